# Optimizing a Trainium2 kernel written in Bass

```python
import math
import jax
import jax.numpy as jnp
from jax import lax
import numpy as np

D_MODEL = 2048
BATCH = 4
SEQ = 2048
DEPTH = 2

GRID_W = 64
EPS = 1e-6
BRANCH_W = D_MODEL // 2
MOD_SCALE = 0.5
ATT_HEAD_DIM = 128
ATT_HEADS = BRANCH_W // ATT_HEAD_DIM
ATT_KV_HEADS = ATT_HEADS // 4
Q_BLOCK = 128
ROPE_THETA = 10000.0
SSD_HEAD_DIM = 64
SSD_HEADS = BRANCH_W // SSD_HEAD_DIM
SSD_GROUPS = 2
SSD_STATE = 128
SSD_CONV = 4
SSD_CHUNK = 128
GDN_HEAD_DIM = 128
GDN_V_HEADS = BRANCH_W // GDN_HEAD_DIM
GDN_QK_HEADS = GDN_V_HEADS // 2
GDN_CONV = 4
GDN_CHUNK = 64
LRU_WIDTH = BRANCH_W
LRU_BLOCKS = 8
LRU_CONV = 4
LRU_C = 8.0

AB_SPLITS = (ATT_HEADS * ATT_HEAD_DIM, ATT_KV_HEADS * ATT_HEAD_DIM, ATT_KV_HEADS * ATT_HEAD_DIM, BRANCH_W,
             BRANCH_W, SSD_GROUPS * SSD_STATE, SSD_GROUPS * SSD_STATE, SSD_HEADS, SSD_HEADS, BRANCH_W)
CD_SPLITS = (GDN_QK_HEADS * GDN_HEAD_DIM, GDN_QK_HEADS * GDN_HEAD_DIM, BRANCH_W, GDN_V_HEADS, GDN_V_HEADS,
             GDN_V_HEADS, GDN_V_HEADS, BRANCH_W, LRU_WIDTH, LRU_WIDTH)
AB_IN = sum(AB_SPLITS)
CD_IN = sum(CD_SPLITS)

kernel_name = "hybrid_bidir_attn_ssd_deltanet_rglru"


def _rmsnorm(x, w):
    xf = x.astype(jnp.float32)
    y = xf * lax.rsqrt(jnp.mean(xf * xf, axis=-1, keepdims=True) + EPS)
    return (y * w.astype(jnp.float32)).astype(x.dtype)


def _l2norm(x):
    return x * lax.rsqrt(jnp.sum(x * x, axis=-1, keepdims=True) + EPS)


def _split(h, sizes):
    outs, start = [], 0
    for s in sizes:
        outs.append(h[..., start:start + s])
        start += s
    return outs


def _flip(t):
    return jnp.flip(t, axis=1)


def _centred_dwconv(x, w, b):
    K, C = w.shape
    y = lax.conv_general_dilated(x, w[:, None, :].astype(x.dtype), window_strides=(1,),
                                 padding=[(K // 2, K - 1 - K // 2)],
                                 dimension_numbers=('NWC', 'WIO', 'NWC'), feature_group_count=C)
    return y + b


def _axial_rope(seq_len):
    rows = seq_len // GRID_W
    t = jnp.arange(rows * GRID_W, dtype=jnp.int32)
    row = (t // GRID_W).astype(jnp.float32)
    col = (t % GRID_W).astype(jnp.float32)
    n_pairs = ATT_HEAD_DIM // 4
    freqs = ROPE_THETA ** (-jnp.arange(n_pairs, dtype=jnp.float32) / n_pairs)
    ang = jnp.concatenate([row[:, None] * freqs, col[:, None] * freqs], axis=-1)
    return jnp.cos(ang), jnp.sin(ang)


def _apply_rope(x, cos, sin):
    xf = x.astype(jnp.float32).reshape(*x.shape[:-1], x.shape[-1] // 2, 2)
    x0, x1 = xf[..., 0], xf[..., 1]
    c, s = cos[None, :, None, :], sin[None, :, None, :]
    out = jnp.stack([x0 * c - x1 * s, x0 * s + x1 * c], axis=-1)
    return out.reshape(x.shape).astype(x.dtype)


def _blocked_gqa(q, k, v):
    Bsz, S, H, Dh = q.shape
    Hkv = k.shape[2]
    G = H // Hkv
    nblk = S // Q_BLOCK
    qb = q.reshape(Bsz, nblk, Q_BLOCK, Hkv, G, Dh).transpose(1, 0, 2, 3, 4, 5)
    scale = Dh ** -0.5

    def block(qi):
        s = jnp.einsum('bqkgd,bskd->bkgqs', qi, k, preferred_element_type=jnp.float32) * scale
        p = jax.nn.softmax(s, axis=-1).astype(v.dtype)
        return jnp.einsum('bkgqs,bskd->bqkgd', p, v)

    o = lax.map(block, qb)
    return o.transpose(1, 0, 2, 3, 4, 5).reshape(Bsz, S, H * Dh)


def _ssd_scan(x, dt, A, Bm, Cm):
    Bsz, S, H, P = x.shape
    N = Bm.shape[-1]
    L = SSD_CHUNK
    nc = S // L
    xd = (x * dt[..., None]).reshape(Bsz, nc, L, H, P)
    a = (dt * A).reshape(Bsz, nc, L, H).transpose(0, 3, 1, 2)
    Bc = Bm.reshape(Bsz, nc, L, H, N)
    Cc = Cm.reshape(Bsz, nc, L, H, N)
    a_cs = jnp.cumsum(a, axis=-1)
    incl = jnp.tril(jnp.ones((L, L), dtype=bool))
    Lmat = jnp.exp(jnp.where(incl, a_cs[..., :, None] - a_cs[..., None, :], -jnp.inf))
    y_diag = jnp.einsum('bclhn,bcshn,bhcls,bcshp->bclhp', Cc, Bc, Lmat, xd)
    decay_states = jnp.exp(a_cs[..., -1:] - a_cs)
    states = jnp.einsum('bclhn,bhcl,bclhp->bchpn', Bc, decay_states, xd)
    chunk_decay = jnp.exp(a_cs[..., -1])

    def step(prev, inp):
        st, dec = inp
        return prev * dec[..., None, None] + st, prev

    init = jnp.zeros((Bsz, H, P, N), x.dtype)
    _, prev_states = lax.scan(step, init, (jnp.moveaxis(states, 1, 0), jnp.moveaxis(chunk_decay, 2, 0)))
    prev_states = jnp.moveaxis(prev_states, 0, 1)
    y_off = jnp.einsum('bclhn,bchpn,bhcl->bclhp', Cc, prev_states, jnp.exp(a_cs))
    return (y_diag + y_off).reshape(Bsz, S, H, P)


def _bidir_ssd(xs, bs, cs, dtf, dtb, z, dt_bias_f, dt_bias_b, a_log_f, a_log_b, d_skip, norm_w):
    f32 = jnp.float32
    Bsz, S, _ = xs.shape
    rep = SSD_HEADS // SSD_GROUPS
    xh = xs.astype(f32).reshape(Bsz, S, SSD_HEADS, SSD_HEAD_DIM)
    bh = jnp.repeat(bs.astype(f32).reshape(Bsz, S, SSD_GROUPS, SSD_STATE), rep, axis=2)
    ch = jnp.repeat(cs.astype(f32).reshape(Bsz, S, SSD_GROUPS, SSD_STATE), rep, axis=2)
    dt_f = jax.nn.softplus(dtf.astype(f32) + dt_bias_f.astype(f32))
    dt_b = jax.nn.softplus(dtb.astype(f32) + dt_bias_b.astype(f32))
    y_f = _ssd_scan(xh, dt_f, -jnp.exp(a_log_f.astype(f32)), bh, ch)
    y_b = _flip(_ssd_scan(_flip(xh), _flip(dt_b), -jnp.exp(a_log_b.astype(f32)), _flip(bh), _flip(ch)))
    y = (y_f + y_b + d_skip.astype(f32)[:, None] * xh).reshape(Bsz, S, BRANCH_W)
    return _rmsnorm(y * jax.nn.silu(z.astype(f32)), norm_w)


def _gated_delta_chunked(q, k, v, g, beta):
    Bsz, S, H, Dk = q.shape
    Dv = v.shape[-1]
    L = GDN_CHUNK
    nc = S // L

    def chunks(t):
        return t.reshape(Bsz, nc, L, H, t.shape[-1]).transpose(0, 3, 1, 2, 4)

    qc, kc, vc = chunks(q), chunks(k), chunks(v)
    gc = g.reshape(Bsz, nc, L, H).transpose(0, 3, 1, 2)
    bc = beta.reshape(Bsz, nc, L, H).transpose(0, 3, 1, 2)
    G = jnp.cumsum(gc, axis=-1)
    incl = jnp.tril(jnp.ones((L, L), dtype=bool))
    decay = jnp.exp(jnp.where(incl, G[..., :, None] - G[..., None, :], -jnp.inf))
    k_beta = kc * bc[..., None]
    strict = jnp.tril(jnp.ones((L, L), dtype=q.dtype), -1)
    lower = jnp.einsum('bhcid,bhcjd->bhcij', k_beta, kc) * decay * strict
    eye = jnp.eye(L, dtype=q.dtype)
    T = lax.linalg.triangular_solve(lower + eye, jnp.broadcast_to(eye, lower.shape),
                                    left_side=True, lower=True, unit_diagonal=True)
    u = jnp.einsum('bhcij,bhcjv->bhciv', T, vc * bc[..., None])
    w = jnp.einsum('bhcij,bhcjd->bhcid', T, k_beta * jnp.exp(G)[..., None])
    attn = jnp.einsum('bhcid,bhcjd->bhcij', qc, kc) * decay
    q_dec = qc * jnp.exp(G)[..., None]
    k_dec = kc * jnp.exp(G[..., -1:] - G)[..., None]
    chunk_dec = jnp.exp(G[..., -1])

    def step(state, inp):
        u_c, w_c, a_c, qd_c, kd_c, d_c = inp
        v_new = u_c - jnp.einsum('bhld,bhdv->bhlv', w_c, state)
        o_c = jnp.einsum('bhld,bhdv->bhlv', qd_c, state) + jnp.einsum('bhls,bhsv->bhlv', a_c, v_new)
        state = state * d_c[..., None, None] + jnp.einsum('bhld,bhlv->bhdv', kd_c, v_new)
        return state, o_c

    init = jnp.zeros((Bsz, H, Dk, Dv), q.dtype)
    xs = (jnp.moveaxis(u, 2, 0), jnp.moveaxis(w, 2, 0), jnp.moveaxis(attn, 2, 0),
          jnp.moveaxis(q_dec, 2, 0), jnp.moveaxis(k_dec, 2, 0), jnp.moveaxis(chunk_dec, 2, 0))
    _, o = lax.scan(step, init, xs)
    return o.transpose(1, 0, 3, 2, 4).reshape(Bsz, S, H, Dv)


def _bidir_gated_delta(q, k, v, bf, bb, af, ab, z, a_log_f, a_log_b, dt_bias_f, dt_bias_b, norm_w):
    f32 = jnp.float32
    Bsz, S, _ = q.shape
    rep = GDN_V_HEADS // GDN_QK_HEADS
    q = _l2norm(q.astype(f32).reshape(Bsz, S, GDN_QK_HEADS, GDN_HEAD_DIM)) * GDN_HEAD_DIM ** -0.5
    k = _l2norm(k.astype(f32).reshape(Bsz, S, GDN_QK_HEADS, GDN_HEAD_DIM))
    q = jnp.repeat(q, rep, axis=2)
    k = jnp.repeat(k, rep, axis=2)
    v = v.astype(f32).reshape(Bsz, S, GDN_V_HEADS, GDN_HEAD_DIM)
    g_f = -jnp.exp(a_log_f.astype(f32)) * jax.nn.softplus(af.astype(f32) + dt_bias_f.astype(f32))
    g_b = -jnp.exp(a_log_b.astype(f32)) * jax.nn.softplus(ab.astype(f32) + dt_bias_b.astype(f32))
    beta_f = jax.nn.sigmoid(bf.astype(f32))
    beta_b = jax.nn.sigmoid(bb.astype(f32))
    o = _gated_delta_chunked(q, k, v, g_f, beta_f) + _flip(
        _gated_delta_chunked(_flip(q), _flip(k), _flip(v), _flip(g_b), _flip(beta_b)))
    zz = z.astype(f32).reshape(Bsz, S, GDN_V_HEADS, GDN_HEAD_DIM)
    o = _rmsnorm(o, norm_w) * jax.nn.silu(zz)
    return o.reshape(Bsz, S, BRANCH_W)


def _linear_combine(e1, e2):
    a1, b1 = e1
    a2, b2 = e2
    return a1 * a2, a2 * b1 + b2


def _rglru(xc, w_a, b_a, w_x, b_x, lam, reverse):
    f32 = jnp.float32
    Bsz, S, W = xc.shape
    xb = xc.reshape(Bsz, S, LRU_BLOCKS, W // LRU_BLOCKS)
    r = jax.nn.sigmoid(jnp.einsum('bsni,nij->bsnj', xb, w_a.astype(f32)).reshape(Bsz, S, W) + b_a.astype(f32))
    i = jax.nn.sigmoid(jnp.einsum('bsni,nij->bsnj', xb, w_x.astype(f32)).reshape(Bsz, S, W) + b_x.astype(f32))
    log_a = -LRU_C * r * jax.nn.softplus(-lam.astype(f32))
    a = jnp.exp(log_a)
    u = jnp.sqrt(-jnp.expm1(2.0 * log_a)) * (i * xc)
    _, h = lax.associative_scan(_linear_combine, (a, u), reverse=reverse, axis=1)
    return h


def _ab_mixer(h, w_in, q_norm, k_norm, conv_w, conv_b, dt_bias_f, dt_bias_b, a_log_f, a_log_b,
              d_skip, ssd_norm, w_out):
    Bsz, S, _ = h.shape
    proj = jnp.einsum('bsd,de->bse', h, w_in)
    q, k, v, ga, xs, bs, cs, dtf, dtb, z = _split(proj, AB_SPLITS)
    q = _rmsnorm(q.reshape(Bsz, S, ATT_HEADS, ATT_HEAD_DIM), q_norm)
    k = _rmsnorm(k.reshape(Bsz, S, ATT_KV_HEADS, ATT_HEAD_DIM), k_norm)
    v = v.reshape(Bsz, S, ATT_KV_HEADS, ATT_HEAD_DIM)
    cos, sin = _axial_rope(S)
    att = _blocked_gqa(_apply_rope(q, cos, sin), _apply_rope(k, cos, sin), v) * jax.nn.silu(ga)
    xbc = jax.nn.silu(_centred_dwconv(jnp.concatenate([xs, bs, cs], axis=-1), conv_w, conv_b))
    xs, bs, cs = _split(xbc, (BRANCH_W, SSD_GROUPS * SSD_STATE, SSD_GROUPS * SSD_STATE))
    ssd = _bidir_ssd(xs, bs, cs, dtf, dtb, z, dt_bias_f, dt_bias_b, a_log_f, a_log_b, d_skip, ssd_norm)
    y = jnp.concatenate([att.astype(h.dtype), ssd.astype(h.dtype)], axis=-1)
    return jnp.einsum('bse,ed->bsd', y, w_out)


def _cd_mixer(h, w_in, conv_w, conv_b, a_log_f, a_log_b, dt_bias_f, dt_bias_b, gdn_norm,
              lru_conv_w, lru_conv_b, wa_f, ba_f, wx_f, bx_f, lam_f, wa_b, ba_b, wx_b, bx_b, lam_b, w_out):
    proj = jnp.einsum('bsd,de->bse', h, w_in)
    q, k, v, bf, bb, af, ab, z, xl, gl = _split(proj, CD_SPLITS)
    qkv = jax.nn.silu(_centred_dwconv(jnp.concatenate([q, k, v], axis=-1), conv_w, conv_b))
    q, k, v = _split(qkv, (GDN_QK_HEADS * GDN_HEAD_DIM, GDN_QK_HEADS * GDN_HEAD_DIM, BRANCH_W))
    gdn = _bidir_gated_delta(q, k, v, bf, bb, af, ab, z, a_log_f, a_log_b, dt_bias_f, dt_bias_b, gdn_norm)
    xc = _centred_dwconv(xl, lru_conv_w, lru_conv_b).astype(jnp.float32)
    hs = _rglru(xc, wa_f, ba_f, wx_f, bx_f, lam_f, False) + _rglru(xc, wa_b, ba_b, wx_b, bx_b, lam_b, True)
    lru = hs * jax.nn.silu(gl.astype(jnp.float32))
    y = jnp.concatenate([gdn.astype(h.dtype), lru.astype(h.dtype)], axis=-1)
    return jnp.einsum('bse,ed->bsd', y, w_out)


def setup_inputs(seed: int = 0) -> dict:
    key = jax.random.key(seed)
    ks = iter(jax.random.split(key, 48))
    ne, no = (DEPTH + 1) // 2, DEPTH // 2
    f32 = jnp.float32
    d = D_MODEL

    def nrm(shape, scale):
        return jax.random.normal(next(ks), shape, f32) * scale

    def gain(shape):
        return 1.0 + nrm(shape, 0.02)

    def dt_bias(shape):
        dt = jnp.exp(jax.random.uniform(next(ks), shape, f32, math.log(1e-3), math.log(1e-1)))
        return dt + jnp.log(-jnp.expm1(-dt))

    def a_log(shape):
        return jnp.log(jax.random.uniform(next(ks), shape, f32, 1.0, 16.0))

    def lru_lambda(shape):
        a_c = jax.random.uniform(next(ks), shape, f32, 0.9, 0.999)
        s = a_c ** (1.0 / LRU_C)
        return jnp.log(s) - jnp.log1p(-s)

    ssd_ch = BRANCH_W + 2 * SSD_GROUPS * SSD_STATE
    gdn_ch = 2 * GDN_QK_HEADS * GDN_HEAD_DIM + BRANCH_W
    bw = LRU_WIDTH // LRU_BLOCKS
    return {
        "x": nrm((BATCH, SEQ, d), 1.0),
        "c": nrm((BATCH, d), 1.0),
        "w_mod": nrm((DEPTH, d, 3 * d), MOD_SCALE * d ** -0.5),
        "b_mod": nrm((DEPTH, 3 * d), 0.01),
        "norm_w": gain((DEPTH, d)),
        "ab_w_in": nrm((ne, d, AB_IN), d ** -0.5),
        "ab_q_norm": gain((ne, ATT_HEAD_DIM)),
        "ab_k_norm": gain((ne, ATT_HEAD_DIM)),
        "ab_conv_w": nrm((ne, SSD_CONV, ssd_ch), SSD_CONV ** -0.5),
        "ab_conv_b": nrm((ne, ssd_ch), 0.01),
        "ab_dt_bias_f": dt_bias((ne, SSD_HEADS)),
        "ab_dt_bias_b": dt_bias((ne, SSD_HEADS)),
        "ab_a_log_f": a_log((ne, SSD_HEADS)),
        "ab_a_log_b": a_log((ne, SSD_HEADS)),
        "ab_d_skip": gain((ne, SSD_HEADS)),
        "ab_ssd_norm": gain((ne, BRANCH_W)),
        "ab_w_out": nrm((ne, 2 * BRANCH_W, d), (2 * BRANCH_W) ** -0.5),
        "cd_w_in": nrm((no, d, CD_IN), d ** -0.5),
        "cd_conv_w": nrm((no, GDN_CONV, gdn_ch), GDN_CONV ** -0.5),
        "cd_conv_b": nrm((no, gdn_ch), 0.01),
        "cd_a_log_f": a_log((no, GDN_V_HEADS)),
        "cd_a_log_b": a_log((no, GDN_V_HEADS)),
        "cd_dt_bias_f": dt_bias((no, GDN_V_HEADS)),
        "cd_dt_bias_b": dt_bias((no, GDN_V_HEADS)),
        "cd_gdn_norm": gain((no, GDN_HEAD_DIM)),
        "cd_lru_conv_w": nrm((no, LRU_CONV, LRU_WIDTH), LRU_CONV ** -0.5),
        "cd_lru_conv_b": nrm((no, LRU_WIDTH), 0.01),
        "cd_lru_wa_f": nrm((no, LRU_BLOCKS, bw, bw), bw ** -0.5),
        "cd_lru_ba_f": nrm((no, LRU_WIDTH), 0.01),
        "cd_lru_wx_f": nrm((no, LRU_BLOCKS, bw, bw), bw ** -0.5),
        "cd_lru_bx_f": nrm((no, LRU_WIDTH), 0.01),
        "cd_lru_lam_f": lru_lambda((no, LRU_WIDTH)),
        "cd_lru_wa_b": nrm((no, LRU_BLOCKS, bw, bw), bw ** -0.5),
        "cd_lru_ba_b": nrm((no, LRU_WIDTH), 0.01),
        "cd_lru_wx_b": nrm((no, LRU_BLOCKS, bw, bw), bw ** -0.5),
        "cd_lru_bx_b": nrm((no, LRU_WIDTH), 0.01),
        "cd_lru_lam_b": lru_lambda((no, LRU_WIDTH)),
        "cd_w_out": nrm((no, 2 * BRANCH_W, d), (2 * BRANCH_W) ** -0.5),
        "final_norm_w": gain((d,)),
    }


def reference(x, c, w_mod, b_mod, norm_w,
              ab_w_in, ab_q_norm, ab_k_norm, ab_conv_w, ab_conv_b, ab_dt_bias_f, ab_dt_bias_b,
              ab_a_log_f, ab_a_log_b, ab_d_skip, ab_ssd_norm, ab_w_out,
              cd_w_in, cd_conv_w, cd_conv_b, cd_a_log_f, cd_a_log_b, cd_dt_bias_f, cd_dt_bias_b,
              cd_gdn_norm, cd_lru_conv_w, cd_lru_conv_b, cd_lru_wa_f, cd_lru_ba_f, cd_lru_wx_f,
              cd_lru_bx_f, cd_lru_lam_f, cd_lru_wa_b, cd_lru_ba_b, cd_lru_wx_b, cd_lru_bx_b,
              cd_lru_lam_b, cd_w_out, final_norm_w):
    d = x.shape[-1]
    cond = jax.nn.silu(c)
    for layer in range(DEPTH):
        mod = jnp.einsum('bd,de->be', cond, w_mod[layer]) + b_mod[layer]
        shift, scale, gate = mod[:, :d], mod[:, d:2 * d], mod[:, 2 * d:]
        h = _rmsnorm(x, norm_w[layer]) * (1.0 + scale[:, None, :]) + shift[:, None, :]
        j = layer // 2
        if layer % 2 == 0:
            y = _ab_mixer(h, ab_w_in[j], ab_q_norm[j], ab_k_norm[j], ab_conv_w[j], ab_conv_b[j],
                          ab_dt_bias_f[j], ab_dt_bias_b[j], ab_a_log_f[j], ab_a_log_b[j],
                          ab_d_skip[j], ab_ssd_norm[j], ab_w_out[j])
        else:
            y = _cd_mixer(h, cd_w_in[j], cd_conv_w[j], cd_conv_b[j], cd_a_log_f[j], cd_a_log_b[j],
                          cd_dt_bias_f[j], cd_dt_bias_b[j], cd_gdn_norm[j], cd_lru_conv_w[j],
                          cd_lru_conv_b[j], cd_lru_wa_f[j], cd_lru_ba_f[j], cd_lru_wx_f[j],
                          cd_lru_bx_f[j], cd_lru_lam_f[j], cd_lru_wa_b[j], cd_lru_ba_b[j],
                          cd_lru_wx_b[j], cd_lru_bx_b[j], cd_lru_lam_b[j], cd_w_out[j])
        x = x + gate[:, None, :] * y.astype(x.dtype)
    return _rmsnorm(x, final_norm_w)
```

```python
import math
import numpy as np
from contextlib import ExitStack, contextmanager
import concourse.bass as bass
import concourse.mybir as mybir
from concourse.bass_utils import run_bass_kernel_spmd

F32 = mybir.dt.float32
BF16 = mybir.dt.bfloat16
AF = mybir.ActivationFunctionType
ALU = mybir.AluOpType

D = 2048
S = 2048
EPS = 1e-6
NEG = -30000.0


class Tn:
    def __init__(self, h, name, psum=False):
        self.h = h
        self.name = name
        self.st = {}
        self.psum = psum

    def __getitem__(self, idx):
        return self.h[idx]


class Prog:
    NDMA = {"sp": 14, "pool": 8}

    def __init__(self, nc, es):
        self.nc = nc
        self.stack = [es]
        self.h = {"pe": nc.tensor, "act": nc.scalar, "dve": nc.vector, "pool": nc.gpsimd, "sp": nc.sync}
        self.sem = {e: es.enter_context(nc.semaphore("s_" + e)) for e in ("pe", "act", "dve", "pool")}
        self.cnt = {e: 0 for e in self.sem}
        self.dsem = {q: [es.enter_context(nc.semaphore(f"d_{q}{i}")) for i in range(n)] for q, n in self.NDMA.items()}
        self.dcnt = {q: 0 for q in self.NDMA}
        self.waited = {e: {} for e in self.h}
        self.semobj = {}
        for s in self.sem.values():
            self.semobj[id(s)] = s
        for l in self.dsem.values():
            for s in l:
                self.semobj[id(s)] = s
        self.uid = 0

    def sb(self, name, shape, dt=F32):
        self.uid += 1
        name = f"{name}_{self.uid}"
        return Tn(self.stack[-1].enter_context(self.nc.sbuf_tensor(name, list(shape), dt)), name)

    def ps(self, name):
        self.uid += 1
        name = f"{name}_{self.uid}"
        return Tn(self.stack[-1].enter_context(self.nc.psum_tensor(name, [128, 512], F32)), name, psum=True)

    def dram(self, name, shape, dt=F32, kind="Internal"):
        return Tn(self.nc.dram_tensor(name, list(shape), dt, kind=kind), name)

    @contextmanager
    def scope(self):
        es = ExitStack()
        self.stack.append(es)
        try:
            yield
        finally:
            self.barrier()
            self.stack.pop()
            es.close()

    def barrier(self):
        toks = [(self.sem[e], self.cnt[e]) for e in self.sem if self.cnt[e] > 0]
        for q, lst in self.dsem.items():
            n = self.dcnt[q]
            for i, s in enumerate(lst):
                uses = (n - i + len(lst) - 1) // len(lst) if n > i else 0
                if uses > 0:
                    toks.append((s, 16 * uses))
        for e, h in self.h.items():
            for s, v in toks:
                if e in self.sem and self.sem[e] is s:
                    continue
                if self.waited[e].get(id(s), 0) >= v:
                    continue
                self.waited[e][id(s)] = v
                h.wait_ge(s, v)

    @staticmethod
    def _norm(x):
        if isinstance(x, tuple):
            return (x[0], None) if x[0].psum else x
        return (x, None)

    def _states(self, t, sub):
        if sub is None:
            return list(t.st.values())
        out = []
        if None in t.st:
            out.append(t.st[None])
        if sub in t.st:
            out.append(t.st[sub])
        return out

    def _deps(self, eng, r, w):
        need = {}

        def add(tok, kind):
            if tok is None:
                return
            sid, val, src = tok
            if src == eng:
                if eng in ("pe", "sp"):
                    return
            if self.waited[eng].get(sid, 0) >= val:
                return
            if need.get(sid, 0) < val:
                need[sid] = val

        for x in r:
            t, sub = self._norm(x)
            for s in self._states(t, sub):
                add(s[0], "raw")
        for x in w:
            t, sub = self._norm(x)
            for s in self._states(t, sub):
                add(s[0], "waw")
                for tok in s[1].values():
                    add(tok, "war")
        for sid, val in need.items():
            self.waited[eng][sid] = val
        return [(self.semobj[sid], val) for sid, val in need.items()]

    def _commit(self, who, tok, r, w):
        for x in r:
            t, sub = self._norm(x)
            if sub is None:
                if None not in t.st:
                    t.st[None] = [None, {}]
                for s in t.st.values():
                    s[1][who] = tok
            else:
                if sub not in t.st:
                    t.st[sub] = [None, {}]
                t.st[sub][1][who] = tok
        for x in w:
            t, sub = self._norm(x)
            if sub is None:
                t.st = {None: [tok, {}]}
            else:
                t.st[sub] = [tok, {}]

    def op(self, eng, fn, r=(), w=()):
        w = list(w) + [x for x in r if self._norm(x)[0].psum]
        waits = self._deps(eng, r, w)
        self.cnt[eng] += 1
        s = self.sem[eng]
        tok = (id(s), self.cnt[eng], eng)
        self._commit(eng, tok, r, w)
        h = self.h[eng]
        for ss, v in waits:
            h.wait_ge(ss, v)
        fn(h).then_inc(s, 1)

    def dma(self, q, out, in_, r=(), w=()):
        n = self.dcnt[q]
        self.dcnt[q] += 1
        pool = self.dsem[q]
        s = pool[n % len(pool)]
        use = n // len(pool)
        waits = self._deps(q, r, w)
        if use > 0 and self.waited[q].get(id(s), 0) < 16 * use:
            waits.append((s, 16 * use))
            self.waited[q][id(s)] = 16 * use
        tok = (id(s), 16 * (use + 1), "dma_" + q)
        self._commit(f"dma_{q}{n % len(pool)}", tok, r, w)
        h = self.h[q]
        for ss, v in waits:
            h.wait_ge(ss, v)
        h.dma_start(out=out, in_=in_).then_inc(s, 16)
        return tok

    def wait_tok(self, eng, tok):
        self.h[eng].wait_ge(self.semobj[tok[0]], tok[1])

    def act(self, out, in_, func, r, w, bias=0.0, scale=1.0, accum_out=None, eng="act"):
        if accum_out is None:
            self.op("act", lambda e: e.activation(out=out, in_=in_, func=func, bias=bias, scale=scale), r=r, w=w)
        else:
            self.op("act", lambda e: e.activation(out=out, in_=in_, func=func, bias=bias, scale=scale, accum_out=accum_out), r=r, w=w)

    def tt(self, eng, out, in0, in1, op, r, w):
        self.op(eng, lambda e: e.tensor_tensor(out=out, in0=in0, in1=in1, op=op), r=r, w=w)

    def ts(self, eng, out, in0, s1, s2, op0, op1, r, w):
        if s2 is None:
            self.op(eng, lambda e: e.tensor_scalar(out=out, in0=in0, scalar1=s1, scalar2=None, op0=op0), r=r, w=w)
        else:
            self.op(eng, lambda e: e.tensor_scalar(out=out, in0=in0, scalar1=s1, scalar2=s2, op0=op0, op1=op1), r=r, w=w)

    def stt(self, eng, out, in0, scalar, in1, op0, op1, r, w):
        self.op(eng, lambda e: e.scalar_tensor_tensor(out=out, in0=in0, scalar=scalar, in1=in1, op0=op0, op1=op1), r=r, w=w)

    def copy(self, eng, out, in_, r, w):
        if eng == "act":
            self.op("act", lambda e: e.copy(out=out, in_=in_), r=r, w=w)
        else:
            self.op(eng, lambda e: e.tensor_copy(out=out, in_=in_), r=r, w=w)

    def mm(self, out, lhsT, rhs, start, stop, r, w):
        self.op("pe", lambda e: e.matmul(out, lhsT=lhsT, rhs=rhs, start=start, stop=stop), r=r, w=w)

    def tr(self, out, in_, ident, r, w):
        self.op("pe", lambda e: e.transpose(out, in_, ident), r=r, w=w)


C_OFF = {}


def _consts():
    i = np.arange(128)
    sI, fI = i[:, None], i[None, :]
    mats = {
        "ident": (sI == fI), "ones": np.ones((128, 128)),
        "Uf": (sI <= fI), "Mf": (sI > fI), "Ub": (sI >= fI), "Mb": (sI < fI),
    }
    R = np.zeros((128, 128))
    for p in range(64):
        R[2 * p, 2 * p + 1] = -1.0
        R[2 * p + 1, 2 * p] = 1.0
    mats["Rt"] = R.T
    negs = {
        "NEGf": NEG * (fI < sI), "NEGb": NEG * (fI > sI),
        "NEGsf": NEG * (fI >= sI), "NEGsb": NEG * (fI <= sI),
    }
    cols = []
    off = 0
    for k, m in mats.items():
        C_OFF[k] = off
        cols.append(np.asarray(m, np.float32))
        off += 128
    for k, m in negs.items():
        C_OFF[k] = off
        cols.append(np.tile(np.asarray(m, np.float32), (1, 4)))
        off += 512
    return np.ascontiguousarray(np.concatenate(cols, axis=1)), off


CONSTS, NCONST = _consts()


def _rope_tables():
    t = np.arange(S)
    row = (t // 64).astype(np.float32)
    col = (t % 64).astype(np.float32)
    n_pairs = 32
    freqs = (np.float32(10000.0) ** (-np.arange(n_pairs, dtype=np.float32) / np.float32(n_pairs))).astype(np.float32)
    ang = np.concatenate([row[:, None] * freqs, col[:, None] * freqs], axis=-1).astype(np.float32)
    cos = np.cos(ang).astype(np.float32)
    sin = np.sin(ang).astype(np.float32)
    cosT = np.repeat(cos, 2, axis=1).T
    sinT = np.repeat(sin, 2, axis=1).T
    return np.ascontiguousarray(np.stack([cosT, sinT], 0))


PV = {}
RV = {}


def _pcols(v, n):
    return np.asarray(v, np.float32).reshape(n, 128).T


def pack_params(inp):
    pv, rv = [], []

    def addp(name, arr):
        PV[name] = (sum(a.shape[1] for a in pv), arr.shape[1])
        pv.append(np.asarray(arr, np.float32))

    def addr(name, vec):
        vec = np.asarray(vec, np.float32).reshape(-1)
        RV[name] = (sum(a.shape[1] for a in rv), vec.shape[0])
        rv.append(np.broadcast_to(vec[None, :], (128, vec.shape[0])))

    addp("q_norm", _pcols(inp["ab_q_norm"][0], 1))
    addp("k_norm", _pcols(inp["ab_k_norm"][0], 1))
    cw = inp["ab_conv_w"][0]
    addp("ab_cw", np.concatenate([_pcols(cw[j], 12)[:, :, None] for j in range(4)], 2).reshape(128, 48))
    addp("ab_cb", _pcols(inp["ab_conv_b"][0], 12))
    addp("d_skip", _pcols(np.repeat(inp["ab_d_skip"][0], 64), 8))
    addp("ssd_norm", _pcols(inp["ab_ssd_norm"][0], 8))
    cw = inp["cd_conv_w"][0]
    addp("cd_cw", np.concatenate([_pcols(cw[j], 16)[:, :, None] for j in range(4)], 2).reshape(128, 64))
    addp("cd_cb", _pcols(inp["cd_conv_b"][0], 16))
    addp("gdn_norm", _pcols(inp["cd_gdn_norm"][0], 1))
    cw = inp["cd_lru_conv_w"][0]
    addp("lru_cw", np.concatenate([_pcols(cw[j], 8)[:, :, None] for j in range(4)], 2).reshape(128, 32))
    addp("lru_cb", _pcols(inp["cd_lru_conv_b"][0], 8))
    for d_ in ("f", "b"):
        addp("ba_" + d_, _pcols(inp["cd_lru_ba_" + d_][0], 8))
        addp("bx_" + d_, _pcols(inp["cd_lru_bx_" + d_][0], 8))
        addp("lam_" + d_, _pcols(inp["cd_lru_lam_" + d_][0], 8))
    addp("fnorm", _pcols(inp["final_norm_w"], 16))
    for l in range(2):
        addp(f"norm{l}", _pcols(inp["norm_w"][l], 16))
        addp(f"bmod{l}", _pcols(inp["b_mod"][l], 48))
    addr("ab_dtb", np.concatenate([inp["ab_dt_bias_f"][0], inp["ab_dt_bias_b"][0]]))
    addr("ab_alog", np.concatenate([inp["ab_a_log_f"][0], inp["ab_a_log_b"][0]]))
    addr("cd_dtb", np.concatenate([inp["cd_dt_bias_f"][0], inp["cd_dt_bias_b"][0]]))
    addr("cd_alog", np.concatenate([inp["cd_a_log_f"][0], inp["cd_a_log_b"][0]]))
    return (np.ascontiguousarray(np.concatenate(pv, 1)), np.ascontiguousarray(np.concatenate(rv, 1)))


def build(npv, nrv, upto="all", debug=False, only=None, lim=None):
    nc = bass.Bass("TRN2", target_bir_lowering=False)
    es = ExitStack()
    dbg = {}
    with es:
        P = Prog(nc, es)
        skind = "ExternalOutput" if debug else "Internal"

        def din(name, shape, dt=F32):
            return P.dram(name, shape, dt, kind="ExternalInput")

        xT_d = din("xT", [D, S])
        cT_d = din("cT", [128, 16])
        wmod_d = din("w_mod", [2, D, 3 * D])
        w_in_d = [din("ab_w_in", [D, 5152]), din("cd_w_in", [D, 5152])]
        w_out_d = [din("ab_w_out", [D, D]), din("cd_w_out", [D, D])]
        lruw_d = din("lru_w", [4, 8, 128, 128])
        consts_d = din("consts", [128, NCONST])
        rope_d = din("rope", [2, 128, S])
        pv_d = din("pvec", [128, npv])
        rv_d = din("rvec", [128, nrv])
        out_d = P.dram("outT", [D, S], F32, kind="ExternalOutput")

        def scratch(name, shape, dt=F32):
            t = P.dram(name, shape, dt, kind=skind)
            dbg[name] = t
            return t

        if only is None:
            projT_d = scratch("projT", [5248, S])
            tokm_d = scratch("tokm", [128, 16, 320])
        else:
            projT_d = din("projT", [5248, S])
            tokm_d = din("tokm", [128, 16, 320])
        yT_d = scratch("yT", [D, S], BF16)
        x1T_d = scratch("x1T", [D, S])
        x2T_d = scratch("x2T", [D, S])

        cst = P.sb("cst", [128, NCONST])
        P.dma("sp", cst[:, 0:1536], consts_d[:, 0:1536], w=[cst])
        P.dma("sp", cst[:, 1536:NCONST], consts_d[:, 1536:NCONST], w=[cst])
        pvt = P.sb("pvt", [128, npv])
        P.dma("sp", pvt[:], pv_d[:], w=[pvt])
        rvt = P.sb("rvt", [128, nrv])
        P.dma("sp", rvt[:], rv_d[:], w=[rvt])
        onesb = P.sb("onesb", [128, 128], BF16)
        P.copy("dve", onesb[:], cst[:, C_OFF["ones"]:C_OFF["ones"] + 128], r=[cst], w=[onesb])
        identb = P.sb("identb", [128, 128], BF16)
        P.copy("dve", identb[:], cst[:, C_OFF["ident"]:C_OFF["ident"] + 128], r=[cst], w=[identb])

        def C(name, n=128):
            return cst[:, C_OFF[name]:C_OFF[name] + n]

        def PVc(name, j=0, n=1):
            o, _ = PV[name]
            return pvt[:, o + j:o + j + n]

        def RVc(name, j=0, n=1):
            o, _ = RV[name]
            return rvt[:, o + j:o + j + n]

        modsb = P.sb("modsb", [128, 2, 48])
        sc1 = P.sb("sc1", [128, 2, 16])

        def stage_mod():
            with P.scope():
                cond = P.sb("cond", [128, 16])
                P.dma("sp", cond[:], cT_d[:], w=[cond])
                P.act(cond[:], cond[:], AF.Silu, r=[cond], w=[cond])
                pm = [P.ps(f"pm{i}") for i in range(4)]
                wst = [P.sb(f"wst{i}", [128, 3 * D]) for i in range(2)]
                red = P.sb("mred", [128, 2, 48])
                for l in range(2):
                    for k in range(16):
                        t = wst[(l * 16 + k) % 2]
                        P.dma("sp", t[:, 0:3072], wmod_d[l, k * 128:(k + 1) * 128, 0:3072], w=[(t, 0)])
                        P.dma("sp", t[:, 3072:6144], wmod_d[l, k * 128:(k + 1) * 128, 3072:6144], w=[(t, 1)])
                        pt = pm[l * 2 + k // 8]
                        for e in range(48):
                            col = (k % 8) * 48 + e
                            P.mm(pt[:, col:col + 1], lhsT=t[:, e * 128:(e + 1) * 128], rhs=cond[:, k:k + 1],
                                 start=True, stop=True, r=[(t, e // 24), cond], w=[pt])
                for l in range(2):
                    for hf in range(2):
                        pt = pm[l * 2 + hf]
                        P.op("dve", lambda e: e.tensor_reduce(out=red[:, hf, :], in_=pt[:, 0:384].rearrange("p (k e) -> p e k", e=48),
                                                             axis=mybir.AxisListType.X, op=ALU.add), r=[pt], w=[(red, hf)])
                    P.tt("dve", modsb[:, l, :], red[:, 0, :], red[:, 1, :], ALU.add, r=[red], w=[(modsb, l)])
                    P.tt("dve", modsb[:, l, :], modsb[:, l, :], PVc(f"bmod{l}", 0, 48), ALU.add, r=[(modsb, l), pvt], w=[(modsb, l)])
                    P.stt("dve", sc1[:, l, :], modsb[:, l, 16:32], 1.0, PVc(f"norm{l}", 0, 16), ALU.add, ALU.mult,
                          r=[(modsb, l), pvt], w=[(sc1, l)])

        def stage_norm(xin_d, scale_ap, bias_ap, out_tile=None, out_dram=None):
            with P.scope():
                xs = [P.sb(f"xn{i}", [128, 16, 256]) for i in range(2)]
                sq = [P.sb(f"sq{i}", [128, 256]) for i in range(2)]
                rstd = [P.sb(f"rstd{i}", [128, 256]) for i in range(2)]
                pss = [P.ps(f"pss{i}") for i in range(2)]
                ob = [P.sb(f"ob{i}", [128, 16, 256]) for i in range(2)] if out_dram is not None else None
                xv = xin_d.h.ap().rearrange("(k p) t -> p k t", p=128)
                for tb in range(8):
                    x = xs[tb % 2]
                    tsl = slice(tb * 256, (tb + 1) * 256)
                    for hf in range(2):
                        P.dma("sp", x[:, hf * 8:(hf + 1) * 8, :], xv[:, hf * 8:(hf + 1) * 8, tsl], r=[xin_d], w=[(x, hf)])
                    ps_ = pss[tb % 2]
                    for k in range(16):
                        s_ = sq[k % 2]
                        P.act(s_[:], x[:, k, :], AF.Square, r=[(x, k // 8)], w=[s_])
                        P.mm(ps_[:, 0:256], lhsT=C("ones"), rhs=s_[:], start=(k == 0), stop=(k == 15), r=[s_, cst], w=[ps_])
                    rs = rstd[tb % 2]
                    P.act(rs[:], ps_[:, 0:256], AF.Ln, r=[ps_], w=[rs], scale=1.0 / D, bias=EPS)
                    P.act(rs[:], rs[:], AF.Exp, r=[rs], w=[rs], scale=-0.5)
                    for k in range(16):
                        P.tt("dve", x[:, k, :], x[:, k, :], rs[:], ALU.mult, r=[(x, k // 8), rs], w=[(x, k // 8)])
                        if out_tile is not None:
                            P.act(out_tile[:, k, tsl], x[:, k, :], AF.Identity, r=[(x, k // 8), modsb, sc1, pvt], w=[(out_tile, tb)],
                                  scale=scale_ap(k), bias=bias_ap(k))
                        else:
                            o = ob[tb % 2]
                            P.act(o[:, k, :], x[:, k, :], AF.Identity, r=[(x, k // 8), pvt], w=[o], scale=scale_ap(k), bias=bias_ap(k))
                    if out_dram is not None:
                        ov = out_dram.h.ap().rearrange("(k p) t -> p k t", p=128)
                        for hf in range(2):
                            P.dma("pool", ov[:, hf * 8:(hf + 1) * 8, tsl], ob[tb % 2][:, hf * 8:(hf + 1) * 8, :], r=[ob[tb % 2]], w=[(out_dram, tb)])

        def stage_inproj(hT, w_d, fm_chunks, tm_specs):
            with P.scope():
                wf = [P.sb(f"wf{i}", [128, 16, 256]) for i in range(2)]
                wb = [P.sb(f"wb{i}", [128, 16, 256], BF16) for i in range(2)]
                ot = [P.sb(f"ot{i}", [128, S]) for i in range(2)]
                otm = P.sb("otm", [128, 16, 256])
                pp = [P.ps(f"pp{i}") for i in range(3)]
                wv = w_d.h.ap().rearrange("(k p) c -> p k c", p=128)
                groups = []
                i = 0
                while i < len(fm_chunks):
                    g = [fm_chunks[i]]
                    if i + 1 < len(fm_chunks) and fm_chunks[i + 1][0] == fm_chunks[i][0] + 128 and fm_chunks[i][1] == 128:
                        g.append(fm_chunks[i + 1])
                        i += 1
                    i += 1
                    groups.append(("fm", g))
                for sp_ in tm_specs:
                    groups.append(("tm", [sp_]))
                npp = 0
                nout = 0
                for gi, (kind, g) in enumerate(groups):
                    c0 = g[0][0]
                    ncol = sum(x[1] for x in g)
                    f_, b_ = wf[gi % 2], wb[gi % 2]
                    for q4 in range(4):
                        P.dma("sp", f_[:, q4 * 4:(q4 + 1) * 4, 0:ncol], wv[:, q4 * 4:(q4 + 1) * 4, c0:c0 + ncol], w=[(f_, q4)])
                    for q4 in range(4):
                        P.copy("dve" if q4 % 2 == 0 else "act", b_[:, q4 * 4:(q4 + 1) * 4, 0:ncol], f_[:, q4 * 4:(q4 + 1) * 4, 0:ncol],
                               r=[(f_, q4)], w=[(b_, q4)])
                    if kind == "fm":
                        for ci, (cc0, cn, row0) in enumerate(g):
                            o_ = ot[nout % 2]
                            nout += 1
                            for tb in range(4):
                                p_ = pp[npp % 3]
                                npp += 1
                                for k in range(16):
                                    P.mm(p_[0:cn, :], lhsT=b_[:, k, ci * 128:ci * 128 + cn], rhs=hT[:, k, tb * 512:(tb + 1) * 512],
                                         start=(k == 0), stop=(k == 15), r=[(b_, k // 4), hT], w=[p_])
                                P.copy("act" if tb % 2 == 0 else "dve", o_[0:cn, tb * 512:(tb + 1) * 512], p_[0:cn, :], r=[p_], w=[(o_, tb)])
                            P.dma("pool", projT_d[row0:row0 + cn, :], o_[0:cn, :], r=[o_], w=[(projT_d, row0 // 128)])
                    else:
                        (cc0, cn, toff) = g[0]
                        o_ = otm
                        ov = o_[:, :, 0:cn]
                        for tb in range(16):
                            p_ = pp[npp % 3]
                            npp += 1
                            for k in range(16):
                                P.mm(p_[:, 0:cn], lhsT=hT[:, k, tb * 128:(tb + 1) * 128], rhs=b_[:, k, 0:cn],
                                     start=(k == 0), stop=(k == 15), r=[(b_, k // 4), hT], w=[p_])
                            P.copy("act" if tb % 2 == 0 else "dve", ov[:, tb, :], p_[:, 0:cn], r=[p_], w=[(o_, tb % 4)])
                        P.dma("pool", tokm_d[:, :, toff:toff + cn], ov, r=[o_], w=[(tokm_d, toff)])

        def stage_outproj(w_d, xin_d, xout_d, l):
            with P.scope():
                yt = P.sb("yt", [128, 16, S], BF16)
                yv = yT_d.h.ap().rearrange("(k p) t -> p k t", p=128)
                for q4 in range(8):
                    P.dma("sp", yt[:, q4 * 2:(q4 + 1) * 2, :], yv[:, q4 * 2:(q4 + 1) * 2, :], r=[yT_d], w=[(yt, q4)])
                wf = [P.sb(f"owf{i}", [128, 16, 128]) for i in range(2)]
                wb = [P.sb(f"owb{i}", [128, 16, 128], BF16) for i in range(2)]
                xo = [P.sb(f"xo{i}", [128, S]) for i in range(2)]
                pp = [P.ps(f"op{i}") for i in range(3)]
                wv = w_d.h.ap().rearrange("(k p) c -> p k c", p=128)
                npp = 0
                for dc in range(16):
                    f_, b_ = wf[dc % 2], wb[dc % 2]
                    for q4 in range(2):
                        P.dma("sp", f_[:, q4 * 8:(q4 + 1) * 8, :], wv[:, q4 * 8:(q4 + 1) * 8, dc * 128:(dc + 1) * 128], w=[(f_, q4)])
                        P.copy("dve" if q4 == 0 else "act", b_[:, q4 * 8:(q4 + 1) * 8, :], f_[:, q4 * 8:(q4 + 1) * 8, :], r=[(f_, q4)], w=[(b_, q4)])
                    x_ = xo[dc % 2]
                    P.dma("sp", x_[:], xin_d[dc * 128:(dc + 1) * 128, :], r=[xin_d], w=[x_])
                    for tb in range(4):
                        p_ = pp[npp % 3]
                        npp += 1
                        for k in range(16):
                            P.mm(p_[:], lhsT=b_[:, k, :], rhs=yt[:, k, tb * 512:(tb + 1) * 512], start=(k == 0), stop=(k == 15),
                                 r=[(b_, k // 8), yt], w=[p_])
                        P.stt("dve", x_[:, tb * 512:(tb + 1) * 512], p_[:], modsb[:, l, 32 + dc:33 + dc], x_[:, tb * 512:(tb + 1) * 512],
                              ALU.mult, ALU.add, r=[p_, x_, modsb], w=[x_])
                    P.dma("pool", xout_d[dc * 128:(dc + 1) * 128, :], x_[:], r=[x_], w=[(xout_d, dc)])

        def conv_fm(src_row0, cwname, cbname, chunk, dst, xp, silu, tagr=()):
            P.dma("sp", xp[:, 2:S + 2], projT_d[src_row0:src_row0 + 128, :], r=[(projT_d, src_row0 // 128)], w=[xp])
            o, _ = PV[cwname]
            wc = lambda j: pvt[:, o + chunk * 4 + j:o + chunk * 4 + j + 1]
            P.ts("dve", dst[:], xp[:, 0:S], wc(0), PVc(cbname, chunk), ALU.mult, ALU.add, r=[xp, pvt], w=[dst])
            for j in range(1, 4):
                P.stt("dve", dst[:], xp[:, j:S + j], wc(j), dst[:], ALU.mult, ALU.add, r=[xp, pvt, dst], w=[dst])
            if silu:
                P.act(dst[:], dst[:], AF.Silu, r=[dst], w=[dst])

        def pad_init(xp):
            P.op("dve", lambda e: e.memset(xp[:, 0:2], 0.0), w=[xp])
            P.op("dve", lambda e: e.memset(xp[:, S + 2:S + 3], 0.0), w=[xp])

        def stage_attn():
            with P.scope():
                rope = P.sb("rope", [128, 2, S])
                P.dma("sp", rope[:, 0, :], rope_d[0], w=[(rope, 0)])
                P.dma("sp", rope[:, 1, :], rope_d[1], w=[(rope, 1)])
                qk = P.sb("qkr", [128, 10, S], BF16)
                vt = P.sb("vt", [128, 16, 256], BF16)
                vf = P.sb("vf", [128, 16, 256])
                P.dma("sp", vf[:], tokm_d[:, :, 0:256], r=[(tokm_d, 0)], w=[vf])
                P.copy("dve", vt[:], vf[:], r=[vf], w=[vt])
                raw = [P.sb(f"raw{i}", [128, 512]) for i in range(2)]
                sq = [P.sb(f"asq{i}", [128, 512]) for i in range(2)]
                rs = [P.sb(f"ars{i}", [128, 512]) for i in range(2)]
                qn = [P.sb(f"aqn{i}", [128, 512]) for i in range(2)]
                t1 = [P.sb(f"at1{i}", [128, 512]) for i in range(2)]
                t2 = [P.sb(f"at2{i}", [128, 512]) for i in range(2)]
                pa = [P.ps(f"pa{i}") for i in range(2)]
                pb = [P.ps(f"pb{i}") for i in range(2)]
                it = 0
                for hh in range(10):
                    row0 = hh * 128 if hh < 8 else 1024 + (hh - 8) * 128
                    wn = PVc("q_norm") if hh < 8 else PVc("k_norm")
                    for tb in range(4):
                        b = it % 2
                        it += 1
                        tsl = slice(tb * 512, (tb + 1) * 512)
                        P.dma("sp", raw[b][:], projT_d[row0:row0 + 128, tsl], r=[(projT_d, row0 // 128)], w=[raw[b]])
                        P.act(sq[b][:], raw[b][:], AF.Square, r=[raw[b]], w=[sq[b]])
                        P.mm(pa[b][:], lhsT=C("ones"), rhs=sq[b][:], start=True, stop=True, r=[sq[b], cst], w=[pa[b]])
                        P.act(rs[b][:], pa[b][:], AF.Ln, r=[pa[b]], w=[rs[b]], scale=1.0 / 128, bias=EPS)
                        P.act(rs[b][:], rs[b][:], AF.Exp, r=[rs[b]], w=[rs[b]], scale=-0.5)
                        P.stt("dve", qn[b][:], raw[b][:], wn, rs[b][:], ALU.mult, ALU.mult, r=[raw[b], rs[b], pvt], w=[qn[b]])
                        P.mm(pb[b][:], lhsT=C("Rt"), rhs=qn[b][:], start=True, stop=True, r=[qn[b], cst], w=[pb[b]])
                        P.tt("dve", t1[b][:], qn[b][:], rope[:, 0, tsl], ALU.mult, r=[qn[b], (rope, 0)], w=[t1[b]])
                        P.tt("dve", t2[b][:], pb[b][:], rope[:, 1, tsl], ALU.mult, r=[pb[b], (rope, 1)], w=[t2[b]])
                        P.tt("dve", qk[:, hh, tsl], t1[b][:], t2[b][:], ALU.add, r=[t1[b], t2[b]], w=[(qk, hh * 4 + tb)])
                pT = [P.sb(f"pT{i}", [128, 512], BF16) for i in range(3)]
                ps_ = [P.ps(f"psc{i}") for i in range(2)]
                po = [P.ps(f"po{i}") for i in range(2)]
                gt = [P.sb(f"gt{i}", [128, 512]) for i in range(2)]
                rd = [P.sb(f"rd{i}", [128, 512]) for i in range(2)]
                yo = [P.sb(f"yo{i}", [128, 512], BF16) for i in range(2)]
                scale = 128.0 ** -0.5
                it = 0
                n3 = 0
                for h in range(8):
                    g = h // 4
                    for qb in range(4):
                        b = it % 2
                        it += 1
                        qsl = slice(qb * 512, (qb + 1) * 512)
                        P.dma("sp", gt[b][:], projT_d[1536 + h * 128:1536 + (h + 1) * 128, qsl], r=[(projT_d, 12 + h)], w=[gt[b]])
                        for kc in range(16):
                            s_ = ps_[kc % 2]
                            P.mm(s_[:], lhsT=qk[:, 8 + g, kc * 128:(kc + 1) * 128], rhs=qk[:, h, qsl], start=True, stop=True,
                                 r=[(qk, (8 + g) * 4 + kc // 4), (qk, h * 4 + qb)], w=[s_])
                            p_ = pT[n3 % 3]
                            n3 += 1
                            P.act(p_[:], s_[:], AF.Exp, r=[s_], w=[p_], scale=scale)
                            P.mm(po[b][:], lhsT=vt[:, kc, g * 128:(g + 1) * 128], rhs=p_[:], start=(kc == 0), stop=(kc == 15),
                                 r=[vt, p_], w=[po[b]])
                            P.mm(pa[b][:], lhsT=onesb[:], rhs=p_[:], start=(kc == 0), stop=(kc == 15), r=[onesb, p_], w=[pa[b]])
                        P.act(gt[b][:], gt[b][:], AF.Silu, r=[gt[b]], w=[gt[b]])
                        P.op("dve", lambda e: e.reciprocal(out=rd[b][:], in_=pa[b][:]), r=[pa[b]], w=[rd[b]])
                        P.tt("dve", rd[b][:], rd[b][:], gt[b][:], ALU.mult, r=[rd[b], gt[b]], w=[rd[b]])
                        P.tt("dve", yo[b][:], po[b][:], rd[b][:], ALU.mult, r=[po[b], rd[b]], w=[yo[b]])
                        P.dma("pool", yT_d[h * 128:(h + 1) * 128, qsl], yo[b][:], r=[yo[b]], w=[(yT_d, h)])

        def stage_ssd():
            with P.scope():
                xp = P.sb("xp", [128, S + 3])
                pad_init(xp)
                BT2 = [P.sb(f"BT{g}", [128, S], BF16) for g in range(2)]
                CTb2 = [P.sb(f"CTb{g}", [128, S], BF16) for g in range(2)]
                Btm2 = [P.sb(f"Btm{g}", [128, 16, 128], BF16) for g in range(2)]
                GT2 = [P.sb(f"GT{g}", [128, 16, 128]) for g in range(2)]
                tmp = P.sb("ctmp", [128, S])
                pg = [P.ps(f"pg{i}") for i in range(4)]
                py = [P.ps(f"py{i}") for i in range(4)]
                for g in range(2):
                    BT, CTb, Btm, GT = BT2[g], CTb2[g], Btm2[g], GT2[g]
                    conv_fm(3584 + g * 128, "ab_cw", "ab_cb", 8 + g, tmp, xp, True)
                    P.copy("act", BT[:], tmp[:], r=[tmp], w=[BT])
                    for c in range(16):
                        p_ = pg[c % 4]
                        P.tr(p_[:, 0:128], tmp[:, c * 128:(c + 1) * 128], C("ident"), r=[tmp, cst], w=[p_])
                        P.copy("dve", Btm[:, c, :], p_[:, 0:128], r=[p_], w=[Btm])
                    conv_fm(3840 + g * 128, "ab_cw", "ab_cb", 10 + g, tmp, xp, True)
                    P.copy("act", CTb[:], tmp[:], r=[tmp], w=[CTb])
                    for c in range(16):
                        p_ = py[c % 4]
                        csl = slice(c * 128, (c + 1) * 128)
                        P.mm(p_[:, 0:128], lhsT=BT[:, csl], rhs=CTb[:, csl], start=True, stop=True, r=[BT, CTb], w=[p_])
                        P.copy("dve", GT[:, c, :], p_[:, 0:128], r=[p_], w=[GT])
                dt = P.sb("dt", [128, 16, 32])
                av = P.sb("av", [128, 16, 32])
                Aneg = P.sb("Aneg", [128, 32])
                P.dma("sp", dt[:], tokm_d[:, :, 256:288], r=[tokm_d], w=[dt])
                P.tt("dve", dt[:], dt[:], RVc("ab_dtb", 0, 32).unsqueeze(1).to_broadcast([128, 16, 32]), ALU.add, r=[dt, rvt], w=[dt])
                P.act(dt[:], dt[:], AF.Exp, r=[dt], w=[dt])
                P.act(dt[:], dt[:], AF.Ln, r=[dt], w=[dt], bias=1.0)
                P.act(Aneg[:], RVc("ab_alog", 0, 32), AF.Exp, r=[rvt], w=[Aneg])
                P.stt("dve", av[:], dt[:], -1.0, Aneg[:].unsqueeze(1).to_broadcast([128, 16, 32]), ALU.mult, ALU.mult, r=[dt, Aneg], w=[av])

                xsT = [P.sb(f"xsT{i}", [128, S]) for i in range(2)]
                Y = [P.sb(f"Y{i}", [128, S]) for i in range(2)]
                xtm = [P.sb(f"xtm{i}", [128, 16, 128], BF16) for i in range(2)]
                xpad = [[P.sb(f"xpad{i}{h}", [128, 16, 128], BF16) for h in range(2)] for i in range(2)]
                Vall = P.sb("Vall", [128, 8, S], BF16)
                NC_ = 4
                Sm = [P.sb(f"Sm{i}", [128, 128]) for i in range(NC_)]
                Spad = [[P.sb(f"Spad{i}{h}", [128, 128], BF16) for h in range(2)] for i in range(NC_)]
                Rt_ = [P.sb(f"R{i}", [128, 2, 128]) for i in range(NC_)]
                EG = [P.sb(f"EG{i}", [128, 2, 128]) for i in range(NC_)]
                DC = [P.sb(f"DC{i}", [128, 2, 128]) for i in range(NC_)]
                kds = [P.sb(f"kds{i}", [128, 2]) for i in range(NC_)]
                AT = [P.sb(f"AT{i}", [128, 2, 128], BF16) for i in range(NC_)]
                QD = [P.sb(f"QD{i}", [128, 2, 128], BF16) for i in range(NC_)]
                KD = [P.sb(f"KD{i}", [128, 2, 128], BF16) for i in range(NC_)]
                nst = 16

                def chain(pi, hp, d, ci):
                    g_, y_ = pg[ci], py[ci]
                    CTb, Btm, GT = CTb2[hp // 4], Btm2[hp // 4], GT2[hp // 4]
                    U = C("Uf") if d == 0 else C("Ub")
                    M = C("Mf") if d == 0 else C("Mb")
                    NG = C("NEGf", 256) if d == 0 else C("NEGb", 256)
                    hcol = slice(d * 16 + hp * 2, d * 16 + hp * 2 + 2)
                    ccol = 127 if d == 0 else 0
                    P.op("dve", lambda e: e.memset(Sm[ci][:], 0.0), w=[Sm[ci]])
                    for h in range(2):
                        P.op("dve", lambda e: e.memset(Spad[ci][h][:], 0.0), w=[Spad[ci][h]])
                    for step in range(nst):
                        c = step if d == 0 else 15 - step
                        csl = slice(c * 128, (c + 1) * 128)
                        a2 = av[:, c, hcol]
                        R_ = Rt_[ci]
                        P.tt("dve", R_[:], U.unsqueeze(1).to_broadcast([128, 2, 128]), a2.unsqueeze(2).to_broadcast([128, 2, 128]),
                             ALU.mult, r=[cst, av], w=[R_])
                        Rf = R_[:].rearrange("p h i -> p (h i)")
                        P.mm(g_[:, 0:256], lhsT=C("ones"), rhs=Rf, start=True, stop=True, r=[R_, cst], w=[g_])
                        P.mm(g_[:, 256:512], lhsT=M, rhs=Rf, start=True, stop=False, r=[R_, cst], w=[g_])
                        P.mm(g_[:, 256:512], lhsT=C("ident"), rhs=NG, start=False, stop=True, r=[cst], w=[g_])
                        P.mm(y_[:, 256:258], lhsT=M, rhs=a2, start=True, stop=True, r=[av, cst], w=[y_])
                        yield
                        P.act(EG[ci][:].rearrange("p h i -> p (h i)"), g_[:, 0:256], AF.Exp, r=[g_], w=[EG[ci]])
                        P.act(DC[ci][:].rearrange("p h i -> p (h i)"), g_[:, 256:512], AF.Exp, r=[g_], w=[DC[ci]])
                        P.act(kds[ci][:], y_[:, 256:258], AF.Exp, r=[y_], w=[kds[ci]])
                        P.tt("dve", kds[ci][:], kds[ci][:], dt[:, c, hcol], ALU.mult, r=[kds[ci], dt], w=[kds[ci]])
                        for h in range(2):
                            P.stt("dve", AT[ci][:, h, :], DC[ci][:, h, :], dt[:, c, d * 16 + hp * 2 + h:d * 16 + hp * 2 + h + 1],
                                  GT[:, c, :], ALU.mult, ALU.mult, r=[DC[ci], dt, GT], w=[AT[ci]])
                        P.tt("dve", QD[ci][:], CTb[:, csl].unsqueeze(1).to_broadcast([128, 2, 128]), EG[ci][:], ALU.mult,
                             r=[CTb, EG[ci]], w=[QD[ci]])
                        P.tt("dve", KD[ci][:], Btm[:, c, :].unsqueeze(1).to_broadcast([128, 2, 128]),
                             kds[ci][:].unsqueeze(2).to_broadcast([128, 2, 128]), ALU.mult, r=[Btm, kds[ci]], w=[KD[ci]])
                        for h in range(2):
                            P.mm(y_[:, 0:128], lhsT=Spad[ci][h][:], rhs=QD[ci][:, h, :], start=(h == 0), stop=False,
                                 r=[Spad[ci][h], QD[ci]], w=[y_])
                        for h in range(2):
                            P.mm(y_[:, 0:128], lhsT=xpad[pi][h][:, c, :], rhs=AT[ci][:, h, :], start=False, stop=(h == 1),
                                 r=[xpad[pi][h], AT[ci]], w=[y_])
                        for h in range(2):
                            P.mm(y_[:, 128 + h * 64:128 + (h + 1) * 64], lhsT=KD[ci][:, h, :], rhs=xtm[pi][:, c, h * 64:(h + 1) * 64],
                                 start=True, stop=True, r=[KD[ci], xtm[pi]], w=[y_])
                        yield
                        P.tt("dve", Y[pi][:, csl], Y[pi][:, csl], y_[:, 0:128], ALU.add, r=[y_, (Y[pi], c)], w=[(Y[pi], c)])
                        for h in range(2):
                            hs = slice(h * 64, (h + 1) * 64)
                            P.stt("dve", Sm[ci][:, hs], Sm[ci][:, hs], EG[ci][:, h, ccol:ccol + 1], y_[:, 128 + h * 64:128 + (h + 1) * 64],
                                  ALU.mult, ALU.add, r=[Sm[ci], EG[ci], y_], w=[Sm[ci]])
                            P.copy("act", Spad[ci][h][:, hs], Sm[ci][:, hs], r=[Sm[ci]], w=[Spad[ci][h]])

                for rnd in range(4):
                    for pi in range(2):
                        hp = rnd * 2 + pi
                        conv_fm(2560 + hp * 128, "ab_cw", "ab_cb", hp, xsT[pi], xp, True)
                        for c in range(16):
                            p_ = pg[c % 4]
                            P.tr(p_[:, 0:128], xsT[pi][:, c * 128:(c + 1) * 128], C("ident"), r=[xsT[pi], cst], w=[p_])
                            P.copy("act", xtm[pi][:, c, :], p_[:, 0:128], r=[p_], w=[xtm[pi]])
                        P.ts("dve", Y[pi][:], xsT[pi][:], PVc("d_skip", hp), None, ALU.mult, None, r=[xsT[pi], pvt], w=[Y[pi]])
                        for h in range(2):
                            P.op("dve", lambda e: e.memset(xpad[pi][h][:], 0.0), w=[xpad[pi][h]])
                            P.copy("dve", xpad[pi][h][:, :, h * 64:(h + 1) * 64], xtm[pi][:, :, h * 64:(h + 1) * 64], r=[xtm[pi]], w=[xpad[pi][h]])
                    gens = [chain(pi, rnd * 2 + pi, d, pi * 2 + d) for pi in range(2) for d in range(2)]
                    while gens:
                        for g_ in list(gens):
                            try:
                                next(g_)
                            except StopIteration:
                                gens.remove(g_)
                    for pi in range(2):
                        hp = rnd * 2 + pi
                        P.dma("sp", tmp[:], projT_d[4128 + hp * 128:4128 + (hp + 1) * 128, :], r=[projT_d], w=[tmp])
                        P.act(tmp[:], tmp[:], AF.Silu, r=[tmp], w=[tmp])
                        P.tt("dve", Vall[:, hp, :], Y[pi][:], tmp[:], ALU.mult, r=[Y[pi], tmp], w=[(Vall, hp)])
                sqb = [P.sb(f"ssq{i}", [128, 512], BF16) for i in range(2)]
                rsb = P.sb("srs", [128, 512])
                ob = [P.sb(f"sob{i}", [128, 512], BF16) for i in range(2)]
                for tb in range(4):
                    tsl = slice(tb * 512, (tb + 1) * 512)
                    for hp in range(8):
                        P.tt("dve", sqb[hp % 2][:], Vall[:, hp, tsl], Vall[:, hp, tsl], ALU.mult, r=[(Vall, hp)], w=[sqb[hp % 2]])
                        P.mm(pg[0][:], lhsT=onesb[:], rhs=sqb[hp % 2][:], start=(hp == 0), stop=(hp == 7), r=[sqb[hp % 2], onesb], w=[pg[0]])
                    P.act(rsb[:], pg[0][:], AF.Ln, r=[pg[0]], w=[rsb], scale=1.0 / 1024, bias=EPS)
                    P.act(rsb[:], rsb[:], AF.Exp, r=[rsb], w=[rsb], scale=-0.5)
                    for hp in range(8):
                        o_ = ob[hp % 2]
                        P.stt("dve", o_[:], Vall[:, hp, tsl], PVc("ssd_norm", hp), rsb[:], ALU.mult, ALU.mult, r=[(Vall, hp), rsb, pvt], w=[o_])
                        P.dma("pool", yT_d[1024 + hp * 128:1024 + (hp + 1) * 128, tsl], o_[:], r=[o_], w=[(yT_d, 8 + hp)])

        def stage_gdn():
            GP = "dve"
            with P.scope():
                xp = P.sb("gxp", [128, S + 3])
                pad_init(xp)
                tmp = P.sb("gtmp", [128, S])
                gt = P.sb("ggt", [128, 16, 32])
                P.dma("sp", gt[:], tokm_d[:, :, 0:32], r=[tokm_d], w=[gt])
                beta = P.sb("gbeta", [128, 16, 16])
                nbeta = P.sb("gnbeta", [128, 16, 16])
                gg = P.sb("ggg", [128, 16, 16])
                An = P.sb("gAn", [128, 16])
                P.act(beta[:], gt[:, :, 0:16], AF.Exp, r=[gt], w=[beta], scale=-1.0)
                P.ts("dve", beta[:], beta[:], 1.0, None, ALU.add, None, r=[beta], w=[beta])
                P.op("dve", lambda e: e.reciprocal(out=beta[:], in_=beta[:]), r=[beta], w=[beta])
                P.ts("dve", nbeta[:], beta[:], -1.0, None, ALU.mult, None, r=[beta], w=[nbeta])
                P.tt("dve", gg[:], gt[:, :, 16:32], RVc("cd_dtb", 0, 16).unsqueeze(1).to_broadcast([128, 16, 16]), ALU.add, r=[gt, rvt], w=[gg])
                P.act(gg[:], gg[:], AF.Exp, r=[gg], w=[gg])
                P.act(gg[:], gg[:], AF.Ln, r=[gg], w=[gg], bias=1.0)
                P.act(An[:], RVc("cd_alog", 0, 16), AF.Exp, r=[rvt], w=[An])
                P.stt("dve", gg[:], gg[:], -1.0, An[:].unsqueeze(1).to_broadcast([128, 16, 16]), ALU.mult, ALU.mult, r=[gg, An], w=[gg])

                qT = P.sb("gqT", [128, S])
                kT = P.sb("gkT", [128, S])
                ktm = P.sb("gktm", [128, 16, 128])
                vtm = P.sb("gvtm", [128, 16, 256])
                O = P.sb("gO", [128, 2, S])
                sq = [P.sb(f"gsq{i}", [128, 512]) for i in range(2)]
                rsb = [P.sb(f"grs{i}", [128, 512]) for i in range(2)]
                NS = 4
                RR = [P.sb(f"gRR{i}", [128, 256]) for i in range(NS)]
                E = [P.sb(f"gE{i}", [128, 388]) for i in range(NS)]
                X = [[P.sb(f"gX{i}_{j}", [128, 128]) for j in range(2)] for i in range(NS)]
                XT = [[P.sb(f"gXT{i}_{j}", [128, 128]) for j in range(2)] for i in range(NS)]
                PT = [[P.sb(f"gPT{i}_{j}", [128, 128]) for j in range(2)] for i in range(NS)]
                AT = [P.sb(f"gAT{i}", [128, 128]) for i in range(NS)]
                vb = [P.sb(f"gvb{i}", [128, 128]) for i in range(NS)]
                kbg = [P.sb(f"gkbg{i}", [128, 128]) for i in range(NS)]
                bge = [P.sb(f"gbge{i}", [128, 1]) for i in range(NS)]
                u_ = [P.sb(f"gu{i}", [128, 128]) for i in range(NS)]
                wT = [P.sb(f"gwT{i}", [128, 128]) for i in range(NS)]
                qd = [P.sb(f"gqd{i}", [128, 128]) for i in range(NS)]
                kdec = [P.sb(f"gkd{i}", [128, 128]) for i in range(NS)]
                vnew = [P.sb(f"gvn{i}", [128, 128]) for i in range(NS)]
                Sst = [P.sb(f"gS{d}", [128, 128]) for d in range(4)]
                ybf = [P.sb(f"gyb{i}", [128, 512], BF16) for i in range(2)]
                pX = [P.ps(f"gpX{i}") for i in range(4)]
                pY = [P.ps(f"gpY{i}") for i in range(4)]
                pA = pX
                qscale = 128.0 ** -0.5
                nhq = 4 if lim is None else lim[0]
                nst = 16 if lim is None else lim[1]
                cut = 0 if (lim is None or len(lim) < 3) else lim[2]

                def l2norm_fm(dst, scl):
                    for tb in range(4):
                        tsl = slice(tb * 512, (tb + 1) * 512)
                        b = tb % 2
                        P.act(sq[b][:], tmp[:, tsl], AF.Square, r=[tmp], w=[sq[b]])
                        P.mm(pA[b][:], lhsT=C("ones"), rhs=sq[b][:], start=True, stop=True, r=[sq[b], cst], w=[pA[b]])
                        P.act(rsb[b][:], pA[b][:], AF.Ln, r=[pA[b]], w=[rsb[b]], bias=EPS)
                        P.act(rsb[b][:], rsb[b][:], AF.Exp, r=[rsb[b]], w=[rsb[b]], scale=-0.5)
                        P.stt("dve", dst[:, tsl], tmp[:, tsl], scl, rsb[b][:], ALU.mult, ALU.mult, r=[tmp, rsb[b]], w=[dst])

                for hq in range(nhq):
                    conv_fm(hq * 128, "cd_cw", "cd_cb", hq, tmp, xp, True)
                    l2norm_fm(qT, qscale)
                    conv_fm(512 + hq * 128, "cd_cw", "cd_cb", 4 + hq, tmp, xp, True)
                    l2norm_fm(kT, 1.0)
                    for c in range(16):
                        p_ = pA[c % 2]
                        P.tr(p_[:, 0:128], kT[:, c * 128:(c + 1) * 128], C("ident"), r=[kT, cst], w=[p_])
                        P.copy("act", ktm[:, c, :], p_[:, 0:128], r=[p_], w=[ktm])
                    for e in range(2):
                        conv_fm(1024 + (2 * hq + e) * 128, "cd_cw", "cd_cb", 8 + 2 * hq + e, tmp, xp, True)
                        for c in range(16):
                            p_ = pA[c % 2]
                            P.tr(p_[:, 0:128], tmp[:, c * 128:(c + 1) * 128], C("ident"), r=[tmp, cst], w=[p_])
                            P.copy("act", vtm[:, c, e * 128:(e + 1) * 128], p_[:, 0:128], r=[p_], w=[vtm])
                    P.op("dve", lambda e_: e_.memset(O[:], 0.0), w=[O])
                    def chain(e, d, ci):
                        hv = 2 * hq + e
                        S_ = Sst[ci]
                        a_ = pX[ci]
                        c_ = pY[ci]
                        bi = ci
                        UM = C("Uf", 256) if d == 0 else C("Ub", 256)
                        U, M = UM[:, 0:128], UM[:, 128:256]
                        NGi = C("NEGf") if d == 0 else C("NEGb")
                        NGs = C("NEGsf") if d == 0 else C("NEGsb")
                        col = d * 8 + hv
                        ccol = 127 if d == 0 else 0
                        P.op("dve", lambda e_: e_.memset(S_[:], 0.0), w=[S_])
                        for step in range(nst):
                            c = step if d == 0 else 15 - step
                            csl = slice(c * 128, (c + 1) * 128)
                            gcol = gg[:, c, col:col + 1]
                            bcol = beta[:, c, col:col + 1]
                            nbcol = nbeta[:, c, col:col + 1]
                            P.ts("pool", RR[bi][:], UM, gcol, None, ALU.mult, None, r=[cst, gg], w=[RR[bi]])
                            R1, R2 = RR[bi][:, 0:128], RR[bi][:, 128:256]
                            P.mm(a_[:, 0:128], lhsT=C("ones"), rhs=R1, start=True, stop=True, r=[RR[bi], cst], w=[a_])
                            P.mm(a_[:, 128:256], lhsT=M, rhs=R1, start=True, stop=False, r=[RR[bi], cst], w=[a_])
                            P.mm(a_[:, 128:256], lhsT=C("ident"), rhs=NGi, start=False, stop=True, r=[cst], w=[a_])
                            P.mm(a_[:, 256:384], lhsT=U, rhs=R2, start=True, stop=False, r=[RR[bi], cst], w=[a_])
                            P.mm(a_[:, 256:384], lhsT=C("ident"), rhs=NGs, start=False, stop=True, r=[cst], w=[a_])
                            P.mm(a_[:, 384:385], lhsT=M, rhs=gcol, start=True, stop=True, r=[gg, cst], w=[a_])
                            P.mm(a_[:, 385:386], lhsT=U, rhs=gcol, start=True, stop=True, r=[gg, cst], w=[a_])
                            yield
                            E_ = E[bi]
                            P.act(E_[:, 0:386], a_[:, 0:386], AF.Exp, r=[a_], w=[E_])
                            EGb, decT, decS = E_[:, 0:128], E_[:, 128:256], E_[:, 256:384]
                            kds, eg = E_[:, 384:385], E_[:, 385:386]
                            P.mm(c_[:, 0:128], lhsT=kT[:, csl], rhs=kT[:, csl], start=True, stop=True, r=[kT], w=[c_])
                            P.mm(c_[:, 128:256], lhsT=kT[:, csl], rhs=qT[:, csl], start=True, stop=True, r=[kT, qT], w=[c_])
                            yield
                            X0 = X[bi][0]
                            P.stt("dve", X0[:], c_[:, 0:128], nbcol, decS, ALU.mult, ALU.mult, r=[c_, nbeta, E_], w=[X0])
                            P.tt("dve", AT[bi][:], c_[:, 128:256], decT, ALU.mult, r=[c_, E_], w=[AT[bi]])
                            P.act(vb[bi][:], vtm[:, c, e * 128:(e + 1) * 128], AF.Copy, r=[vtm, beta], w=[vb[bi]], scale=bcol)
                            P.tt("dve", bge[bi][:], bcol, eg, ALU.mult, r=[beta, E_], w=[bge[bi]])
                            P.act(kbg[bi][:], ktm[:, c, :], AF.Copy, r=[ktm, bge[bi]], w=[kbg[bi]], scale=bge[bi][:])
                            P.tt("dve", qd[bi][:], qT[:, csl], EGb, ALU.mult, r=[qT, E_], w=[qd[bi]])
                            P.act(kdec[bi][:], ktm[:, c, :], AF.Copy, r=[ktm, E_], w=[kdec[bi]], scale=kds)
                            P.tr(a_[:, 0:128], X0[:], C("ident"), r=[X0, cst], w=[a_])
                            yield
                            P.copy("act", XT[bi][0][:], a_[:, 0:128], r=[a_], w=[XT[bi][0]])
                            P.tt("dve", PT[bi][0][:], a_[:, 0:128], C("ident"), ALU.add, r=[a_, cst], w=[PT[bi][0]])
                            for k in range(1, 7):
                                xo, xn = X[bi][(k - 1) % 2], X[bi][k % 2]
                                to, tn = XT[bi][(k - 1) % 2], XT[bi][k % 2]
                                po_, pn = PT[bi][(k - 1) % 2], PT[bi][k % 2]
                                P.mm(c_[:, 128:256], lhsT=to[:], rhs=xo[:], start=True, stop=True, r=[to, xo], w=[c_])
                                if k < 6:
                                    P.mm(c_[:, 0:128], lhsT=xo[:], rhs=to[:], start=True, stop=True, r=[to, xo], w=[c_])
                                yield
                                P.copy("act", xn[:], c_[:, 128:256], r=[c_], w=[xn])
                                if k < 6:
                                    P.copy("act", tn[:], c_[:, 0:128], r=[c_], w=[tn])
                                P.mm(a_[:, 256:384], lhsT=xn[:], rhs=po_[:], start=True, stop=True, r=[xn, po_], w=[a_])
                                yield
                                P.tt("dve", pn[:], a_[:, 256:384], po_[:], ALU.add, r=[a_, po_], w=[pn])
                            TT = PT[bi][0]
                            P.mm(c_[:, 256:384], lhsT=TT[:], rhs=vb[bi][:], start=True, stop=True, r=[TT, vb[bi]], w=[c_])
                            P.mm(c_[:, 384:512], lhsT=kbg[bi][:], rhs=TT[:], start=True, stop=True, r=[TT, kbg[bi]], w=[c_])
                            yield
                            P.copy("act", u_[bi][:], c_[:, 256:384], r=[c_], w=[u_[bi]])
                            P.copy("act", wT[bi][:], c_[:, 384:512], r=[c_], w=[wT[bi]])
                            P.mm(a_[:, 0:128], lhsT=wT[bi][:], rhs=S_[:], start=True, stop=True, r=[wT[bi], S_], w=[a_])
                            yield
                            P.tt("dve", vnew[bi][:], u_[bi][:], a_[:, 0:128], ALU.subtract, r=[u_[bi], a_], w=[vnew[bi]])
                            P.mm(c_[:, 0:128], lhsT=S_[:], rhs=qd[bi][:], start=True, stop=False, r=[S_, qd[bi]], w=[c_])
                            P.mm(c_[:, 0:128], lhsT=vnew[bi][:], rhs=AT[bi][:], start=False, stop=True, r=[vnew[bi], AT[bi]], w=[c_])
                            P.mm(c_[:, 128:256], lhsT=kdec[bi][:], rhs=vnew[bi][:], start=True, stop=True, r=[kdec[bi], vnew[bi]], w=[c_])
                            yield
                            P.tt("dve", O[:, e, csl], O[:, e, csl], c_[:, 0:128], ALU.add, r=[c_, (O, e * 16 + c)], w=[(O, e * 16 + c)])
                            P.stt("dve", S_[:], S_[:], EGb[:, ccol:ccol + 1], c_[:, 128:256], ALU.mult, ALU.add, r=[S_, E_, c_], w=[S_])

                    gens = [chain(e, d, e * 2 + d) for e in range(2) for d in range(2)]
                    while gens:
                        for g_ in list(gens):
                            try:
                                next(g_)
                            except StopIteration:
                                gens.remove(g_)
                    for e in range(2):
                        hv = 2 * hq + e
                        P.dma("sp", tmp[:], projT_d[2080 + hv * 128:2080 + (hv + 1) * 128, :], r=[projT_d], w=[tmp])
                        P.act(tmp[:], tmp[:], AF.Silu, r=[tmp], w=[tmp])
                        for tb in range(4):
                            tsl = slice(tb * 512, (tb + 1) * 512)
                            b = tb % 2
                            P.act(sq[b][:], O[:, e, tsl], AF.Square, r=[O], w=[sq[b]])
                            P.mm(pA[b][:], lhsT=C("ones"), rhs=sq[b][:], start=True, stop=True, r=[sq[b], cst], w=[pA[b]])
                            P.act(rsb[b][:], pA[b][:], AF.Ln, r=[pA[b]], w=[rsb[b]], scale=1.0 / 128, bias=EPS)
                            P.act(rsb[b][:], rsb[b][:], AF.Exp, r=[rsb[b]], w=[rsb[b]], scale=-0.5)
                            P.stt("dve", sq[b][:], O[:, e, tsl], PVc("gdn_norm"), rsb[b][:], ALU.mult, ALU.mult, r=[O, rsb[b], pvt], w=[sq[b]])
                            P.tt("dve", ybf[b][:], sq[b][:], tmp[:, tsl], ALU.mult, r=[sq[b], tmp], w=[ybf[b]])
                            P.dma("pool", yT_d[hv * 128:(hv + 1) * 128, tsl], ybf[b][:], r=[ybf[b]], w=[(yT_d, hv)])

        def stage_lru():
            with P.scope():
                xp = P.sb("lxp", [128, S + 3])
                pad_init(xp)
                xc = P.sb("lxc", [128, S])
                wl = P.sb("lw", [128, 4, 8, 128])
                for m_ in range(4):
                    P.dma("sp", wl[:, m_, :, :], lruw_d.h.ap()[m_].rearrange("n i j -> i n j"), w=[(wl, m_)])
                nsp = P.sb("lnsp", [128, 16])
                for d, nm in enumerate(("lam_f", "lam_b")):
                    P.act(nsp[:, d * 8:(d + 1) * 8], PVc(nm, 0, 8), AF.Exp, r=[pvt], w=[(nsp, d)], scale=-1.0)
                    P.act(nsp[:, d * 8:(d + 1) * 8], nsp[:, d * 8:(d + 1) * 8], AF.Ln, r=[(nsp, d)], w=[(nsp, d)], bias=1.0)
                    P.ts("dve", nsp[:, d * 8:(d + 1) * 8], nsp[:, d * 8:(d + 1) * 8], -8.0, None, ALU.mult, None, r=[(nsp, d)], w=[(nsp, d)])
                rr = P.sb("lrr", [128, S])
                ig = P.sb("lig", [128, S])
                aa = P.sb("laa", [128, S])
                mm_ = P.sb("lmm", [128, S])
                hh = [P.sb(f"lhh{d}", [128, S]) for d in range(2)]
                gl = P.sb("lgl", [128, S])
                yb = P.sb("lyb", [128, S], BF16)
                pp = [P.ps(f"lp{i}") for i in range(4)]

                def rev(t):
                    return bass.AP(t.h, S - 1, [[S, 128], [-1, S]])

                for n in range(8 if lim is None else lim[0]):
                    conv_fm(3104 + n * 128, "lru_cw", "lru_cb", n, xc, xp, False)
                    P.dma("sp", gl[:], projT_d[4128 + n * 128:4128 + (n + 1) * 128, :], r=[projT_d], w=[gl])
                    for d in range(2):
                        sfx = "f" if d == 0 else "b"
                        for tb in range(4):
                            tsl = slice(tb * 512, (tb + 1) * 512)
                            p1, p2 = pp[(tb % 2) * 2], pp[(tb % 2) * 2 + 1]
                            P.mm(p1[:], lhsT=wl[:, 2 * d, n, :], rhs=xc[:, tsl], start=True, stop=True, r=[(wl, 2 * d), xc], w=[p1])
                            P.mm(p2[:], lhsT=wl[:, 2 * d + 1, n, :], rhs=xc[:, tsl], start=True, stop=True, r=[(wl, 2 * d + 1), xc], w=[p2])
                            P.act(rr[:, tsl], p1[:], AF.Sigmoid, r=[p1, pvt], w=[(rr, tb)], bias=PVc("ba_" + sfx, n))
                            P.act(ig[:, tsl], p2[:], AF.Sigmoid, r=[p2, pvt], w=[(ig, tb)], bias=PVc("bx_" + sfx, n))
                        P.act(aa[:], rr[:], AF.Exp, r=[rr, nsp], w=[aa], scale=nsp[:, d * 8 + n:d * 8 + n + 1])
                        P.tt("pool", mm_[:], aa[:], aa[:], ALU.mult, r=[aa], w=[mm_])
                        P.act(mm_[:], mm_[:], AF.Ln, r=[mm_], w=[mm_], scale=-1.0, bias=1.0)
                        P.act(mm_[:], mm_[:], AF.Exp, r=[mm_], w=[mm_], scale=0.5)
                        P.tt("pool", ig[:], ig[:], xc[:], ALU.mult, r=[ig, xc], w=[ig])
                        P.tt("dve", mm_[:], mm_[:], ig[:], ALU.mult, r=[mm_, ig], w=[mm_])
                        if d == 0:
                            P.op("dve", lambda e: e.tensor_tensor_scan(out=hh[0][:], data0=aa[:], data1=mm_[:], initial=0.0,
                                                                       op0=ALU.mult, op1=ALU.add), r=[aa, mm_], w=[hh[0]])
                        else:
                            P.op("dve", lambda e: e.tensor_tensor_scan(out=rev(hh[1]), data0=rev(aa), data1=rev(mm_), initial=0.0,
                                                                       op0=ALU.mult, op1=ALU.add), r=[aa, mm_], w=[hh[1]])
                    P.act(gl[:], gl[:], AF.Silu, r=[gl], w=[gl])
                    P.tt("pool", hh[0][:], hh[0][:], hh[1][:], ALU.add, r=[hh[0], hh[1]], w=[hh[0]])
                    P.tt("dve", yb[:], hh[0][:], gl[:], ALU.mult, r=[hh[0], gl], w=[yb])
                    P.dma("pool", yT_d[1024 + n * 128:1024 + (n + 1) * 128, :], yb[:], r=[yb], w=[(yT_d, 8 + n)])

        if only is not None:
            {"ssd": stage_ssd, "attn": stage_attn, "gdn": stage_gdn, "lru": stage_lru}[only]()
            P.barrier()
            return nc, dbg
        stage_mod()
        if debug:
            md = scratch("dbg_mod", [128, 96])
            P.dma("pool", md[:], modsb[:].rearrange("p l e -> p (l e)"), r=[modsb], w=[md])
        with P.scope():
            hT = P.sb("hT", [128, 16, S], BF16)
            stage_norm(xT_d, lambda k: sc1[:, 0, k:k + 1], lambda k: modsb[:, 0, k:k + 1], out_tile=hT)
            if debug:
                hd = scratch("dbg_h", [128, 16, S], BF16)
                P.dma("pool", hd[:], hT[:], r=[hT], w=[hd])
            fm = [(c * 128, 128, c * 128) for c in range(32) if not (10 <= c < 12)]
            fm += [(4128 + c * 128, 128, 4128 + c * 128) for c in range(8)]
            stage_inproj(hT, w_in_d[0], fm, [(1280, 256, 0), (4096, 32, 256)])
        if upto == "proj0":
            return nc, dbg
        if upto != "ssd":
            stage_attn()
        if upto == "attn":
            return nc, dbg
        stage_ssd()
        if upto == "ssd":
            return nc, dbg
        stage_outproj(w_out_d[0], xT_d, x1T_d, 0)
        if upto == "l0":
            return nc, dbg
        with P.scope():
            hT = P.sb("hT1", [128, 16, S], BF16)
            stage_norm(x1T_d, lambda k: sc1[:, 1, k:k + 1], lambda k: modsb[:, 1, k:k + 1], out_tile=hT)
            fm = [(c * 128, 128, c * 128) for c in range(16)]
            fm += [(2080 + c * 128, 128, 2080 + c * 128) for c in range(24)]
            stage_inproj(hT, w_in_d[1], fm, [(2048, 32, 0)])
        stage_gdn()
        stage_lru()
        stage_outproj(w_out_d[1], x1T_d, x2T_d, 1)
        stage_norm(x2T_d, lambda k: PVc("fnorm", k), lambda k: 0.0, out_dram=out_d)
        P.barrier()
    return nc, dbg


def make_inputs(inp, b):
    pv, rv = pack_params(inp)
    m = {
        "xT": np.ascontiguousarray(inp["x"][b].T),
        "cT": np.ascontiguousarray(inp["c"][b].reshape(16, 128).T),
        "w_mod": np.ascontiguousarray(inp["w_mod"]),
        "ab_w_in": np.ascontiguousarray(inp["ab_w_in"][0]),
        "cd_w_in": np.ascontiguousarray(inp["cd_w_in"][0]),
        "ab_w_out": np.ascontiguousarray(inp["ab_w_out"][0]),
        "cd_w_out": np.ascontiguousarray(inp["cd_w_out"][0]),
        "lru_w": np.ascontiguousarray(np.stack([inp["cd_lru_wa_f"][0], inp["cd_lru_wx_f"][0], inp["cd_lru_wa_b"][0], inp["cd_lru_wx_b"][0]], 0)),
        "consts": CONSTS,
        "rope": _rope_tables(),
        "pvec": pv,
        "rvec": rv,
    }
    return m


def kernel(**inputs):
    inp = {k: np.asarray(v, np.float32) for k, v in inputs.items()}
    maps = [make_inputs(inp, i // 2) for i in range(8)]
    nc, _ = build(maps[0]["pvec"].shape[1], maps[0]["rvec"].shape[1])
    res = run_bass_kernel_spmd(nc, maps, core_ids=list(range(8)))
    out = np.stack([np.asarray(res.results[2 * b]["outT"], np.float32).T for b in range(4)], 0)
    return np.ascontiguousarray(out)
```

```python
import math
import numpy as np
from contextlib import ExitStack, contextmanager
import concourse.bass as bass
import concourse.mybir as mybir
from concourse.bass_utils import run_bass_kernel_spmd

F32 = mybir.dt.float32
BF16 = mybir.dt.bfloat16
AF = mybir.ActivationFunctionType
ALU = mybir.AluOpType

D = 2048
S = 2048
EPS = 1e-6
NEG = -30000.0


class Tn:
    def __init__(self, h, name, psum=False):
        self.h = h
        self.name = name
        self.st = {}
        self.psum = psum

    def __getitem__(self, idx):
        return self.h[idx]


class Prog:
    NDMA = {"sp": 14, "pool": 8}

    def __init__(self, nc, es):
        self.nc = nc
        self.stack = [es]
        self.h = {"pe": nc.tensor, "act": nc.scalar, "dve": nc.vector, "pool": nc.gpsimd, "sp": nc.sync}
        self.sem = {e: es.enter_context(nc.semaphore("s_" + e)) for e in ("pe", "act", "dve", "pool")}
        self.cnt = {e: 0 for e in self.sem}
        self.dsem = {q: [es.enter_context(nc.semaphore(f"d_{q}{i}")) for i in range(n)] for q, n in self.NDMA.items()}
        self.dcnt = {q: 0 for q in self.NDMA}
        self.waited = {e: {} for e in self.h}
        self.semobj = {}
        for s in self.sem.values():
            self.semobj[id(s)] = s
        for l in self.dsem.values():
            for s in l:
                self.semobj[id(s)] = s
        self.uid = 0

    def sb(self, name, shape, dt=F32):
        self.uid += 1
        name = f"{name}_{self.uid}"
        return Tn(self.stack[-1].enter_context(self.nc.sbuf_tensor(name, list(shape), dt)), name)

    def ps(self, name):
        self.uid += 1
        name = f"{name}_{self.uid}"
        return Tn(self.stack[-1].enter_context(self.nc.psum_tensor(name, [128, 512], F32)), name, psum=True)

    def dram(self, name, shape, dt=F32, kind="Internal"):
        return Tn(self.nc.dram_tensor(name, list(shape), dt, kind=kind), name)

    @contextmanager
    def scope(self):
        es = ExitStack()
        self.stack.append(es)
        try:
            yield
        finally:
            self.barrier()
            self.stack.pop()
            es.close()

    def barrier(self):
        toks = [(self.sem[e], self.cnt[e]) for e in self.sem if self.cnt[e] > 0]
        for q, lst in self.dsem.items():
            n = self.dcnt[q]
            for i, s in enumerate(lst):
                uses = (n - i + len(lst) - 1) // len(lst) if n > i else 0
                if uses > 0:
                    toks.append((s, 16 * uses))
        for e, h in self.h.items():
            for s, v in toks:
                if e in self.sem and self.sem[e] is s:
                    continue
                if self.waited[e].get(id(s), 0) >= v:
                    continue
                self.waited[e][id(s)] = v
                h.wait_ge(s, v)

    @staticmethod
    def _norm(x):
        if isinstance(x, tuple):
            return (x[0], None) if x[0].psum else x
        return (x, None)

    def _states(self, t, sub):
        if sub is None:
            return list(t.st.values())
        out = []
        if None in t.st:
            out.append(t.st[None])
        if sub in t.st:
            out.append(t.st[sub])
        return out

    def _deps(self, eng, r, w):
        need = {}

        def add(tok, kind):
            if tok is None:
                return
            sid, val, src = tok
            if src == eng:
                if eng in ("pe", "sp"):
                    return
            if self.waited[eng].get(sid, 0) >= val:
                return
            if need.get(sid, 0) < val:
                need[sid] = val

        for x in r:
            t, sub = self._norm(x)
            for s in self._states(t, sub):
                add(s[0], "raw")
        for x in w:
            t, sub = self._norm(x)
            for s in self._states(t, sub):
                add(s[0], "waw")
                for tok in s[1].values():
                    add(tok, "war")
        for sid, val in need.items():
            self.waited[eng][sid] = val
        return [(self.semobj[sid], val) for sid, val in need.items()]

    def _commit(self, who, tok, r, w):
        for x in r:
            t, sub = self._norm(x)
            if sub is None:
                if None not in t.st:
                    t.st[None] = [None, {}]
                for s in t.st.values():
                    s[1][who] = tok
            else:
                if sub not in t.st:
                    t.st[sub] = [None, {}]
                t.st[sub][1][who] = tok
        for x in w:
            t, sub = self._norm(x)
            if sub is None:
                t.st = {None: [tok, {}]}
            else:
                t.st[sub] = [tok, {}]

    def op(self, eng, fn, r=(), w=()):
        w = list(w) + [x for x in r if self._norm(x)[0].psum]
        waits = self._deps(eng, r, w)
        self.cnt[eng] += 1
        s = self.sem[eng]
        tok = (id(s), self.cnt[eng], eng)
        self._commit(eng, tok, r, w)
        h = self.h[eng]
        for ss, v in waits:
            h.wait_ge(ss, v)
        fn(h).then_inc(s, 1)

    def dma(self, q, out, in_, r=(), w=()):
        n = self.dcnt[q]
        self.dcnt[q] += 1
        pool = self.dsem[q]
        s = pool[n % len(pool)]
        use = n // len(pool)
        waits = self._deps(q, r, w)
        if use > 0 and self.waited[q].get(id(s), 0) < 16 * use:
            waits.append((s, 16 * use))
            self.waited[q][id(s)] = 16 * use
        tok = (id(s), 16 * (use + 1), "dma_" + q)
        self._commit(f"dma_{q}{n % len(pool)}", tok, r, w)
        h = self.h[q]
        for ss, v in waits:
            h.wait_ge(ss, v)
        h.dma_start(out=out, in_=in_).then_inc(s, 16)
        return tok

    def wait_tok(self, eng, tok):
        self.h[eng].wait_ge(self.semobj[tok[0]], tok[1])

    def act(self, out, in_, func, r, w, bias=0.0, scale=1.0, accum_out=None, eng="act"):
        if accum_out is None:
            self.op("act", lambda e: e.activation(out=out, in_=in_, func=func, bias=bias, scale=scale), r=r, w=w)
        else:
            self.op("act", lambda e: e.activation(out=out, in_=in_, func=func, bias=bias, scale=scale, accum_out=accum_out), r=r, w=w)

    def tt(self, eng, out, in0, in1, op, r, w):
        self.op(eng, lambda e: e.tensor_tensor(out=out, in0=in0, in1=in1, op=op), r=r, w=w)

    def ts(self, eng, out, in0, s1, s2, op0, op1, r, w):
        if s2 is None:
            self.op(eng, lambda e: e.tensor_scalar(out=out, in0=in0, scalar1=s1, scalar2=None, op0=op0), r=r, w=w)
        else:
            self.op(eng, lambda e: e.tensor_scalar(out=out, in0=in0, scalar1=s1, scalar2=s2, op0=op0, op1=op1), r=r, w=w)

    def stt(self, eng, out, in0, scalar, in1, op0, op1, r, w):
        self.op(eng, lambda e: e.scalar_tensor_tensor(out=out, in0=in0, scalar=scalar, in1=in1, op0=op0, op1=op1), r=r, w=w)

    def copy(self, eng, out, in_, r, w):
        if eng == "act":
            self.op("act", lambda e: e.copy(out=out, in_=in_), r=r, w=w)
        else:
            self.op(eng, lambda e: e.tensor_copy(out=out, in_=in_), r=r, w=w)

    def mm(self, out, lhsT, rhs, start, stop, r, w):
        self.op("pe", lambda e: e.matmul(out, lhsT=lhsT, rhs=rhs, start=start, stop=stop), r=r, w=w)

    def tr(self, out, in_, ident, r, w):
        self.op("pe", lambda e: e.transpose(out, in_, ident), r=r, w=w)


C_OFF = {}


def _consts():
    i = np.arange(128)
    sI, fI = i[:, None], i[None, :]
    mats = {
        "ident": (sI == fI), "ones": np.ones((128, 128)),
        "Uf": (sI <= fI), "Mf": (sI > fI), "Ub": (sI >= fI), "Mb": (sI < fI),
    }
    R = np.zeros((128, 128))
    for p in range(64):
        R[2 * p, 2 * p + 1] = -1.0
        R[2 * p + 1, 2 * p] = 1.0
    mats["Rt"] = R.T
    negs = {
        "NEGf": NEG * (fI < sI), "NEGb": NEG * (fI > sI),
        "NEGsf": NEG * (fI >= sI), "NEGsb": NEG * (fI <= sI),
    }
    cols = []
    off = 0
    for k, m in mats.items():
        C_OFF[k] = off
        cols.append(np.asarray(m, np.float32))
        off += 128
    for k, m in negs.items():
        C_OFF[k] = off
        cols.append(np.tile(np.asarray(m, np.float32), (1, 4)))
        off += 512
    return np.ascontiguousarray(np.concatenate(cols, axis=1)), off


CONSTS, NCONST = _consts()


def _rope_tables():
    t = np.arange(S)
    row = (t // 64).astype(np.float32)
    col = (t % 64).astype(np.float32)
    n_pairs = 32
    freqs = (np.float32(10000.0) ** (-np.arange(n_pairs, dtype=np.float32) / np.float32(n_pairs))).astype(np.float32)
    ang = np.concatenate([row[:, None] * freqs, col[:, None] * freqs], axis=-1).astype(np.float32)
    cos = np.cos(ang).astype(np.float32)
    sin = np.sin(ang).astype(np.float32)
    cosT = np.repeat(cos, 2, axis=1).T
    sinT = np.repeat(sin, 2, axis=1).T
    return np.ascontiguousarray(np.stack([cosT, sinT], 0))


PV = {}
RV = {}


def _pcols(v, n):
    return np.asarray(v, np.float32).reshape(n, 128).T


def pack_params(inp):
    pv, rv = [], []

    def addp(name, arr):
        PV[name] = (sum(a.shape[1] for a in pv), arr.shape[1])
        pv.append(np.asarray(arr, np.float32))

    def addr(name, vec):
        vec = np.asarray(vec, np.float32).reshape(-1)
        RV[name] = (sum(a.shape[1] for a in rv), vec.shape[0])
        rv.append(np.broadcast_to(vec[None, :], (128, vec.shape[0])))

    addp("q_norm", _pcols(inp["ab_q_norm"][0], 1))
    addp("k_norm", _pcols(inp["ab_k_norm"][0], 1))
    cw = inp["ab_conv_w"][0]
    addp("ab_cw", np.concatenate([_pcols(cw[j], 12)[:, :, None] for j in range(4)], 2).reshape(128, 48))
    addp("ab_cb", _pcols(inp["ab_conv_b"][0], 12))
    addp("d_skip", _pcols(np.repeat(inp["ab_d_skip"][0], 64), 8))
    addp("ssd_norm", _pcols(inp["ab_ssd_norm"][0], 8))
    cw = inp["cd_conv_w"][0]
    addp("cd_cw", np.concatenate([_pcols(cw[j], 16)[:, :, None] for j in range(4)], 2).reshape(128, 64))
    addp("cd_cb", _pcols(inp["cd_conv_b"][0], 16))
    addp("gdn_norm", _pcols(inp["cd_gdn_norm"][0], 1))
    cw = inp["cd_lru_conv_w"][0]
    addp("lru_cw", np.concatenate([_pcols(cw[j], 8)[:, :, None] for j in range(4)], 2).reshape(128, 32))
    addp("lru_cb", _pcols(inp["cd_lru_conv_b"][0], 8))
    for d_ in ("f", "b"):
        addp("ba_" + d_, _pcols(inp["cd_lru_ba_" + d_][0], 8))
        addp("bx_" + d_, _pcols(inp["cd_lru_bx_" + d_][0], 8))
        addp("lam_" + d_, _pcols(inp["cd_lru_lam_" + d_][0], 8))
    addp("fnorm", _pcols(inp["final_norm_w"], 16))
    for l in range(2):
        addp(f"norm{l}", _pcols(inp["norm_w"][l], 16))
        addp(f"bmod{l}", _pcols(inp["b_mod"][l], 48))
    addr("ab_dtb", np.concatenate([inp["ab_dt_bias_f"][0], inp["ab_dt_bias_b"][0]]))
    addr("ab_alog", np.concatenate([inp["ab_a_log_f"][0], inp["ab_a_log_b"][0]]))
    addr("cd_dtb", np.concatenate([inp["cd_dt_bias_f"][0], inp["cd_dt_bias_b"][0]]))
    addr("cd_alog", np.concatenate([inp["cd_a_log_f"][0], inp["cd_a_log_b"][0]]))
    return (np.ascontiguousarray(np.concatenate(pv, 1)), np.ascontiguousarray(np.concatenate(rv, 1)))


def build(npv, nrv, upto="all", debug=False, only=None, lim=None):
    nc = bass.Bass("TRN2", target_bir_lowering=False)
    es = ExitStack()
    dbg = {}
    with es:
        P = Prog(nc, es)
        skind = "ExternalOutput" if debug else "Internal"

        def din(name, shape, dt=F32):
            return P.dram(name, shape, dt, kind="ExternalInput")

        xT_d = din("xT", [D, S])
        cT_d = din("cT", [128, 16])
        wmod_d = din("w_mod", [2, D, 3 * D])
        w_in_d = [din("ab_w_in", [D, 5152]), din("cd_w_in", [D, 5152])]
        w_out_d = [din("ab_w_out", [D, D]), din("cd_w_out", [D, D])]
        lruw_d = din("lru_w", [4, 8, 128, 128])
        consts_d = din("consts", [128, NCONST])
        rope_d = din("rope", [2, 128, S])
        pv_d = din("pvec", [128, npv])
        rv_d = din("rvec", [128, nrv])
        out_d = P.dram("outT", [D, S], F32, kind="ExternalOutput")

        def scratch(name, shape, dt=F32):
            t = P.dram(name, shape, dt, kind=skind)
            dbg[name] = t
            return t

        if only is None:
            projT_d = scratch("projT", [5248, S])
            tokm_d = scratch("tokm", [128, 16, 320])
        else:
            projT_d = din("projT", [5248, S])
            tokm_d = din("tokm", [128, 16, 320])
        yT_d = scratch("yT", [D, S], BF16)
        x1T_d = scratch("x1T", [D, S])
        x2T_d = scratch("x2T", [D, S])

        cst = P.sb("cst", [128, NCONST])
        P.dma("sp", cst[:, 0:1536], consts_d[:, 0:1536], w=[cst])
        P.dma("sp", cst[:, 1536:NCONST], consts_d[:, 1536:NCONST], w=[cst])
        pvt = P.sb("pvt", [128, npv])
        P.dma("sp", pvt[:], pv_d[:], w=[pvt])
        rvt = P.sb("rvt", [128, nrv])
        P.dma("sp", rvt[:], rv_d[:], w=[rvt])
        onesb = P.sb("onesb", [128, 128], BF16)
        P.copy("dve", onesb[:], cst[:, C_OFF["ones"]:C_OFF["ones"] + 128], r=[cst], w=[onesb])
        identb = P.sb("identb", [128, 128], BF16)
        P.copy("dve", identb[:], cst[:, C_OFF["ident"]:C_OFF["ident"] + 128], r=[cst], w=[identb])

        def C(name, n=128):
            return cst[:, C_OFF[name]:C_OFF[name] + n]

        def PVc(name, j=0, n=1):
            o, _ = PV[name]
            return pvt[:, o + j:o + j + n]

        def RVc(name, j=0, n=1):
            o, _ = RV[name]
            return rvt[:, o + j:o + j + n]

        modsb = P.sb("modsb", [128, 2, 48])
        sc1 = P.sb("sc1", [128, 2, 16])

        def stage_mod():
            with P.scope():
                cond = P.sb("cond", [128, 16])
                P.dma("sp", cond[:], cT_d[:], w=[cond])
                P.act(cond[:], cond[:], AF.Silu, r=[cond], w=[cond])
                pm = [P.ps(f"pm{i}") for i in range(4)]
                wst = [P.sb(f"wst{i}", [128, 3 * D]) for i in range(2)]
                red = P.sb("mred", [128, 2, 48])
                for l in range(2):
                    for k in range(16):
                        t = wst[(l * 16 + k) % 2]
                        P.dma("sp", t[:, 0:3072], wmod_d[l, k * 128:(k + 1) * 128, 0:3072], w=[(t, 0)])
                        P.dma("sp", t[:, 3072:6144], wmod_d[l, k * 128:(k + 1) * 128, 3072:6144], w=[(t, 1)])
                        pt = pm[l * 2 + k // 8]
                        for e in range(48):
                            col = (k % 8) * 48 + e
                            P.mm(pt[:, col:col + 1], lhsT=t[:, e * 128:(e + 1) * 128], rhs=cond[:, k:k + 1],
                                 start=True, stop=True, r=[(t, e // 24), cond], w=[pt])
                for l in range(2):
                    for hf in range(2):
                        pt = pm[l * 2 + hf]
                        P.op("dve", lambda e: e.tensor_reduce(out=red[:, hf, :], in_=pt[:, 0:384].rearrange("p (k e) -> p e k", e=48),
                                                             axis=mybir.AxisListType.X, op=ALU.add), r=[pt], w=[(red, hf)])
                    P.tt("dve", modsb[:, l, :], red[:, 0, :], red[:, 1, :], ALU.add, r=[red], w=[(modsb, l)])
                    P.tt("dve", modsb[:, l, :], modsb[:, l, :], PVc(f"bmod{l}", 0, 48), ALU.add, r=[(modsb, l), pvt], w=[(modsb, l)])
                    P.stt("dve", sc1[:, l, :], modsb[:, l, 16:32], 1.0, PVc(f"norm{l}", 0, 16), ALU.add, ALU.mult,
                          r=[(modsb, l), pvt], w=[(sc1, l)])

        def stage_norm(xin_d, scale_ap, bias_ap, out_tile=None, out_dram=None):
            with P.scope():
                xs = [P.sb(f"xn{i}", [128, 16, 256]) for i in range(2)]
                sq = [P.sb(f"sq{i}", [128, 256]) for i in range(2)]
                rstd = [P.sb(f"rstd{i}", [128, 256]) for i in range(2)]
                pss = [P.ps(f"pss{i}") for i in range(2)]
                ob = [P.sb(f"ob{i}", [128, 16, 256]) for i in range(2)] if out_dram is not None else None
                xv = xin_d.h.ap().rearrange("(k p) t -> p k t", p=128)
                for tb in range(8):
                    x = xs[tb % 2]
                    tsl = slice(tb * 256, (tb + 1) * 256)
                    for hf in range(2):
                        P.dma("sp", x[:, hf * 8:(hf + 1) * 8, :], xv[:, hf * 8:(hf + 1) * 8, tsl], r=[xin_d], w=[(x, hf)])
                    ps_ = pss[tb % 2]
                    for k in range(16):
                        s_ = sq[k % 2]
                        P.act(s_[:], x[:, k, :], AF.Square, r=[(x, k // 8)], w=[s_])
                        P.mm(ps_[:, 0:256], lhsT=C("ones"), rhs=s_[:], start=(k == 0), stop=(k == 15), r=[s_, cst], w=[ps_])
                    rs = rstd[tb % 2]
                    P.act(rs[:], ps_[:, 0:256], AF.Ln, r=[ps_], w=[rs], scale=1.0 / D, bias=EPS)
                    P.act(rs[:], rs[:], AF.Exp, r=[rs], w=[rs], scale=-0.5)
                    for k in range(16):
                        P.tt("dve", x[:, k, :], x[:, k, :], rs[:], ALU.mult, r=[(x, k // 8), rs], w=[(x, k // 8)])
                        if out_tile is not None:
                            P.act(out_tile[:, k, tsl], x[:, k, :], AF.Identity, r=[(x, k // 8), modsb, sc1, pvt], w=[(out_tile, tb)],
                                  scale=scale_ap(k), bias=bias_ap(k))
                        else:
                            o = ob[tb % 2]
                            P.act(o[:, k, :], x[:, k, :], AF.Identity, r=[(x, k // 8), pvt], w=[o], scale=scale_ap(k), bias=bias_ap(k))
                    if out_dram is not None:
                        ov = out_dram.h.ap().rearrange("(k p) t -> p k t", p=128)
                        for hf in range(2):
                            P.dma("pool", ov[:, hf * 8:(hf + 1) * 8, tsl], ob[tb % 2][:, hf * 8:(hf + 1) * 8, :], r=[ob[tb % 2]], w=[(out_dram, tb)])

        def stage_inproj(hT, w_d, fm_chunks, tm_specs):
            with P.scope():
                wf = [P.sb(f"wf{i}", [128, 16, 256]) for i in range(2)]
                wb = [P.sb(f"wb{i}", [128, 16, 256], BF16) for i in range(2)]
                ot = [P.sb(f"ot{i}", [128, S]) for i in range(2)]
                otm = P.sb("otm", [128, 16, 256])
                pp = [P.ps(f"pp{i}") for i in range(3)]
                wv = w_d.h.ap().rearrange("(k p) c -> p k c", p=128)
                groups = []
                i = 0
                while i < len(fm_chunks):
                    g = [fm_chunks[i]]
                    if i + 1 < len(fm_chunks) and fm_chunks[i + 1][0] == fm_chunks[i][0] + 128 and fm_chunks[i][1] == 128:
                        g.append(fm_chunks[i + 1])
                        i += 1
                    i += 1
                    groups.append(("fm", g))
                for sp_ in tm_specs:
                    groups.append(("tm", [sp_]))
                npp = 0
                nout = 0
                for gi, (kind, g) in enumerate(groups):
                    c0 = g[0][0]
                    ncol = sum(x[1] for x in g)
                    f_, b_ = wf[gi % 2], wb[gi % 2]
                    for q4 in range(4):
                        P.dma("sp", f_[:, q4 * 4:(q4 + 1) * 4, 0:ncol], wv[:, q4 * 4:(q4 + 1) * 4, c0:c0 + ncol], w=[(f_, q4)])
                    for q4 in range(4):
                        P.copy("dve" if q4 % 2 == 0 else "act", b_[:, q4 * 4:(q4 + 1) * 4, 0:ncol], f_[:, q4 * 4:(q4 + 1) * 4, 0:ncol],
                               r=[(f_, q4)], w=[(b_, q4)])
                    if kind == "fm":
                        for ci, (cc0, cn, row0) in enumerate(g):
                            o_ = ot[nout % 2]
                            nout += 1
                            for tb in range(4):
                                p_ = pp[npp % 3]
                                npp += 1
                                for k in range(16):
                                    P.mm(p_[0:cn, :], lhsT=b_[:, k, ci * 128:ci * 128 + cn], rhs=hT[:, k, tb * 512:(tb + 1) * 512],
                                         start=(k == 0), stop=(k == 15), r=[(b_, k // 4), hT], w=[p_])
                                P.copy("act" if tb % 2 == 0 else "dve", o_[0:cn, tb * 512:(tb + 1) * 512], p_[0:cn, :], r=[p_], w=[(o_, tb)])
                            P.dma("pool", projT_d[row0:row0 + cn, :], o_[0:cn, :], r=[o_], w=[(projT_d, row0 // 128)])
                    else:
                        (cc0, cn, toff) = g[0]
                        o_ = otm
                        ov = o_[:, :, 0:cn]
                        for tb in range(16):
                            p_ = pp[npp % 3]
                            npp += 1
                            for k in range(16):
                                P.mm(p_[:, 0:cn], lhsT=hT[:, k, tb * 128:(tb + 1) * 128], rhs=b_[:, k, 0:cn],
                                     start=(k == 0), stop=(k == 15), r=[(b_, k // 4), hT], w=[p_])
                            P.copy("act" if tb % 2 == 0 else "dve", ov[:, tb, :], p_[:, 0:cn], r=[p_], w=[(o_, tb % 4)])
                        P.dma("pool", tokm_d[:, :, toff:toff + cn], ov, r=[o_], w=[(tokm_d, toff)])

        def stage_outproj(w_d, xin_d, xout_d, l):
            with P.scope():
                yt = P.sb("yt", [128, 16, S], BF16)
                yv = yT_d.h.ap().rearrange("(k p) t -> p k t", p=128)
                for q4 in range(8):
                    P.dma("sp", yt[:, q4 * 2:(q4 + 1) * 2, :], yv[:, q4 * 2:(q4 + 1) * 2, :], r=[yT_d], w=[(yt, q4)])
                wf = [P.sb(f"owf{i}", [128, 16, 128]) for i in range(2)]
                wb = [P.sb(f"owb{i}", [128, 16, 128], BF16) for i in range(2)]
                xo = [P.sb(f"xo{i}", [128, S]) for i in range(2)]
                pp = [P.ps(f"op{i}") for i in range(3)]
                wv = w_d.h.ap().rearrange("(k p) c -> p k c", p=128)
                npp = 0
                for dc in range(16):
                    f_, b_ = wf[dc % 2], wb[dc % 2]
                    for q4 in range(2):
                        P.dma("sp", f_[:, q4 * 8:(q4 + 1) * 8, :], wv[:, q4 * 8:(q4 + 1) * 8, dc * 128:(dc + 1) * 128], w=[(f_, q4)])
                        P.copy("dve" if q4 == 0 else "act", b_[:, q4 * 8:(q4 + 1) * 8, :], f_[:, q4 * 8:(q4 + 1) * 8, :], r=[(f_, q4)], w=[(b_, q4)])
                    x_ = xo[dc % 2]
                    P.dma("sp", x_[:], xin_d[dc * 128:(dc + 1) * 128, :], r=[xin_d], w=[x_])
                    for tb in range(4):
                        p_ = pp[npp % 3]
                        npp += 1
                        for k in range(16):
                            P.mm(p_[:], lhsT=b_[:, k, :], rhs=yt[:, k, tb * 512:(tb + 1) * 512], start=(k == 0), stop=(k == 15),
                                 r=[(b_, k // 8), yt], w=[p_])
                        P.stt("dve", x_[:, tb * 512:(tb + 1) * 512], p_[:], modsb[:, l, 32 + dc:33 + dc], x_[:, tb * 512:(tb + 1) * 512],
                              ALU.mult, ALU.add, r=[p_, x_, modsb], w=[x_])
                    P.dma("pool", xout_d[dc * 128:(dc + 1) * 128, :], x_[:], r=[x_], w=[(xout_d, dc)])

        def conv_fm(src_row0, cwname, cbname, chunk, dst, xp, silu, tagr=()):
            P.dma("sp", xp[:, 2:S + 2], projT_d[src_row0:src_row0 + 128, :], r=[(projT_d, src_row0 // 128)], w=[xp])
            o, _ = PV[cwname]
            wc = lambda j: pvt[:, o + chunk * 4 + j:o + chunk * 4 + j + 1]
            P.ts("dve", dst[:], xp[:, 0:S], wc(0), PVc(cbname, chunk), ALU.mult, ALU.add, r=[xp, pvt], w=[dst])
            for j in range(1, 4):
                P.stt("dve", dst[:], xp[:, j:S + j], wc(j), dst[:], ALU.mult, ALU.add, r=[xp, pvt, dst], w=[dst])
            if silu:
                P.act(dst[:], dst[:], AF.Silu, r=[dst], w=[dst])

        def pad_init(xp):
            P.op("dve", lambda e: e.memset(xp[:, 0:2], 0.0), w=[xp])
            P.op("dve", lambda e: e.memset(xp[:, S + 2:S + 3], 0.0), w=[xp])

        def stage_attn():
            with P.scope():
                rope = P.sb("rope", [128, 2, S])
                P.dma("sp", rope[:, 0, :], rope_d[0], w=[(rope, 0)])
                P.dma("sp", rope[:, 1, :], rope_d[1], w=[(rope, 1)])
                qk = P.sb("qkr", [128, 10, S], BF16)
                vt = P.sb("vt", [128, 16, 256], BF16)
                vf = P.sb("vf", [128, 16, 256])
                P.dma("sp", vf[:], tokm_d[:, :, 0:256], r=[(tokm_d, 0)], w=[vf])
                P.copy("dve", vt[:], vf[:], r=[vf], w=[vt])
                raw = [P.sb(f"raw{i}", [128, 512]) for i in range(2)]
                sq = [P.sb(f"asq{i}", [128, 512]) for i in range(2)]
                rs = [P.sb(f"ars{i}", [128, 512]) for i in range(2)]
                qn = [P.sb(f"aqn{i}", [128, 512]) for i in range(2)]
                t1 = [P.sb(f"at1{i}", [128, 512]) for i in range(2)]
                t2 = [P.sb(f"at2{i}", [128, 512]) for i in range(2)]
                pa = [P.ps(f"pa{i}") for i in range(2)]
                pb = [P.ps(f"pb{i}") for i in range(2)]
                it = 0
                for hh in range(10):
                    row0 = hh * 128 if hh < 8 else 1024 + (hh - 8) * 128
                    wn = PVc("q_norm") if hh < 8 else PVc("k_norm")
                    for tb in range(4):
                        b = it % 2
                        it += 1
                        tsl = slice(tb * 512, (tb + 1) * 512)
                        P.dma("sp", raw[b][:], projT_d[row0:row0 + 128, tsl], r=[(projT_d, row0 // 128)], w=[raw[b]])
                        P.act(sq[b][:], raw[b][:], AF.Square, r=[raw[b]], w=[sq[b]])
                        P.mm(pa[b][:], lhsT=C("ones"), rhs=sq[b][:], start=True, stop=True, r=[sq[b], cst], w=[pa[b]])
                        P.act(rs[b][:], pa[b][:], AF.Ln, r=[pa[b]], w=[rs[b]], scale=1.0 / 128, bias=EPS)
                        P.act(rs[b][:], rs[b][:], AF.Exp, r=[rs[b]], w=[rs[b]], scale=-0.5)
                        P.stt("dve", qn[b][:], raw[b][:], wn, rs[b][:], ALU.mult, ALU.mult, r=[raw[b], rs[b], pvt], w=[qn[b]])
                        P.mm(pb[b][:], lhsT=C("Rt"), rhs=qn[b][:], start=True, stop=True, r=[qn[b], cst], w=[pb[b]])
                        P.tt("dve", t1[b][:], qn[b][:], rope[:, 0, tsl], ALU.mult, r=[qn[b], (rope, 0)], w=[t1[b]])
                        P.tt("dve", t2[b][:], pb[b][:], rope[:, 1, tsl], ALU.mult, r=[pb[b], (rope, 1)], w=[t2[b]])
                        P.tt("dve", qk[:, hh, tsl], t1[b][:], t2[b][:], ALU.add, r=[t1[b], t2[b]], w=[(qk, hh * 4 + tb)])
                pT = [P.sb(f"pT{i}", [128, 512], BF16) for i in range(3)]
                ps_ = [P.ps(f"psc{i}") for i in range(2)]
                po = [P.ps(f"po{i}") for i in range(2)]
                gt = [P.sb(f"gt{i}", [128, 512]) for i in range(2)]
                rd = [P.sb(f"rd{i}", [128, 512]) for i in range(2)]
                yo = [P.sb(f"yo{i}", [128, 512], BF16) for i in range(2)]
                scale = 128.0 ** -0.5
                it = 0
                n3 = 0
                groups = [(h, qb) for h in range(8) for qb in range(4)]
                items = [(gi, kc) for gi in range(len(groups)) for kc in range(16)]

                def epilogue(gi):
                    h, qb = groups[gi]
                    b = gi % 2
                    qsl = slice(qb * 512, (qb + 1) * 512)
                    P.ts("dve", gt2[b][:], gt2[b][:], 1.0, None, ALU.add, None, r=[gt2[b]], w=[gt2[b]])
                    P.op("dve", lambda e: e.reciprocal(out=gt2[b][:], in_=gt2[b][:]), r=[gt2[b]], w=[gt2[b]])
                    P.op("dve", lambda e: e.reciprocal(out=rd[b][:], in_=pa[b][:]), r=[pa[b]], w=[rd[b]])
                    P.tt("dve", rd[b][:], rd[b][:], gt2[b][:], ALU.mult, r=[rd[b], gt2[b]], w=[rd[b]])
                    P.tt("dve", rd[b][:], rd[b][:], gt[b][:], ALU.mult, r=[rd[b], gt[b]], w=[rd[b]])
                    P.tt("dve", yo[b][:], po[b][:], rd[b][:], ALU.mult, r=[po[b], rd[b]], w=[yo[b]])
                    P.dma("pool", yT_d[h * 128:(h + 1) * 128, qsl], yo[b][:], r=[yo[b]], w=[(yT_d, h)])

                def tail(idx):
                    gi, kc = items[idx]
                    h, qb = groups[gi]
                    g = h // 4
                    b = gi % 2
                    p_ = pT[idx % 3]
                    P.mm(po[b][:], lhsT=vt[:, kc, g * 128:(g + 1) * 128], rhs=p_[:], start=(kc == 0), stop=(kc == 15),
                         r=[vt, p_], w=[po[b]])
                    P.mm(pa[b][:], lhsT=onesb[:], rhs=p_[:], start=(kc == 0), stop=(kc == 15), r=[onesb, p_], w=[pa[b]])
                    if kc == 15:
                        epilogue(gi)

                gt2 = [P.sb(f"gtb{i}", [128, 512]) for i in range(2)]
                for idx, (gi, kc) in enumerate(items):
                    h, qb = groups[gi]
                    g = h // 4
                    b = gi % 2
                    qsl = slice(qb * 512, (qb + 1) * 512)
                    if kc == 0:
                        P.dma("sp", gt[b][:], projT_d[1536 + h * 128:1536 + (h + 1) * 128, qsl], r=[(projT_d, 12 + h)], w=[gt[b]])
                    s_ = ps_[idx % 2]
                    P.mm(s_[:], lhsT=qk[:, 8 + g, kc * 128:(kc + 1) * 128], rhs=qk[:, h, qsl], start=True, stop=True,
                         r=[(qk, (8 + g) * 4 + kc // 4), (qk, h * 4 + qb)], w=[s_])
                    if idx > 0:
                        tail(idx - 1)
                    P.act(pT[idx % 3][:], s_[:], AF.Exp, r=[s_], w=[pT[idx % 3]], scale=scale)
                    if kc == 0:
                        P.act(gt2[b][:], gt[b][:], AF.Exp, r=[gt[b]], w=[gt2[b]], scale=-1.0)
                tail(len(items) - 1)

        def stage_ssd():
            with P.scope():
                xp = P.sb("xp", [128, S + 3])
                pad_init(xp)
                BT2 = [P.sb(f"BT{g}", [128, S], BF16) for g in range(2)]
                CTb2 = [P.sb(f"CTb{g}", [128, S], BF16) for g in range(2)]
                Btm2 = [P.sb(f"Btm{g}", [128, 16, 128], BF16) for g in range(2)]
                GT2 = [P.sb(f"GT{g}", [128, 16, 128]) for g in range(2)]
                tmp = P.sb("ctmp", [128, S])
                pg = [P.ps(f"pg{i}") for i in range(4)]
                py = [P.ps(f"py{i}") for i in range(4)]
                for g in range(2):
                    BT, CTb, Btm, GT = BT2[g], CTb2[g], Btm2[g], GT2[g]
                    conv_fm(3584 + g * 128, "ab_cw", "ab_cb", 8 + g, tmp, xp, True)
                    P.copy("act", BT[:], tmp[:], r=[tmp], w=[BT])
                    for c in range(16):
                        p_ = pg[c % 4]
                        P.tr(p_[:, 0:128], tmp[:, c * 128:(c + 1) * 128], C("ident"), r=[tmp, cst], w=[p_])
                        P.copy("dve", Btm[:, c, :], p_[:, 0:128], r=[p_], w=[Btm])
                    conv_fm(3840 + g * 128, "ab_cw", "ab_cb", 10 + g, tmp, xp, True)
                    P.copy("act", CTb[:], tmp[:], r=[tmp], w=[CTb])
                    for c in range(16):
                        p_ = py[c % 4]
                        csl = slice(c * 128, (c + 1) * 128)
                        P.mm(p_[:, 0:128], lhsT=BT[:, csl], rhs=CTb[:, csl], start=True, stop=True, r=[BT, CTb], w=[p_])
                        P.copy("dve", GT[:, c, :], p_[:, 0:128], r=[p_], w=[GT])
                dt = P.sb("dt", [128, 16, 32])
                av = P.sb("av", [128, 16, 32])
                Aneg = P.sb("Aneg", [128, 32])
                P.dma("sp", dt[:], tokm_d[:, :, 256:288], r=[tokm_d], w=[dt])
                P.tt("dve", dt[:], dt[:], RVc("ab_dtb", 0, 32).unsqueeze(1).to_broadcast([128, 16, 32]), ALU.add, r=[dt, rvt], w=[dt])
                P.act(dt[:], dt[:], AF.Exp, r=[dt], w=[dt])
                P.act(dt[:], dt[:], AF.Ln, r=[dt], w=[dt], bias=1.0)
                P.act(Aneg[:], RVc("ab_alog", 0, 32), AF.Exp, r=[rvt], w=[Aneg])
                P.stt("dve", av[:], dt[:], -1.0, Aneg[:].unsqueeze(1).to_broadcast([128, 16, 32]), ALU.mult, ALU.mult, r=[dt, Aneg], w=[av])

                xsT = [P.sb(f"xsT{i}", [128, S]) for i in range(2)]
                Y = [P.sb(f"Y{i}", [128, S]) for i in range(2)]
                xtm = [P.sb(f"xtm{i}", [128, 16, 128], BF16) for i in range(2)]
                xpad = [[P.sb(f"xpad{i}{h}", [128, 16, 128], BF16) for h in range(2)] for i in range(2)]
                Vall = P.sb("Vall", [128, 8, S], BF16)
                NC_ = 4
                Sm = [P.sb(f"Sm{i}", [128, 128]) for i in range(NC_)]
                Spad = [[P.sb(f"Spad{i}{h}", [128, 128], BF16) for h in range(2)] for i in range(NC_)]
                Rt_ = [P.sb(f"R{i}", [128, 2, 128]) for i in range(NC_)]
                EG = [P.sb(f"EG{i}", [128, 2, 128]) for i in range(NC_)]
                DC = [P.sb(f"DC{i}", [128, 2, 128]) for i in range(NC_)]
                kds = [P.sb(f"kds{i}", [128, 2]) for i in range(NC_)]
                AT = [P.sb(f"AT{i}", [128, 2, 128], BF16) for i in range(NC_)]
                QD = [P.sb(f"QD{i}", [128, 2, 128], BF16) for i in range(NC_)]
                KD = [P.sb(f"KD{i}", [128, 2, 128], BF16) for i in range(NC_)]
                nst = 16

                def chain(pi, hp, d, ci):
                    g_, y_ = pg[ci], py[ci]
                    CTb, Btm, GT = CTb2[hp // 4], Btm2[hp // 4], GT2[hp // 4]
                    U = C("Uf") if d == 0 else C("Ub")
                    M = C("Mf") if d == 0 else C("Mb")
                    NG = C("NEGf", 256) if d == 0 else C("NEGb", 256)
                    hcol = slice(d * 16 + hp * 2, d * 16 + hp * 2 + 2)
                    ccol = 127 if d == 0 else 0
                    P.op("dve", lambda e: e.memset(Sm[ci][:], 0.0), w=[Sm[ci]])
                    for h in range(2):
                        P.op("dve", lambda e: e.memset(Spad[ci][h][:], 0.0), w=[Spad[ci][h]])
                    for step in range(nst):
                        c = step if d == 0 else 15 - step
                        csl = slice(c * 128, (c + 1) * 128)
                        a2 = av[:, c, hcol]
                        R_ = Rt_[ci]
                        P.tt("dve", R_[:], U.unsqueeze(1).to_broadcast([128, 2, 128]), a2.unsqueeze(2).to_broadcast([128, 2, 128]),
                             ALU.mult, r=[cst, av], w=[R_])
                        Rf = R_[:].rearrange("p h i -> p (h i)")
                        P.mm(g_[:, 0:256], lhsT=C("ones"), rhs=Rf, start=True, stop=True, r=[R_, cst], w=[g_])
                        P.mm(g_[:, 256:512], lhsT=M, rhs=Rf, start=True, stop=False, r=[R_, cst], w=[g_])
                        P.mm(g_[:, 256:512], lhsT=C("ident"), rhs=NG, start=False, stop=True, r=[cst], w=[g_])
                        P.mm(y_[:, 256:258], lhsT=M, rhs=a2, start=True, stop=True, r=[av, cst], w=[y_])
                        yield
                        P.act(EG[ci][:].rearrange("p h i -> p (h i)"), g_[:, 0:256], AF.Exp, r=[g_], w=[EG[ci]])
                        P.act(DC[ci][:].rearrange("p h i -> p (h i)"), g_[:, 256:512], AF.Exp, r=[g_], w=[DC[ci]])
                        P.act(kds[ci][:], y_[:, 256:258], AF.Exp, r=[y_], w=[kds[ci]])
                        P.tt("dve", kds[ci][:], kds[ci][:], dt[:, c, hcol], ALU.mult, r=[kds[ci], dt], w=[kds[ci]])
                        for h in range(2):
                            P.stt("dve", AT[ci][:, h, :], DC[ci][:, h, :], dt[:, c, d * 16 + hp * 2 + h:d * 16 + hp * 2 + h + 1],
                                  GT[:, c, :], ALU.mult, ALU.mult, r=[DC[ci], dt, GT], w=[AT[ci]])
                        P.tt("dve", QD[ci][:], CTb[:, csl].unsqueeze(1).to_broadcast([128, 2, 128]), EG[ci][:], ALU.mult,
                             r=[CTb, EG[ci]], w=[QD[ci]])
                        P.tt("dve", KD[ci][:], Btm[:, c, :].unsqueeze(1).to_broadcast([128, 2, 128]),
                             kds[ci][:].unsqueeze(2).to_broadcast([128, 2, 128]), ALU.mult, r=[Btm, kds[ci]], w=[KD[ci]])
                        for h in range(2):
                            P.mm(y_[:, 0:128], lhsT=Spad[ci][h][:], rhs=QD[ci][:, h, :], start=(h == 0), stop=False,
                                 r=[Spad[ci][h], QD[ci]], w=[y_])
                        for h in range(2):
                            P.mm(y_[:, 0:128], lhsT=xpad[pi][h][:, c, :], rhs=AT[ci][:, h, :], start=False, stop=(h == 1),
                                 r=[xpad[pi][h], AT[ci]], w=[y_])
                        for h in range(2):
                            P.mm(y_[:, 128 + h * 64:128 + (h + 1) * 64], lhsT=KD[ci][:, h, :], rhs=xtm[pi][:, c, h * 64:(h + 1) * 64],
                                 start=True, stop=True, r=[KD[ci], xtm[pi]], w=[y_])
                        yield
                        P.tt("dve", Y[pi][:, csl], Y[pi][:, csl], y_[:, 0:128], ALU.add, r=[y_, (Y[pi], c)], w=[(Y[pi], c)])
                        for h in range(2):
                            hs = slice(h * 64, (h + 1) * 64)
                            P.stt("dve", Sm[ci][:, hs], Sm[ci][:, hs], EG[ci][:, h, ccol:ccol + 1], y_[:, 128 + h * 64:128 + (h + 1) * 64],
                                  ALU.mult, ALU.add, r=[Sm[ci], EG[ci], y_], w=[Sm[ci]])
                            P.copy("act", Spad[ci][h][:, hs], Sm[ci][:, hs], r=[Sm[ci]], w=[Spad[ci][h]])

                for rnd in range(4):
                    for pi in range(2):
                        hp = rnd * 2 + pi
                        conv_fm(2560 + hp * 128, "ab_cw", "ab_cb", hp, xsT[pi], xp, True)
                        for c in range(16):
                            p_ = pg[c % 4]
                            P.tr(p_[:, 0:128], xsT[pi][:, c * 128:(c + 1) * 128], C("ident"), r=[xsT[pi], cst], w=[p_])
                            P.copy("act", xtm[pi][:, c, :], p_[:, 0:128], r=[p_], w=[xtm[pi]])
                        P.ts("dve", Y[pi][:], xsT[pi][:], PVc("d_skip", hp), None, ALU.mult, None, r=[xsT[pi], pvt], w=[Y[pi]])
                        for h in range(2):
                            P.op("dve", lambda e: e.memset(xpad[pi][h][:], 0.0), w=[xpad[pi][h]])
                            P.copy("dve", xpad[pi][h][:, :, h * 64:(h + 1) * 64], xtm[pi][:, :, h * 64:(h + 1) * 64], r=[xtm[pi]], w=[xpad[pi][h]])
                    gens = [chain(pi, rnd * 2 + pi, d, pi * 2 + d) for pi in range(2) for d in range(2)]
                    while gens:
                        for g_ in list(gens):
                            try:
                                next(g_)
                            except StopIteration:
                                gens.remove(g_)
                    for pi in range(2):
                        hp = rnd * 2 + pi
                        P.dma("sp", tmp[:], projT_d[4128 + hp * 128:4128 + (hp + 1) * 128, :], r=[projT_d], w=[tmp])
                        P.act(tmp[:], tmp[:], AF.Silu, r=[tmp], w=[tmp])
                        P.tt("dve", Vall[:, hp, :], Y[pi][:], tmp[:], ALU.mult, r=[Y[pi], tmp], w=[(Vall, hp)])
                sqb = [P.sb(f"ssq{i}", [128, 512], BF16) for i in range(2)]
                rsb = P.sb("srs", [128, 512])
                ob = [P.sb(f"sob{i}", [128, 512], BF16) for i in range(2)]
                for tb in range(4):
                    tsl = slice(tb * 512, (tb + 1) * 512)
                    for hp in range(8):
                        P.tt("dve", sqb[hp % 2][:], Vall[:, hp, tsl], Vall[:, hp, tsl], ALU.mult, r=[(Vall, hp)], w=[sqb[hp % 2]])
                        P.mm(pg[0][:], lhsT=onesb[:], rhs=sqb[hp % 2][:], start=(hp == 0), stop=(hp == 7), r=[sqb[hp % 2], onesb], w=[pg[0]])
                    P.act(rsb[:], pg[0][:], AF.Ln, r=[pg[0]], w=[rsb], scale=1.0 / 1024, bias=EPS)
                    P.act(rsb[:], rsb[:], AF.Exp, r=[rsb], w=[rsb], scale=-0.5)
                    for hp in range(8):
                        o_ = ob[hp % 2]
                        P.stt("dve", o_[:], Vall[:, hp, tsl], PVc("ssd_norm", hp), rsb[:], ALU.mult, ALU.mult, r=[(Vall, hp), rsb, pvt], w=[o_])
                        P.dma("pool", yT_d[1024 + hp * 128:1024 + (hp + 1) * 128, tsl], o_[:], r=[o_], w=[(yT_d, 8 + hp)])

        def stage_gdn():
            GP = "dve"
            with P.scope():
                xp = P.sb("gxp", [128, S + 3])
                pad_init(xp)
                tmp = P.sb("gtmp", [128, S])
                gt = P.sb("ggt", [128, 16, 32])
                P.dma("sp", gt[:], tokm_d[:, :, 0:32], r=[tokm_d], w=[gt])
                beta = P.sb("gbeta", [128, 16, 16])
                nbeta = P.sb("gnbeta", [128, 16, 16])
                gg = P.sb("ggg", [128, 16, 16])
                An = P.sb("gAn", [128, 16])
                P.act(beta[:], gt[:, :, 0:16], AF.Exp, r=[gt], w=[beta], scale=-1.0)
                P.ts("dve", beta[:], beta[:], 1.0, None, ALU.add, None, r=[beta], w=[beta])
                P.op("dve", lambda e: e.reciprocal(out=beta[:], in_=beta[:]), r=[beta], w=[beta])
                P.ts("dve", nbeta[:], beta[:], -1.0, None, ALU.mult, None, r=[beta], w=[nbeta])
                P.tt("dve", gg[:], gt[:, :, 16:32], RVc("cd_dtb", 0, 16).unsqueeze(1).to_broadcast([128, 16, 16]), ALU.add, r=[gt, rvt], w=[gg])
                P.act(gg[:], gg[:], AF.Exp, r=[gg], w=[gg])
                P.act(gg[:], gg[:], AF.Ln, r=[gg], w=[gg], bias=1.0)
                P.act(An[:], RVc("cd_alog", 0, 16), AF.Exp, r=[rvt], w=[An])
                P.stt("dve", gg[:], gg[:], -1.0, An[:].unsqueeze(1).to_broadcast([128, 16, 16]), ALU.mult, ALU.mult, r=[gg, An], w=[gg])

                qT = P.sb("gqT", [128, S])
                kT = P.sb("gkT", [128, S])
                ktm = P.sb("gktm", [128, 16, 128])
                vtm = P.sb("gvtm", [128, 16, 256])
                O = P.sb("gO", [128, 2, S])
                sq = [P.sb(f"gsq{i}", [128, 512]) for i in range(2)]
                rsb = [P.sb(f"grs{i}", [128, 512]) for i in range(2)]
                NS = 4
                RR = [P.sb(f"gRR{i}", [128, 256]) for i in range(NS)]
                E = [P.sb(f"gE{i}", [128, 388]) for i in range(NS)]
                X = [[P.sb(f"gX{i}_{j}", [128, 128]) for j in range(2)] for i in range(NS)]
                XT = [[P.sb(f"gXT{i}_{j}", [128, 128]) for j in range(2)] for i in range(NS)]
                PT = [[P.sb(f"gPT{i}_{j}", [128, 128]) for j in range(2)] for i in range(NS)]
                AT = [P.sb(f"gAT{i}", [128, 128]) for i in range(NS)]
                vb = [P.sb(f"gvb{i}", [128, 128]) for i in range(NS)]
                kbg = [P.sb(f"gkbg{i}", [128, 128]) for i in range(NS)]
                bge = [P.sb(f"gbge{i}", [128, 1]) for i in range(NS)]
                u_ = [P.sb(f"gu{i}", [128, 128]) for i in range(NS)]
                wT = [P.sb(f"gwT{i}", [128, 128]) for i in range(NS)]
                qd = [P.sb(f"gqd{i}", [128, 128]) for i in range(NS)]
                kdec = [P.sb(f"gkd{i}", [128, 128]) for i in range(NS)]
                vnew = [P.sb(f"gvn{i}", [128, 128]) for i in range(NS)]
                Sst = [P.sb(f"gS{d}", [128, 128]) for d in range(4)]
                ybf = [P.sb(f"gyb{i}", [128, 512], BF16) for i in range(2)]
                pX = [P.ps(f"gpX{i}") for i in range(4)]
                pY = [P.ps(f"gpY{i}") for i in range(4)]
                pA = pX
                qscale = 128.0 ** -0.5
                nhq = 4 if lim is None else lim[0]
                nst = 16 if lim is None else lim[1]
                cut = 0 if (lim is None or len(lim) < 3) else lim[2]

                def l2norm_fm(dst, scl):
                    for tb in range(4):
                        tsl = slice(tb * 512, (tb + 1) * 512)
                        b = tb % 2
                        P.act(sq[b][:], tmp[:, tsl], AF.Square, r=[tmp], w=[sq[b]])
                        P.mm(pA[b][:], lhsT=C("ones"), rhs=sq[b][:], start=True, stop=True, r=[sq[b], cst], w=[pA[b]])
                        P.act(rsb[b][:], pA[b][:], AF.Ln, r=[pA[b]], w=[rsb[b]], bias=EPS)
                        P.act(rsb[b][:], rsb[b][:], AF.Exp, r=[rsb[b]], w=[rsb[b]], scale=-0.5)
                        P.stt("dve", dst[:, tsl], tmp[:, tsl], scl, rsb[b][:], ALU.mult, ALU.mult, r=[tmp, rsb[b]], w=[dst])

                for hq in range(nhq):
                    conv_fm(hq * 128, "cd_cw", "cd_cb", hq, tmp, xp, True)
                    l2norm_fm(qT, qscale)
                    conv_fm(512 + hq * 128, "cd_cw", "cd_cb", 4 + hq, tmp, xp, True)
                    l2norm_fm(kT, 1.0)
                    for c in range(16):
                        p_ = pA[c % 2]
                        P.tr(p_[:, 0:128], kT[:, c * 128:(c + 1) * 128], C("ident"), r=[kT, cst], w=[p_])
                        P.copy("act", ktm[:, c, :], p_[:, 0:128], r=[p_], w=[ktm])
                    for e in range(2):
                        conv_fm(1024 + (2 * hq + e) * 128, "cd_cw", "cd_cb", 8 + 2 * hq + e, tmp, xp, True)
                        for c in range(16):
                            p_ = pA[c % 2]
                            P.tr(p_[:, 0:128], tmp[:, c * 128:(c + 1) * 128], C("ident"), r=[tmp, cst], w=[p_])
                            P.copy("act", vtm[:, c, e * 128:(e + 1) * 128], p_[:, 0:128], r=[p_], w=[vtm])
                    P.op("dve", lambda e_: e_.memset(O[:], 0.0), w=[O])
                    def chain(e, d, ci):
                        hv = 2 * hq + e
                        S_ = Sst[ci]
                        a_ = pX[ci]
                        c_ = pY[ci]
                        bi = ci
                        UM = C("Uf", 256) if d == 0 else C("Ub", 256)
                        U, M = UM[:, 0:128], UM[:, 128:256]
                        NGi = C("NEGf") if d == 0 else C("NEGb")
                        NGs = C("NEGsf") if d == 0 else C("NEGsb")
                        col = d * 8 + hv
                        ccol = 127 if d == 0 else 0
                        P.op("dve", lambda e_: e_.memset(S_[:], 0.0), w=[S_])
                        for step in range(nst):
                            c = step if d == 0 else 15 - step
                            csl = slice(c * 128, (c + 1) * 128)
                            gcol = gg[:, c, col:col + 1]
                            bcol = beta[:, c, col:col + 1]
                            nbcol = nbeta[:, c, col:col + 1]
                            P.ts("pool", RR[bi][:], UM, gcol, None, ALU.mult, None, r=[cst, gg], w=[RR[bi]])
                            R1, R2 = RR[bi][:, 0:128], RR[bi][:, 128:256]
                            P.mm(a_[:, 0:128], lhsT=C("ones"), rhs=R1, start=True, stop=True, r=[RR[bi], cst], w=[a_])
                            P.mm(a_[:, 128:256], lhsT=M, rhs=R1, start=True, stop=False, r=[RR[bi], cst], w=[a_])
                            P.mm(a_[:, 128:256], lhsT=C("ident"), rhs=NGi, start=False, stop=True, r=[cst], w=[a_])
                            P.mm(a_[:, 256:384], lhsT=U, rhs=R2, start=True, stop=False, r=[RR[bi], cst], w=[a_])
                            P.mm(a_[:, 256:384], lhsT=C("ident"), rhs=NGs, start=False, stop=True, r=[cst], w=[a_])
                            P.mm(a_[:, 384:385], lhsT=M, rhs=gcol, start=True, stop=True, r=[gg, cst], w=[a_])
                            P.mm(a_[:, 385:386], lhsT=U, rhs=gcol, start=True, stop=True, r=[gg, cst], w=[a_])
                            yield
                            E_ = E[bi]
                            P.act(E_[:, 0:386], a_[:, 0:386], AF.Exp, r=[a_], w=[E_])
                            EGb, decT, decS = E_[:, 0:128], E_[:, 128:256], E_[:, 256:384]
                            kds, eg = E_[:, 384:385], E_[:, 385:386]
                            P.mm(c_[:, 0:128], lhsT=kT[:, csl], rhs=kT[:, csl], start=True, stop=True, r=[kT], w=[c_])
                            P.mm(c_[:, 128:256], lhsT=kT[:, csl], rhs=qT[:, csl], start=True, stop=True, r=[kT, qT], w=[c_])
                            yield
                            X0 = X[bi][0]
                            P.stt("dve", X0[:], c_[:, 0:128], nbcol, decS, ALU.mult, ALU.mult, r=[c_, nbeta, E_], w=[X0])
                            P.tt("dve", AT[bi][:], c_[:, 128:256], decT, ALU.mult, r=[c_, E_], w=[AT[bi]])
                            P.act(vb[bi][:], vtm[:, c, e * 128:(e + 1) * 128], AF.Copy, r=[vtm, beta], w=[vb[bi]], scale=bcol)
                            P.tt("dve", bge[bi][:], bcol, eg, ALU.mult, r=[beta, E_], w=[bge[bi]])
                            P.act(kbg[bi][:], ktm[:, c, :], AF.Copy, r=[ktm, bge[bi]], w=[kbg[bi]], scale=bge[bi][:])
                            P.tt("dve", qd[bi][:], qT[:, csl], EGb, ALU.mult, r=[qT, E_], w=[qd[bi]])
                            P.act(kdec[bi][:], ktm[:, c, :], AF.Copy, r=[ktm, E_], w=[kdec[bi]], scale=kds)
                            P.tr(a_[:, 0:128], X0[:], C("ident"), r=[X0, cst], w=[a_])
                            yield
                            P.copy("act", XT[bi][0][:], a_[:, 0:128], r=[a_], w=[XT[bi][0]])
                            P.tt("dve", PT[bi][0][:], a_[:, 0:128], C("ident"), ALU.add, r=[a_, cst], w=[PT[bi][0]])
                            for k in range(1, 7):
                                xo, xn = X[bi][(k - 1) % 2], X[bi][k % 2]
                                to, tn = XT[bi][(k - 1) % 2], XT[bi][k % 2]
                                po_, pn = PT[bi][(k - 1) % 2], PT[bi][k % 2]
                                P.mm(c_[:, 128:256], lhsT=to[:], rhs=xo[:], start=True, stop=True, r=[to, xo], w=[c_])
                                if k < 6:
                                    P.mm(c_[:, 0:128], lhsT=xo[:], rhs=to[:], start=True, stop=True, r=[to, xo], w=[c_])
                                yield
                                P.copy("act", xn[:], c_[:, 128:256], r=[c_], w=[xn])
                                if k < 6:
                                    P.copy("act", tn[:], c_[:, 0:128], r=[c_], w=[tn])
                                P.mm(a_[:, 256:384], lhsT=xn[:], rhs=po_[:], start=True, stop=True, r=[xn, po_], w=[a_])
                                yield
                                P.tt("dve", pn[:], a_[:, 256:384], po_[:], ALU.add, r=[a_, po_], w=[pn])
                            TT = PT[bi][0]
                            P.mm(c_[:, 256:384], lhsT=TT[:], rhs=vb[bi][:], start=True, stop=True, r=[TT, vb[bi]], w=[c_])
                            P.mm(c_[:, 384:512], lhsT=kbg[bi][:], rhs=TT[:], start=True, stop=True, r=[TT, kbg[bi]], w=[c_])
                            yield
                            P.copy("act", u_[bi][:], c_[:, 256:384], r=[c_], w=[u_[bi]])
                            P.copy("act", wT[bi][:], c_[:, 384:512], r=[c_], w=[wT[bi]])
                            P.mm(a_[:, 0:128], lhsT=wT[bi][:], rhs=S_[:], start=True, stop=True, r=[wT[bi], S_], w=[a_])
                            yield
                            P.tt("dve", vnew[bi][:], u_[bi][:], a_[:, 0:128], ALU.subtract, r=[u_[bi], a_], w=[vnew[bi]])
                            P.mm(c_[:, 0:128], lhsT=S_[:], rhs=qd[bi][:], start=True, stop=False, r=[S_, qd[bi]], w=[c_])
                            P.mm(c_[:, 0:128], lhsT=vnew[bi][:], rhs=AT[bi][:], start=False, stop=True, r=[vnew[bi], AT[bi]], w=[c_])
                            P.mm(c_[:, 128:256], lhsT=kdec[bi][:], rhs=vnew[bi][:], start=True, stop=True, r=[kdec[bi], vnew[bi]], w=[c_])
                            yield
                            P.tt("dve", O[:, e, csl], O[:, e, csl], c_[:, 0:128], ALU.add, r=[c_, (O, e * 16 + c)], w=[(O, e * 16 + c)])
                            P.stt("dve", S_[:], S_[:], EGb[:, ccol:ccol + 1], c_[:, 128:256], ALU.mult, ALU.add, r=[S_, E_, c_], w=[S_])

                    gens = [chain(e, d, e * 2 + d) for e in range(2) for d in range(2)]
                    while gens:
                        for g_ in list(gens):
                            try:
                                next(g_)
                            except StopIteration:
                                gens.remove(g_)
                    for e in range(2):
                        hv = 2 * hq + e
                        P.dma("sp", tmp[:], projT_d[2080 + hv * 128:2080 + (hv + 1) * 128, :], r=[projT_d], w=[tmp])
                        P.act(tmp[:], tmp[:], AF.Silu, r=[tmp], w=[tmp])
                        for tb in range(4):
                            tsl = slice(tb * 512, (tb + 1) * 512)
                            b = tb % 2
                            P.act(sq[b][:], O[:, e, tsl], AF.Square, r=[O], w=[sq[b]])
                            P.mm(pA[b][:], lhsT=C("ones"), rhs=sq[b][:], start=True, stop=True, r=[sq[b], cst], w=[pA[b]])
                            P.act(rsb[b][:], pA[b][:], AF.Ln, r=[pA[b]], w=[rsb[b]], scale=1.0 / 128, bias=EPS)
                            P.act(rsb[b][:], rsb[b][:], AF.Exp, r=[rsb[b]], w=[rsb[b]], scale=-0.5)
                            P.stt("dve", sq[b][:], O[:, e, tsl], PVc("gdn_norm"), rsb[b][:], ALU.mult, ALU.mult, r=[O, rsb[b], pvt], w=[sq[b]])
                            P.tt("dve", ybf[b][:], sq[b][:], tmp[:, tsl], ALU.mult, r=[sq[b], tmp], w=[ybf[b]])
                            P.dma("pool", yT_d[hv * 128:(hv + 1) * 128, tsl], ybf[b][:], r=[ybf[b]], w=[(yT_d, hv)])

        def stage_lru():
            with P.scope():
                xp = P.sb("lxp", [128, S + 3])
                pad_init(xp)
                xc = P.sb("lxc", [128, S])
                wl = P.sb("lw", [128, 4, 8, 128])
                for m_ in range(4):
                    P.dma("sp", wl[:, m_, :, :], lruw_d.h.ap()[m_].rearrange("n i j -> i n j"), w=[(wl, m_)])
                nsp = P.sb("lnsp", [128, 16])
                for d, nm in enumerate(("lam_f", "lam_b")):
                    P.act(nsp[:, d * 8:(d + 1) * 8], PVc(nm, 0, 8), AF.Exp, r=[pvt], w=[(nsp, d)], scale=-1.0)
                    P.act(nsp[:, d * 8:(d + 1) * 8], nsp[:, d * 8:(d + 1) * 8], AF.Ln, r=[(nsp, d)], w=[(nsp, d)], bias=1.0)
                    P.ts("dve", nsp[:, d * 8:(d + 1) * 8], nsp[:, d * 8:(d + 1) * 8], -8.0, None, ALU.mult, None, r=[(nsp, d)], w=[(nsp, d)])
                rr = P.sb("lrr", [128, S])
                ig = P.sb("lig", [128, S])
                aa = P.sb("laa", [128, S])
                mm_ = P.sb("lmm", [128, S])
                hh = [P.sb(f"lhh{d}", [128, S]) for d in range(2)]
                gl = P.sb("lgl", [128, S])
                yb = P.sb("lyb", [128, S], BF16)
                pp = [P.ps(f"lp{i}") for i in range(4)]

                def rev(t):
                    return bass.AP(t.h, S - 1, [[S, 128], [-1, S]])

                for n in range(8 if lim is None else lim[0]):
                    conv_fm(3104 + n * 128, "lru_cw", "lru_cb", n, xc, xp, False)
                    P.dma("sp", gl[:], projT_d[4128 + n * 128:4128 + (n + 1) * 128, :], r=[projT_d], w=[gl])
                    for d in range(2):
                        sfx = "f" if d == 0 else "b"
                        for tb in range(4):
                            tsl = slice(tb * 512, (tb + 1) * 512)
                            p1, p2 = pp[(tb % 2) * 2], pp[(tb % 2) * 2 + 1]
                            P.mm(p1[:], lhsT=wl[:, 2 * d, n, :], rhs=xc[:, tsl], start=True, stop=True, r=[(wl, 2 * d), xc], w=[p1])
                            P.mm(p2[:], lhsT=wl[:, 2 * d + 1, n, :], rhs=xc[:, tsl], start=True, stop=True, r=[(wl, 2 * d + 1), xc], w=[p2])
                            P.act(rr[:, tsl], p1[:], AF.Sigmoid, r=[p1, pvt], w=[(rr, tb)], bias=PVc("ba_" + sfx, n))
                            P.act(ig[:, tsl], p2[:], AF.Sigmoid, r=[p2, pvt], w=[(ig, tb)], bias=PVc("bx_" + sfx, n))
                        P.act(aa[:], rr[:], AF.Exp, r=[rr, nsp], w=[aa], scale=nsp[:, d * 8 + n:d * 8 + n + 1])
                        P.tt("pool", mm_[:], aa[:], aa[:], ALU.mult, r=[aa], w=[mm_])
                        P.act(mm_[:], mm_[:], AF.Ln, r=[mm_], w=[mm_], scale=-1.0, bias=1.0)
                        P.act(mm_[:], mm_[:], AF.Exp, r=[mm_], w=[mm_], scale=0.5)
                        P.tt("pool", ig[:], ig[:], xc[:], ALU.mult, r=[ig, xc], w=[ig])
                        P.tt("dve", mm_[:], mm_[:], ig[:], ALU.mult, r=[mm_, ig], w=[mm_])
                        if d == 0:
                            P.op("dve", lambda e: e.tensor_tensor_scan(out=hh[0][:], data0=aa[:], data1=mm_[:], initial=0.0,
                                                                       op0=ALU.mult, op1=ALU.add), r=[aa, mm_], w=[hh[0]])
                        else:
                            P.op("dve", lambda e: e.tensor_tensor_scan(out=rev(hh[1]), data0=rev(aa), data1=rev(mm_), initial=0.0,
                                                                       op0=ALU.mult, op1=ALU.add), r=[aa, mm_], w=[hh[1]])
                    P.act(gl[:], gl[:], AF.Silu, r=[gl], w=[gl])
                    P.tt("pool", hh[0][:], hh[0][:], hh[1][:], ALU.add, r=[hh[0], hh[1]], w=[hh[0]])
                    P.tt("dve", yb[:], hh[0][:], gl[:], ALU.mult, r=[hh[0], gl], w=[yb])
                    P.dma("pool", yT_d[1024 + n * 128:1024 + (n + 1) * 128, :], yb[:], r=[yb], w=[(yT_d, 8 + n)])

        if only is not None:
            {"ssd": stage_ssd, "attn": stage_attn, "gdn": stage_gdn, "lru": stage_lru}[only]()
            P.barrier()
            return nc, dbg
        stage_mod()
        if debug:
            md = scratch("dbg_mod", [128, 96])
            P.dma("pool", md[:], modsb[:].rearrange("p l e -> p (l e)"), r=[modsb], w=[md])
        with P.scope():
            hT = P.sb("hT", [128, 16, S], BF16)
            stage_norm(xT_d, lambda k: sc1[:, 0, k:k + 1], lambda k: modsb[:, 0, k:k + 1], out_tile=hT)
            if debug:
                hd = scratch("dbg_h", [128, 16, S], BF16)
                P.dma("pool", hd[:], hT[:], r=[hT], w=[hd])
            fm = [(c * 128, 128, c * 128) for c in range(32) if not (10 <= c < 12)]
            fm += [(4128 + c * 128, 128, 4128 + c * 128) for c in range(8)]
            stage_inproj(hT, w_in_d[0], fm, [(1280, 256, 0), (4096, 32, 256)])
        if upto == "proj0":
            return nc, dbg
        if upto != "ssd":
            stage_attn()
        if upto == "attn":
            return nc, dbg
        stage_ssd()
        if upto == "ssd":
            return nc, dbg
        stage_outproj(w_out_d[0], xT_d, x1T_d, 0)
        if upto == "l0":
            return nc, dbg
        with P.scope():
            hT = P.sb("hT1", [128, 16, S], BF16)
            stage_norm(x1T_d, lambda k: sc1[:, 1, k:k + 1], lambda k: modsb[:, 1, k:k + 1], out_tile=hT)
            fm = [(c * 128, 128, c * 128) for c in range(16)]
            fm += [(2080 + c * 128, 128, 2080 + c * 128) for c in range(24)]
            stage_inproj(hT, w_in_d[1], fm, [(2048, 32, 0)])
        stage_gdn()
        stage_lru()
        stage_outproj(w_out_d[1], x1T_d, x2T_d, 1)
        stage_norm(x2T_d, lambda k: PVc("fnorm", k), lambda k: 0.0, out_dram=out_d)
        P.barrier()
    return nc, dbg


def make_inputs(inp, b):
    pv, rv = pack_params(inp)
    m = {
        "xT": np.ascontiguousarray(inp["x"][b].T),
        "cT": np.ascontiguousarray(inp["c"][b].reshape(16, 128).T),
        "w_mod": np.ascontiguousarray(inp["w_mod"]),
        "ab_w_in": np.ascontiguousarray(inp["ab_w_in"][0]),
        "cd_w_in": np.ascontiguousarray(inp["cd_w_in"][0]),
        "ab_w_out": np.ascontiguousarray(inp["ab_w_out"][0]),
        "cd_w_out": np.ascontiguousarray(inp["cd_w_out"][0]),
        "lru_w": np.ascontiguousarray(np.stack([inp["cd_lru_wa_f"][0], inp["cd_lru_wx_f"][0], inp["cd_lru_wa_b"][0], inp["cd_lru_wx_b"][0]], 0)),
        "consts": CONSTS,
        "rope": _rope_tables(),
        "pvec": pv,
        "rvec": rv,
    }
    return m


def kernel(**inputs):
    inp = {k: np.asarray(v, np.float32) for k, v in inputs.items()}
    maps = [make_inputs(inp, i // 2) for i in range(8)]
    nc, _ = build(maps[0]["pvec"].shape[1], maps[0]["rvec"].shape[1])
    res = run_bass_kernel_spmd(nc, maps, core_ids=list(range(8)))
    out = np.stack([np.asarray(res.results[2 * b]["outT"], np.float32).T for b in range(4)], 0)
    return np.ascontiguousarray(out)
```

```python
import math
import numpy as np
from contextlib import ExitStack, contextmanager
import concourse.bass as bass
import concourse.mybir as mybir
from concourse.bass_utils import run_bass_kernel_spmd

F32 = mybir.dt.float32
BF16 = mybir.dt.bfloat16
AF = mybir.ActivationFunctionType
ALU = mybir.AluOpType

D = 2048
S = 2048
EPS = 1e-6
NEG = -30000.0


class Tn:
    def __init__(self, h, name, psum=False):
        self.h = h
        self.name = name
        self.st = {}
        self.psum = psum

    def __getitem__(self, idx):
        return self.h[idx]


class Prog:
    NDMA = {"sp": 14, "pool": 8}

    def __init__(self, nc, es):
        self.nc = nc
        self.stack = [es]
        self.h = {"pe": nc.tensor, "act": nc.scalar, "dve": nc.vector, "pool": nc.gpsimd, "sp": nc.sync}
        self.sem = {e: es.enter_context(nc.semaphore("s_" + e)) for e in ("pe", "act", "dve", "pool")}
        self.cnt = {e: 0 for e in self.sem}
        self.dsem = {q: [es.enter_context(nc.semaphore(f"d_{q}{i}")) for i in range(n)] for q, n in self.NDMA.items()}
        self.dcnt = {q: 0 for q in self.NDMA}
        self.waited = {e: {} for e in self.h}
        self.semobj = {}
        for s in self.sem.values():
            self.semobj[id(s)] = s
        for l in self.dsem.values():
            for s in l:
                self.semobj[id(s)] = s
        self.uid = 0

    def sb(self, name, shape, dt=F32):
        self.uid += 1
        name = f"{name}_{self.uid}"
        return Tn(self.stack[-1].enter_context(self.nc.sbuf_tensor(name, list(shape), dt)), name)

    def ps(self, name):
        self.uid += 1
        name = f"{name}_{self.uid}"
        return Tn(self.stack[-1].enter_context(self.nc.psum_tensor(name, [128, 512], F32)), name, psum=True)

    def dram(self, name, shape, dt=F32, kind="Internal"):
        return Tn(self.nc.dram_tensor(name, list(shape), dt, kind=kind), name)

    @contextmanager
    def scope(self):
        es = ExitStack()
        self.stack.append(es)
        try:
            yield
        finally:
            self.barrier()
            self.stack.pop()
            es.close()

    def barrier(self):
        toks = [(self.sem[e], self.cnt[e]) for e in self.sem if self.cnt[e] > 0]
        for q, lst in self.dsem.items():
            n = self.dcnt[q]
            for i, s in enumerate(lst):
                uses = (n - i + len(lst) - 1) // len(lst) if n > i else 0
                if uses > 0:
                    toks.append((s, 16 * uses))
        for e, h in self.h.items():
            for s, v in toks:
                if e in self.sem and self.sem[e] is s:
                    continue
                if self.waited[e].get(id(s), 0) >= v:
                    continue
                self.waited[e][id(s)] = v
                h.wait_ge(s, v)

    @staticmethod
    def _norm(x):
        if isinstance(x, tuple):
            return (x[0], None) if x[0].psum else x
        return (x, None)

    def _states(self, t, sub):
        if sub is None:
            return list(t.st.values())
        out = []
        if None in t.st:
            out.append(t.st[None])
        if sub in t.st:
            out.append(t.st[sub])
        return out

    def _deps(self, eng, r, w):
        need = {}

        def add(tok, kind):
            if tok is None:
                return
            sid, val, src = tok
            if src == eng:
                if eng in ("pe", "sp"):
                    return
            if self.waited[eng].get(sid, 0) >= val:
                return
            if need.get(sid, 0) < val:
                need[sid] = val

        for x in r:
            t, sub = self._norm(x)
            for s in self._states(t, sub):
                add(s[0], "raw")
        for x in w:
            t, sub = self._norm(x)
            for s in self._states(t, sub):
                add(s[0], "waw")
                for tok in s[1].values():
                    add(tok, "war")
        for sid, val in need.items():
            self.waited[eng][sid] = val
        return [(self.semobj[sid], val) for sid, val in need.items()]

    def _commit(self, who, tok, r, w):
        for x in r:
            t, sub = self._norm(x)
            if sub is None:
                if None not in t.st:
                    t.st[None] = [None, {}]
                for s in t.st.values():
                    s[1][who] = tok
            else:
                if sub not in t.st:
                    t.st[sub] = [None, {}]
                t.st[sub][1][who] = tok
        for x in w:
            t, sub = self._norm(x)
            if sub is None:
                t.st = {None: [tok, {}]}
            else:
                t.st[sub] = [tok, {}]

    def op(self, eng, fn, r=(), w=()):
        w = list(w) + [x for x in r if self._norm(x)[0].psum]
        waits = self._deps(eng, r, w)
        self.cnt[eng] += 1
        s = self.sem[eng]
        tok = (id(s), self.cnt[eng], eng)
        self._commit(eng, tok, r, w)
        h = self.h[eng]
        for ss, v in waits:
            h.wait_ge(ss, v)
        fn(h).then_inc(s, 1)

    def dma(self, q, out, in_, r=(), w=()):
        n = self.dcnt[q]
        self.dcnt[q] += 1
        pool = self.dsem[q]
        s = pool[n % len(pool)]
        use = n // len(pool)
        waits = self._deps(q, r, w)
        if use > 0 and self.waited[q].get(id(s), 0) < 16 * use:
            waits.append((s, 16 * use))
            self.waited[q][id(s)] = 16 * use
        tok = (id(s), 16 * (use + 1), "dma_" + q)
        self._commit(f"dma_{q}{n % len(pool)}", tok, r, w)
        h = self.h[q]
        for ss, v in waits:
            h.wait_ge(ss, v)
        h.dma_start(out=out, in_=in_).then_inc(s, 16)
        return tok

    def wait_tok(self, eng, tok):
        self.h[eng].wait_ge(self.semobj[tok[0]], tok[1])

    def act(self, out, in_, func, r, w, bias=0.0, scale=1.0, accum_out=None, eng="act"):
        if accum_out is None:
            self.op("act", lambda e: e.activation(out=out, in_=in_, func=func, bias=bias, scale=scale), r=r, w=w)
        else:
            self.op("act", lambda e: e.activation(out=out, in_=in_, func=func, bias=bias, scale=scale, accum_out=accum_out), r=r, w=w)

    def tt(self, eng, out, in0, in1, op, r, w):
        self.op(eng, lambda e: e.tensor_tensor(out=out, in0=in0, in1=in1, op=op), r=r, w=w)

    def ts(self, eng, out, in0, s1, s2, op0, op1, r, w):
        if s2 is None:
            self.op(eng, lambda e: e.tensor_scalar(out=out, in0=in0, scalar1=s1, scalar2=None, op0=op0), r=r, w=w)
        else:
            self.op(eng, lambda e: e.tensor_scalar(out=out, in0=in0, scalar1=s1, scalar2=s2, op0=op0, op1=op1), r=r, w=w)

    def stt(self, eng, out, in0, scalar, in1, op0, op1, r, w):
        self.op(eng, lambda e: e.scalar_tensor_tensor(out=out, in0=in0, scalar=scalar, in1=in1, op0=op0, op1=op1), r=r, w=w)

    def copy(self, eng, out, in_, r, w):
        if eng == "act":
            self.op("act", lambda e: e.copy(out=out, in_=in_), r=r, w=w)
        else:
            self.op(eng, lambda e: e.tensor_copy(out=out, in_=in_), r=r, w=w)

    def mm(self, out, lhsT, rhs, start, stop, r, w):
        self.op("pe", lambda e: e.matmul(out, lhsT=lhsT, rhs=rhs, start=start, stop=stop), r=r, w=w)

    def tr(self, out, in_, ident, r, w):
        self.op("pe", lambda e: e.transpose(out, in_, ident), r=r, w=w)


C_OFF = {}


def _consts():
    i = np.arange(128)
    sI, fI = i[:, None], i[None, :]
    mats = {
        "ident": (sI == fI), "ones": np.ones((128, 128)),
        "Uf": (sI <= fI), "Mf": (sI > fI), "Ub": (sI >= fI), "Mb": (sI < fI),
    }
    R = np.zeros((128, 128))
    for p in range(64):
        R[2 * p, 2 * p + 1] = -1.0
        R[2 * p + 1, 2 * p] = 1.0
    mats["Rt"] = R.T
    negs = {
        "NEGf": NEG * (fI < sI), "NEGb": NEG * (fI > sI),
        "NEGsf": NEG * (fI >= sI), "NEGsb": NEG * (fI <= sI),
    }
    cols = []
    off = 0
    for k, m in mats.items():
        C_OFF[k] = off
        cols.append(np.asarray(m, np.float32))
        off += 128
    for k, m in negs.items():
        C_OFF[k] = off
        cols.append(np.tile(np.asarray(m, np.float32), (1, 4)))
        off += 512
    return np.ascontiguousarray(np.concatenate(cols, axis=1)), off


CONSTS, NCONST = _consts()


def _rope_tables():
    t = np.arange(S)
    row = (t // 64).astype(np.float32)
    col = (t % 64).astype(np.float32)
    n_pairs = 32
    freqs = (np.float32(10000.0) ** (-np.arange(n_pairs, dtype=np.float32) / np.float32(n_pairs))).astype(np.float32)
    ang = np.concatenate([row[:, None] * freqs, col[:, None] * freqs], axis=-1).astype(np.float32)
    cos = np.cos(ang).astype(np.float32)
    sin = np.sin(ang).astype(np.float32)
    cosT = np.repeat(cos, 2, axis=1).T
    sinT = np.repeat(sin, 2, axis=1).T
    return np.ascontiguousarray(np.stack([cosT, sinT], 0))


PV = {}
RV = {}


def _pcols(v, n):
    return np.asarray(v, np.float32).reshape(n, 128).T


def pack_params(inp):
    pv, rv = [], []

    def addp(name, arr):
        PV[name] = (sum(a.shape[1] for a in pv), arr.shape[1])
        pv.append(np.asarray(arr, np.float32))

    def addr(name, vec):
        vec = np.asarray(vec, np.float32).reshape(-1)
        RV[name] = (sum(a.shape[1] for a in rv), vec.shape[0])
        rv.append(np.broadcast_to(vec[None, :], (128, vec.shape[0])))

    addp("q_norm", _pcols(inp["ab_q_norm"][0], 1))
    addp("k_norm", _pcols(inp["ab_k_norm"][0], 1))
    cw = inp["ab_conv_w"][0]
    addp("ab_cw", np.concatenate([_pcols(cw[j], 12)[:, :, None] for j in range(4)], 2).reshape(128, 48))
    addp("ab_cb", _pcols(inp["ab_conv_b"][0], 12))
    addp("d_skip", _pcols(np.repeat(inp["ab_d_skip"][0], 64), 8))
    addp("ssd_norm", _pcols(inp["ab_ssd_norm"][0], 8))
    cw = inp["cd_conv_w"][0]
    addp("cd_cw", np.concatenate([_pcols(cw[j], 16)[:, :, None] for j in range(4)], 2).reshape(128, 64))
    addp("cd_cb", _pcols(inp["cd_conv_b"][0], 16))
    addp("gdn_norm", _pcols(inp["cd_gdn_norm"][0], 1))
    cw = inp["cd_lru_conv_w"][0]
    addp("lru_cw", np.concatenate([_pcols(cw[j], 8)[:, :, None] for j in range(4)], 2).reshape(128, 32))
    addp("lru_cb", _pcols(inp["cd_lru_conv_b"][0], 8))
    for d_ in ("f", "b"):
        addp("ba_" + d_, _pcols(inp["cd_lru_ba_" + d_][0], 8))
        addp("bx_" + d_, _pcols(inp["cd_lru_bx_" + d_][0], 8))
        addp("lam_" + d_, _pcols(inp["cd_lru_lam_" + d_][0], 8))
    addp("fnorm", _pcols(inp["final_norm_w"], 16))
    for l in range(2):
        addp(f"norm{l}", _pcols(inp["norm_w"][l], 16))
        addp(f"bmod{l}", _pcols(inp["b_mod"][l], 48))
    addr("ab_dtb", np.concatenate([inp["ab_dt_bias_f"][0], inp["ab_dt_bias_b"][0]]))
    addr("ab_alog", np.concatenate([inp["ab_a_log_f"][0], inp["ab_a_log_b"][0]]))
    addr("cd_dtb", np.concatenate([inp["cd_dt_bias_f"][0], inp["cd_dt_bias_b"][0]]))
    addr("cd_alog", np.concatenate([inp["cd_a_log_f"][0], inp["cd_a_log_b"][0]]))
    return (np.ascontiguousarray(np.concatenate(pv, 1)), np.ascontiguousarray(np.concatenate(rv, 1)))


def build(npv, nrv, upto="all", debug=False, only=None, lim=None):
    nc = bass.Bass("TRN2", target_bir_lowering=False)
    es = ExitStack()
    dbg = {}
    with es:
        P = Prog(nc, es)
        skind = "ExternalOutput" if debug else "Internal"

        def din(name, shape, dt=F32):
            return P.dram(name, shape, dt, kind="ExternalInput")

        xT_d = din("xT", [D, S])
        cT_d = din("cT", [128, 16])
        wmod_d = din("w_mod", [2, D, 3 * D])
        w_in_d = [din("ab_w_in", [D, 5152]), din("cd_w_in", [D, 5152])]
        w_out_d = [din("ab_w_out", [D, D]), din("cd_w_out", [D, D])]
        lruw_d = din("lru_w", [4, 8, 128, 128])
        consts_d = din("consts", [128, NCONST])
        rope_d = din("rope", [2, 128, S])
        pv_d = din("pvec", [128, npv])
        rv_d = din("rvec", [128, nrv])
        out_d = P.dram("outT", [D, S], F32, kind="ExternalOutput")

        def scratch(name, shape, dt=F32):
            t = P.dram(name, shape, dt, kind=skind)
            dbg[name] = t
            return t

        if only is None:
            projT_d = scratch("projT", [5248, S])
            tokm_d = scratch("tokm", [128, 16, 320])
        else:
            projT_d = din("projT", [5248, S])
            tokm_d = din("tokm", [128, 16, 320])
        yT_d = scratch("yT", [D, S], BF16)
        x1T_d = scratch("x1T", [D, S])
        x2T_d = scratch("x2T", [D, S])

        cst = P.sb("cst", [128, NCONST])
        P.dma("sp", cst[:, 0:1536], consts_d[:, 0:1536], w=[cst])
        P.dma("sp", cst[:, 1536:NCONST], consts_d[:, 1536:NCONST], w=[cst])
        pvt = P.sb("pvt", [128, npv])
        P.dma("sp", pvt[:], pv_d[:], w=[pvt])
        rvt = P.sb("rvt", [128, nrv])
        P.dma("sp", rvt[:], rv_d[:], w=[rvt])
        onesb = P.sb("onesb", [128, 128], BF16)
        P.copy("dve", onesb[:], cst[:, C_OFF["ones"]:C_OFF["ones"] + 128], r=[cst], w=[onesb])
        identb = P.sb("identb", [128, 128], BF16)
        P.copy("dve", identb[:], cst[:, C_OFF["ident"]:C_OFF["ident"] + 128], r=[cst], w=[identb])

        def C(name, n=128):
            return cst[:, C_OFF[name]:C_OFF[name] + n]

        def PVc(name, j=0, n=1):
            o, _ = PV[name]
            return pvt[:, o + j:o + j + n]

        def RVc(name, j=0, n=1):
            o, _ = RV[name]
            return rvt[:, o + j:o + j + n]

        modsb = P.sb("modsb", [128, 2, 48])
        sc1 = P.sb("sc1", [128, 2, 16])

        def stage_mod():
            with P.scope():
                cond = P.sb("cond", [128, 16])
                P.dma("sp", cond[:], cT_d[:], w=[cond])
                P.act(cond[:], cond[:], AF.Silu, r=[cond], w=[cond])
                pm = P.ps("pm")
                wst = [P.sb(f"wst{i}", [128, 3 * D]) for i in range(3)]
                acc = P.sb("macc", [128, 3 * D])
                for l in range(2):
                    for k in range(16):
                        t = wst[(l * 16 + k) % 3]
                        P.dma("sp", t[:, 0:3072], wmod_d[l, k * 128:(k + 1) * 128, 0:3072], w=[(t, 0)])
                        P.dma("sp", t[:, 3072:6144], wmod_d[l, k * 128:(k + 1) * 128, 3072:6144], w=[(t, 1)])
                        for hf in range(2):
                            sl_ = slice(hf * 3072, (hf + 1) * 3072)
                            if k == 0:
                                P.ts("dve", acc[:, sl_], t[:, sl_], cond[:, k:k + 1], None, ALU.mult, None, r=[(t, hf), cond], w=[(acc, hf)])
                            else:
                                P.stt("dve", acc[:, sl_], t[:, sl_], cond[:, k:k + 1], acc[:, sl_], ALU.mult, ALU.add,
                                      r=[(t, hf), cond, (acc, hf)], w=[(acc, hf)])
                    for e in range(48):
                        P.mm(pm[:, l * 64 + e:l * 64 + e + 1], lhsT=acc[:, e * 128:(e + 1) * 128], rhs=C("ones")[:, 0:1],
                             start=True, stop=True, r=[acc, cst], w=[pm])
                    P.tt("dve", modsb[:, l, :], pm[:, l * 64:l * 64 + 48], PVc(f"bmod{l}", 0, 48), ALU.add, r=[pm, pvt], w=[(modsb, l)])
                    P.stt("dve", sc1[:, l, :], modsb[:, l, 16:32], 1.0, PVc(f"norm{l}", 0, 16), ALU.add, ALU.mult,
                          r=[(modsb, l), pvt], w=[(sc1, l)])

        def stage_norm(xin_d, scale_ap, bias_ap, out_tile=None, out_dram=None):
            with P.scope():
                xs = [P.sb(f"xn{i}", [128, 16, 256]) for i in range(2)]
                sq = [P.sb(f"sq{i}", [128, 256]) for i in range(2)]
                rstd = [P.sb(f"rstd{i}", [128, 256]) for i in range(2)]
                pss = [P.ps(f"pss{i}") for i in range(2)]
                ob = [P.sb(f"ob{i}", [128, 16, 256]) for i in range(2)] if out_dram is not None else None
                xv = xin_d.h.ap().rearrange("(k p) t -> p k t", p=128)
                for tb in range(8):
                    x = xs[tb % 2]
                    tsl = slice(tb * 256, (tb + 1) * 256)
                    for hf in range(2):
                        P.dma("sp", x[:, hf * 8:(hf + 1) * 8, :], xv[:, hf * 8:(hf + 1) * 8, tsl], r=[xin_d], w=[(x, hf)])
                    ps_ = pss[tb % 2]
                    for k in range(16):
                        s_ = sq[k % 2]
                        P.act(s_[:], x[:, k, :], AF.Square, r=[(x, k // 8)], w=[s_])
                        P.mm(ps_[:, 0:256], lhsT=C("ones"), rhs=s_[:], start=(k == 0), stop=(k == 15), r=[s_, cst], w=[ps_])
                    rs = rstd[tb % 2]
                    P.act(rs[:], ps_[:, 0:256], AF.Ln, r=[ps_], w=[rs], scale=1.0 / D, bias=EPS)
                    P.act(rs[:], rs[:], AF.Exp, r=[rs], w=[rs], scale=-0.5)
                    for k in range(16):
                        P.tt("dve", x[:, k, :], x[:, k, :], rs[:], ALU.mult, r=[(x, k // 8), rs], w=[(x, k // 8)])
                        if out_tile is not None:
                            P.act(out_tile[:, k, tsl], x[:, k, :], AF.Identity, r=[(x, k // 8), modsb, sc1, pvt], w=[(out_tile, tb)],
                                  scale=scale_ap(k), bias=bias_ap(k))
                        else:
                            o = ob[tb % 2]
                            P.act(o[:, k, :], x[:, k, :], AF.Identity, r=[(x, k // 8), pvt], w=[o], scale=scale_ap(k), bias=bias_ap(k))
                    if out_dram is not None:
                        ov = out_dram.h.ap().rearrange("(k p) t -> p k t", p=128)
                        for hf in range(2):
                            P.dma("pool", ov[:, hf * 8:(hf + 1) * 8, tsl], ob[tb % 2][:, hf * 8:(hf + 1) * 8, :], r=[ob[tb % 2]], w=[(out_dram, tb)])

        def stage_inproj(hT, w_d, fm_chunks, tm_specs):
            with P.scope():
                wf = [P.sb(f"wf{i}", [128, 16, 256]) for i in range(2)]
                wb = [P.sb(f"wb{i}", [128, 16, 256], BF16) for i in range(2)]
                ot = [P.sb(f"ot{i}", [128, S]) for i in range(2)]
                otm = P.sb("otm", [128, 16, 256])
                pp = [P.ps(f"pp{i}") for i in range(3)]
                wv = w_d.h.ap().rearrange("(k p) c -> p k c", p=128)
                groups = []
                i = 0
                while i < len(fm_chunks):
                    g = [fm_chunks[i]]
                    if i + 1 < len(fm_chunks) and fm_chunks[i + 1][0] == fm_chunks[i][0] + 128 and fm_chunks[i][1] == 128:
                        g.append(fm_chunks[i + 1])
                        i += 1
                    i += 1
                    groups.append(("fm", g))
                for sp_ in tm_specs:
                    groups.append(("tm", [sp_]))
                npp = 0
                nout = 0
                for gi, (kind, g) in enumerate(groups):
                    c0 = g[0][0]
                    ncol = sum(x[1] for x in g)
                    f_, b_ = wf[gi % 2], wb[gi % 2]
                    for q4 in range(4):
                        P.dma("sp", f_[:, q4 * 4:(q4 + 1) * 4, 0:ncol], wv[:, q4 * 4:(q4 + 1) * 4, c0:c0 + ncol], w=[(f_, q4)])
                    for q4 in range(4):
                        P.copy("dve" if q4 % 2 == 0 else "act", b_[:, q4 * 4:(q4 + 1) * 4, 0:ncol], f_[:, q4 * 4:(q4 + 1) * 4, 0:ncol],
                               r=[(f_, q4)], w=[(b_, q4)])
                    if kind == "fm":
                        for ci, (cc0, cn, row0) in enumerate(g):
                            o_ = ot[nout % 2]
                            nout += 1
                            for tb in range(4):
                                p_ = pp[npp % 3]
                                npp += 1
                                for k in range(16):
                                    P.mm(p_[0:cn, :], lhsT=b_[:, k, ci * 128:ci * 128 + cn], rhs=hT[:, k, tb * 512:(tb + 1) * 512],
                                         start=(k == 0), stop=(k == 15), r=[(b_, k // 4), hT], w=[p_])
                                P.copy("act" if tb % 2 == 0 else "dve", o_[0:cn, tb * 512:(tb + 1) * 512], p_[0:cn, :], r=[p_], w=[(o_, tb)])
                            P.dma("pool", projT_d[row0:row0 + cn, :], o_[0:cn, :], r=[o_], w=[(projT_d, row0 // 128)])
                    else:
                        (cc0, cn, toff) = g[0]
                        o_ = otm
                        ov = o_[:, :, 0:cn]
                        for tb in range(16):
                            p_ = pp[npp % 3]
                            npp += 1
                            for k in range(16):
                                P.mm(p_[:, 0:cn], lhsT=hT[:, k, tb * 128:(tb + 1) * 128], rhs=b_[:, k, 0:cn],
                                     start=(k == 0), stop=(k == 15), r=[(b_, k // 4), hT], w=[p_])
                            P.copy("act" if tb % 2 == 0 else "dve", ov[:, tb, :], p_[:, 0:cn], r=[p_], w=[(o_, tb % 4)])
                        P.dma("pool", tokm_d[:, :, toff:toff + cn], ov, r=[o_], w=[(tokm_d, toff)])

        def stage_outproj(w_d, xin_d, xout_d, l):
            with P.scope():
                yt = P.sb("yt", [128, 16, S], BF16)
                yv = yT_d.h.ap().rearrange("(k p) t -> p k t", p=128)
                for q4 in range(8):
                    P.dma("sp", yt[:, q4 * 2:(q4 + 1) * 2, :], yv[:, q4 * 2:(q4 + 1) * 2, :], r=[yT_d], w=[(yt, q4)])
                wf = [P.sb(f"owf{i}", [128, 16, 128]) for i in range(2)]
                wb = [P.sb(f"owb{i}", [128, 16, 128], BF16) for i in range(2)]
                xo = [P.sb(f"xo{i}", [128, S]) for i in range(2)]
                pp = [P.ps(f"op{i}") for i in range(3)]
                wv = w_d.h.ap().rearrange("(k p) c -> p k c", p=128)
                npp = 0
                for dc in range(16):
                    f_, b_ = wf[dc % 2], wb[dc % 2]
                    for q4 in range(2):
                        P.dma("sp", f_[:, q4 * 8:(q4 + 1) * 8, :], wv[:, q4 * 8:(q4 + 1) * 8, dc * 128:(dc + 1) * 128], w=[(f_, q4)])
                        P.copy("dve" if q4 == 0 else "act", b_[:, q4 * 8:(q4 + 1) * 8, :], f_[:, q4 * 8:(q4 + 1) * 8, :], r=[(f_, q4)], w=[(b_, q4)])
                    x_ = xo[dc % 2]
                    P.dma("sp", x_[:], xin_d[dc * 128:(dc + 1) * 128, :], r=[xin_d], w=[x_])
                    for tb in range(4):
                        p_ = pp[npp % 3]
                        npp += 1
                        for k in range(16):
                            P.mm(p_[:], lhsT=b_[:, k, :], rhs=yt[:, k, tb * 512:(tb + 1) * 512], start=(k == 0), stop=(k == 15),
                                 r=[(b_, k // 8), yt], w=[p_])
                        P.stt("dve", x_[:, tb * 512:(tb + 1) * 512], p_[:], modsb[:, l, 32 + dc:33 + dc], x_[:, tb * 512:(tb + 1) * 512],
                              ALU.mult, ALU.add, r=[p_, x_, modsb], w=[x_])
                    P.dma("pool", xout_d[dc * 128:(dc + 1) * 128, :], x_[:], r=[x_], w=[(xout_d, dc)])

        def conv_fm(src_row0, cwname, cbname, chunk, dst, xp, silu, tagr=()):
            P.dma("sp", xp[:, 2:S + 2], projT_d[src_row0:src_row0 + 128, :], r=[(projT_d, src_row0 // 128)], w=[xp])
            o, _ = PV[cwname]
            wc = lambda j: pvt[:, o + chunk * 4 + j:o + chunk * 4 + j + 1]
            P.ts("dve", dst[:], xp[:, 0:S], wc(0), PVc(cbname, chunk), ALU.mult, ALU.add, r=[xp, pvt], w=[dst])
            for j in range(1, 4):
                P.stt("dve", dst[:], xp[:, j:S + j], wc(j), dst[:], ALU.mult, ALU.add, r=[xp, pvt, dst], w=[dst])
            if silu:
                P.act(dst[:], dst[:], AF.Silu, r=[dst], w=[dst])

        def pad_init(xp):
            P.op("dve", lambda e: e.memset(xp[:, 0:2], 0.0), w=[xp])
            P.op("dve", lambda e: e.memset(xp[:, S + 2:S + 3], 0.0), w=[xp])

        def stage_attn():
            with P.scope():
                rope = P.sb("rope", [128, 2, S])
                P.dma("sp", rope[:, 0, :], rope_d[0], w=[(rope, 0)])
                P.dma("sp", rope[:, 1, :], rope_d[1], w=[(rope, 1)])
                qk = P.sb("qkr", [128, 10, S], BF16)
                vt = P.sb("vt", [128, 16, 256], BF16)
                vf = P.sb("vf", [128, 16, 256])
                P.dma("sp", vf[:], tokm_d[:, :, 0:256], r=[(tokm_d, 0)], w=[vf])
                P.copy("dve", vt[:], vf[:], r=[vf], w=[vt])
                raw = [P.sb(f"raw{i}", [128, 512]) for i in range(2)]
                sq = [P.sb(f"asq{i}", [128, 512]) for i in range(2)]
                rs = [P.sb(f"ars{i}", [128, 512]) for i in range(2)]
                qn = [P.sb(f"aqn{i}", [128, 512]) for i in range(2)]
                t1 = [P.sb(f"at1{i}", [128, 512]) for i in range(2)]
                t2 = [P.sb(f"at2{i}", [128, 512]) for i in range(2)]
                pa = [P.ps(f"pa{i}") for i in range(2)]
                pb = [P.ps(f"pb{i}") for i in range(2)]
                it = 0
                for hh in range(10):
                    row0 = hh * 128 if hh < 8 else 1024 + (hh - 8) * 128
                    wn = PVc("q_norm") if hh < 8 else PVc("k_norm")
                    for tb in range(4):
                        b = it % 2
                        it += 1
                        tsl = slice(tb * 512, (tb + 1) * 512)
                        P.dma("sp", raw[b][:], projT_d[row0:row0 + 128, tsl], r=[(projT_d, row0 // 128)], w=[raw[b]])
                        P.act(sq[b][:], raw[b][:], AF.Square, r=[raw[b]], w=[sq[b]])
                        P.mm(pa[b][:], lhsT=C("ones"), rhs=sq[b][:], start=True, stop=True, r=[sq[b], cst], w=[pa[b]])
                        P.act(rs[b][:], pa[b][:], AF.Ln, r=[pa[b]], w=[rs[b]], scale=1.0 / 128, bias=EPS)
                        P.act(rs[b][:], rs[b][:], AF.Exp, r=[rs[b]], w=[rs[b]], scale=-0.5)
                        P.stt("dve", qn[b][:], raw[b][:], wn, rs[b][:], ALU.mult, ALU.mult, r=[raw[b], rs[b], pvt], w=[qn[b]])
                        P.mm(pb[b][:], lhsT=C("Rt"), rhs=qn[b][:], start=True, stop=True, r=[qn[b], cst], w=[pb[b]])
                        P.tt("dve", t1[b][:], qn[b][:], rope[:, 0, tsl], ALU.mult, r=[qn[b], (rope, 0)], w=[t1[b]])
                        P.tt("dve", t2[b][:], pb[b][:], rope[:, 1, tsl], ALU.mult, r=[pb[b], (rope, 1)], w=[t2[b]])
                        P.tt("dve", qk[:, hh, tsl], t1[b][:], t2[b][:], ALU.add, r=[t1[b], t2[b]], w=[(qk, hh * 4 + tb)])
                pT = [P.sb(f"pT{i}", [128, 512], BF16) for i in range(3)]
                ps_ = [P.ps(f"psc{i}") for i in range(2)]
                po = [P.ps(f"po{i}") for i in range(2)]
                gt = [P.sb(f"gt{i}", [128, 512]) for i in range(2)]
                rd = [P.sb(f"rd{i}", [128, 512]) for i in range(2)]
                yo = [P.sb(f"yo{i}", [128, 512], BF16) for i in range(2)]
                scale = 128.0 ** -0.5
                it = 0
                n3 = 0
                groups = [(h, qb) for h in range(8) for qb in range(4)]
                items = [(gi, kc) for gi in range(len(groups)) for kc in range(16)]

                def epilogue(gi):
                    h, qb = groups[gi]
                    b = gi % 2
                    qsl = slice(qb * 512, (qb + 1) * 512)
                    P.ts("dve", gt2[b][:], gt2[b][:], 1.0, None, ALU.add, None, r=[gt2[b]], w=[gt2[b]])
                    P.op("dve", lambda e: e.reciprocal(out=gt2[b][:], in_=gt2[b][:]), r=[gt2[b]], w=[gt2[b]])
                    P.op("dve", lambda e: e.reciprocal(out=rd[b][:], in_=pa[b][:]), r=[pa[b]], w=[rd[b]])
                    P.tt("dve", rd[b][:], rd[b][:], gt2[b][:], ALU.mult, r=[rd[b], gt2[b]], w=[rd[b]])
                    P.tt("dve", rd[b][:], rd[b][:], gt[b][:], ALU.mult, r=[rd[b], gt[b]], w=[rd[b]])
                    P.tt("dve", yo[b][:], po[b][:], rd[b][:], ALU.mult, r=[po[b], rd[b]], w=[yo[b]])
                    P.dma("pool", yT_d[h * 128:(h + 1) * 128, qsl], yo[b][:], r=[yo[b]], w=[(yT_d, h)])

                def tail(idx):
                    gi, kc = items[idx]
                    h, qb = groups[gi]
                    g = h // 4
                    b = gi % 2
                    p_ = pT[idx % 3]
                    P.mm(po[b][:], lhsT=vt[:, kc, g * 128:(g + 1) * 128], rhs=p_[:], start=(kc == 0), stop=(kc == 15),
                         r=[vt, p_], w=[po[b]])
                    P.mm(pa[b][:], lhsT=onesb[:], rhs=p_[:], start=(kc == 0), stop=(kc == 15), r=[onesb, p_], w=[pa[b]])
                    if kc == 15:
                        epilogue(gi)

                gt2 = [P.sb(f"gtb{i}", [128, 512]) for i in range(2)]
                for idx, (gi, kc) in enumerate(items):
                    h, qb = groups[gi]
                    g = h // 4
                    b = gi % 2
                    qsl = slice(qb * 512, (qb + 1) * 512)
                    if kc == 0:
                        P.dma("sp", gt[b][:], projT_d[1536 + h * 128:1536 + (h + 1) * 128, qsl], r=[(projT_d, 12 + h)], w=[gt[b]])
                    s_ = ps_[idx % 2]
                    P.mm(s_[:], lhsT=qk[:, 8 + g, kc * 128:(kc + 1) * 128], rhs=qk[:, h, qsl], start=True, stop=True,
                         r=[(qk, (8 + g) * 4 + kc // 4), (qk, h * 4 + qb)], w=[s_])
                    if idx > 0:
                        tail(idx - 1)
                    P.act(pT[idx % 3][:], s_[:], AF.Exp, r=[s_], w=[pT[idx % 3]], scale=scale)
                    if kc == 0:
                        P.act(gt2[b][:], gt[b][:], AF.Exp, r=[gt[b]], w=[gt2[b]], scale=-1.0)
                tail(len(items) - 1)

        def stage_ssd():
            with P.scope():
                xp = P.sb("xp", [128, S + 3])
                pad_init(xp)
                BT2 = [P.sb(f"BT{g}", [128, S], BF16) for g in range(2)]
                CTb2 = [P.sb(f"CTb{g}", [128, S], BF16) for g in range(2)]
                Btm2 = [P.sb(f"Btm{g}", [128, 16, 128], BF16) for g in range(2)]
                GT2 = [P.sb(f"GT{g}", [128, 16, 128]) for g in range(2)]
                tmp = P.sb("ctmp", [128, S])
                pg = [P.ps(f"pg{i}") for i in range(4)]
                py = [P.ps(f"py{i}") for i in range(4)]
                for g in range(2):
                    BT, CTb, Btm, GT = BT2[g], CTb2[g], Btm2[g], GT2[g]
                    conv_fm(3584 + g * 128, "ab_cw", "ab_cb", 8 + g, tmp, xp, True)
                    P.copy("act", BT[:], tmp[:], r=[tmp], w=[BT])
                    for c in range(16):
                        p_ = pg[c % 4]
                        P.tr(p_[:, 0:128], tmp[:, c * 128:(c + 1) * 128], C("ident"), r=[tmp, cst], w=[p_])
                        P.copy("dve", Btm[:, c, :], p_[:, 0:128], r=[p_], w=[Btm])
                    conv_fm(3840 + g * 128, "ab_cw", "ab_cb", 10 + g, tmp, xp, True)
                    P.copy("act", CTb[:], tmp[:], r=[tmp], w=[CTb])
                    for c in range(16):
                        p_ = py[c % 4]
                        csl = slice(c * 128, (c + 1) * 128)
                        P.mm(p_[:, 0:128], lhsT=BT[:, csl], rhs=CTb[:, csl], start=True, stop=True, r=[BT, CTb], w=[p_])
                        P.copy("dve", GT[:, c, :], p_[:, 0:128], r=[p_], w=[GT])
                dt = P.sb("dt", [128, 16, 32])
                av = P.sb("av", [128, 16, 32])
                Aneg = P.sb("Aneg", [128, 32])
                P.dma("sp", dt[:], tokm_d[:, :, 256:288], r=[tokm_d], w=[dt])
                P.tt("dve", dt[:], dt[:], RVc("ab_dtb", 0, 32).unsqueeze(1).to_broadcast([128, 16, 32]), ALU.add, r=[dt, rvt], w=[dt])
                P.act(dt[:], dt[:], AF.Exp, r=[dt], w=[dt])
                P.act(dt[:], dt[:], AF.Ln, r=[dt], w=[dt], bias=1.0)
                P.act(Aneg[:], RVc("ab_alog", 0, 32), AF.Exp, r=[rvt], w=[Aneg])
                P.stt("dve", av[:], dt[:], -1.0, Aneg[:].unsqueeze(1).to_broadcast([128, 16, 32]), ALU.mult, ALU.mult, r=[dt, Aneg], w=[av])

                xsT = [P.sb(f"xsT{i}", [128, S]) for i in range(2)]
                Y = [P.sb(f"Y{i}", [128, S]) for i in range(2)]
                xtm = [P.sb(f"xtm{i}", [128, 16, 128], BF16) for i in range(2)]
                xpad = [[P.sb(f"xpad{i}{h}", [128, 16, 128], BF16) for h in range(2)] for i in range(2)]
                Vall = P.sb("Vall", [128, 8, S], BF16)
                NC_ = 4
                Sm = [P.sb(f"Sm{i}", [128, 128]) for i in range(NC_)]
                Spad = [[P.sb(f"Spad{i}{h}", [128, 128], BF16) for h in range(2)] for i in range(NC_)]
                Rt_ = [P.sb(f"R{i}", [128, 2, 128]) for i in range(NC_)]
                EG = [P.sb(f"EG{i}", [128, 2, 128]) for i in range(NC_)]
                DC = [P.sb(f"DC{i}", [128, 2, 128]) for i in range(NC_)]
                kds = [P.sb(f"kds{i}", [128, 2]) for i in range(NC_)]
                AT = [P.sb(f"AT{i}", [128, 2, 128], BF16) for i in range(NC_)]
                QD = [P.sb(f"QD{i}", [128, 2, 128], BF16) for i in range(NC_)]
                KD = [P.sb(f"KD{i}", [128, 2, 128], BF16) for i in range(NC_)]
                nst = 16

                def chain(pi, hp, d, ci):
                    g_, y_ = pg[ci], py[ci]
                    CTb, Btm, GT = CTb2[hp // 4], Btm2[hp // 4], GT2[hp // 4]
                    U = C("Uf") if d == 0 else C("Ub")
                    M = C("Mf") if d == 0 else C("Mb")
                    NG = C("NEGf", 256) if d == 0 else C("NEGb", 256)
                    hcol = slice(d * 16 + hp * 2, d * 16 + hp * 2 + 2)
                    ccol = 127 if d == 0 else 0
                    P.op("dve", lambda e: e.memset(Sm[ci][:], 0.0), w=[Sm[ci]])
                    for h in range(2):
                        P.op("dve", lambda e: e.memset(Spad[ci][h][:], 0.0), w=[Spad[ci][h]])
                    for step in range(nst):
                        c = step if d == 0 else 15 - step
                        csl = slice(c * 128, (c + 1) * 128)
                        a2 = av[:, c, hcol]
                        R_ = Rt_[ci]
                        P.tt("dve", R_[:], U.unsqueeze(1).to_broadcast([128, 2, 128]), a2.unsqueeze(2).to_broadcast([128, 2, 128]),
                             ALU.mult, r=[cst, av], w=[R_])
                        Rf = R_[:].rearrange("p h i -> p (h i)")
                        P.mm(g_[:, 0:256], lhsT=C("ones"), rhs=Rf, start=True, stop=True, r=[R_, cst], w=[g_])
                        P.mm(g_[:, 256:512], lhsT=M, rhs=Rf, start=True, stop=False, r=[R_, cst], w=[g_])
                        P.mm(g_[:, 256:512], lhsT=C("ident"), rhs=NG, start=False, stop=True, r=[cst], w=[g_])
                        P.mm(y_[:, 256:258], lhsT=M, rhs=a2, start=True, stop=True, r=[av, cst], w=[y_])
                        yield
                        P.act(EG[ci][:].rearrange("p h i -> p (h i)"), g_[:, 0:256], AF.Exp, r=[g_], w=[EG[ci]])
                        P.act(DC[ci][:].rearrange("p h i -> p (h i)"), g_[:, 256:512], AF.Exp, r=[g_], w=[DC[ci]])
                        P.act(kds[ci][:], y_[:, 256:258], AF.Exp, r=[y_], w=[kds[ci]])
                        P.tt("dve", kds[ci][:], kds[ci][:], dt[:, c, hcol], ALU.mult, r=[kds[ci], dt], w=[kds[ci]])
                        for h in range(2):
                            P.stt("dve", AT[ci][:, h, :], DC[ci][:, h, :], dt[:, c, d * 16 + hp * 2 + h:d * 16 + hp * 2 + h + 1],
                                  GT[:, c, :], ALU.mult, ALU.mult, r=[DC[ci], dt, GT], w=[AT[ci]])
                        P.tt("dve", QD[ci][:], CTb[:, csl].unsqueeze(1).to_broadcast([128, 2, 128]), EG[ci][:], ALU.mult,
                             r=[CTb, EG[ci]], w=[QD[ci]])
                        P.tt("dve", KD[ci][:], Btm[:, c, :].unsqueeze(1).to_broadcast([128, 2, 128]),
                             kds[ci][:].unsqueeze(2).to_broadcast([128, 2, 128]), ALU.mult, r=[Btm, kds[ci]], w=[KD[ci]])
                        for h in range(2):
                            P.mm(y_[:, 0:128], lhsT=Spad[ci][h][:], rhs=QD[ci][:, h, :], start=(h == 0), stop=False,
                                 r=[Spad[ci][h], QD[ci]], w=[y_])
                        for h in range(2):
                            P.mm(y_[:, 0:128], lhsT=xpad[pi][h][:, c, :], rhs=AT[ci][:, h, :], start=False, stop=(h == 1),
                                 r=[xpad[pi][h], AT[ci]], w=[y_])
                        for h in range(2):
                            P.mm(y_[:, 128 + h * 64:128 + (h + 1) * 64], lhsT=KD[ci][:, h, :], rhs=xtm[pi][:, c, h * 64:(h + 1) * 64],
                                 start=True, stop=True, r=[KD[ci], xtm[pi]], w=[y_])
                        yield
                        P.tt("dve", Y[pi][:, csl], Y[pi][:, csl], y_[:, 0:128], ALU.add, r=[y_, (Y[pi], c)], w=[(Y[pi], c)])
                        for h in range(2):
                            hs = slice(h * 64, (h + 1) * 64)
                            P.stt("dve", Sm[ci][:, hs], Sm[ci][:, hs], EG[ci][:, h, ccol:ccol + 1], y_[:, 128 + h * 64:128 + (h + 1) * 64],
                                  ALU.mult, ALU.add, r=[Sm[ci], EG[ci], y_], w=[Sm[ci]])
                            P.copy("act", Spad[ci][h][:, hs], Sm[ci][:, hs], r=[Sm[ci]], w=[Spad[ci][h]])

                for rnd in range(4):
                    for pi in range(2):
                        hp = rnd * 2 + pi
                        conv_fm(2560 + hp * 128, "ab_cw", "ab_cb", hp, xsT[pi], xp, True)
                        for c in range(16):
                            p_ = pg[c % 4]
                            P.tr(p_[:, 0:128], xsT[pi][:, c * 128:(c + 1) * 128], C("ident"), r=[xsT[pi], cst], w=[p_])
                            P.copy("act", xtm[pi][:, c, :], p_[:, 0:128], r=[p_], w=[xtm[pi]])
                        P.ts("dve", Y[pi][:], xsT[pi][:], PVc("d_skip", hp), None, ALU.mult, None, r=[xsT[pi], pvt], w=[Y[pi]])
                        for h in range(2):
                            P.op("dve", lambda e: e.memset(xpad[pi][h][:], 0.0), w=[xpad[pi][h]])
                            P.copy("dve", xpad[pi][h][:, :, h * 64:(h + 1) * 64], xtm[pi][:, :, h * 64:(h + 1) * 64], r=[xtm[pi]], w=[xpad[pi][h]])
                    gens = [chain(pi, rnd * 2 + pi, d, pi * 2 + d) for pi in range(2) for d in range(2)]
                    while gens:
                        for g_ in list(gens):
                            try:
                                next(g_)
                            except StopIteration:
                                gens.remove(g_)
                    for pi in range(2):
                        hp = rnd * 2 + pi
                        P.dma("sp", tmp[:], projT_d[4128 + hp * 128:4128 + (hp + 1) * 128, :], r=[projT_d], w=[tmp])
                        P.act(tmp[:], tmp[:], AF.Silu, r=[tmp], w=[tmp])
                        P.tt("dve", Vall[:, hp, :], Y[pi][:], tmp[:], ALU.mult, r=[Y[pi], tmp], w=[(Vall, hp)])
                sqb = [P.sb(f"ssq{i}", [128, 512], BF16) for i in range(2)]
                rsb = P.sb("srs", [128, 512])
                ob = [P.sb(f"sob{i}", [128, 512], BF16) for i in range(2)]
                for tb in range(4):
                    tsl = slice(tb * 512, (tb + 1) * 512)
                    for hp in range(8):
                        P.tt("dve", sqb[hp % 2][:], Vall[:, hp, tsl], Vall[:, hp, tsl], ALU.mult, r=[(Vall, hp)], w=[sqb[hp % 2]])
                        P.mm(pg[0][:], lhsT=onesb[:], rhs=sqb[hp % 2][:], start=(hp == 0), stop=(hp == 7), r=[sqb[hp % 2], onesb], w=[pg[0]])
                    P.act(rsb[:], pg[0][:], AF.Ln, r=[pg[0]], w=[rsb], scale=1.0 / 1024, bias=EPS)
                    P.act(rsb[:], rsb[:], AF.Exp, r=[rsb], w=[rsb], scale=-0.5)
                    for hp in range(8):
                        o_ = ob[hp % 2]
                        P.stt("dve", o_[:], Vall[:, hp, tsl], PVc("ssd_norm", hp), rsb[:], ALU.mult, ALU.mult, r=[(Vall, hp), rsb, pvt], w=[o_])
                        P.dma("pool", yT_d[1024 + hp * 128:1024 + (hp + 1) * 128, tsl], o_[:], r=[o_], w=[(yT_d, 8 + hp)])

        def stage_gdn():
            GP = "dve"
            with P.scope():
                xp = P.sb("gxp", [128, S + 3])
                pad_init(xp)
                tmp = P.sb("gtmp", [128, S])
                gt = P.sb("ggt", [128, 16, 32])
                P.dma("sp", gt[:], tokm_d[:, :, 0:32], r=[tokm_d], w=[gt])
                beta = P.sb("gbeta", [128, 16, 16])
                nbeta = P.sb("gnbeta", [128, 16, 16])
                gg = P.sb("ggg", [128, 16, 16])
                An = P.sb("gAn", [128, 16])
                P.act(beta[:], gt[:, :, 0:16], AF.Exp, r=[gt], w=[beta], scale=-1.0)
                P.ts("dve", beta[:], beta[:], 1.0, None, ALU.add, None, r=[beta], w=[beta])
                P.op("dve", lambda e: e.reciprocal(out=beta[:], in_=beta[:]), r=[beta], w=[beta])
                P.ts("dve", nbeta[:], beta[:], -1.0, None, ALU.mult, None, r=[beta], w=[nbeta])
                P.tt("dve", gg[:], gt[:, :, 16:32], RVc("cd_dtb", 0, 16).unsqueeze(1).to_broadcast([128, 16, 16]), ALU.add, r=[gt, rvt], w=[gg])
                P.act(gg[:], gg[:], AF.Exp, r=[gg], w=[gg])
                P.act(gg[:], gg[:], AF.Ln, r=[gg], w=[gg], bias=1.0)
                P.act(An[:], RVc("cd_alog", 0, 16), AF.Exp, r=[rvt], w=[An])
                P.stt("dve", gg[:], gg[:], -1.0, An[:].unsqueeze(1).to_broadcast([128, 16, 16]), ALU.mult, ALU.mult, r=[gg, An], w=[gg])

                qT = P.sb("gqT", [128, S])
                kT = P.sb("gkT", [128, S])
                ktm = P.sb("gktm", [128, 16, 128])
                vtm = P.sb("gvtm", [128, 16, 256])
                O = P.sb("gO", [128, 2, S])
                sq = [P.sb(f"gsq{i}", [128, 512]) for i in range(2)]
                rsb = [P.sb(f"grs{i}", [128, 512]) for i in range(2)]
                NS = 4
                RR = [P.sb(f"gRR{i}", [128, 256]) for i in range(NS)]
                E = [P.sb(f"gE{i}", [128, 388]) for i in range(NS)]
                X = [[P.sb(f"gX{i}_{j}", [128, 128]) for j in range(2)] for i in range(NS)]
                XT = [[P.sb(f"gXT{i}_{j}", [128, 128]) for j in range(2)] for i in range(NS)]
                PT = [[P.sb(f"gPT{i}_{j}", [128, 128]) for j in range(2)] for i in range(NS)]
                AT = [P.sb(f"gAT{i}", [128, 128]) for i in range(NS)]
                vb = [P.sb(f"gvb{i}", [128, 128]) for i in range(NS)]
                kbg = [P.sb(f"gkbg{i}", [128, 128]) for i in range(NS)]
                bge = [P.sb(f"gbge{i}", [128, 1]) for i in range(NS)]
                u_ = [P.sb(f"gu{i}", [128, 128]) for i in range(NS)]
                wT = [P.sb(f"gwT{i}", [128, 128]) for i in range(NS)]
                qd = [P.sb(f"gqd{i}", [128, 128]) for i in range(NS)]
                kdec = [P.sb(f"gkd{i}", [128, 128]) for i in range(NS)]
                vnew = [P.sb(f"gvn{i}", [128, 128]) for i in range(NS)]
                Sst = [P.sb(f"gS{d}", [128, 128]) for d in range(4)]
                ybf = [P.sb(f"gyb{i}", [128, 512], BF16) for i in range(2)]
                pX = [P.ps(f"gpX{i}") for i in range(4)]
                pY = [P.ps(f"gpY{i}") for i in range(4)]
                pA = pX
                qscale = 128.0 ** -0.5
                nhq = 4 if lim is None else lim[0]
                nst = 16 if lim is None else lim[1]
                cut = 0 if (lim is None or len(lim) < 3) else lim[2]

                def l2norm_fm(dst, scl):
                    for tb in range(4):
                        tsl = slice(tb * 512, (tb + 1) * 512)
                        b = tb % 2
                        P.act(sq[b][:], tmp[:, tsl], AF.Square, r=[tmp], w=[sq[b]])
                        P.mm(pA[b][:], lhsT=C("ones"), rhs=sq[b][:], start=True, stop=True, r=[sq[b], cst], w=[pA[b]])
                        P.act(rsb[b][:], pA[b][:], AF.Ln, r=[pA[b]], w=[rsb[b]], bias=EPS)
                        P.act(rsb[b][:], rsb[b][:], AF.Exp, r=[rsb[b]], w=[rsb[b]], scale=-0.5)
                        P.stt("dve", dst[:, tsl], tmp[:, tsl], scl, rsb[b][:], ALU.mult, ALU.mult, r=[tmp, rsb[b]], w=[dst])

                for hq in range(nhq):
                    conv_fm(hq * 128, "cd_cw", "cd_cb", hq, tmp, xp, True)
                    l2norm_fm(qT, qscale)
                    conv_fm(512 + hq * 128, "cd_cw", "cd_cb", 4 + hq, tmp, xp, True)
                    l2norm_fm(kT, 1.0)
                    for c in range(16):
                        p_ = pA[c % 2]
                        P.tr(p_[:, 0:128], kT[:, c * 128:(c + 1) * 128], C("ident"), r=[kT, cst], w=[p_])
                        P.copy("act", ktm[:, c, :], p_[:, 0:128], r=[p_], w=[ktm])
                    for e in range(2):
                        conv_fm(1024 + (2 * hq + e) * 128, "cd_cw", "cd_cb", 8 + 2 * hq + e, tmp, xp, True)
                        for c in range(16):
                            p_ = pA[c % 2]
                            P.tr(p_[:, 0:128], tmp[:, c * 128:(c + 1) * 128], C("ident"), r=[tmp, cst], w=[p_])
                            P.copy("act", vtm[:, c, e * 128:(e + 1) * 128], p_[:, 0:128], r=[p_], w=[vtm])
                    P.op("dve", lambda e_: e_.memset(O[:], 0.0), w=[O])
                    def chain(e, d, ci):
                        hv = 2 * hq + e
                        S_ = Sst[ci]
                        a_ = pX[ci]
                        c_ = pY[ci]
                        bi = ci
                        UM = C("Uf", 256) if d == 0 else C("Ub", 256)
                        U, M = UM[:, 0:128], UM[:, 128:256]
                        NGi = C("NEGf") if d == 0 else C("NEGb")
                        NGs = C("NEGsf") if d == 0 else C("NEGsb")
                        col = d * 8 + hv
                        ccol = 127 if d == 0 else 0
                        P.op("dve", lambda e_: e_.memset(S_[:], 0.0), w=[S_])
                        for step in range(nst):
                            c = step if d == 0 else 15 - step
                            csl = slice(c * 128, (c + 1) * 128)
                            gcol = gg[:, c, col:col + 1]
                            bcol = beta[:, c, col:col + 1]
                            nbcol = nbeta[:, c, col:col + 1]
                            P.ts("pool", RR[bi][:], UM, gcol, None, ALU.mult, None, r=[cst, gg], w=[RR[bi]])
                            R1, R2 = RR[bi][:, 0:128], RR[bi][:, 128:256]
                            P.mm(a_[:, 0:128], lhsT=C("ones"), rhs=R1, start=True, stop=True, r=[RR[bi], cst], w=[a_])
                            P.mm(a_[:, 128:256], lhsT=M, rhs=R1, start=True, stop=False, r=[RR[bi], cst], w=[a_])
                            P.mm(a_[:, 128:256], lhsT=C("ident"), rhs=NGi, start=False, stop=True, r=[cst], w=[a_])
                            P.mm(a_[:, 256:384], lhsT=U, rhs=R2, start=True, stop=False, r=[RR[bi], cst], w=[a_])
                            P.mm(a_[:, 256:384], lhsT=C("ident"), rhs=NGs, start=False, stop=True, r=[cst], w=[a_])
                            P.mm(a_[:, 384:385], lhsT=M, rhs=gcol, start=True, stop=True, r=[gg, cst], w=[a_])
                            P.mm(a_[:, 385:386], lhsT=U, rhs=gcol, start=True, stop=True, r=[gg, cst], w=[a_])
                            yield
                            E_ = E[bi]
                            P.act(E_[:, 0:386], a_[:, 0:386], AF.Exp, r=[a_], w=[E_])
                            EGb, decT, decS = E_[:, 0:128], E_[:, 128:256], E_[:, 256:384]
                            kds, eg = E_[:, 384:385], E_[:, 385:386]
                            P.mm(c_[:, 0:128], lhsT=kT[:, csl], rhs=kT[:, csl], start=True, stop=True, r=[kT], w=[c_])
                            P.mm(c_[:, 128:256], lhsT=kT[:, csl], rhs=qT[:, csl], start=True, stop=True, r=[kT, qT], w=[c_])
                            yield
                            X0 = X[bi][0]
                            P.stt("dve", X0[:], c_[:, 0:128], nbcol, decS, ALU.mult, ALU.mult, r=[c_, nbeta, E_], w=[X0])
                            P.tt("dve", AT[bi][:], c_[:, 128:256], decT, ALU.mult, r=[c_, E_], w=[AT[bi]])
                            P.act(vb[bi][:], vtm[:, c, e * 128:(e + 1) * 128], AF.Copy, r=[vtm, beta], w=[vb[bi]], scale=bcol)
                            P.tt("dve", bge[bi][:], bcol, eg, ALU.mult, r=[beta, E_], w=[bge[bi]])
                            P.act(kbg[bi][:], ktm[:, c, :], AF.Copy, r=[ktm, bge[bi]], w=[kbg[bi]], scale=bge[bi][:])
                            P.tt("dve", qd[bi][:], qT[:, csl], EGb, ALU.mult, r=[qT, E_], w=[qd[bi]])
                            P.act(kdec[bi][:], ktm[:, c, :], AF.Copy, r=[ktm, E_], w=[kdec[bi]], scale=kds)
                            P.tr(a_[:, 0:128], X0[:], C("ident"), r=[X0, cst], w=[a_])
                            yield
                            P.copy("act", XT[bi][0][:], a_[:, 0:128], r=[a_], w=[XT[bi][0]])
                            P.tt("dve", PT[bi][0][:], a_[:, 0:128], C("ident"), ALU.add, r=[a_, cst], w=[PT[bi][0]])
                            for k in range(1, 7):
                                xo, xn = X[bi][(k - 1) % 2], X[bi][k % 2]
                                to, tn = XT[bi][(k - 1) % 2], XT[bi][k % 2]
                                po_, pn = PT[bi][(k - 1) % 2], PT[bi][k % 2]
                                P.mm(c_[:, 128:256], lhsT=to[:], rhs=xo[:], start=True, stop=True, r=[to, xo], w=[c_])
                                if k < 6:
                                    P.mm(c_[:, 0:128], lhsT=xo[:], rhs=to[:], start=True, stop=True, r=[to, xo], w=[c_])
                                yield
                                P.copy("act", xn[:], c_[:, 128:256], r=[c_], w=[xn])
                                if k < 6:
                                    P.copy("act", tn[:], c_[:, 0:128], r=[c_], w=[tn])
                                P.mm(a_[:, 256:384], lhsT=xn[:], rhs=po_[:], start=True, stop=True, r=[xn, po_], w=[a_])
                                yield
                                P.tt("dve", pn[:], a_[:, 256:384], po_[:], ALU.add, r=[a_, po_], w=[pn])
                            TT = PT[bi][0]
                            P.mm(c_[:, 256:384], lhsT=TT[:], rhs=vb[bi][:], start=True, stop=True, r=[TT, vb[bi]], w=[c_])
                            P.mm(c_[:, 384:512], lhsT=kbg[bi][:], rhs=TT[:], start=True, stop=True, r=[TT, kbg[bi]], w=[c_])
                            yield
                            P.copy("act", u_[bi][:], c_[:, 256:384], r=[c_], w=[u_[bi]])
                            P.copy("act", wT[bi][:], c_[:, 384:512], r=[c_], w=[wT[bi]])
                            P.mm(a_[:, 0:128], lhsT=wT[bi][:], rhs=S_[:], start=True, stop=True, r=[wT[bi], S_], w=[a_])
                            yield
                            P.tt("dve", vnew[bi][:], u_[bi][:], a_[:, 0:128], ALU.subtract, r=[u_[bi], a_], w=[vnew[bi]])
                            P.mm(c_[:, 0:128], lhsT=S_[:], rhs=qd[bi][:], start=True, stop=False, r=[S_, qd[bi]], w=[c_])
                            P.mm(c_[:, 0:128], lhsT=vnew[bi][:], rhs=AT[bi][:], start=False, stop=True, r=[vnew[bi], AT[bi]], w=[c_])
                            P.mm(c_[:, 128:256], lhsT=kdec[bi][:], rhs=vnew[bi][:], start=True, stop=True, r=[kdec[bi], vnew[bi]], w=[c_])
                            yield
                            P.tt("dve", O[:, e, csl], O[:, e, csl], c_[:, 0:128], ALU.add, r=[c_, (O, e * 16 + c)], w=[(O, e * 16 + c)])
                            P.stt("dve", S_[:], S_[:], EGb[:, ccol:ccol + 1], c_[:, 128:256], ALU.mult, ALU.add, r=[S_, E_, c_], w=[S_])

                    gens = [chain(e, d, e * 2 + d) for e in range(2) for d in range(2)]
                    while gens:
                        for g_ in list(gens):
                            try:
                                next(g_)
                            except StopIteration:
                                gens.remove(g_)
                    for e in range(2):
                        hv = 2 * hq + e
                        P.dma("sp", tmp[:], projT_d[2080 + hv * 128:2080 + (hv + 1) * 128, :], r=[projT_d], w=[tmp])
                        P.act(tmp[:], tmp[:], AF.Silu, r=[tmp], w=[tmp])
                        for tb in range(4):
                            tsl = slice(tb * 512, (tb + 1) * 512)
                            b = tb % 2
                            P.act(sq[b][:], O[:, e, tsl], AF.Square, r=[O], w=[sq[b]])
                            P.mm(pA[b][:], lhsT=C("ones"), rhs=sq[b][:], start=True, stop=True, r=[sq[b], cst], w=[pA[b]])
                            P.act(rsb[b][:], pA[b][:], AF.Ln, r=[pA[b]], w=[rsb[b]], scale=1.0 / 128, bias=EPS)
                            P.act(rsb[b][:], rsb[b][:], AF.Exp, r=[rsb[b]], w=[rsb[b]], scale=-0.5)
                            P.stt("dve", sq[b][:], O[:, e, tsl], PVc("gdn_norm"), rsb[b][:], ALU.mult, ALU.mult, r=[O, rsb[b], pvt], w=[sq[b]])
                            P.tt("dve", ybf[b][:], sq[b][:], tmp[:, tsl], ALU.mult, r=[sq[b], tmp], w=[ybf[b]])
                            P.dma("pool", yT_d[hv * 128:(hv + 1) * 128, tsl], ybf[b][:], r=[ybf[b]], w=[(yT_d, hv)])

        def stage_lru():
            with P.scope():
                xp = P.sb("lxp", [128, S + 3])
                pad_init(xp)
                xc = P.sb("lxc", [128, S])
                wl = P.sb("lw", [128, 4, 8, 128])
                for m_ in range(4):
                    P.dma("sp", wl[:, m_, :, :], lruw_d.h.ap()[m_].rearrange("n i j -> i n j"), w=[(wl, m_)])
                nsp = P.sb("lnsp", [128, 16])
                for d, nm in enumerate(("lam_f", "lam_b")):
                    P.act(nsp[:, d * 8:(d + 1) * 8], PVc(nm, 0, 8), AF.Exp, r=[pvt], w=[(nsp, d)], scale=-1.0)
                    P.act(nsp[:, d * 8:(d + 1) * 8], nsp[:, d * 8:(d + 1) * 8], AF.Ln, r=[(nsp, d)], w=[(nsp, d)], bias=1.0)
                    P.ts("dve", nsp[:, d * 8:(d + 1) * 8], nsp[:, d * 8:(d + 1) * 8], -8.0, None, ALU.mult, None, r=[(nsp, d)], w=[(nsp, d)])
                rr = P.sb("lrr", [128, S])
                ig = P.sb("lig", [128, S])
                aa = P.sb("laa", [128, S])
                mm_ = P.sb("lmm", [128, S])
                hh = [P.sb(f"lhh{d}", [128, S]) for d in range(2)]
                gl = P.sb("lgl", [128, S])
                yb = P.sb("lyb", [128, S], BF16)
                pp = [P.ps(f"lp{i}") for i in range(4)]

                def rev(t):
                    return bass.AP(t.h, S - 1, [[S, 128], [-1, S]])

                for n in range(8 if lim is None else lim[0]):
                    conv_fm(3104 + n * 128, "lru_cw", "lru_cb", n, xc, xp, False)
                    P.dma("sp", gl[:], projT_d[4128 + n * 128:4128 + (n + 1) * 128, :], r=[projT_d], w=[gl])
                    for d in range(2):
                        sfx = "f" if d == 0 else "b"
                        for tb in range(4):
                            tsl = slice(tb * 512, (tb + 1) * 512)
                            p1, p2 = pp[(tb % 2) * 2], pp[(tb % 2) * 2 + 1]
                            P.mm(p1[:], lhsT=wl[:, 2 * d, n, :], rhs=xc[:, tsl], start=True, stop=True, r=[(wl, 2 * d), xc], w=[p1])
                            P.mm(p2[:], lhsT=wl[:, 2 * d + 1, n, :], rhs=xc[:, tsl], start=True, stop=True, r=[(wl, 2 * d + 1), xc], w=[p2])
                            P.act(rr[:, tsl], p1[:], AF.Sigmoid, r=[p1, pvt], w=[(rr, tb)], bias=PVc("ba_" + sfx, n))
                            P.act(ig[:, tsl], p2[:], AF.Sigmoid, r=[p2, pvt], w=[(ig, tb)], bias=PVc("bx_" + sfx, n))
                        P.act(aa[:], rr[:], AF.Exp, r=[rr, nsp], w=[aa], scale=nsp[:, d * 8 + n:d * 8 + n + 1])
                        P.tt("pool", mm_[:], aa[:], aa[:], ALU.mult, r=[aa], w=[mm_])
                        P.act(mm_[:], mm_[:], AF.Ln, r=[mm_], w=[mm_], scale=-1.0, bias=1.0)
                        P.act(mm_[:], mm_[:], AF.Exp, r=[mm_], w=[mm_], scale=0.5)
                        P.tt("pool", ig[:], ig[:], xc[:], ALU.mult, r=[ig, xc], w=[ig])
                        P.tt("dve", mm_[:], mm_[:], ig[:], ALU.mult, r=[mm_, ig], w=[mm_])
                        if d == 0:
                            P.op("dve", lambda e: e.tensor_tensor_scan(out=hh[0][:], data0=aa[:], data1=mm_[:], initial=0.0,
                                                                       op0=ALU.mult, op1=ALU.add), r=[aa, mm_], w=[hh[0]])
                        else:
                            P.op("dve", lambda e: e.tensor_tensor_scan(out=rev(hh[1]), data0=rev(aa), data1=rev(mm_), initial=0.0,
                                                                       op0=ALU.mult, op1=ALU.add), r=[aa, mm_], w=[hh[1]])
                    P.act(gl[:], gl[:], AF.Silu, r=[gl], w=[gl])
                    P.tt("pool", hh[0][:], hh[0][:], hh[1][:], ALU.add, r=[hh[0], hh[1]], w=[hh[0]])
                    P.tt("dve", yb[:], hh[0][:], gl[:], ALU.mult, r=[hh[0], gl], w=[yb])
                    P.dma("pool", yT_d[1024 + n * 128:1024 + (n + 1) * 128, :], yb[:], r=[yb], w=[(yT_d, 8 + n)])

        if only is not None:
            {"ssd": stage_ssd, "attn": stage_attn, "gdn": stage_gdn, "lru": stage_lru}[only]()
            P.barrier()
            return nc, dbg
        stage_mod()
        if debug:
            md = scratch("dbg_mod", [128, 96])
            P.dma("pool", md[:], modsb[:].rearrange("p l e -> p (l e)"), r=[modsb], w=[md])
        with P.scope():
            hT = P.sb("hT", [128, 16, S], BF16)
            stage_norm(xT_d, lambda k: sc1[:, 0, k:k + 1], lambda k: modsb[:, 0, k:k + 1], out_tile=hT)
            if debug:
                hd = scratch("dbg_h", [128, 16, S], BF16)
                P.dma("pool", hd[:], hT[:], r=[hT], w=[hd])
            fm = [(c * 128, 128, c * 128) for c in range(32) if not (10 <= c < 12)]
            fm += [(4128 + c * 128, 128, 4128 + c * 128) for c in range(8)]
            stage_inproj(hT, w_in_d[0], fm, [(1280, 256, 0), (4096, 32, 256)])
        if upto == "proj0":
            return nc, dbg
        if upto != "ssd":
            stage_attn()
        if upto == "attn":
            return nc, dbg
        stage_ssd()
        if upto == "ssd":
            return nc, dbg
        stage_outproj(w_out_d[0], xT_d, x1T_d, 0)
        if upto == "l0":
            return nc, dbg
        with P.scope():
            hT = P.sb("hT1", [128, 16, S], BF16)
            stage_norm(x1T_d, lambda k: sc1[:, 1, k:k + 1], lambda k: modsb[:, 1, k:k + 1], out_tile=hT)
            fm = [(c * 128, 128, c * 128) for c in range(16)]
            fm += [(2080 + c * 128, 128, 2080 + c * 128) for c in range(24)]
            stage_inproj(hT, w_in_d[1], fm, [(2048, 32, 0)])
        stage_gdn()
        stage_lru()
        stage_outproj(w_out_d[1], x1T_d, x2T_d, 1)
        stage_norm(x2T_d, lambda k: PVc("fnorm", k), lambda k: 0.0, out_dram=out_d)
        P.barrier()
    return nc, dbg


def make_inputs(inp, b):
    pv, rv = pack_params(inp)
    m = {
        "xT": np.ascontiguousarray(inp["x"][b].T),
        "cT": np.ascontiguousarray(inp["c"][b].reshape(16, 128).T),
        "w_mod": np.ascontiguousarray(inp["w_mod"]),
        "ab_w_in": np.ascontiguousarray(inp["ab_w_in"][0]),
        "cd_w_in": np.ascontiguousarray(inp["cd_w_in"][0]),
        "ab_w_out": np.ascontiguousarray(inp["ab_w_out"][0]),
        "cd_w_out": np.ascontiguousarray(inp["cd_w_out"][0]),
        "lru_w": np.ascontiguousarray(np.stack([inp["cd_lru_wa_f"][0], inp["cd_lru_wx_f"][0], inp["cd_lru_wa_b"][0], inp["cd_lru_wx_b"][0]], 0)),
        "consts": CONSTS,
        "rope": _rope_tables(),
        "pvec": pv,
        "rvec": rv,
    }
    return m


def kernel(**inputs):
    inp = {k: np.asarray(v, np.float32) for k, v in inputs.items()}
    maps = [make_inputs(inp, i // 2) for i in range(8)]
    nc, _ = build(maps[0]["pvec"].shape[1], maps[0]["rvec"].shape[1])
    res = run_bass_kernel_spmd(nc, maps, core_ids=list(range(8)))
    out = np.stack([np.asarray(res.results[2 * b]["outT"], np.float32).T for b in range(4)], 0)
    return np.ascontiguousarray(out)
```

```python
import math
import numpy as np
from contextlib import ExitStack, contextmanager
import concourse.bass as bass
import concourse.mybir as mybir
from concourse.bass_utils import run_bass_kernel_spmd

F32 = mybir.dt.float32
BF16 = mybir.dt.bfloat16
AF = mybir.ActivationFunctionType
ALU = mybir.AluOpType

D = 2048
S = 2048
EPS = 1e-6
NEG = -30000.0


class Tn:
    def __init__(self, h, name, psum=False):
        self.h = h
        self.name = name
        self.st = {}
        self.psum = psum

    def __getitem__(self, idx):
        return self.h[idx]


class Prog:
    NDMA = {"sp": 14, "pool": 8}

    def __init__(self, nc, es):
        self.nc = nc
        self.stack = [es]
        self.h = {"pe": nc.tensor, "act": nc.scalar, "dve": nc.vector, "pool": nc.gpsimd, "sp": nc.sync}
        self.sem = {e: es.enter_context(nc.semaphore("s_" + e)) for e in ("pe", "act", "dve", "pool")}
        self.cnt = {e: 0 for e in self.sem}
        self.dsem = {q: [es.enter_context(nc.semaphore(f"d_{q}{i}")) for i in range(n)] for q, n in self.NDMA.items()}
        self.dcnt = {q: 0 for q in self.NDMA}
        self.waited = {e: {} for e in self.h}
        self.semobj = {}
        for s in self.sem.values():
            self.semobj[id(s)] = s
        for l in self.dsem.values():
            for s in l:
                self.semobj[id(s)] = s
        self.uid = 0

    def sb(self, name, shape, dt=F32):
        self.uid += 1
        name = f"{name}_{self.uid}"
        return Tn(self.stack[-1].enter_context(self.nc.sbuf_tensor(name, list(shape), dt)), name)

    def ps(self, name):
        self.uid += 1
        name = f"{name}_{self.uid}"
        return Tn(self.stack[-1].enter_context(self.nc.psum_tensor(name, [128, 512], F32)), name, psum=True)

    def dram(self, name, shape, dt=F32, kind="Internal"):
        return Tn(self.nc.dram_tensor(name, list(shape), dt, kind=kind), name)

    @contextmanager
    def scope(self):
        es = ExitStack()
        self.stack.append(es)
        try:
            yield
        finally:
            self.barrier()
            self.stack.pop()
            es.close()

    def barrier(self):
        toks = [(self.sem[e], self.cnt[e]) for e in self.sem if self.cnt[e] > 0]
        for q, lst in self.dsem.items():
            n = self.dcnt[q]
            for i, s in enumerate(lst):
                uses = (n - i + len(lst) - 1) // len(lst) if n > i else 0
                if uses > 0:
                    toks.append((s, 16 * uses))
        for e, h in self.h.items():
            for s, v in toks:
                if e in self.sem and self.sem[e] is s:
                    continue
                if self.waited[e].get(id(s), 0) >= v:
                    continue
                self.waited[e][id(s)] = v
                h.wait_ge(s, v)

    @staticmethod
    def _norm(x):
        if isinstance(x, tuple):
            return (x[0], None) if x[0].psum else x
        return (x, None)

    def _states(self, t, sub):
        if sub is None:
            return list(t.st.values())
        out = []
        if None in t.st:
            out.append(t.st[None])
        if sub in t.st:
            out.append(t.st[sub])
        return out

    def _deps(self, eng, r, w):
        need = {}

        def add(tok, kind):
            if tok is None:
                return
            sid, val, src = tok
            if src == eng:
                if eng in ("pe", "sp"):
                    return
            if self.waited[eng].get(sid, 0) >= val:
                return
            if need.get(sid, 0) < val:
                need[sid] = val

        for x in r:
            t, sub = self._norm(x)
            for s in self._states(t, sub):
                add(s[0], "raw")
        for x in w:
            t, sub = self._norm(x)
            for s in self._states(t, sub):
                add(s[0], "waw")
                for tok in s[1].values():
                    add(tok, "war")
        for sid, val in need.items():
            self.waited[eng][sid] = val
        return [(self.semobj[sid], val) for sid, val in need.items()]

    def _commit(self, who, tok, r, w):
        for x in r:
            t, sub = self._norm(x)
            if sub is None:
                if None not in t.st:
                    t.st[None] = [None, {}]
                for s in t.st.values():
                    s[1][who] = tok
            else:
                if sub not in t.st:
                    t.st[sub] = [None, {}]
                t.st[sub][1][who] = tok
        for x in w:
            t, sub = self._norm(x)
            if sub is None:
                t.st = {None: [tok, {}]}
            else:
                t.st[sub] = [tok, {}]

    def op(self, eng, fn, r=(), w=()):
        w = list(w) + [x for x in r if self._norm(x)[0].psum]
        waits = self._deps(eng, r, w)
        self.cnt[eng] += 1
        s = self.sem[eng]
        tok = (id(s), self.cnt[eng], eng)
        self._commit(eng, tok, r, w)
        h = self.h[eng]
        for ss, v in waits:
            h.wait_ge(ss, v)
        fn(h).then_inc(s, 1)

    def dma(self, q, out, in_, r=(), w=()):
        n = self.dcnt[q]
        self.dcnt[q] += 1
        pool = self.dsem[q]
        s = pool[n % len(pool)]
        use = n // len(pool)
        waits = self._deps(q, r, w)
        if use > 0 and self.waited[q].get(id(s), 0) < 16 * use:
            waits.append((s, 16 * use))
            self.waited[q][id(s)] = 16 * use
        tok = (id(s), 16 * (use + 1), "dma_" + q)
        self._commit(f"dma_{q}{n % len(pool)}", tok, r, w)
        h = self.h[q]
        for ss, v in waits:
            h.wait_ge(ss, v)
        h.dma_start(out=out, in_=in_).then_inc(s, 16)
        return tok

    def wait_tok(self, eng, tok):
        self.h[eng].wait_ge(self.semobj[tok[0]], tok[1])

    def act(self, out, in_, func, r, w, bias=0.0, scale=1.0, accum_out=None, eng="act"):
        if accum_out is None:
            self.op("act", lambda e: e.activation(out=out, in_=in_, func=func, bias=bias, scale=scale), r=r, w=w)
        else:
            self.op("act", lambda e: e.activation(out=out, in_=in_, func=func, bias=bias, scale=scale, accum_out=accum_out), r=r, w=w)

    def tt(self, eng, out, in0, in1, op, r, w):
        self.op(eng, lambda e: e.tensor_tensor(out=out, in0=in0, in1=in1, op=op), r=r, w=w)

    def ts(self, eng, out, in0, s1, s2, op0, op1, r, w):
        if s2 is None:
            self.op(eng, lambda e: e.tensor_scalar(out=out, in0=in0, scalar1=s1, scalar2=None, op0=op0), r=r, w=w)
        else:
            self.op(eng, lambda e: e.tensor_scalar(out=out, in0=in0, scalar1=s1, scalar2=s2, op0=op0, op1=op1), r=r, w=w)

    def stt(self, eng, out, in0, scalar, in1, op0, op1, r, w):
        self.op(eng, lambda e: e.scalar_tensor_tensor(out=out, in0=in0, scalar=scalar, in1=in1, op0=op0, op1=op1), r=r, w=w)

    def copy(self, eng, out, in_, r, w):
        if eng == "act":
            self.op("act", lambda e: e.copy(out=out, in_=in_), r=r, w=w)
        else:
            self.op(eng, lambda e: e.tensor_copy(out=out, in_=in_), r=r, w=w)

    def mm(self, out, lhsT, rhs, start, stop, r, w):
        self.op("pe", lambda e: e.matmul(out, lhsT=lhsT, rhs=rhs, start=start, stop=stop), r=r, w=w)

    def tr(self, out, in_, ident, r, w):
        self.op("pe", lambda e: e.transpose(out, in_, ident), r=r, w=w)


C_OFF = {}


def _consts():
    i = np.arange(128)
    sI, fI = i[:, None], i[None, :]
    mats = {
        "ident": (sI == fI), "ones": np.ones((128, 128)),
        "Uf": (sI <= fI), "Mf": (sI > fI), "Ub": (sI >= fI), "Mb": (sI < fI),
    }
    R = np.zeros((128, 128))
    for p in range(64):
        R[2 * p, 2 * p + 1] = -1.0
        R[2 * p + 1, 2 * p] = 1.0
    mats["Rt"] = R.T
    negs = {
        "NEGf": NEG * (fI < sI), "NEGb": NEG * (fI > sI),
        "NEGsf": NEG * (fI >= sI), "NEGsb": NEG * (fI <= sI),
    }
    cols = []
    off = 0
    for k, m in mats.items():
        C_OFF[k] = off
        cols.append(np.asarray(m, np.float32))
        off += 128
    for k, m in negs.items():
        C_OFF[k] = off
        cols.append(np.tile(np.asarray(m, np.float32), (1, 4)))
        off += 512
    return np.ascontiguousarray(np.concatenate(cols, axis=1)), off


CONSTS, NCONST = _consts()


def _rope_tables():
    t = np.arange(S)
    row = (t // 64).astype(np.float32)
    col = (t % 64).astype(np.float32)
    n_pairs = 32
    freqs = (np.float32(10000.0) ** (-np.arange(n_pairs, dtype=np.float32) / np.float32(n_pairs))).astype(np.float32)
    ang = np.concatenate([row[:, None] * freqs, col[:, None] * freqs], axis=-1).astype(np.float32)
    cos = np.cos(ang).astype(np.float32)
    sin = np.sin(ang).astype(np.float32)
    cosT = np.repeat(cos, 2, axis=1).T
    sinT = np.repeat(sin, 2, axis=1).T
    return np.ascontiguousarray(np.stack([cosT, sinT], 0))


PV = {}
RV = {}


def _pcols(v, n):
    return np.asarray(v, np.float32).reshape(n, 128).T


def pack_params(inp):
    pv, rv = [], []

    def addp(name, arr):
        PV[name] = (sum(a.shape[1] for a in pv), arr.shape[1])
        pv.append(np.asarray(arr, np.float32))

    def addr(name, vec):
        vec = np.asarray(vec, np.float32).reshape(-1)
        RV[name] = (sum(a.shape[1] for a in rv), vec.shape[0])
        rv.append(np.broadcast_to(vec[None, :], (128, vec.shape[0])))

    addp("q_norm", _pcols(inp["ab_q_norm"][0], 1))
    addp("k_norm", _pcols(inp["ab_k_norm"][0], 1))
    cw = inp["ab_conv_w"][0]
    addp("ab_cw", np.concatenate([_pcols(cw[j], 12)[:, :, None] for j in range(4)], 2).reshape(128, 48))
    addp("ab_cb", _pcols(inp["ab_conv_b"][0], 12))
    addp("d_skip", _pcols(np.repeat(inp["ab_d_skip"][0], 64), 8))
    addp("ssd_norm", _pcols(inp["ab_ssd_norm"][0], 8))
    cw = inp["cd_conv_w"][0]
    addp("cd_cw", np.concatenate([_pcols(cw[j], 16)[:, :, None] for j in range(4)], 2).reshape(128, 64))
    addp("cd_cb", _pcols(inp["cd_conv_b"][0], 16))
    addp("gdn_norm", _pcols(inp["cd_gdn_norm"][0], 1))
    cw = inp["cd_lru_conv_w"][0]
    addp("lru_cw", np.concatenate([_pcols(cw[j], 8)[:, :, None] for j in range(4)], 2).reshape(128, 32))
    addp("lru_cb", _pcols(inp["cd_lru_conv_b"][0], 8))
    for d_ in ("f", "b"):
        addp("ba_" + d_, _pcols(inp["cd_lru_ba_" + d_][0], 8))
        addp("bx_" + d_, _pcols(inp["cd_lru_bx_" + d_][0], 8))
        addp("lam_" + d_, _pcols(inp["cd_lru_lam_" + d_][0], 8))
    addp("fnorm", _pcols(inp["final_norm_w"], 16))
    for l in range(2):
        addp(f"norm{l}", _pcols(inp["norm_w"][l], 16))
        addp(f"bmod{l}", _pcols(inp["b_mod"][l], 48))
    addr("ab_dtb", np.concatenate([inp["ab_dt_bias_f"][0], inp["ab_dt_bias_b"][0]]))
    addr("ab_alog", np.concatenate([inp["ab_a_log_f"][0], inp["ab_a_log_b"][0]]))
    addr("cd_dtb", np.concatenate([inp["cd_dt_bias_f"][0], inp["cd_dt_bias_b"][0]]))
    addr("cd_alog", np.concatenate([inp["cd_a_log_f"][0], inp["cd_a_log_b"][0]]))
    return (np.ascontiguousarray(np.concatenate(pv, 1)), np.ascontiguousarray(np.concatenate(rv, 1)))


def build(npv, nrv, upto="all", debug=False, only=None, lim=None):
    nc = bass.Bass("TRN2", target_bir_lowering=False)
    es = ExitStack()
    dbg = {}
    with es:
        P = Prog(nc, es)
        skind = "ExternalOutput" if debug else "Internal"

        def din(name, shape, dt=F32):
            return P.dram(name, shape, dt, kind="ExternalInput")

        xT_d = din("xT", [D, S])
        cT_d = din("cT", [128, 16])
        wmod_d = din("w_mod", [2, D, 3 * D])
        w_in_d = [din("ab_w_in", [D, 5152]), din("cd_w_in", [D, 5152])]
        w_out_d = [din("ab_w_out", [D, D]), din("cd_w_out", [D, D])]
        lruw_d = din("lru_w", [4, 8, 128, 128])
        consts_d = din("consts", [128, NCONST])
        rope_d = din("rope", [2, 128, S])
        pv_d = din("pvec", [128, npv])
        rv_d = din("rvec", [128, nrv])
        out_d = P.dram("outT", [D, S], F32, kind="ExternalOutput")

        def scratch(name, shape, dt=F32):
            t = P.dram(name, shape, dt, kind=skind)
            dbg[name] = t
            return t

        if only is None:
            projT_d = scratch("projT", [5248, S])
            tokm_d = scratch("tokm", [128, 16, 320])
        else:
            projT_d = din("projT", [5248, S])
            tokm_d = din("tokm", [128, 16, 320])
        yT_d = scratch("yT", [D, S], BF16)
        x1T_d = scratch("x1T", [D, S])
        x2T_d = scratch("x2T", [D, S])

        cst = P.sb("cst", [128, NCONST])
        P.dma("sp", cst[:, 0:1536], consts_d[:, 0:1536], w=[cst])
        P.dma("sp", cst[:, 1536:NCONST], consts_d[:, 1536:NCONST], w=[cst])
        pvt = P.sb("pvt", [128, npv])
        P.dma("sp", pvt[:], pv_d[:], w=[pvt])
        rvt = P.sb("rvt", [128, nrv])
        P.dma("sp", rvt[:], rv_d[:], w=[rvt])
        onesb = P.sb("onesb", [128, 128], BF16)
        P.copy("dve", onesb[:], cst[:, C_OFF["ones"]:C_OFF["ones"] + 128], r=[cst], w=[onesb])
        identb = P.sb("identb", [128, 128], BF16)
        P.copy("dve", identb[:], cst[:, C_OFF["ident"]:C_OFF["ident"] + 128], r=[cst], w=[identb])

        def C(name, n=128):
            return cst[:, C_OFF[name]:C_OFF[name] + n]

        def PVc(name, j=0, n=1):
            o, _ = PV[name]
            return pvt[:, o + j:o + j + n]

        def RVc(name, j=0, n=1):
            o, _ = RV[name]
            return rvt[:, o + j:o + j + n]

        modsb = P.sb("modsb", [128, 2, 48])
        sc1 = P.sb("sc1", [128, 2, 16])

        def stage_mod():
            with P.scope():
                cond = P.sb("cond", [128, 16])
                P.dma("sp", cond[:], cT_d[:], w=[cond])
                P.act(cond[:], cond[:], AF.Silu, r=[cond], w=[cond])
                pm = P.ps("pm")
                wst = [P.sb(f"wst{i}", [128, 3 * D]) for i in range(3)]
                acc = P.sb("macc", [128, 3 * D])
                for l in range(2):
                    for k in range(16):
                        t = wst[(l * 16 + k) % 3]
                        P.dma("sp", t[:, 0:3072], wmod_d[l, k * 128:(k + 1) * 128, 0:3072], w=[(t, 0)])
                        P.dma("sp", t[:, 3072:6144], wmod_d[l, k * 128:(k + 1) * 128, 3072:6144], w=[(t, 1)])
                        for hf in range(2):
                            sl_ = slice(hf * 3072, (hf + 1) * 3072)
                            if k == 0:
                                P.ts("dve", acc[:, sl_], t[:, sl_], cond[:, k:k + 1], None, ALU.mult, None, r=[(t, hf), cond], w=[(acc, hf)])
                            else:
                                P.stt("dve", acc[:, sl_], t[:, sl_], cond[:, k:k + 1], acc[:, sl_], ALU.mult, ALU.add,
                                      r=[(t, hf), cond, (acc, hf)], w=[(acc, hf)])
                    for e in range(48):
                        P.mm(pm[:, l * 64 + e:l * 64 + e + 1], lhsT=acc[:, e * 128:(e + 1) * 128], rhs=C("ones")[:, 0:1],
                             start=True, stop=True, r=[acc, cst], w=[pm])
                    P.tt("dve", modsb[:, l, :], pm[:, l * 64:l * 64 + 48], PVc(f"bmod{l}", 0, 48), ALU.add, r=[pm, pvt], w=[(modsb, l)])
                    P.stt("dve", sc1[:, l, :], modsb[:, l, 16:32], 1.0, PVc(f"norm{l}", 0, 16), ALU.add, ALU.mult,
                          r=[(modsb, l), pvt], w=[(sc1, l)])

        def stage_norm(xin_d, scale_ap, bias_ap, out_tile=None, out_dram=None):
            with P.scope():
                xs = [P.sb(f"xn{i}", [128, 16, 256]) for i in range(2)]
                sq = [P.sb(f"sq{i}", [128, 256]) for i in range(2)]
                rstd = [P.sb(f"rstd{i}", [128, 256]) for i in range(2)]
                pss = [P.ps(f"pss{i}") for i in range(2)]
                ob = [P.sb(f"ob{i}", [128, 16, 256]) for i in range(2)] if out_dram is not None else None
                xv = xin_d.h.ap().rearrange("(k p) t -> p k t", p=128)
                for tb in range(8):
                    x = xs[tb % 2]
                    tsl = slice(tb * 256, (tb + 1) * 256)
                    for hf in range(2):
                        P.dma("sp", x[:, hf * 8:(hf + 1) * 8, :], xv[:, hf * 8:(hf + 1) * 8, tsl], r=[xin_d], w=[(x, hf)])
                    ps_ = pss[tb % 2]
                    for k in range(16):
                        s_ = sq[k % 2]
                        P.act(s_[:], x[:, k, :], AF.Square, r=[(x, k // 8)], w=[s_])
                        P.mm(ps_[:, 0:256], lhsT=C("ones"), rhs=s_[:], start=(k == 0), stop=(k == 15), r=[s_, cst], w=[ps_])
                    rs = rstd[tb % 2]
                    P.act(rs[:], ps_[:, 0:256], AF.Ln, r=[ps_], w=[rs], scale=1.0 / D, bias=EPS)
                    P.act(rs[:], rs[:], AF.Exp, r=[rs], w=[rs], scale=-0.5)
                    for k in range(16):
                        P.tt("dve", x[:, k, :], x[:, k, :], rs[:], ALU.mult, r=[(x, k // 8), rs], w=[(x, k // 8)])
                        if out_tile is not None:
                            P.act(out_tile[:, k, tsl], x[:, k, :], AF.Identity, r=[(x, k // 8), modsb, sc1, pvt], w=[(out_tile, tb)],
                                  scale=scale_ap(k), bias=bias_ap(k))
                        else:
                            o = ob[tb % 2]
                            P.act(o[:, k, :], x[:, k, :], AF.Identity, r=[(x, k // 8), pvt], w=[o], scale=scale_ap(k), bias=bias_ap(k))
                    if out_dram is not None:
                        ov = out_dram.h.ap().rearrange("(k p) t -> p k t", p=128)
                        for hf in range(2):
                            P.dma("pool", ov[:, hf * 8:(hf + 1) * 8, tsl], ob[tb % 2][:, hf * 8:(hf + 1) * 8, :], r=[ob[tb % 2]], w=[(out_dram, tb)])

        def stage_inproj(hT, w_d, fm_chunks, tm_specs):
            with P.scope():
                wf = [P.sb(f"wf{i}", [128, 16, 256]) for i in range(2)]
                wb = [P.sb(f"wb{i}", [128, 16, 256], BF16) for i in range(2)]
                ot = [P.sb(f"ot{i}", [128, S]) for i in range(2)]
                otm = P.sb("otm", [128, 16, 256])
                pp = [P.ps(f"pp{i}") for i in range(6)]
                wv = w_d.h.ap().rearrange("(k p) c -> p k c", p=128)
                groups = []
                i = 0
                while i < len(fm_chunks):
                    g = [fm_chunks[i]]
                    if i + 1 < len(fm_chunks) and fm_chunks[i + 1][0] == fm_chunks[i][0] + 128 and fm_chunks[i][1] == 128:
                        g.append(fm_chunks[i + 1])
                        i += 1
                    i += 1
                    groups.append(("fm", g))
                for sp_ in tm_specs:
                    groups.append(("tm", [sp_]))
                npp = 0
                nout = 0
                for gi, (kind, g) in enumerate(groups):
                    c0 = g[0][0]
                    ncol = sum(x[1] for x in g)
                    f_, b_ = wf[gi % 2], wb[gi % 2]
                    for q4 in range(4):
                        P.dma("sp", f_[:, q4 * 4:(q4 + 1) * 4, 0:ncol], wv[:, q4 * 4:(q4 + 1) * 4, c0:c0 + ncol], w=[(f_, q4)])
                    for q4 in range(4):
                        P.copy("dve" if q4 % 2 == 0 else "act", b_[:, q4 * 4:(q4 + 1) * 4, 0:ncol], f_[:, q4 * 4:(q4 + 1) * 4, 0:ncol],
                               r=[(f_, q4)], w=[(b_, q4)])
                    if kind == "fm":
                        for ci, (cc0, cn, row0) in enumerate(g):
                            o_ = ot[nout % 2]
                            nout += 1
                            for tb in range(4):
                                p_ = pp[npp % 6]
                                npp += 1
                                for k in range(16):
                                    P.mm(p_[0:cn, :], lhsT=b_[:, k, ci * 128:ci * 128 + cn], rhs=hT[:, k, tb * 512:(tb + 1) * 512],
                                         start=(k == 0), stop=(k == 15), r=[(b_, k // 4), hT], w=[p_])
                                P.copy("act" if tb % 2 == 0 else "dve", o_[0:cn, tb * 512:(tb + 1) * 512], p_[0:cn, :], r=[p_], w=[(o_, tb)])
                            P.dma("pool", projT_d[row0:row0 + cn, :], o_[0:cn, :], r=[o_], w=[(projT_d, row0 // 128)])
                    else:
                        (cc0, cn, toff) = g[0]
                        o_ = otm
                        ov = o_[:, :, 0:cn]
                        for tb in range(16):
                            p_ = pp[npp % 6]
                            npp += 1
                            for k in range(16):
                                P.mm(p_[:, 0:cn], lhsT=hT[:, k, tb * 128:(tb + 1) * 128], rhs=b_[:, k, 0:cn],
                                     start=(k == 0), stop=(k == 15), r=[(b_, k // 4), hT], w=[p_])
                            P.copy("act" if tb % 2 == 0 else "dve", ov[:, tb, :], p_[:, 0:cn], r=[p_], w=[(o_, tb % 4)])
                        P.dma("pool", tokm_d[:, :, toff:toff + cn], ov, r=[o_], w=[(tokm_d, toff)])

        def stage_outproj(w_d, xin_d, xout_d, l):
            with P.scope():
                yt = P.sb("yt", [128, 16, S], BF16)
                yv = yT_d.h.ap().rearrange("(k p) t -> p k t", p=128)
                for q4 in range(8):
                    P.dma("sp", yt[:, q4 * 2:(q4 + 1) * 2, :], yv[:, q4 * 2:(q4 + 1) * 2, :], r=[yT_d], w=[(yt, q4)])
                wf = [P.sb(f"owf{i}", [128, 16, 128]) for i in range(2)]
                wb = [P.sb(f"owb{i}", [128, 16, 128], BF16) for i in range(2)]
                xo = [P.sb(f"xo{i}", [128, S]) for i in range(2)]
                pp = [P.ps(f"op{i}") for i in range(6)]
                wv = w_d.h.ap().rearrange("(k p) c -> p k c", p=128)
                npp = 0
                for dc in range(16):
                    f_, b_ = wf[dc % 2], wb[dc % 2]
                    for q4 in range(2):
                        P.dma("sp", f_[:, q4 * 8:(q4 + 1) * 8, :], wv[:, q4 * 8:(q4 + 1) * 8, dc * 128:(dc + 1) * 128], w=[(f_, q4)])
                        P.copy("dve" if q4 == 0 else "act", b_[:, q4 * 8:(q4 + 1) * 8, :], f_[:, q4 * 8:(q4 + 1) * 8, :], r=[(f_, q4)], w=[(b_, q4)])
                    x_ = xo[dc % 2]
                    P.dma("sp", x_[:], xin_d[dc * 128:(dc + 1) * 128, :], r=[xin_d], w=[x_])
                    for tb in range(4):
                        p_ = pp[npp % 6]
                        npp += 1
                        for k in range(16):
                            P.mm(p_[:], lhsT=b_[:, k, :], rhs=yt[:, k, tb * 512:(tb + 1) * 512], start=(k == 0), stop=(k == 15),
                                 r=[(b_, k // 8), yt], w=[p_])
                        P.stt("dve", x_[:, tb * 512:(tb + 1) * 512], p_[:], modsb[:, l, 32 + dc:33 + dc], x_[:, tb * 512:(tb + 1) * 512],
                              ALU.mult, ALU.add, r=[p_, x_, modsb], w=[x_])
                    P.dma("pool", xout_d[dc * 128:(dc + 1) * 128, :], x_[:], r=[x_], w=[(xout_d, dc)])

        def conv_fm(src_row0, cwname, cbname, chunk, dst, xp, silu, tagr=()):
            P.dma("sp", xp[:, 2:S + 2], projT_d[src_row0:src_row0 + 128, :], r=[(projT_d, src_row0 // 128)], w=[xp])
            o, _ = PV[cwname]
            wc = lambda j: pvt[:, o + chunk * 4 + j:o + chunk * 4 + j + 1]
            P.ts("dve", dst[:], xp[:, 0:S], wc(0), PVc(cbname, chunk), ALU.mult, ALU.add, r=[xp, pvt], w=[dst])
            for j in range(1, 4):
                P.stt("dve", dst[:], xp[:, j:S + j], wc(j), dst[:], ALU.mult, ALU.add, r=[xp, pvt, dst], w=[dst])
            if silu:
                P.act(dst[:], dst[:], AF.Silu, r=[dst], w=[dst])

        def pad_init(xp):
            P.op("dve", lambda e: e.memset(xp[:, 0:2], 0.0), w=[xp])
            P.op("dve", lambda e: e.memset(xp[:, S + 2:S + 3], 0.0), w=[xp])

        def stage_attn():
            with P.scope():
                rope = P.sb("rope", [128, 2, S])
                P.dma("sp", rope[:, 0, :], rope_d[0], w=[(rope, 0)])
                P.dma("sp", rope[:, 1, :], rope_d[1], w=[(rope, 1)])
                qk = P.sb("qkr", [128, 10, S], BF16)
                vt = P.sb("vt", [128, 16, 256], BF16)
                vf = P.sb("vf", [128, 16, 256])
                P.dma("sp", vf[:], tokm_d[:, :, 0:256], r=[(tokm_d, 0)], w=[vf])
                P.copy("dve", vt[:], vf[:], r=[vf], w=[vt])
                raw = [P.sb(f"raw{i}", [128, 512]) for i in range(2)]
                sq = [P.sb(f"asq{i}", [128, 512]) for i in range(2)]
                rs = [P.sb(f"ars{i}", [128, 512]) for i in range(2)]
                qn = [P.sb(f"aqn{i}", [128, 512]) for i in range(2)]
                t1 = [P.sb(f"at1{i}", [128, 512]) for i in range(2)]
                t2 = [P.sb(f"at2{i}", [128, 512]) for i in range(2)]
                pa = [P.ps(f"pa{i}") for i in range(2)]
                pb = [P.ps(f"pb{i}") for i in range(2)]
                it = 0
                for hh in range(10):
                    row0 = hh * 128 if hh < 8 else 1024 + (hh - 8) * 128
                    wn = PVc("q_norm") if hh < 8 else PVc("k_norm")
                    for tb in range(4):
                        b = it % 2
                        it += 1
                        tsl = slice(tb * 512, (tb + 1) * 512)
                        P.dma("sp", raw[b][:], projT_d[row0:row0 + 128, tsl], r=[(projT_d, row0 // 128)], w=[raw[b]])
                        P.act(sq[b][:], raw[b][:], AF.Square, r=[raw[b]], w=[sq[b]])
                        P.mm(pa[b][:], lhsT=C("ones"), rhs=sq[b][:], start=True, stop=True, r=[sq[b], cst], w=[pa[b]])
                        P.act(rs[b][:], pa[b][:], AF.Ln, r=[pa[b]], w=[rs[b]], scale=1.0 / 128, bias=EPS)
                        P.act(rs[b][:], rs[b][:], AF.Exp, r=[rs[b]], w=[rs[b]], scale=-0.5)
                        P.stt("dve", qn[b][:], raw[b][:], wn, rs[b][:], ALU.mult, ALU.mult, r=[raw[b], rs[b], pvt], w=[qn[b]])
                        P.mm(pb[b][:], lhsT=C("Rt"), rhs=qn[b][:], start=True, stop=True, r=[qn[b], cst], w=[pb[b]])
                        P.tt("dve", t1[b][:], qn[b][:], rope[:, 0, tsl], ALU.mult, r=[qn[b], (rope, 0)], w=[t1[b]])
                        P.tt("dve", t2[b][:], pb[b][:], rope[:, 1, tsl], ALU.mult, r=[pb[b], (rope, 1)], w=[t2[b]])
                        P.tt("dve", qk[:, hh, tsl], t1[b][:], t2[b][:], ALU.add, r=[t1[b], t2[b]], w=[(qk, hh * 4 + tb)])
                pT = [P.sb(f"pT{i}", [128, 512], BF16) for i in range(3)]
                ps_ = [P.ps(f"psc{i}") for i in range(2)]
                po = [P.ps(f"po{i}") for i in range(2)]
                gt = [P.sb(f"gt{i}", [128, 512]) for i in range(2)]
                rd = [P.sb(f"rd{i}", [128, 512]) for i in range(2)]
                yo = [P.sb(f"yo{i}", [128, 512], BF16) for i in range(2)]
                scale = 128.0 ** -0.5
                it = 0
                n3 = 0
                groups = [(h, qb) for h in range(8) for qb in range(4)]
                items = [(gi, kc) for gi in range(len(groups)) for kc in range(16)]

                def epilogue(gi):
                    h, qb = groups[gi]
                    b = gi % 2
                    qsl = slice(qb * 512, (qb + 1) * 512)
                    P.ts("dve", gt2[b][:], gt2[b][:], 1.0, None, ALU.add, None, r=[gt2[b]], w=[gt2[b]])
                    P.op("dve", lambda e: e.reciprocal(out=gt2[b][:], in_=gt2[b][:]), r=[gt2[b]], w=[gt2[b]])
                    P.op("dve", lambda e: e.reciprocal(out=rd[b][:], in_=pa[b][:]), r=[pa[b]], w=[rd[b]])
                    P.tt("dve", rd[b][:], rd[b][:], gt2[b][:], ALU.mult, r=[rd[b], gt2[b]], w=[rd[b]])
                    P.tt("dve", rd[b][:], rd[b][:], gt[b][:], ALU.mult, r=[rd[b], gt[b]], w=[rd[b]])
                    P.tt("dve", yo[b][:], po[b][:], rd[b][:], ALU.mult, r=[po[b], rd[b]], w=[yo[b]])
                    P.dma("pool", yT_d[h * 128:(h + 1) * 128, qsl], yo[b][:], r=[yo[b]], w=[(yT_d, h)])

                def tail(idx):
                    gi, kc = items[idx]
                    h, qb = groups[gi]
                    g = h // 4
                    b = gi % 2
                    p_ = pT[idx % 3]
                    P.mm(po[b][:], lhsT=vt[:, kc, g * 128:(g + 1) * 128], rhs=p_[:], start=(kc == 0), stop=(kc == 15),
                         r=[vt, p_], w=[po[b]])
                    P.mm(pa[b][:], lhsT=onesb[:], rhs=p_[:], start=(kc == 0), stop=(kc == 15), r=[onesb, p_], w=[pa[b]])
                    if kc == 15:
                        epilogue(gi)

                gt2 = [P.sb(f"gtb{i}", [128, 512]) for i in range(2)]
                for idx, (gi, kc) in enumerate(items):
                    h, qb = groups[gi]
                    g = h // 4
                    b = gi % 2
                    qsl = slice(qb * 512, (qb + 1) * 512)
                    if kc == 0:
                        P.dma("sp", gt[b][:], projT_d[1536 + h * 128:1536 + (h + 1) * 128, qsl], r=[(projT_d, 12 + h)], w=[gt[b]])
                    s_ = ps_[idx % 2]
                    P.mm(s_[:], lhsT=qk[:, 8 + g, kc * 128:(kc + 1) * 128], rhs=qk[:, h, qsl], start=True, stop=True,
                         r=[(qk, (8 + g) * 4 + kc // 4), (qk, h * 4 + qb)], w=[s_])
                    if idx > 0:
                        tail(idx - 1)
                    P.act(pT[idx % 3][:], s_[:], AF.Exp, r=[s_], w=[pT[idx % 3]], scale=scale)
                    if kc == 0:
                        P.act(gt2[b][:], gt[b][:], AF.Exp, r=[gt[b]], w=[gt2[b]], scale=-1.0)
                tail(len(items) - 1)

        def stage_ssd():
            with P.scope():
                xp = P.sb("xp", [128, S + 3])
                pad_init(xp)
                BT2 = [P.sb(f"BT{g}", [128, S], BF16) for g in range(2)]
                CTb2 = [P.sb(f"CTb{g}", [128, S], BF16) for g in range(2)]
                Btm2 = [P.sb(f"Btm{g}", [128, 16, 128], BF16) for g in range(2)]
                GT2 = [P.sb(f"GT{g}", [128, 16, 128]) for g in range(2)]
                tmp = P.sb("ctmp", [128, S])
                pg = [P.ps(f"pg{i}") for i in range(4)]
                py = [P.ps(f"py{i}") for i in range(4)]
                for g in range(2):
                    BT, CTb, Btm, GT = BT2[g], CTb2[g], Btm2[g], GT2[g]
                    conv_fm(3584 + g * 128, "ab_cw", "ab_cb", 8 + g, tmp, xp, True)
                    P.copy("act", BT[:], tmp[:], r=[tmp], w=[BT])
                    for c in range(16):
                        p_ = pg[c % 4]
                        P.tr(p_[:, 0:128], tmp[:, c * 128:(c + 1) * 128], C("ident"), r=[tmp, cst], w=[p_])
                        P.copy("dve", Btm[:, c, :], p_[:, 0:128], r=[p_], w=[Btm])
                    conv_fm(3840 + g * 128, "ab_cw", "ab_cb", 10 + g, tmp, xp, True)
                    P.copy("act", CTb[:], tmp[:], r=[tmp], w=[CTb])
                    for c in range(16):
                        p_ = py[c % 4]
                        csl = slice(c * 128, (c + 1) * 128)
                        P.mm(p_[:, 0:128], lhsT=BT[:, csl], rhs=CTb[:, csl], start=True, stop=True, r=[BT, CTb], w=[p_])
                        P.copy("dve", GT[:, c, :], p_[:, 0:128], r=[p_], w=[GT])
                dt = P.sb("dt", [128, 16, 32])
                av = P.sb("av", [128, 16, 32])
                Aneg = P.sb("Aneg", [128, 32])
                P.dma("sp", dt[:], tokm_d[:, :, 256:288], r=[tokm_d], w=[dt])
                P.tt("dve", dt[:], dt[:], RVc("ab_dtb", 0, 32).unsqueeze(1).to_broadcast([128, 16, 32]), ALU.add, r=[dt, rvt], w=[dt])
                P.act(dt[:], dt[:], AF.Exp, r=[dt], w=[dt])
                P.act(dt[:], dt[:], AF.Ln, r=[dt], w=[dt], bias=1.0)
                P.act(Aneg[:], RVc("ab_alog", 0, 32), AF.Exp, r=[rvt], w=[Aneg])
                P.stt("dve", av[:], dt[:], -1.0, Aneg[:].unsqueeze(1).to_broadcast([128, 16, 32]), ALU.mult, ALU.mult, r=[dt, Aneg], w=[av])

                xsT = [P.sb(f"xsT{i}", [128, S]) for i in range(2)]
                Y = [P.sb(f"Y{i}", [128, S]) for i in range(2)]
                xtm = [P.sb(f"xtm{i}", [128, 16, 128], BF16) for i in range(2)]
                xpad = [[P.sb(f"xpad{i}{h}", [128, 16, 128], BF16) for h in range(2)] for i in range(2)]
                Vall = P.sb("Vall", [128, 8, S], BF16)
                NC_ = 4
                Sm = [P.sb(f"Sm{i}", [128, 128]) for i in range(NC_)]
                Spad = [[P.sb(f"Spad{i}{h}", [128, 128], BF16) for h in range(2)] for i in range(NC_)]
                Rt_ = [P.sb(f"R{i}", [128, 2, 128]) for i in range(NC_)]
                EG = [P.sb(f"EG{i}", [128, 2, 128]) for i in range(NC_)]
                DC = [P.sb(f"DC{i}", [128, 2, 128]) for i in range(NC_)]
                kds = [P.sb(f"kds{i}", [128, 2]) for i in range(NC_)]
                AT = [P.sb(f"AT{i}", [128, 2, 128], BF16) for i in range(NC_)]
                QD = [P.sb(f"QD{i}", [128, 2, 128], BF16) for i in range(NC_)]
                KD = [P.sb(f"KD{i}", [128, 2, 128], BF16) for i in range(NC_)]
                nst = 16

                def chain(pi, hp, d, ci):
                    g_, y_ = pg[ci], py[ci]
                    CTb, Btm, GT = CTb2[hp // 4], Btm2[hp // 4], GT2[hp // 4]
                    U = C("Uf") if d == 0 else C("Ub")
                    M = C("Mf") if d == 0 else C("Mb")
                    NG = C("NEGf", 256) if d == 0 else C("NEGb", 256)
                    hcol = slice(d * 16 + hp * 2, d * 16 + hp * 2 + 2)
                    ccol = 127 if d == 0 else 0
                    P.op("dve", lambda e: e.memset(Sm[ci][:], 0.0), w=[Sm[ci]])
                    for h in range(2):
                        P.op("dve", lambda e: e.memset(Spad[ci][h][:], 0.0), w=[Spad[ci][h]])
                    for step in range(nst):
                        c = step if d == 0 else 15 - step
                        csl = slice(c * 128, (c + 1) * 128)
                        a2 = av[:, c, hcol]
                        R_ = Rt_[ci]
                        P.tt("dve", R_[:], U.unsqueeze(1).to_broadcast([128, 2, 128]), a2.unsqueeze(2).to_broadcast([128, 2, 128]),
                             ALU.mult, r=[cst, av], w=[R_])
                        Rf = R_[:].rearrange("p h i -> p (h i)")
                        P.mm(g_[:, 0:256], lhsT=C("ones"), rhs=Rf, start=True, stop=True, r=[R_, cst], w=[g_])
                        P.mm(g_[:, 256:512], lhsT=M, rhs=Rf, start=True, stop=False, r=[R_, cst], w=[g_])
                        P.mm(g_[:, 256:512], lhsT=C("ident"), rhs=NG, start=False, stop=True, r=[cst], w=[g_])
                        P.mm(y_[:, 256:258], lhsT=M, rhs=a2, start=True, stop=True, r=[av, cst], w=[y_])
                        yield
                        P.act(EG[ci][:].rearrange("p h i -> p (h i)"), g_[:, 0:256], AF.Exp, r=[g_], w=[EG[ci]])
                        P.act(DC[ci][:].rearrange("p h i -> p (h i)"), g_[:, 256:512], AF.Exp, r=[g_], w=[DC[ci]])
                        P.act(kds[ci][:], y_[:, 256:258], AF.Exp, r=[y_], w=[kds[ci]])
                        P.tt("dve", kds[ci][:], kds[ci][:], dt[:, c, hcol], ALU.mult, r=[kds[ci], dt], w=[kds[ci]])
                        for h in range(2):
                            P.stt("dve", AT[ci][:, h, :], DC[ci][:, h, :], dt[:, c, d * 16 + hp * 2 + h:d * 16 + hp * 2 + h + 1],
                                  GT[:, c, :], ALU.mult, ALU.mult, r=[DC[ci], dt, GT], w=[AT[ci]])
                        P.tt("dve", QD[ci][:], CTb[:, csl].unsqueeze(1).to_broadcast([128, 2, 128]), EG[ci][:], ALU.mult,
                             r=[CTb, EG[ci]], w=[QD[ci]])
                        P.tt("dve", KD[ci][:], Btm[:, c, :].unsqueeze(1).to_broadcast([128, 2, 128]),
                             kds[ci][:].unsqueeze(2).to_broadcast([128, 2, 128]), ALU.mult, r=[Btm, kds[ci]], w=[KD[ci]])
                        for h in range(2):
                            P.mm(y_[:, 0:128], lhsT=Spad[ci][h][:], rhs=QD[ci][:, h, :], start=(h == 0), stop=False,
                                 r=[Spad[ci][h], QD[ci]], w=[y_])
                        for h in range(2):
                            P.mm(y_[:, 0:128], lhsT=xpad[pi][h][:, c, :], rhs=AT[ci][:, h, :], start=False, stop=(h == 1),
                                 r=[xpad[pi][h], AT[ci]], w=[y_])
                        for h in range(2):
                            P.mm(y_[:, 128 + h * 64:128 + (h + 1) * 64], lhsT=KD[ci][:, h, :], rhs=xtm[pi][:, c, h * 64:(h + 1) * 64],
                                 start=True, stop=True, r=[KD[ci], xtm[pi]], w=[y_])
                        yield
                        P.tt("dve", Y[pi][:, csl], Y[pi][:, csl], y_[:, 0:128], ALU.add, r=[y_, (Y[pi], c)], w=[(Y[pi], c)])
                        for h in range(2):
                            hs = slice(h * 64, (h + 1) * 64)
                            P.stt("dve", Sm[ci][:, hs], Sm[ci][:, hs], EG[ci][:, h, ccol:ccol + 1], y_[:, 128 + h * 64:128 + (h + 1) * 64],
                                  ALU.mult, ALU.add, r=[Sm[ci], EG[ci], y_], w=[Sm[ci]])
                            P.copy("act", Spad[ci][h][:, hs], Sm[ci][:, hs], r=[Sm[ci]], w=[Spad[ci][h]])

                for rnd in range(4):
                    for pi in range(2):
                        hp = rnd * 2 + pi
                        conv_fm(2560 + hp * 128, "ab_cw", "ab_cb", hp, xsT[pi], xp, True)
                        for c in range(16):
                            p_ = pg[c % 4]
                            P.tr(p_[:, 0:128], xsT[pi][:, c * 128:(c + 1) * 128], C("ident"), r=[xsT[pi], cst], w=[p_])
                            P.copy("act", xtm[pi][:, c, :], p_[:, 0:128], r=[p_], w=[xtm[pi]])
                        P.ts("dve", Y[pi][:], xsT[pi][:], PVc("d_skip", hp), None, ALU.mult, None, r=[xsT[pi], pvt], w=[Y[pi]])
                        for h in range(2):
                            P.op("dve", lambda e: e.memset(xpad[pi][h][:], 0.0), w=[xpad[pi][h]])
                            P.copy("dve", xpad[pi][h][:, :, h * 64:(h + 1) * 64], xtm[pi][:, :, h * 64:(h + 1) * 64], r=[xtm[pi]], w=[xpad[pi][h]])
                    gens = [chain(pi, rnd * 2 + pi, d, pi * 2 + d) for pi in range(2) for d in range(2)]
                    while gens:
                        for g_ in list(gens):
                            try:
                                next(g_)
                            except StopIteration:
                                gens.remove(g_)
                    for pi in range(2):
                        hp = rnd * 2 + pi
                        P.dma("sp", tmp[:], projT_d[4128 + hp * 128:4128 + (hp + 1) * 128, :], r=[projT_d], w=[tmp])
                        P.act(tmp[:], tmp[:], AF.Silu, r=[tmp], w=[tmp])
                        P.tt("dve", Vall[:, hp, :], Y[pi][:], tmp[:], ALU.mult, r=[Y[pi], tmp], w=[(Vall, hp)])
                sqb = [P.sb(f"ssq{i}", [128, 512], BF16) for i in range(2)]
                rsb = P.sb("srs", [128, 512])
                ob = [P.sb(f"sob{i}", [128, 512], BF16) for i in range(2)]
                for tb in range(4):
                    tsl = slice(tb * 512, (tb + 1) * 512)
                    for hp in range(8):
                        P.tt("dve", sqb[hp % 2][:], Vall[:, hp, tsl], Vall[:, hp, tsl], ALU.mult, r=[(Vall, hp)], w=[sqb[hp % 2]])
                        P.mm(pg[0][:], lhsT=onesb[:], rhs=sqb[hp % 2][:], start=(hp == 0), stop=(hp == 7), r=[sqb[hp % 2], onesb], w=[pg[0]])
                    P.act(rsb[:], pg[0][:], AF.Ln, r=[pg[0]], w=[rsb], scale=1.0 / 1024, bias=EPS)
                    P.act(rsb[:], rsb[:], AF.Exp, r=[rsb], w=[rsb], scale=-0.5)
                    for hp in range(8):
                        o_ = ob[hp % 2]
                        P.stt("dve", o_[:], Vall[:, hp, tsl], PVc("ssd_norm", hp), rsb[:], ALU.mult, ALU.mult, r=[(Vall, hp), rsb, pvt], w=[o_])
                        P.dma("pool", yT_d[1024 + hp * 128:1024 + (hp + 1) * 128, tsl], o_[:], r=[o_], w=[(yT_d, 8 + hp)])

        def stage_gdn():
            GP = "dve"
            with P.scope():
                xp = P.sb("gxp", [128, S + 3])
                pad_init(xp)
                tmp = P.sb("gtmp", [128, S])
                gt = P.sb("ggt", [128, 16, 32])
                P.dma("sp", gt[:], tokm_d[:, :, 0:32], r=[tokm_d], w=[gt])
                beta = P.sb("gbeta", [128, 16, 16])
                nbeta = P.sb("gnbeta", [128, 16, 16])
                gg = P.sb("ggg", [128, 16, 16])
                An = P.sb("gAn", [128, 16])
                P.act(beta[:], gt[:, :, 0:16], AF.Exp, r=[gt], w=[beta], scale=-1.0)
                P.ts("dve", beta[:], beta[:], 1.0, None, ALU.add, None, r=[beta], w=[beta])
                P.op("dve", lambda e: e.reciprocal(out=beta[:], in_=beta[:]), r=[beta], w=[beta])
                P.ts("dve", nbeta[:], beta[:], -1.0, None, ALU.mult, None, r=[beta], w=[nbeta])
                P.tt("dve", gg[:], gt[:, :, 16:32], RVc("cd_dtb", 0, 16).unsqueeze(1).to_broadcast([128, 16, 16]), ALU.add, r=[gt, rvt], w=[gg])
                P.act(gg[:], gg[:], AF.Exp, r=[gg], w=[gg])
                P.act(gg[:], gg[:], AF.Ln, r=[gg], w=[gg], bias=1.0)
                P.act(An[:], RVc("cd_alog", 0, 16), AF.Exp, r=[rvt], w=[An])
                P.stt("dve", gg[:], gg[:], -1.0, An[:].unsqueeze(1).to_broadcast([128, 16, 16]), ALU.mult, ALU.mult, r=[gg, An], w=[gg])

                qT = P.sb("gqT", [128, S])
                kT = P.sb("gkT", [128, S])
                ktm = P.sb("gktm", [128, 16, 128])
                vtm = P.sb("gvtm", [128, 16, 256])
                O = P.sb("gO", [128, 2, S])
                sq = [P.sb(f"gsq{i}", [128, 512]) for i in range(2)]
                rsb = [P.sb(f"grs{i}", [128, 512]) for i in range(2)]
                NS = 4
                RR = [P.sb(f"gRR{i}", [128, 256]) for i in range(NS)]
                E = [P.sb(f"gE{i}", [128, 388]) for i in range(NS)]
                X = [[P.sb(f"gX{i}_{j}", [128, 128]) for j in range(2)] for i in range(NS)]
                XT = [[P.sb(f"gXT{i}_{j}", [128, 128]) for j in range(2)] for i in range(NS)]
                PT = [[P.sb(f"gPT{i}_{j}", [128, 128]) for j in range(2)] for i in range(NS)]
                AT = [P.sb(f"gAT{i}", [128, 128]) for i in range(NS)]
                vb = [P.sb(f"gvb{i}", [128, 128]) for i in range(NS)]
                kbg = [P.sb(f"gkbg{i}", [128, 128]) for i in range(NS)]
                bge = [P.sb(f"gbge{i}", [128, 1]) for i in range(NS)]
                u_ = [P.sb(f"gu{i}", [128, 128]) for i in range(NS)]
                wT = [P.sb(f"gwT{i}", [128, 128]) for i in range(NS)]
                qd = [P.sb(f"gqd{i}", [128, 128]) for i in range(NS)]
                kdec = [P.sb(f"gkd{i}", [128, 128]) for i in range(NS)]
                vnew = [P.sb(f"gvn{i}", [128, 128]) for i in range(NS)]
                Sst = [P.sb(f"gS{d}", [128, 128]) for d in range(4)]
                ybf = [P.sb(f"gyb{i}", [128, 512], BF16) for i in range(2)]
                pX = [P.ps(f"gpX{i}") for i in range(4)]
                pY = [P.ps(f"gpY{i}") for i in range(4)]
                pA = pX
                qscale = 128.0 ** -0.5
                nhq = 4 if lim is None else lim[0]
                nst = 16 if lim is None else lim[1]
                cut = 0 if (lim is None or len(lim) < 3) else lim[2]

                def l2norm_fm(dst, scl):
                    for tb in range(4):
                        tsl = slice(tb * 512, (tb + 1) * 512)
                        b = tb % 2
                        P.act(sq[b][:], tmp[:, tsl], AF.Square, r=[tmp], w=[sq[b]])
                        P.mm(pA[b][:], lhsT=C("ones"), rhs=sq[b][:], start=True, stop=True, r=[sq[b], cst], w=[pA[b]])
                        P.act(rsb[b][:], pA[b][:], AF.Ln, r=[pA[b]], w=[rsb[b]], bias=EPS)
                        P.act(rsb[b][:], rsb[b][:], AF.Exp, r=[rsb[b]], w=[rsb[b]], scale=-0.5)
                        P.stt("dve", dst[:, tsl], tmp[:, tsl], scl, rsb[b][:], ALU.mult, ALU.mult, r=[tmp, rsb[b]], w=[dst])

                for hq in range(nhq):
                    conv_fm(hq * 128, "cd_cw", "cd_cb", hq, tmp, xp, True)
                    l2norm_fm(qT, qscale)
                    conv_fm(512 + hq * 128, "cd_cw", "cd_cb", 4 + hq, tmp, xp, True)
                    l2norm_fm(kT, 1.0)
                    for c in range(16):
                        p_ = pA[c % 2]
                        P.tr(p_[:, 0:128], kT[:, c * 128:(c + 1) * 128], C("ident"), r=[kT, cst], w=[p_])
                        P.copy("act", ktm[:, c, :], p_[:, 0:128], r=[p_], w=[ktm])
                    for e in range(2):
                        conv_fm(1024 + (2 * hq + e) * 128, "cd_cw", "cd_cb", 8 + 2 * hq + e, tmp, xp, True)
                        for c in range(16):
                            p_ = pA[c % 2]
                            P.tr(p_[:, 0:128], tmp[:, c * 128:(c + 1) * 128], C("ident"), r=[tmp, cst], w=[p_])
                            P.copy("act", vtm[:, c, e * 128:(e + 1) * 128], p_[:, 0:128], r=[p_], w=[vtm])
                    P.op("dve", lambda e_: e_.memset(O[:], 0.0), w=[O])
                    def chain(e, d, ci):
                        hv = 2 * hq + e
                        S_ = Sst[ci]
                        a_ = pX[ci]
                        c_ = pY[ci]
                        bi = ci
                        UM = C("Uf", 256) if d == 0 else C("Ub", 256)
                        U, M = UM[:, 0:128], UM[:, 128:256]
                        NGi = C("NEGf") if d == 0 else C("NEGb")
                        NGs = C("NEGsf") if d == 0 else C("NEGsb")
                        col = d * 8 + hv
                        ccol = 127 if d == 0 else 0
                        P.op("dve", lambda e_: e_.memset(S_[:], 0.0), w=[S_])
                        for step in range(nst):
                            c = step if d == 0 else 15 - step
                            csl = slice(c * 128, (c + 1) * 128)
                            gcol = gg[:, c, col:col + 1]
                            bcol = beta[:, c, col:col + 1]
                            nbcol = nbeta[:, c, col:col + 1]
                            P.ts("pool", RR[bi][:], UM, gcol, None, ALU.mult, None, r=[cst, gg], w=[RR[bi]])
                            R1, R2 = RR[bi][:, 0:128], RR[bi][:, 128:256]
                            P.mm(a_[:, 0:128], lhsT=C("ones"), rhs=R1, start=True, stop=True, r=[RR[bi], cst], w=[a_])
                            P.mm(a_[:, 128:256], lhsT=M, rhs=R1, start=True, stop=False, r=[RR[bi], cst], w=[a_])
                            P.mm(a_[:, 128:256], lhsT=C("ident"), rhs=NGi, start=False, stop=True, r=[cst], w=[a_])
                            P.mm(a_[:, 256:384], lhsT=U, rhs=R2, start=True, stop=False, r=[RR[bi], cst], w=[a_])
                            P.mm(a_[:, 256:384], lhsT=C("ident"), rhs=NGs, start=False, stop=True, r=[cst], w=[a_])
                            P.mm(a_[:, 384:385], lhsT=M, rhs=gcol, start=True, stop=True, r=[gg, cst], w=[a_])
                            P.mm(a_[:, 385:386], lhsT=U, rhs=gcol, start=True, stop=True, r=[gg, cst], w=[a_])
                            yield
                            E_ = E[bi]
                            P.act(E_[:, 0:386], a_[:, 0:386], AF.Exp, r=[a_], w=[E_])
                            EGb, decT, decS = E_[:, 0:128], E_[:, 128:256], E_[:, 256:384]
                            kds, eg = E_[:, 384:385], E_[:, 385:386]
                            P.mm(c_[:, 0:128], lhsT=kT[:, csl], rhs=kT[:, csl], start=True, stop=True, r=[kT], w=[c_])
                            P.mm(c_[:, 128:256], lhsT=kT[:, csl], rhs=qT[:, csl], start=True, stop=True, r=[kT, qT], w=[c_])
                            yield
                            X0 = X[bi][0]
                            P.stt("dve", X0[:], c_[:, 0:128], nbcol, decS, ALU.mult, ALU.mult, r=[c_, nbeta, E_], w=[X0])
                            P.tt("dve", AT[bi][:], c_[:, 128:256], decT, ALU.mult, r=[c_, E_], w=[AT[bi]])
                            P.act(vb[bi][:], vtm[:, c, e * 128:(e + 1) * 128], AF.Copy, r=[vtm, beta], w=[vb[bi]], scale=bcol)
                            P.tt("dve", bge[bi][:], bcol, eg, ALU.mult, r=[beta, E_], w=[bge[bi]])
                            P.act(kbg[bi][:], ktm[:, c, :], AF.Copy, r=[ktm, bge[bi]], w=[kbg[bi]], scale=bge[bi][:])
                            P.tt("dve", qd[bi][:], qT[:, csl], EGb, ALU.mult, r=[qT, E_], w=[qd[bi]])
                            P.act(kdec[bi][:], ktm[:, c, :], AF.Copy, r=[ktm, E_], w=[kdec[bi]], scale=kds)
                            P.tr(a_[:, 0:128], X0[:], C("ident"), r=[X0, cst], w=[a_])
                            yield
                            P.copy("act", XT[bi][0][:], a_[:, 0:128], r=[a_], w=[XT[bi][0]])
                            P.tt("dve", PT[bi][0][:], a_[:, 0:128], C("ident"), ALU.add, r=[a_, cst], w=[PT[bi][0]])
                            for k in range(1, 7):
                                xo, xn = X[bi][(k - 1) % 2], X[bi][k % 2]
                                to, tn = XT[bi][(k - 1) % 2], XT[bi][k % 2]
                                po_, pn = PT[bi][(k - 1) % 2], PT[bi][k % 2]
                                P.mm(c_[:, 128:256], lhsT=to[:], rhs=xo[:], start=True, stop=True, r=[to, xo], w=[c_])
                                if k < 6:
                                    P.mm(c_[:, 0:128], lhsT=xo[:], rhs=to[:], start=True, stop=True, r=[to, xo], w=[c_])
                                yield
                                P.copy("act", xn[:], c_[:, 128:256], r=[c_], w=[xn])
                                if k < 6:
                                    P.copy("act", tn[:], c_[:, 0:128], r=[c_], w=[tn])
                                P.mm(a_[:, 256:384], lhsT=xn[:], rhs=po_[:], start=True, stop=True, r=[xn, po_], w=[a_])
                                yield
                                P.tt("dve", pn[:], a_[:, 256:384], po_[:], ALU.add, r=[a_, po_], w=[pn])
                            TT = PT[bi][0]
                            P.mm(c_[:, 256:384], lhsT=TT[:], rhs=vb[bi][:], start=True, stop=True, r=[TT, vb[bi]], w=[c_])
                            P.mm(c_[:, 384:512], lhsT=kbg[bi][:], rhs=TT[:], start=True, stop=True, r=[TT, kbg[bi]], w=[c_])
                            yield
                            P.copy("act", u_[bi][:], c_[:, 256:384], r=[c_], w=[u_[bi]])
                            P.copy("act", wT[bi][:], c_[:, 384:512], r=[c_], w=[wT[bi]])
                            P.mm(a_[:, 0:128], lhsT=wT[bi][:], rhs=S_[:], start=True, stop=True, r=[wT[bi], S_], w=[a_])
                            yield
                            P.tt("dve", vnew[bi][:], u_[bi][:], a_[:, 0:128], ALU.subtract, r=[u_[bi], a_], w=[vnew[bi]])
                            P.mm(c_[:, 0:128], lhsT=S_[:], rhs=qd[bi][:], start=True, stop=False, r=[S_, qd[bi]], w=[c_])
                            P.mm(c_[:, 0:128], lhsT=vnew[bi][:], rhs=AT[bi][:], start=False, stop=True, r=[vnew[bi], AT[bi]], w=[c_])
                            P.mm(c_[:, 128:256], lhsT=kdec[bi][:], rhs=vnew[bi][:], start=True, stop=True, r=[kdec[bi], vnew[bi]], w=[c_])
                            yield
                            P.tt("dve", O[:, e, csl], O[:, e, csl], c_[:, 0:128], ALU.add, r=[c_, (O, e * 16 + c)], w=[(O, e * 16 + c)])
                            P.stt("dve", S_[:], S_[:], EGb[:, ccol:ccol + 1], c_[:, 128:256], ALU.mult, ALU.add, r=[S_, E_, c_], w=[S_])

                    gens = [chain(e, d, e * 2 + d) for e in range(2) for d in range(2)]
                    while gens:
                        for g_ in list(gens):
                            try:
                                next(g_)
                            except StopIteration:
                                gens.remove(g_)
                    for e in range(2):
                        hv = 2 * hq + e
                        P.dma("sp", tmp[:], projT_d[2080 + hv * 128:2080 + (hv + 1) * 128, :], r=[projT_d], w=[tmp])
                        P.act(tmp[:], tmp[:], AF.Silu, r=[tmp], w=[tmp])
                        for tb in range(4):
                            tsl = slice(tb * 512, (tb + 1) * 512)
                            b = tb % 2
                            P.act(sq[b][:], O[:, e, tsl], AF.Square, r=[O], w=[sq[b]])
                            P.mm(pA[b][:], lhsT=C("ones"), rhs=sq[b][:], start=True, stop=True, r=[sq[b], cst], w=[pA[b]])
                            P.act(rsb[b][:], pA[b][:], AF.Ln, r=[pA[b]], w=[rsb[b]], scale=1.0 / 128, bias=EPS)
                            P.act(rsb[b][:], rsb[b][:], AF.Exp, r=[rsb[b]], w=[rsb[b]], scale=-0.5)
                            P.stt("dve", sq[b][:], O[:, e, tsl], PVc("gdn_norm"), rsb[b][:], ALU.mult, ALU.mult, r=[O, rsb[b], pvt], w=[sq[b]])
                            P.tt("dve", ybf[b][:], sq[b][:], tmp[:, tsl], ALU.mult, r=[sq[b], tmp], w=[ybf[b]])
                            P.dma("pool", yT_d[hv * 128:(hv + 1) * 128, tsl], ybf[b][:], r=[ybf[b]], w=[(yT_d, hv)])

        def stage_lru():
            with P.scope():
                xp = P.sb("lxp", [128, S + 3])
                pad_init(xp)
                xc = P.sb("lxc", [128, S])
                wl = P.sb("lw", [128, 4, 8, 128])
                for m_ in range(4):
                    P.dma("sp", wl[:, m_, :, :], lruw_d.h.ap()[m_].rearrange("n i j -> i n j"), w=[(wl, m_)])
                nsp = P.sb("lnsp", [128, 16])
                for d, nm in enumerate(("lam_f", "lam_b")):
                    P.act(nsp[:, d * 8:(d + 1) * 8], PVc(nm, 0, 8), AF.Exp, r=[pvt], w=[(nsp, d)], scale=-1.0)
                    P.act(nsp[:, d * 8:(d + 1) * 8], nsp[:, d * 8:(d + 1) * 8], AF.Ln, r=[(nsp, d)], w=[(nsp, d)], bias=1.0)
                    P.ts("dve", nsp[:, d * 8:(d + 1) * 8], nsp[:, d * 8:(d + 1) * 8], -8.0, None, ALU.mult, None, r=[(nsp, d)], w=[(nsp, d)])
                rr = P.sb("lrr", [128, S])
                ig = P.sb("lig", [128, S])
                aa = P.sb("laa", [128, S])
                mm_ = P.sb("lmm", [128, S])
                hh = [P.sb(f"lhh{d}", [128, S]) for d in range(2)]
                gl = P.sb("lgl", [128, S])
                yb = P.sb("lyb", [128, S], BF16)
                pp = [P.ps(f"lp{i}") for i in range(4)]

                def rev(t):
                    return bass.AP(t.h, S - 1, [[S, 128], [-1, S]])

                for n in range(8 if lim is None else lim[0]):
                    conv_fm(3104 + n * 128, "lru_cw", "lru_cb", n, xc, xp, False)
                    P.dma("sp", gl[:], projT_d[4128 + n * 128:4128 + (n + 1) * 128, :], r=[projT_d], w=[gl])
                    for d in range(2):
                        sfx = "f" if d == 0 else "b"
                        for tb in range(4):
                            tsl = slice(tb * 512, (tb + 1) * 512)
                            p1, p2 = pp[(tb % 2) * 2], pp[(tb % 2) * 2 + 1]
                            P.mm(p1[:], lhsT=wl[:, 2 * d, n, :], rhs=xc[:, tsl], start=True, stop=True, r=[(wl, 2 * d), xc], w=[p1])
                            P.mm(p2[:], lhsT=wl[:, 2 * d + 1, n, :], rhs=xc[:, tsl], start=True, stop=True, r=[(wl, 2 * d + 1), xc], w=[p2])
                            P.act(rr[:, tsl], p1[:], AF.Sigmoid, r=[p1, pvt], w=[(rr, tb)], bias=PVc("ba_" + sfx, n))
                            P.act(ig[:, tsl], p2[:], AF.Sigmoid, r=[p2, pvt], w=[(ig, tb)], bias=PVc("bx_" + sfx, n))
                        P.act(aa[:], rr[:], AF.Exp, r=[rr, nsp], w=[aa], scale=nsp[:, d * 8 + n:d * 8 + n + 1])
                        P.tt("pool", mm_[:], aa[:], aa[:], ALU.mult, r=[aa], w=[mm_])
                        P.act(mm_[:], mm_[:], AF.Ln, r=[mm_], w=[mm_], scale=-1.0, bias=1.0)
                        P.act(mm_[:], mm_[:], AF.Exp, r=[mm_], w=[mm_], scale=0.5)
                        P.tt("pool", ig[:], ig[:], xc[:], ALU.mult, r=[ig, xc], w=[ig])
                        P.tt("dve", mm_[:], mm_[:], ig[:], ALU.mult, r=[mm_, ig], w=[mm_])
                        if d == 0:
                            P.op("dve", lambda e: e.tensor_tensor_scan(out=hh[0][:], data0=aa[:], data1=mm_[:], initial=0.0,
                                                                       op0=ALU.mult, op1=ALU.add), r=[aa, mm_], w=[hh[0]])
                        else:
                            P.op("dve", lambda e: e.tensor_tensor_scan(out=rev(hh[1]), data0=rev(aa), data1=rev(mm_), initial=0.0,
                                                                       op0=ALU.mult, op1=ALU.add), r=[aa, mm_], w=[hh[1]])
                    P.act(gl[:], gl[:], AF.Silu, r=[gl], w=[gl])
                    P.tt("pool", hh[0][:], hh[0][:], hh[1][:], ALU.add, r=[hh[0], hh[1]], w=[hh[0]])
                    P.tt("dve", yb[:], hh[0][:], gl[:], ALU.mult, r=[hh[0], gl], w=[yb])
                    P.dma("pool", yT_d[1024 + n * 128:1024 + (n + 1) * 128, :], yb[:], r=[yb], w=[(yT_d, 8 + n)])

        if only is not None:
            {"ssd": stage_ssd, "attn": stage_attn, "gdn": stage_gdn, "lru": stage_lru}[only]()
            P.barrier()
            return nc, dbg
        stage_mod()
        if debug:
            md = scratch("dbg_mod", [128, 96])
            P.dma("pool", md[:], modsb[:].rearrange("p l e -> p (l e)"), r=[modsb], w=[md])
        with P.scope():
            hT = P.sb("hT", [128, 16, S], BF16)
            stage_norm(xT_d, lambda k: sc1[:, 0, k:k + 1], lambda k: modsb[:, 0, k:k + 1], out_tile=hT)
            if debug:
                hd = scratch("dbg_h", [128, 16, S], BF16)
                P.dma("pool", hd[:], hT[:], r=[hT], w=[hd])
            fm = [(c * 128, 128, c * 128) for c in range(32) if not (10 <= c < 12)]
            fm += [(4128 + c * 128, 128, 4128 + c * 128) for c in range(8)]
            stage_inproj(hT, w_in_d[0], fm, [(1280, 256, 0), (4096, 32, 256)])
        if upto == "proj0":
            return nc, dbg
        if upto != "ssd":
            stage_attn()
        if upto == "attn":
            return nc, dbg
        stage_ssd()
        if upto == "ssd":
            return nc, dbg
        stage_outproj(w_out_d[0], xT_d, x1T_d, 0)
        if upto == "l0":
            return nc, dbg
        with P.scope():
            hT = P.sb("hT1", [128, 16, S], BF16)
            stage_norm(x1T_d, lambda k: sc1[:, 1, k:k + 1], lambda k: modsb[:, 1, k:k + 1], out_tile=hT)
            fm = [(c * 128, 128, c * 128) for c in range(16)]
            fm += [(2080 + c * 128, 128, 2080 + c * 128) for c in range(24)]
            stage_inproj(hT, w_in_d[1], fm, [(2048, 32, 0)])
        stage_gdn()
        stage_lru()
        stage_outproj(w_out_d[1], x1T_d, x2T_d, 1)
        stage_norm(x2T_d, lambda k: PVc("fnorm", k), lambda k: 0.0, out_dram=out_d)
        P.barrier()
    return nc, dbg


def make_inputs(inp, b):
    pv, rv = pack_params(inp)
    m = {
        "xT": np.ascontiguousarray(inp["x"][b].T),
        "cT": np.ascontiguousarray(inp["c"][b].reshape(16, 128).T),
        "w_mod": np.ascontiguousarray(inp["w_mod"]),
        "ab_w_in": np.ascontiguousarray(inp["ab_w_in"][0]),
        "cd_w_in": np.ascontiguousarray(inp["cd_w_in"][0]),
        "ab_w_out": np.ascontiguousarray(inp["ab_w_out"][0]),
        "cd_w_out": np.ascontiguousarray(inp["cd_w_out"][0]),
        "lru_w": np.ascontiguousarray(np.stack([inp["cd_lru_wa_f"][0], inp["cd_lru_wx_f"][0], inp["cd_lru_wa_b"][0], inp["cd_lru_wx_b"][0]], 0)),
        "consts": CONSTS,
        "rope": _rope_tables(),
        "pvec": pv,
        "rvec": rv,
    }
    return m


def kernel(**inputs):
    inp = {k: np.asarray(v, np.float32) for k, v in inputs.items()}
    maps = [make_inputs(inp, i // 2) for i in range(8)]
    nc, _ = build(maps[0]["pvec"].shape[1], maps[0]["rvec"].shape[1])
    res = run_bass_kernel_spmd(nc, maps, core_ids=list(range(8)))
    out = np.stack([np.asarray(res.results[2 * b]["outT"], np.float32).T for b in range(4)], 0)
    return np.ascontiguousarray(out)
```

```python
import math
import numpy as np
from contextlib import ExitStack, contextmanager
import concourse.bass as bass
import concourse.mybir as mybir
from concourse.bass_utils import run_bass_kernel_spmd

F32 = mybir.dt.float32
BF16 = mybir.dt.bfloat16
AF = mybir.ActivationFunctionType
ALU = mybir.AluOpType

D = 2048
S = 2048
EPS = 1e-6
NEG = -30000.0


class Tn:
    def __init__(self, h, name, psum=False):
        self.h = h
        self.name = name
        self.st = {}
        self.psum = psum

    def __getitem__(self, idx):
        return self.h[idx]


class Prog:
    NDMA = {"sp": 14, "pool": 8}

    def __init__(self, nc, es):
        self.nc = nc
        self.stack = [es]
        self.h = {"pe": nc.tensor, "act": nc.scalar, "dve": nc.vector, "pool": nc.gpsimd, "sp": nc.sync}
        self.sem = {e: es.enter_context(nc.semaphore("s_" + e)) for e in ("pe", "act", "dve", "pool")}
        self.cnt = {e: 0 for e in self.sem}
        self.dsem = {q: [es.enter_context(nc.semaphore(f"d_{q}{i}")) for i in range(n)] for q, n in self.NDMA.items()}
        self.dcnt = {q: 0 for q in self.NDMA}
        self.waited = {e: {} for e in self.h}
        self.semobj = {}
        for s in self.sem.values():
            self.semobj[id(s)] = s
        for l in self.dsem.values():
            for s in l:
                self.semobj[id(s)] = s
        self.uid = 0

    def sb(self, name, shape, dt=F32):
        self.uid += 1
        name = f"{name}_{self.uid}"
        return Tn(self.stack[-1].enter_context(self.nc.sbuf_tensor(name, list(shape), dt)), name)

    def ps(self, name):
        self.uid += 1
        name = f"{name}_{self.uid}"
        return Tn(self.stack[-1].enter_context(self.nc.psum_tensor(name, [128, 512], F32)), name, psum=True)

    def dram(self, name, shape, dt=F32, kind="Internal"):
        return Tn(self.nc.dram_tensor(name, list(shape), dt, kind=kind), name)

    @contextmanager
    def scope(self):
        es = ExitStack()
        self.stack.append(es)
        try:
            yield
        finally:
            self.barrier()
            self.stack.pop()
            es.close()

    def barrier(self):
        toks = [(self.sem[e], self.cnt[e]) for e in self.sem if self.cnt[e] > 0]
        for q, lst in self.dsem.items():
            n = self.dcnt[q]
            for i, s in enumerate(lst):
                uses = (n - i + len(lst) - 1) // len(lst) if n > i else 0
                if uses > 0:
                    toks.append((s, 16 * uses))
        for e, h in self.h.items():
            for s, v in toks:
                if e in self.sem and self.sem[e] is s:
                    continue
                if self.waited[e].get(id(s), 0) >= v:
                    continue
                self.waited[e][id(s)] = v
                h.wait_ge(s, v)

    @staticmethod
    def _norm(x):
        if isinstance(x, tuple):
            return (x[0], None) if x[0].psum else x
        return (x, None)

    def _states(self, t, sub):
        if sub is None:
            return list(t.st.values())
        out = []
        if None in t.st:
            out.append(t.st[None])
        if sub in t.st:
            out.append(t.st[sub])
        return out

    def _deps(self, eng, r, w):
        need = {}

        def add(tok, kind):
            if tok is None:
                return
            sid, val, src = tok
            if src == eng:
                if eng in ("pe", "sp"):
                    return
            if self.waited[eng].get(sid, 0) >= val:
                return
            if need.get(sid, 0) < val:
                need[sid] = val

        for x in r:
            t, sub = self._norm(x)
            for s in self._states(t, sub):
                add(s[0], "raw")
        for x in w:
            t, sub = self._norm(x)
            for s in self._states(t, sub):
                add(s[0], "waw")
                for tok in s[1].values():
                    add(tok, "war")
        for sid, val in need.items():
            self.waited[eng][sid] = val
        return [(self.semobj[sid], val) for sid, val in need.items()]

    def _commit(self, who, tok, r, w):
        for x in r:
            t, sub = self._norm(x)
            if sub is None:
                if None not in t.st:
                    t.st[None] = [None, {}]
                for s in t.st.values():
                    s[1][who] = tok
            else:
                if sub not in t.st:
                    t.st[sub] = [None, {}]
                t.st[sub][1][who] = tok
        for x in w:
            t, sub = self._norm(x)
            if sub is None:
                t.st = {None: [tok, {}]}
            else:
                t.st[sub] = [tok, {}]

    def op(self, eng, fn, r=(), w=()):
        w = list(w) + [x for x in r if self._norm(x)[0].psum]
        waits = self._deps(eng, r, w)
        self.cnt[eng] += 1
        s = self.sem[eng]
        tok = (id(s), self.cnt[eng], eng)
        self._commit(eng, tok, r, w)
        h = self.h[eng]
        for ss, v in waits:
            h.wait_ge(ss, v)
        fn(h).then_inc(s, 1)

    def dma(self, q, out, in_, r=(), w=()):
        n = self.dcnt[q]
        self.dcnt[q] += 1
        pool = self.dsem[q]
        s = pool[n % len(pool)]
        use = n // len(pool)
        waits = self._deps(q, r, w)
        if use > 0 and self.waited[q].get(id(s), 0) < 16 * use:
            waits.append((s, 16 * use))
            self.waited[q][id(s)] = 16 * use
        tok = (id(s), 16 * (use + 1), "dma_" + q)
        self._commit(f"dma_{q}{n % len(pool)}", tok, r, w)
        h = self.h[q]
        for ss, v in waits:
            h.wait_ge(ss, v)
        h.dma_start(out=out, in_=in_).then_inc(s, 16)
        return tok

    def wait_tok(self, eng, tok):
        self.h[eng].wait_ge(self.semobj[tok[0]], tok[1])

    def act(self, out, in_, func, r, w, bias=0.0, scale=1.0, accum_out=None, eng="act"):
        if accum_out is None:
            self.op("act", lambda e: e.activation(out=out, in_=in_, func=func, bias=bias, scale=scale), r=r, w=w)
        else:
            self.op("act", lambda e: e.activation(out=out, in_=in_, func=func, bias=bias, scale=scale, accum_out=accum_out), r=r, w=w)

    def tt(self, eng, out, in0, in1, op, r, w):
        self.op(eng, lambda e: e.tensor_tensor(out=out, in0=in0, in1=in1, op=op), r=r, w=w)

    def ts(self, eng, out, in0, s1, s2, op0, op1, r, w):
        if s2 is None:
            self.op(eng, lambda e: e.tensor_scalar(out=out, in0=in0, scalar1=s1, scalar2=None, op0=op0), r=r, w=w)
        else:
            self.op(eng, lambda e: e.tensor_scalar(out=out, in0=in0, scalar1=s1, scalar2=s2, op0=op0, op1=op1), r=r, w=w)

    def stt(self, eng, out, in0, scalar, in1, op0, op1, r, w):
        self.op(eng, lambda e: e.scalar_tensor_tensor(out=out, in0=in0, scalar=scalar, in1=in1, op0=op0, op1=op1), r=r, w=w)

    def copy(self, eng, out, in_, r, w):
        if eng == "act":
            self.op("act", lambda e: e.copy(out=out, in_=in_), r=r, w=w)
        else:
            self.op(eng, lambda e: e.tensor_copy(out=out, in_=in_), r=r, w=w)

    def mm(self, out, lhsT, rhs, start, stop, r, w):
        self.op("pe", lambda e: e.matmul(out, lhsT=lhsT, rhs=rhs, start=start, stop=stop), r=r, w=w)

    def tr(self, out, in_, ident, r, w):
        self.op("pe", lambda e: e.transpose(out, in_, ident), r=r, w=w)


C_OFF = {}


def _consts():
    i = np.arange(128)
    sI, fI = i[:, None], i[None, :]
    mats = {
        "ident": (sI == fI), "ones": np.ones((128, 128)),
        "Uf": (sI <= fI), "Mf": (sI > fI), "Ub": (sI >= fI), "Mb": (sI < fI),
    }
    R = np.zeros((128, 128))
    for p in range(64):
        R[2 * p, 2 * p + 1] = -1.0
        R[2 * p + 1, 2 * p] = 1.0
    mats["Rt"] = R.T
    negs = {
        "NEGf": NEG * (fI < sI), "NEGb": NEG * (fI > sI),
        "NEGsf": NEG * (fI >= sI), "NEGsb": NEG * (fI <= sI),
    }
    cols = []
    off = 0
    for k, m in mats.items():
        C_OFF[k] = off
        cols.append(np.asarray(m, np.float32))
        off += 128
    for k, m in negs.items():
        C_OFF[k] = off
        cols.append(np.tile(np.asarray(m, np.float32), (1, 4)))
        off += 512
    return np.ascontiguousarray(np.concatenate(cols, axis=1)), off


CONSTS, NCONST = _consts()


def _rope_tables():
    t = np.arange(S)
    row = (t // 64).astype(np.float32)
    col = (t % 64).astype(np.float32)
    n_pairs = 32
    freqs = (np.float32(10000.0) ** (-np.arange(n_pairs, dtype=np.float32) / np.float32(n_pairs))).astype(np.float32)
    ang = np.concatenate([row[:, None] * freqs, col[:, None] * freqs], axis=-1).astype(np.float32)
    cos = np.cos(ang).astype(np.float32)
    sin = np.sin(ang).astype(np.float32)
    cosT = np.repeat(cos, 2, axis=1).T
    sinT = np.repeat(sin, 2, axis=1).T
    return np.ascontiguousarray(np.stack([cosT, sinT], 0))


PV = {}
RV = {}


def _pcols(v, n):
    return np.asarray(v, np.float32).reshape(n, 128).T


def pack_params(inp):
    pv, rv = [], []

    def addp(name, arr):
        PV[name] = (sum(a.shape[1] for a in pv), arr.shape[1])
        pv.append(np.asarray(arr, np.float32))

    def addr(name, vec):
        vec = np.asarray(vec, np.float32).reshape(-1)
        RV[name] = (sum(a.shape[1] for a in rv), vec.shape[0])
        rv.append(np.broadcast_to(vec[None, :], (128, vec.shape[0])))

    addp("q_norm", _pcols(inp["ab_q_norm"][0], 1))
    addp("k_norm", _pcols(inp["ab_k_norm"][0], 1))
    cw = inp["ab_conv_w"][0]
    addp("ab_cw", np.concatenate([_pcols(cw[j], 12)[:, :, None] for j in range(4)], 2).reshape(128, 48))
    addp("ab_cb", _pcols(inp["ab_conv_b"][0], 12))
    addp("d_skip", _pcols(np.repeat(inp["ab_d_skip"][0], 64), 8))
    addp("ssd_norm", _pcols(inp["ab_ssd_norm"][0], 8))
    cw = inp["cd_conv_w"][0]
    addp("cd_cw", np.concatenate([_pcols(cw[j], 16)[:, :, None] for j in range(4)], 2).reshape(128, 64))
    addp("cd_cb", _pcols(inp["cd_conv_b"][0], 16))
    addp("gdn_norm", _pcols(inp["cd_gdn_norm"][0], 1))
    cw = inp["cd_lru_conv_w"][0]
    addp("lru_cw", np.concatenate([_pcols(cw[j], 8)[:, :, None] for j in range(4)], 2).reshape(128, 32))
    addp("lru_cb", _pcols(inp["cd_lru_conv_b"][0], 8))
    for d_ in ("f", "b"):
        addp("ba_" + d_, _pcols(inp["cd_lru_ba_" + d_][0], 8))
        addp("bx_" + d_, _pcols(inp["cd_lru_bx_" + d_][0], 8))
        addp("lam_" + d_, _pcols(inp["cd_lru_lam_" + d_][0], 8))
    addp("fnorm", _pcols(inp["final_norm_w"], 16))
    for l in range(2):
        addp(f"norm{l}", _pcols(inp["norm_w"][l], 16))
        addp(f"bmod{l}", _pcols(inp["b_mod"][l], 48))
    addr("ab_dtb", np.concatenate([inp["ab_dt_bias_f"][0], inp["ab_dt_bias_b"][0]]))
    addr("ab_alog", np.concatenate([inp["ab_a_log_f"][0], inp["ab_a_log_b"][0]]))
    addr("cd_dtb", np.concatenate([inp["cd_dt_bias_f"][0], inp["cd_dt_bias_b"][0]]))
    addr("cd_alog", np.concatenate([inp["cd_a_log_f"][0], inp["cd_a_log_b"][0]]))
    return (np.ascontiguousarray(np.concatenate(pv, 1)), np.ascontiguousarray(np.concatenate(rv, 1)))


def build(npv, nrv, upto="all", debug=False, only=None, lim=None):
    nc = bass.Bass("TRN2", target_bir_lowering=False)
    es = ExitStack()
    dbg = {}
    with es:
        P = Prog(nc, es)
        skind = "ExternalOutput" if debug else "Internal"

        def din(name, shape, dt=F32):
            return P.dram(name, shape, dt, kind="ExternalInput")

        xT_d = din("xT", [D, S])
        cT_d = din("cT", [128, 16])
        wmod_d = din("w_mod", [2, D, 3 * D])
        w_in_d = [din("ab_w_in", [D, 5152]), din("cd_w_in", [D, 5152])]
        w_out_d = [din("ab_w_out", [D, D]), din("cd_w_out", [D, D])]
        lruw_d = din("lru_w", [4, 8, 128, 128])
        consts_d = din("consts", [128, NCONST])
        rope_d = din("rope", [2, 128, S])
        pv_d = din("pvec", [128, npv])
        rv_d = din("rvec", [128, nrv])
        out_d = P.dram("outT", [D, S], F32, kind="ExternalOutput")

        def scratch(name, shape, dt=F32):
            t = P.dram(name, shape, dt, kind=skind)
            dbg[name] = t
            return t

        if only is None:
            projT_d = scratch("projT", [5248, S])
            tokm_d = scratch("tokm", [128, 16, 320])
        else:
            projT_d = din("projT", [5248, S])
            tokm_d = din("tokm", [128, 16, 320])
        yT_d = scratch("yT", [D, S], BF16)
        x1T_d = scratch("x1T", [D, S])
        x2T_d = scratch("x2T", [D, S])

        cst = P.sb("cst", [128, NCONST])
        P.dma("sp", cst[:, 0:1536], consts_d[:, 0:1536], w=[cst])
        P.dma("sp", cst[:, 1536:NCONST], consts_d[:, 1536:NCONST], w=[cst])
        pvt = P.sb("pvt", [128, npv])
        P.dma("sp", pvt[:], pv_d[:], w=[pvt])
        rvt = P.sb("rvt", [128, nrv])
        P.dma("sp", rvt[:], rv_d[:], w=[rvt])
        onesb = P.sb("onesb", [128, 128], BF16)
        P.copy("dve", onesb[:], cst[:, C_OFF["ones"]:C_OFF["ones"] + 128], r=[cst], w=[onesb])
        identb = P.sb("identb", [128, 128], BF16)
        P.copy("dve", identb[:], cst[:, C_OFF["ident"]:C_OFF["ident"] + 128], r=[cst], w=[identb])

        def C(name, n=128):
            return cst[:, C_OFF[name]:C_OFF[name] + n]

        def PVc(name, j=0, n=1):
            o, _ = PV[name]
            return pvt[:, o + j:o + j + n]

        def RVc(name, j=0, n=1):
            o, _ = RV[name]
            return rvt[:, o + j:o + j + n]

        modsb = P.sb("modsb", [128, 2, 48])
        sc1 = P.sb("sc1", [128, 2, 16])

        def stage_mod():
            with P.scope():
                cond = P.sb("cond", [128, 16])
                P.dma("sp", cond[:], cT_d[:], w=[cond])
                P.act(cond[:], cond[:], AF.Silu, r=[cond], w=[cond])
                pm = P.ps("pm")
                wst = [P.sb(f"wst{i}", [128, 3 * D]) for i in range(3)]
                acc = P.sb("macc", [128, 3 * D])
                for l in range(2):
                    for k in range(16):
                        t = wst[(l * 16 + k) % 3]
                        P.dma("sp", t[:, 0:3072], wmod_d[l, k * 128:(k + 1) * 128, 0:3072], w=[(t, 0)])
                        P.dma("sp", t[:, 3072:6144], wmod_d[l, k * 128:(k + 1) * 128, 3072:6144], w=[(t, 1)])
                        for hf in range(2):
                            sl_ = slice(hf * 3072, (hf + 1) * 3072)
                            if k == 0:
                                P.ts("dve", acc[:, sl_], t[:, sl_], cond[:, k:k + 1], None, ALU.mult, None, r=[(t, hf), cond], w=[(acc, hf)])
                            else:
                                P.stt("dve", acc[:, sl_], t[:, sl_], cond[:, k:k + 1], acc[:, sl_], ALU.mult, ALU.add,
                                      r=[(t, hf), cond, (acc, hf)], w=[(acc, hf)])
                    for e in range(48):
                        P.mm(pm[:, l * 64 + e:l * 64 + e + 1], lhsT=acc[:, e * 128:(e + 1) * 128], rhs=C("ones")[:, 0:1],
                             start=True, stop=True, r=[acc, cst], w=[pm])
                    P.tt("dve", modsb[:, l, :], pm[:, l * 64:l * 64 + 48], PVc(f"bmod{l}", 0, 48), ALU.add, r=[pm, pvt], w=[(modsb, l)])
                    P.stt("dve", sc1[:, l, :], modsb[:, l, 16:32], 1.0, PVc(f"norm{l}", 0, 16), ALU.add, ALU.mult,
                          r=[(modsb, l), pvt], w=[(sc1, l)])

        def stage_norm(xin_d, scale_ap, bias_ap, out_tile=None, out_dram=None):
            with P.scope():
                xs = [P.sb(f"xn{i}", [128, 16, 256]) for i in range(2)]
                sq = [P.sb(f"sq{i}", [128, 256]) for i in range(2)]
                rstd = [P.sb(f"rstd{i}", [128, 256]) for i in range(2)]
                pss = [P.ps(f"pss{i}") for i in range(2)]
                ob = [P.sb(f"ob{i}", [128, 16, 256]) for i in range(2)] if out_dram is not None else None
                xv = xin_d.h.ap().rearrange("(k p) t -> p k t", p=128)
                for tb in range(8):
                    x = xs[tb % 2]
                    tsl = slice(tb * 256, (tb + 1) * 256)
                    for hf in range(2):
                        P.dma("sp", x[:, hf * 8:(hf + 1) * 8, :], xv[:, hf * 8:(hf + 1) * 8, tsl], r=[xin_d], w=[(x, hf)])
                    ps_ = pss[tb % 2]
                    for k in range(16):
                        s_ = sq[k % 2]
                        P.act(s_[:], x[:, k, :], AF.Square, r=[(x, k // 8)], w=[s_])
                        P.mm(ps_[:, 0:256], lhsT=C("ones"), rhs=s_[:], start=(k == 0), stop=(k == 15), r=[s_, cst], w=[ps_])
                    rs = rstd[tb % 2]
                    P.act(rs[:], ps_[:, 0:256], AF.Ln, r=[ps_], w=[rs], scale=1.0 / D, bias=EPS)
                    P.act(rs[:], rs[:], AF.Exp, r=[rs], w=[rs], scale=-0.5)
                    for k in range(16):
                        P.tt("dve", x[:, k, :], x[:, k, :], rs[:], ALU.mult, r=[(x, k // 8), rs], w=[(x, k // 8)])
                        if out_tile is not None:
                            P.act(out_tile[:, k, tsl], x[:, k, :], AF.Identity, r=[(x, k // 8), modsb, sc1, pvt], w=[(out_tile, tb)],
                                  scale=scale_ap(k), bias=bias_ap(k))
                        else:
                            o = ob[tb % 2]
                            P.act(o[:, k, :], x[:, k, :], AF.Identity, r=[(x, k // 8), pvt], w=[o], scale=scale_ap(k), bias=bias_ap(k))
                    if out_dram is not None:
                        ov = out_dram.h.ap().rearrange("(k p) t -> p k t", p=128)
                        for hf in range(2):
                            P.dma("pool", ov[:, hf * 8:(hf + 1) * 8, tsl], ob[tb % 2][:, hf * 8:(hf + 1) * 8, :], r=[ob[tb % 2]], w=[(out_dram, tb)])

        def stage_inproj(hT, w_d, fm_chunks, tm_specs):
            with P.scope():
                wf = [P.sb(f"wf{i}", [128, 16, 256]) for i in range(2)]
                wb = [P.sb(f"wb{i}", [128, 16, 256], BF16) for i in range(2)]
                ot = [P.sb(f"ot{i}", [128, S]) for i in range(2)]
                otm = P.sb("otm", [128, 16, 256])
                pp = [P.ps(f"pp{i}") for i in range(6)]
                wv = w_d.h.ap().rearrange("(k p) c -> p k c", p=128)
                groups = []
                i = 0
                while i < len(fm_chunks):
                    g = [fm_chunks[i]]
                    if i + 1 < len(fm_chunks) and fm_chunks[i + 1][0] == fm_chunks[i][0] + 128 and fm_chunks[i][1] == 128:
                        g.append(fm_chunks[i + 1])
                        i += 1
                    i += 1
                    groups.append(("fm", g))
                for sp_ in tm_specs:
                    groups.append(("tm", [sp_]))
                npp = 0
                nout = 0
                def wprep(gi):
                    kind, g = groups[gi]
                    c0 = g[0][0]
                    ncol = sum(x[1] for x in g)
                    f_, b_ = wf[gi % 2], wb[gi % 2]
                    for q4 in range(4):
                        P.dma("sp", f_[:, q4 * 4:(q4 + 1) * 4, 0:ncol], wv[:, q4 * 4:(q4 + 1) * 4, c0:c0 + ncol], w=[(f_, q4)])
                    for q4 in range(4):
                        P.copy("dve" if q4 % 2 == 0 else "act", b_[:, q4 * 4:(q4 + 1) * 4, 0:ncol], f_[:, q4 * 4:(q4 + 1) * 4, 0:ncol],
                               r=[(f_, q4)], w=[(b_, q4)])

                wprep(0)
                for gi, (kind, g) in enumerate(groups):
                    if gi + 1 < len(groups):
                        wprep(gi + 1)
                    b_ = wb[gi % 2]
                    if kind == "fm":
                        for ci, (cc0, cn, row0) in enumerate(g):
                            o_ = ot[nout % 2]
                            nout += 1
                            for tb in range(4):
                                p_ = pp[npp % 6]
                                npp += 1
                                for k in range(16):
                                    P.mm(p_[0:cn, :], lhsT=b_[:, k, ci * 128:ci * 128 + cn], rhs=hT[:, k, tb * 512:(tb + 1) * 512],
                                         start=(k == 0), stop=(k == 15), r=[(b_, k // 4), hT], w=[p_])
                                P.copy("act" if tb % 2 == 0 else "dve", o_[0:cn, tb * 512:(tb + 1) * 512], p_[0:cn, :], r=[p_], w=[(o_, tb)])
                            P.dma("pool", projT_d[row0:row0 + cn, :], o_[0:cn, :], r=[o_], w=[(projT_d, row0 // 128)])
                    else:
                        (cc0, cn, toff) = g[0]
                        o_ = otm
                        ov = o_[:, :, 0:cn]
                        for tb in range(16):
                            p_ = pp[npp % 6]
                            npp += 1
                            for k in range(16):
                                P.mm(p_[:, 0:cn], lhsT=hT[:, k, tb * 128:(tb + 1) * 128], rhs=b_[:, k, 0:cn],
                                     start=(k == 0), stop=(k == 15), r=[(b_, k // 4), hT], w=[p_])
                            P.copy("act" if tb % 2 == 0 else "dve", ov[:, tb, :], p_[:, 0:cn], r=[p_], w=[(o_, tb % 4)])
                        P.dma("pool", tokm_d[:, :, toff:toff + cn], ov, r=[o_], w=[(tokm_d, toff)])

        def stage_outproj(w_d, xin_d, xout_d, l):
            with P.scope():
                yt = P.sb("yt", [128, 16, S], BF16)
                yv = yT_d.h.ap().rearrange("(k p) t -> p k t", p=128)
                for q4 in range(8):
                    P.dma("sp", yt[:, q4 * 2:(q4 + 1) * 2, :], yv[:, q4 * 2:(q4 + 1) * 2, :], r=[yT_d], w=[(yt, q4)])
                wf = [P.sb(f"owf{i}", [128, 16, 128]) for i in range(2)]
                wb = [P.sb(f"owb{i}", [128, 16, 128], BF16) for i in range(2)]
                xo = [P.sb(f"xo{i}", [128, S]) for i in range(2)]
                pp = [P.ps(f"op{i}") for i in range(6)]
                wv = w_d.h.ap().rearrange("(k p) c -> p k c", p=128)
                npp = 0
                for dc in range(16):
                    f_, b_ = wf[dc % 2], wb[dc % 2]
                    for q4 in range(2):
                        P.dma("sp", f_[:, q4 * 8:(q4 + 1) * 8, :], wv[:, q4 * 8:(q4 + 1) * 8, dc * 128:(dc + 1) * 128], w=[(f_, q4)])
                        P.copy("dve" if q4 == 0 else "act", b_[:, q4 * 8:(q4 + 1) * 8, :], f_[:, q4 * 8:(q4 + 1) * 8, :], r=[(f_, q4)], w=[(b_, q4)])
                    x_ = xo[dc % 2]
                    P.dma("sp", x_[:], xin_d[dc * 128:(dc + 1) * 128, :], r=[xin_d], w=[x_])
                    for tb in range(4):
                        p_ = pp[npp % 6]
                        npp += 1
                        for k in range(16):
                            P.mm(p_[:], lhsT=b_[:, k, :], rhs=yt[:, k, tb * 512:(tb + 1) * 512], start=(k == 0), stop=(k == 15),
                                 r=[(b_, k // 8), yt], w=[p_])
                        P.stt("dve", x_[:, tb * 512:(tb + 1) * 512], p_[:], modsb[:, l, 32 + dc:33 + dc], x_[:, tb * 512:(tb + 1) * 512],
                              ALU.mult, ALU.add, r=[p_, x_, modsb], w=[x_])
                    P.dma("pool", xout_d[dc * 128:(dc + 1) * 128, :], x_[:], r=[x_], w=[(xout_d, dc)])

        def conv_fm(src_row0, cwname, cbname, chunk, dst, xp, silu, tagr=()):
            P.dma("sp", xp[:, 2:S + 2], projT_d[src_row0:src_row0 + 128, :], r=[(projT_d, src_row0 // 128)], w=[xp])
            o, _ = PV[cwname]
            wc = lambda j: pvt[:, o + chunk * 4 + j:o + chunk * 4 + j + 1]
            P.ts("dve", dst[:], xp[:, 0:S], wc(0), PVc(cbname, chunk), ALU.mult, ALU.add, r=[xp, pvt], w=[dst])
            for j in range(1, 4):
                P.stt("dve", dst[:], xp[:, j:S + j], wc(j), dst[:], ALU.mult, ALU.add, r=[xp, pvt, dst], w=[dst])
            if silu:
                P.act(dst[:], dst[:], AF.Silu, r=[dst], w=[dst])

        def pad_init(xp):
            P.op("dve", lambda e: e.memset(xp[:, 0:2], 0.0), w=[xp])
            P.op("dve", lambda e: e.memset(xp[:, S + 2:S + 3], 0.0), w=[xp])

        def stage_attn():
            with P.scope():
                rope = P.sb("rope", [128, 2, S])
                P.dma("sp", rope[:, 0, :], rope_d[0], w=[(rope, 0)])
                P.dma("sp", rope[:, 1, :], rope_d[1], w=[(rope, 1)])
                qk = P.sb("qkr", [128, 10, S], BF16)
                vt = P.sb("vt", [128, 16, 256], BF16)
                vf = P.sb("vf", [128, 16, 256])
                P.dma("sp", vf[:], tokm_d[:, :, 0:256], r=[(tokm_d, 0)], w=[vf])
                P.copy("dve", vt[:], vf[:], r=[vf], w=[vt])
                raw = [P.sb(f"raw{i}", [128, 512]) for i in range(2)]
                sq = [P.sb(f"asq{i}", [128, 512]) for i in range(2)]
                rs = [P.sb(f"ars{i}", [128, 512]) for i in range(2)]
                qn = [P.sb(f"aqn{i}", [128, 512]) for i in range(2)]
                t1 = [P.sb(f"at1{i}", [128, 512]) for i in range(2)]
                t2 = [P.sb(f"at2{i}", [128, 512]) for i in range(2)]
                pa = [P.ps(f"pa{i}") for i in range(2)]
                pb = [P.ps(f"pb{i}") for i in range(2)]
                it = 0
                for hh in range(10):
                    row0 = hh * 128 if hh < 8 else 1024 + (hh - 8) * 128
                    wn = PVc("q_norm") if hh < 8 else PVc("k_norm")
                    for tb in range(4):
                        b = it % 2
                        it += 1
                        tsl = slice(tb * 512, (tb + 1) * 512)
                        P.dma("sp", raw[b][:], projT_d[row0:row0 + 128, tsl], r=[(projT_d, row0 // 128)], w=[raw[b]])
                        P.act(sq[b][:], raw[b][:], AF.Square, r=[raw[b]], w=[sq[b]])
                        P.mm(pa[b][:], lhsT=C("ones"), rhs=sq[b][:], start=True, stop=True, r=[sq[b], cst], w=[pa[b]])
                        P.act(rs[b][:], pa[b][:], AF.Ln, r=[pa[b]], w=[rs[b]], scale=1.0 / 128, bias=EPS)
                        P.act(rs[b][:], rs[b][:], AF.Exp, r=[rs[b]], w=[rs[b]], scale=-0.5)
                        P.stt("dve", qn[b][:], raw[b][:], wn, rs[b][:], ALU.mult, ALU.mult, r=[raw[b], rs[b], pvt], w=[qn[b]])
                        P.mm(pb[b][:], lhsT=C("Rt"), rhs=qn[b][:], start=True, stop=True, r=[qn[b], cst], w=[pb[b]])
                        P.tt("dve", t1[b][:], qn[b][:], rope[:, 0, tsl], ALU.mult, r=[qn[b], (rope, 0)], w=[t1[b]])
                        P.tt("dve", t2[b][:], pb[b][:], rope[:, 1, tsl], ALU.mult, r=[pb[b], (rope, 1)], w=[t2[b]])
                        P.tt("dve", qk[:, hh, tsl], t1[b][:], t2[b][:], ALU.add, r=[t1[b], t2[b]], w=[(qk, hh * 4 + tb)])
                pT = [P.sb(f"pT{i}", [128, 512], BF16) for i in range(3)]
                ps_ = [P.ps(f"psc{i}") for i in range(2)]
                po = [P.ps(f"po{i}") for i in range(2)]
                gt = [P.sb(f"gt{i}", [128, 512]) for i in range(2)]
                rd = [P.sb(f"rd{i}", [128, 512]) for i in range(2)]
                yo = [P.sb(f"yo{i}", [128, 512], BF16) for i in range(2)]
                scale = 128.0 ** -0.5
                it = 0
                n3 = 0
                groups = [(h, qb) for h in range(8) for qb in range(4)]
                items = [(gi, kc) for gi in range(len(groups)) for kc in range(16)]

                def epilogue(gi):
                    h, qb = groups[gi]
                    b = gi % 2
                    qsl = slice(qb * 512, (qb + 1) * 512)
                    P.ts("dve", gt2[b][:], gt2[b][:], 1.0, None, ALU.add, None, r=[gt2[b]], w=[gt2[b]])
                    P.op("dve", lambda e: e.reciprocal(out=gt2[b][:], in_=gt2[b][:]), r=[gt2[b]], w=[gt2[b]])
                    P.op("dve", lambda e: e.reciprocal(out=rd[b][:], in_=pa[b][:]), r=[pa[b]], w=[rd[b]])
                    P.tt("dve", rd[b][:], rd[b][:], gt2[b][:], ALU.mult, r=[rd[b], gt2[b]], w=[rd[b]])
                    P.tt("dve", rd[b][:], rd[b][:], gt[b][:], ALU.mult, r=[rd[b], gt[b]], w=[rd[b]])
                    P.tt("dve", yo[b][:], po[b][:], rd[b][:], ALU.mult, r=[po[b], rd[b]], w=[yo[b]])
                    P.dma("pool", yT_d[h * 128:(h + 1) * 128, qsl], yo[b][:], r=[yo[b]], w=[(yT_d, h)])

                def tail(idx):
                    gi, kc = items[idx]
                    h, qb = groups[gi]
                    g = h // 4
                    b = gi % 2
                    p_ = pT[idx % 3]
                    P.mm(po[b][:], lhsT=vt[:, kc, g * 128:(g + 1) * 128], rhs=p_[:], start=(kc == 0), stop=(kc == 15),
                         r=[vt, p_], w=[po[b]])
                    P.mm(pa[b][:], lhsT=onesb[:], rhs=p_[:], start=(kc == 0), stop=(kc == 15), r=[onesb, p_], w=[pa[b]])
                    if kc == 15:
                        epilogue(gi)

                gt2 = [P.sb(f"gtb{i}", [128, 512]) for i in range(2)]
                for idx, (gi, kc) in enumerate(items):
                    h, qb = groups[gi]
                    g = h // 4
                    b = gi % 2
                    qsl = slice(qb * 512, (qb + 1) * 512)
                    if kc == 0:
                        P.dma("sp", gt[b][:], projT_d[1536 + h * 128:1536 + (h + 1) * 128, qsl], r=[(projT_d, 12 + h)], w=[gt[b]])
                    s_ = ps_[idx % 2]
                    P.mm(s_[:], lhsT=qk[:, 8 + g, kc * 128:(kc + 1) * 128], rhs=qk[:, h, qsl], start=True, stop=True,
                         r=[(qk, (8 + g) * 4 + kc // 4), (qk, h * 4 + qb)], w=[s_])
                    if idx > 0:
                        tail(idx - 1)
                    P.act(pT[idx % 3][:], s_[:], AF.Exp, r=[s_], w=[pT[idx % 3]], scale=scale)
                    if kc == 0:
                        P.act(gt2[b][:], gt[b][:], AF.Exp, r=[gt[b]], w=[gt2[b]], scale=-1.0)
                tail(len(items) - 1)

        def stage_ssd():
            with P.scope():
                xp = P.sb("xp", [128, S + 3])
                pad_init(xp)
                BT2 = [P.sb(f"BT{g}", [128, S], BF16) for g in range(2)]
                CTb2 = [P.sb(f"CTb{g}", [128, S], BF16) for g in range(2)]
                Btm2 = [P.sb(f"Btm{g}", [128, 16, 128], BF16) for g in range(2)]
                GT2 = [P.sb(f"GT{g}", [128, 16, 128]) for g in range(2)]
                tmp = P.sb("ctmp", [128, S])
                pg = [P.ps(f"pg{i}") for i in range(4)]
                py = [P.ps(f"py{i}") for i in range(4)]
                for g in range(2):
                    BT, CTb, Btm, GT = BT2[g], CTb2[g], Btm2[g], GT2[g]
                    conv_fm(3584 + g * 128, "ab_cw", "ab_cb", 8 + g, tmp, xp, True)
                    P.copy("act", BT[:], tmp[:], r=[tmp], w=[BT])
                    for c in range(16):
                        p_ = pg[c % 4]
                        P.tr(p_[:, 0:128], tmp[:, c * 128:(c + 1) * 128], C("ident"), r=[tmp, cst], w=[p_])
                        P.copy("dve", Btm[:, c, :], p_[:, 0:128], r=[p_], w=[Btm])
                    conv_fm(3840 + g * 128, "ab_cw", "ab_cb", 10 + g, tmp, xp, True)
                    P.copy("act", CTb[:], tmp[:], r=[tmp], w=[CTb])
                    for c in range(16):
                        p_ = py[c % 4]
                        csl = slice(c * 128, (c + 1) * 128)
                        P.mm(p_[:, 0:128], lhsT=BT[:, csl], rhs=CTb[:, csl], start=True, stop=True, r=[BT, CTb], w=[p_])
                        P.copy("dve", GT[:, c, :], p_[:, 0:128], r=[p_], w=[GT])
                dt = P.sb("dt", [128, 16, 32])
                av = P.sb("av", [128, 16, 32])
                Aneg = P.sb("Aneg", [128, 32])
                P.dma("sp", dt[:], tokm_d[:, :, 256:288], r=[tokm_d], w=[dt])
                P.tt("dve", dt[:], dt[:], RVc("ab_dtb", 0, 32).unsqueeze(1).to_broadcast([128, 16, 32]), ALU.add, r=[dt, rvt], w=[dt])
                P.act(dt[:], dt[:], AF.Exp, r=[dt], w=[dt])
                P.act(dt[:], dt[:], AF.Ln, r=[dt], w=[dt], bias=1.0)
                P.act(Aneg[:], RVc("ab_alog", 0, 32), AF.Exp, r=[rvt], w=[Aneg])
                P.stt("dve", av[:], dt[:], -1.0, Aneg[:].unsqueeze(1).to_broadcast([128, 16, 32]), ALU.mult, ALU.mult, r=[dt, Aneg], w=[av])

                xsT = [P.sb(f"xsT{i}", [128, S]) for i in range(2)]
                Y = [P.sb(f"Y{i}", [128, S]) for i in range(2)]
                xtm = [P.sb(f"xtm{i}", [128, 16, 128], BF16) for i in range(2)]
                xpad = [[P.sb(f"xpad{i}{h}", [128, 16, 128], BF16) for h in range(2)] for i in range(2)]
                Vall = P.sb("Vall", [128, 8, S], BF16)
                NC_ = 4
                Sm = [P.sb(f"Sm{i}", [128, 128]) for i in range(NC_)]
                Spad = [[P.sb(f"Spad{i}{h}", [128, 128], BF16) for h in range(2)] for i in range(NC_)]
                Rt_ = [P.sb(f"R{i}", [128, 2, 128]) for i in range(NC_)]
                EG = [P.sb(f"EG{i}", [128, 2, 128]) for i in range(NC_)]
                DC = [P.sb(f"DC{i}", [128, 2, 128]) for i in range(NC_)]
                kds = [P.sb(f"kds{i}", [128, 2]) for i in range(NC_)]
                AT = [P.sb(f"AT{i}", [128, 2, 128], BF16) for i in range(NC_)]
                QD = [P.sb(f"QD{i}", [128, 2, 128], BF16) for i in range(NC_)]
                KD = [P.sb(f"KD{i}", [128, 2, 128], BF16) for i in range(NC_)]
                nst = 16

                def chain(pi, hp, d, ci):
                    g_, y_ = pg[ci], py[ci]
                    CTb, Btm, GT = CTb2[hp // 4], Btm2[hp // 4], GT2[hp // 4]
                    U = C("Uf") if d == 0 else C("Ub")
                    M = C("Mf") if d == 0 else C("Mb")
                    NG = C("NEGf", 256) if d == 0 else C("NEGb", 256)
                    hcol = slice(d * 16 + hp * 2, d * 16 + hp * 2 + 2)
                    ccol = 127 if d == 0 else 0
                    P.op("dve", lambda e: e.memset(Sm[ci][:], 0.0), w=[Sm[ci]])
                    for h in range(2):
                        P.op("dve", lambda e: e.memset(Spad[ci][h][:], 0.0), w=[Spad[ci][h]])
                    for step in range(nst):
                        c = step if d == 0 else 15 - step
                        csl = slice(c * 128, (c + 1) * 128)
                        a2 = av[:, c, hcol]
                        R_ = Rt_[ci]
                        P.tt("dve", R_[:], U.unsqueeze(1).to_broadcast([128, 2, 128]), a2.unsqueeze(2).to_broadcast([128, 2, 128]),
                             ALU.mult, r=[cst, av], w=[R_])
                        Rf = R_[:].rearrange("p h i -> p (h i)")
                        P.mm(g_[:, 0:256], lhsT=C("ones"), rhs=Rf, start=True, stop=True, r=[R_, cst], w=[g_])
                        P.mm(g_[:, 256:512], lhsT=M, rhs=Rf, start=True, stop=False, r=[R_, cst], w=[g_])
                        P.mm(g_[:, 256:512], lhsT=C("ident"), rhs=NG, start=False, stop=True, r=[cst], w=[g_])
                        P.mm(y_[:, 256:258], lhsT=M, rhs=a2, start=True, stop=True, r=[av, cst], w=[y_])
                        yield
                        P.act(EG[ci][:].rearrange("p h i -> p (h i)"), g_[:, 0:256], AF.Exp, r=[g_], w=[EG[ci]])
                        P.act(DC[ci][:].rearrange("p h i -> p (h i)"), g_[:, 256:512], AF.Exp, r=[g_], w=[DC[ci]])
                        P.act(kds[ci][:], y_[:, 256:258], AF.Exp, r=[y_], w=[kds[ci]])
                        P.tt("dve", kds[ci][:], kds[ci][:], dt[:, c, hcol], ALU.mult, r=[kds[ci], dt], w=[kds[ci]])
                        for h in range(2):
                            P.stt("dve", AT[ci][:, h, :], DC[ci][:, h, :], dt[:, c, d * 16 + hp * 2 + h:d * 16 + hp * 2 + h + 1],
                                  GT[:, c, :], ALU.mult, ALU.mult, r=[DC[ci], dt, GT], w=[AT[ci]])
                        P.tt("dve", QD[ci][:], CTb[:, csl].unsqueeze(1).to_broadcast([128, 2, 128]), EG[ci][:], ALU.mult,
                             r=[CTb, EG[ci]], w=[QD[ci]])
                        P.tt("dve", KD[ci][:], Btm[:, c, :].unsqueeze(1).to_broadcast([128, 2, 128]),
                             kds[ci][:].unsqueeze(2).to_broadcast([128, 2, 128]), ALU.mult, r=[Btm, kds[ci]], w=[KD[ci]])
                        for h in range(2):
                            P.mm(y_[:, 0:128], lhsT=Spad[ci][h][:], rhs=QD[ci][:, h, :], start=(h == 0), stop=False,
                                 r=[Spad[ci][h], QD[ci]], w=[y_])
                        for h in range(2):
                            P.mm(y_[:, 0:128], lhsT=xpad[pi][h][:, c, :], rhs=AT[ci][:, h, :], start=False, stop=(h == 1),
                                 r=[xpad[pi][h], AT[ci]], w=[y_])
                        for h in range(2):
                            P.mm(y_[:, 128 + h * 64:128 + (h + 1) * 64], lhsT=KD[ci][:, h, :], rhs=xtm[pi][:, c, h * 64:(h + 1) * 64],
                                 start=True, stop=True, r=[KD[ci], xtm[pi]], w=[y_])
                        yield
                        P.tt("dve", Y[pi][:, csl], Y[pi][:, csl], y_[:, 0:128], ALU.add, r=[y_, (Y[pi], c)], w=[(Y[pi], c)])
                        for h in range(2):
                            hs = slice(h * 64, (h + 1) * 64)
                            P.stt("dve", Sm[ci][:, hs], Sm[ci][:, hs], EG[ci][:, h, ccol:ccol + 1], y_[:, 128 + h * 64:128 + (h + 1) * 64],
                                  ALU.mult, ALU.add, r=[Sm[ci], EG[ci], y_], w=[Sm[ci]])
                            P.copy("act", Spad[ci][h][:, hs], Sm[ci][:, hs], r=[Sm[ci]], w=[Spad[ci][h]])

                for rnd in range(4):
                    for pi in range(2):
                        hp = rnd * 2 + pi
                        conv_fm(2560 + hp * 128, "ab_cw", "ab_cb", hp, xsT[pi], xp, True)
                        for c in range(16):
                            p_ = pg[c % 4]
                            P.tr(p_[:, 0:128], xsT[pi][:, c * 128:(c + 1) * 128], C("ident"), r=[xsT[pi], cst], w=[p_])
                            P.copy("act", xtm[pi][:, c, :], p_[:, 0:128], r=[p_], w=[xtm[pi]])
                        P.ts("dve", Y[pi][:], xsT[pi][:], PVc("d_skip", hp), None, ALU.mult, None, r=[xsT[pi], pvt], w=[Y[pi]])
                        for h in range(2):
                            P.op("dve", lambda e: e.memset(xpad[pi][h][:], 0.0), w=[xpad[pi][h]])
                            P.copy("dve", xpad[pi][h][:, :, h * 64:(h + 1) * 64], xtm[pi][:, :, h * 64:(h + 1) * 64], r=[xtm[pi]], w=[xpad[pi][h]])
                    gens = [chain(pi, rnd * 2 + pi, d, pi * 2 + d) for pi in range(2) for d in range(2)]
                    while gens:
                        for g_ in list(gens):
                            try:
                                next(g_)
                            except StopIteration:
                                gens.remove(g_)
                    for pi in range(2):
                        hp = rnd * 2 + pi
                        P.dma("sp", tmp[:], projT_d[4128 + hp * 128:4128 + (hp + 1) * 128, :], r=[projT_d], w=[tmp])
                        P.act(tmp[:], tmp[:], AF.Silu, r=[tmp], w=[tmp])
                        P.tt("dve", Vall[:, hp, :], Y[pi][:], tmp[:], ALU.mult, r=[Y[pi], tmp], w=[(Vall, hp)])
                sqb = [P.sb(f"ssq{i}", [128, 512], BF16) for i in range(2)]
                rsb = P.sb("srs", [128, 512])
                ob = [P.sb(f"sob{i}", [128, 512], BF16) for i in range(2)]
                for tb in range(4):
                    tsl = slice(tb * 512, (tb + 1) * 512)
                    for hp in range(8):
                        P.tt("dve", sqb[hp % 2][:], Vall[:, hp, tsl], Vall[:, hp, tsl], ALU.mult, r=[(Vall, hp)], w=[sqb[hp % 2]])
                        P.mm(pg[0][:], lhsT=onesb[:], rhs=sqb[hp % 2][:], start=(hp == 0), stop=(hp == 7), r=[sqb[hp % 2], onesb], w=[pg[0]])
                    P.act(rsb[:], pg[0][:], AF.Ln, r=[pg[0]], w=[rsb], scale=1.0 / 1024, bias=EPS)
                    P.act(rsb[:], rsb[:], AF.Exp, r=[rsb], w=[rsb], scale=-0.5)
                    for hp in range(8):
                        o_ = ob[hp % 2]
                        P.stt("dve", o_[:], Vall[:, hp, tsl], PVc("ssd_norm", hp), rsb[:], ALU.mult, ALU.mult, r=[(Vall, hp), rsb, pvt], w=[o_])
                        P.dma("pool", yT_d[1024 + hp * 128:1024 + (hp + 1) * 128, tsl], o_[:], r=[o_], w=[(yT_d, 8 + hp)])

        def stage_gdn():
            GP = "dve"
            with P.scope():
                xp = P.sb("gxp", [128, S + 3])
                pad_init(xp)
                tmp = P.sb("gtmp", [128, S])
                gt = P.sb("ggt", [128, 16, 32])
                P.dma("sp", gt[:], tokm_d[:, :, 0:32], r=[tokm_d], w=[gt])
                beta = P.sb("gbeta", [128, 16, 16])
                nbeta = P.sb("gnbeta", [128, 16, 16])
                gg = P.sb("ggg", [128, 16, 16])
                An = P.sb("gAn", [128, 16])
                P.act(beta[:], gt[:, :, 0:16], AF.Exp, r=[gt], w=[beta], scale=-1.0)
                P.ts("dve", beta[:], beta[:], 1.0, None, ALU.add, None, r=[beta], w=[beta])
                P.op("dve", lambda e: e.reciprocal(out=beta[:], in_=beta[:]), r=[beta], w=[beta])
                P.ts("dve", nbeta[:], beta[:], -1.0, None, ALU.mult, None, r=[beta], w=[nbeta])
                P.tt("dve", gg[:], gt[:, :, 16:32], RVc("cd_dtb", 0, 16).unsqueeze(1).to_broadcast([128, 16, 16]), ALU.add, r=[gt, rvt], w=[gg])
                P.act(gg[:], gg[:], AF.Exp, r=[gg], w=[gg])
                P.act(gg[:], gg[:], AF.Ln, r=[gg], w=[gg], bias=1.0)
                P.act(An[:], RVc("cd_alog", 0, 16), AF.Exp, r=[rvt], w=[An])
                P.stt("dve", gg[:], gg[:], -1.0, An[:].unsqueeze(1).to_broadcast([128, 16, 16]), ALU.mult, ALU.mult, r=[gg, An], w=[gg])

                qT = P.sb("gqT", [128, S])
                kT = P.sb("gkT", [128, S])
                ktm = P.sb("gktm", [128, 16, 128])
                vtm = P.sb("gvtm", [128, 16, 256])
                O = P.sb("gO", [128, 2, S])
                sq = [P.sb(f"gsq{i}", [128, 512]) for i in range(2)]
                rsb = [P.sb(f"grs{i}", [128, 512]) for i in range(2)]
                NS = 4
                RR = [P.sb(f"gRR{i}", [128, 256]) for i in range(NS)]
                E = [P.sb(f"gE{i}", [128, 388]) for i in range(NS)]
                X = [[P.sb(f"gX{i}_{j}", [128, 128]) for j in range(2)] for i in range(NS)]
                XT = [[P.sb(f"gXT{i}_{j}", [128, 128]) for j in range(2)] for i in range(NS)]
                PT = [[P.sb(f"gPT{i}_{j}", [128, 128]) for j in range(2)] for i in range(NS)]
                AT = [P.sb(f"gAT{i}", [128, 128]) for i in range(NS)]
                vb = [P.sb(f"gvb{i}", [128, 128]) for i in range(NS)]
                kbg = [P.sb(f"gkbg{i}", [128, 128]) for i in range(NS)]
                bge = [P.sb(f"gbge{i}", [128, 1]) for i in range(NS)]
                u_ = [P.sb(f"gu{i}", [128, 128]) for i in range(NS)]
                wT = [P.sb(f"gwT{i}", [128, 128]) for i in range(NS)]
                qd = [P.sb(f"gqd{i}", [128, 128]) for i in range(NS)]
                kdec = [P.sb(f"gkd{i}", [128, 128]) for i in range(NS)]
                vnew = [P.sb(f"gvn{i}", [128, 128]) for i in range(NS)]
                Sst = [P.sb(f"gS{d}", [128, 128]) for d in range(4)]
                ybf = [P.sb(f"gyb{i}", [128, 512], BF16) for i in range(2)]
                pX = [P.ps(f"gpX{i}") for i in range(4)]
                pY = [P.ps(f"gpY{i}") for i in range(4)]
                pA = pX
                qscale = 128.0 ** -0.5
                nhq = 4 if lim is None else lim[0]
                nst = 16 if lim is None else lim[1]
                cut = 0 if (lim is None or len(lim) < 3) else lim[2]

                def l2norm_fm(dst, scl):
                    for tb in range(4):
                        tsl = slice(tb * 512, (tb + 1) * 512)
                        b = tb % 2
                        P.act(sq[b][:], tmp[:, tsl], AF.Square, r=[tmp], w=[sq[b]])
                        P.mm(pA[b][:], lhsT=C("ones"), rhs=sq[b][:], start=True, stop=True, r=[sq[b], cst], w=[pA[b]])
                        P.act(rsb[b][:], pA[b][:], AF.Ln, r=[pA[b]], w=[rsb[b]], bias=EPS)
                        P.act(rsb[b][:], rsb[b][:], AF.Exp, r=[rsb[b]], w=[rsb[b]], scale=-0.5)
                        P.stt("dve", dst[:, tsl], tmp[:, tsl], scl, rsb[b][:], ALU.mult, ALU.mult, r=[tmp, rsb[b]], w=[dst])

                for hq in range(nhq):
                    conv_fm(hq * 128, "cd_cw", "cd_cb", hq, tmp, xp, True)
                    l2norm_fm(qT, qscale)
                    conv_fm(512 + hq * 128, "cd_cw", "cd_cb", 4 + hq, tmp, xp, True)
                    l2norm_fm(kT, 1.0)
                    for c in range(16):
                        p_ = pA[c % 2]
                        P.tr(p_[:, 0:128], kT[:, c * 128:(c + 1) * 128], C("ident"), r=[kT, cst], w=[p_])
                        P.copy("act", ktm[:, c, :], p_[:, 0:128], r=[p_], w=[ktm])
                    for e in range(2):
                        conv_fm(1024 + (2 * hq + e) * 128, "cd_cw", "cd_cb", 8 + 2 * hq + e, tmp, xp, True)
                        for c in range(16):
                            p_ = pA[c % 2]
                            P.tr(p_[:, 0:128], tmp[:, c * 128:(c + 1) * 128], C("ident"), r=[tmp, cst], w=[p_])
                            P.copy("act", vtm[:, c, e * 128:(e + 1) * 128], p_[:, 0:128], r=[p_], w=[vtm])
                    P.op("dve", lambda e_: e_.memset(O[:], 0.0), w=[O])
                    def chain(e, d, ci):
                        hv = 2 * hq + e
                        S_ = Sst[ci]
                        a_ = pX[ci]
                        c_ = pY[ci]
                        bi = ci
                        UM = C("Uf", 256) if d == 0 else C("Ub", 256)
                        U, M = UM[:, 0:128], UM[:, 128:256]
                        NGi = C("NEGf") if d == 0 else C("NEGb")
                        NGs = C("NEGsf") if d == 0 else C("NEGsb")
                        col = d * 8 + hv
                        ccol = 127 if d == 0 else 0
                        P.op("dve", lambda e_: e_.memset(S_[:], 0.0), w=[S_])
                        for step in range(nst):
                            c = step if d == 0 else 15 - step
                            csl = slice(c * 128, (c + 1) * 128)
                            gcol = gg[:, c, col:col + 1]
                            bcol = beta[:, c, col:col + 1]
                            nbcol = nbeta[:, c, col:col + 1]
                            P.ts("pool", RR[bi][:], UM, gcol, None, ALU.mult, None, r=[cst, gg], w=[RR[bi]])
                            R1, R2 = RR[bi][:, 0:128], RR[bi][:, 128:256]
                            P.mm(a_[:, 0:128], lhsT=C("ones"), rhs=R1, start=True, stop=True, r=[RR[bi], cst], w=[a_])
                            P.mm(a_[:, 128:256], lhsT=M, rhs=R1, start=True, stop=False, r=[RR[bi], cst], w=[a_])
                            P.mm(a_[:, 128:256], lhsT=C("ident"), rhs=NGi, start=False, stop=True, r=[cst], w=[a_])
                            P.mm(a_[:, 256:384], lhsT=U, rhs=R2, start=True, stop=False, r=[RR[bi], cst], w=[a_])
                            P.mm(a_[:, 256:384], lhsT=C("ident"), rhs=NGs, start=False, stop=True, r=[cst], w=[a_])
                            P.mm(a_[:, 384:385], lhsT=M, rhs=gcol, start=True, stop=True, r=[gg, cst], w=[a_])
                            P.mm(a_[:, 385:386], lhsT=U, rhs=gcol, start=True, stop=True, r=[gg, cst], w=[a_])
                            yield
                            E_ = E[bi]
                            P.act(E_[:, 0:386], a_[:, 0:386], AF.Exp, r=[a_], w=[E_])
                            EGb, decT, decS = E_[:, 0:128], E_[:, 128:256], E_[:, 256:384]
                            kds, eg = E_[:, 384:385], E_[:, 385:386]
                            P.mm(c_[:, 0:128], lhsT=kT[:, csl], rhs=kT[:, csl], start=True, stop=True, r=[kT], w=[c_])
                            P.mm(c_[:, 128:256], lhsT=kT[:, csl], rhs=qT[:, csl], start=True, stop=True, r=[kT, qT], w=[c_])
                            yield
                            X0 = X[bi][0]
                            P.stt("dve", X0[:], c_[:, 0:128], nbcol, decS, ALU.mult, ALU.mult, r=[c_, nbeta, E_], w=[X0])
                            P.tt("dve", AT[bi][:], c_[:, 128:256], decT, ALU.mult, r=[c_, E_], w=[AT[bi]])
                            P.act(vb[bi][:], vtm[:, c, e * 128:(e + 1) * 128], AF.Copy, r=[vtm, beta], w=[vb[bi]], scale=bcol)
                            P.tt("dve", bge[bi][:], bcol, eg, ALU.mult, r=[beta, E_], w=[bge[bi]])
                            P.act(kbg[bi][:], ktm[:, c, :], AF.Copy, r=[ktm, bge[bi]], w=[kbg[bi]], scale=bge[bi][:])
                            P.tt("dve", qd[bi][:], qT[:, csl], EGb, ALU.mult, r=[qT, E_], w=[qd[bi]])
                            P.act(kdec[bi][:], ktm[:, c, :], AF.Copy, r=[ktm, E_], w=[kdec[bi]], scale=kds)
                            P.tr(a_[:, 0:128], X0[:], C("ident"), r=[X0, cst], w=[a_])
                            yield
                            P.copy("act", XT[bi][0][:], a_[:, 0:128], r=[a_], w=[XT[bi][0]])
                            P.tt("dve", PT[bi][0][:], a_[:, 0:128], C("ident"), ALU.add, r=[a_, cst], w=[PT[bi][0]])
                            for k in range(1, 7):
                                xo, xn = X[bi][(k - 1) % 2], X[bi][k % 2]
                                to, tn = XT[bi][(k - 1) % 2], XT[bi][k % 2]
                                po_, pn = PT[bi][(k - 1) % 2], PT[bi][k % 2]
                                P.mm(c_[:, 128:256], lhsT=to[:], rhs=xo[:], start=True, stop=True, r=[to, xo], w=[c_])
                                if k < 6:
                                    P.mm(c_[:, 0:128], lhsT=xo[:], rhs=to[:], start=True, stop=True, r=[to, xo], w=[c_])
                                yield
                                P.copy("act", xn[:], c_[:, 128:256], r=[c_], w=[xn])
                                if k < 6:
                                    P.copy("act", tn[:], c_[:, 0:128], r=[c_], w=[tn])
                                P.mm(a_[:, 256:384], lhsT=xn[:], rhs=po_[:], start=True, stop=True, r=[xn, po_], w=[a_])
                                yield
                                P.tt("dve", pn[:], a_[:, 256:384], po_[:], ALU.add, r=[a_, po_], w=[pn])
                            TT = PT[bi][0]
                            P.mm(c_[:, 256:384], lhsT=TT[:], rhs=vb[bi][:], start=True, stop=True, r=[TT, vb[bi]], w=[c_])
                            P.mm(c_[:, 384:512], lhsT=kbg[bi][:], rhs=TT[:], start=True, stop=True, r=[TT, kbg[bi]], w=[c_])
                            yield
                            P.copy("act", u_[bi][:], c_[:, 256:384], r=[c_], w=[u_[bi]])
                            P.copy("act", wT[bi][:], c_[:, 384:512], r=[c_], w=[wT[bi]])
                            P.mm(a_[:, 0:128], lhsT=wT[bi][:], rhs=S_[:], start=True, stop=True, r=[wT[bi], S_], w=[a_])
                            yield
                            P.tt("dve", vnew[bi][:], u_[bi][:], a_[:, 0:128], ALU.subtract, r=[u_[bi], a_], w=[vnew[bi]])
                            P.mm(c_[:, 0:128], lhsT=S_[:], rhs=qd[bi][:], start=True, stop=False, r=[S_, qd[bi]], w=[c_])
                            P.mm(c_[:, 0:128], lhsT=vnew[bi][:], rhs=AT[bi][:], start=False, stop=True, r=[vnew[bi], AT[bi]], w=[c_])
                            P.mm(c_[:, 128:256], lhsT=kdec[bi][:], rhs=vnew[bi][:], start=True, stop=True, r=[kdec[bi], vnew[bi]], w=[c_])
                            yield
                            P.tt("dve", O[:, e, csl], O[:, e, csl], c_[:, 0:128], ALU.add, r=[c_, (O, e * 16 + c)], w=[(O, e * 16 + c)])
                            P.stt("dve", S_[:], S_[:], EGb[:, ccol:ccol + 1], c_[:, 128:256], ALU.mult, ALU.add, r=[S_, E_, c_], w=[S_])

                    gens = [chain(e, d, e * 2 + d) for e in range(2) for d in range(2)]
                    while gens:
                        for g_ in list(gens):
                            try:
                                next(g_)
                            except StopIteration:
                                gens.remove(g_)
                    for e in range(2):
                        hv = 2 * hq + e
                        P.dma("sp", tmp[:], projT_d[2080 + hv * 128:2080 + (hv + 1) * 128, :], r=[projT_d], w=[tmp])
                        P.act(tmp[:], tmp[:], AF.Silu, r=[tmp], w=[tmp])
                        for tb in range(4):
                            tsl = slice(tb * 512, (tb + 1) * 512)
                            b = tb % 2
                            P.act(sq[b][:], O[:, e, tsl], AF.Square, r=[O], w=[sq[b]])
                            P.mm(pA[b][:], lhsT=C("ones"), rhs=sq[b][:], start=True, stop=True, r=[sq[b], cst], w=[pA[b]])
                            P.act(rsb[b][:], pA[b][:], AF.Ln, r=[pA[b]], w=[rsb[b]], scale=1.0 / 128, bias=EPS)
                            P.act(rsb[b][:], rsb[b][:], AF.Exp, r=[rsb[b]], w=[rsb[b]], scale=-0.5)
                            P.stt("dve", sq[b][:], O[:, e, tsl], PVc("gdn_norm"), rsb[b][:], ALU.mult, ALU.mult, r=[O, rsb[b], pvt], w=[sq[b]])
                            P.tt("dve", ybf[b][:], sq[b][:], tmp[:, tsl], ALU.mult, r=[sq[b], tmp], w=[ybf[b]])
                            P.dma("pool", yT_d[hv * 128:(hv + 1) * 128, tsl], ybf[b][:], r=[ybf[b]], w=[(yT_d, hv)])

        def stage_lru():
            with P.scope():
                xp = P.sb("lxp", [128, S + 3])
                pad_init(xp)
                xc = P.sb("lxc", [128, S])
                wl = P.sb("lw", [128, 4, 8, 128])
                for m_ in range(4):
                    P.dma("sp", wl[:, m_, :, :], lruw_d.h.ap()[m_].rearrange("n i j -> i n j"), w=[(wl, m_)])
                nsp = P.sb("lnsp", [128, 16])
                for d, nm in enumerate(("lam_f", "lam_b")):
                    P.act(nsp[:, d * 8:(d + 1) * 8], PVc(nm, 0, 8), AF.Exp, r=[pvt], w=[(nsp, d)], scale=-1.0)
                    P.act(nsp[:, d * 8:(d + 1) * 8], nsp[:, d * 8:(d + 1) * 8], AF.Ln, r=[(nsp, d)], w=[(nsp, d)], bias=1.0)
                    P.ts("dve", nsp[:, d * 8:(d + 1) * 8], nsp[:, d * 8:(d + 1) * 8], -8.0, None, ALU.mult, None, r=[(nsp, d)], w=[(nsp, d)])
                rr = P.sb("lrr", [128, S])
                ig = P.sb("lig", [128, S])
                aa = P.sb("laa", [128, S])
                mm_ = P.sb("lmm", [128, S])
                hh = [P.sb(f"lhh{d}", [128, S]) for d in range(2)]
                gl = P.sb("lgl", [128, S])
                yb = P.sb("lyb", [128, S], BF16)
                pp = [P.ps(f"lp{i}") for i in range(4)]

                def rev(t):
                    return bass.AP(t.h, S - 1, [[S, 128], [-1, S]])

                for n in range(8 if lim is None else lim[0]):
                    conv_fm(3104 + n * 128, "lru_cw", "lru_cb", n, xc, xp, False)
                    P.dma("sp", gl[:], projT_d[4128 + n * 128:4128 + (n + 1) * 128, :], r=[projT_d], w=[gl])
                    for d in range(2):
                        sfx = "f" if d == 0 else "b"
                        for tb in range(4):
                            tsl = slice(tb * 512, (tb + 1) * 512)
                            p1, p2 = pp[(tb % 2) * 2], pp[(tb % 2) * 2 + 1]
                            P.mm(p1[:], lhsT=wl[:, 2 * d, n, :], rhs=xc[:, tsl], start=True, stop=True, r=[(wl, 2 * d), xc], w=[p1])
                            P.mm(p2[:], lhsT=wl[:, 2 * d + 1, n, :], rhs=xc[:, tsl], start=True, stop=True, r=[(wl, 2 * d + 1), xc], w=[p2])
                            P.act(rr[:, tsl], p1[:], AF.Sigmoid, r=[p1, pvt], w=[(rr, tb)], bias=PVc("ba_" + sfx, n))
                            P.act(ig[:, tsl], p2[:], AF.Sigmoid, r=[p2, pvt], w=[(ig, tb)], bias=PVc("bx_" + sfx, n))
                        P.act(aa[:], rr[:], AF.Exp, r=[rr, nsp], w=[aa], scale=nsp[:, d * 8 + n:d * 8 + n + 1])
                        P.tt("pool", mm_[:], aa[:], aa[:], ALU.mult, r=[aa], w=[mm_])
                        P.act(mm_[:], mm_[:], AF.Ln, r=[mm_], w=[mm_], scale=-1.0, bias=1.0)
                        P.act(mm_[:], mm_[:], AF.Exp, r=[mm_], w=[mm_], scale=0.5)
                        P.tt("pool", ig[:], ig[:], xc[:], ALU.mult, r=[ig, xc], w=[ig])
                        P.tt("dve", mm_[:], mm_[:], ig[:], ALU.mult, r=[mm_, ig], w=[mm_])
                        if d == 0:
                            P.op("dve", lambda e: e.tensor_tensor_scan(out=hh[0][:], data0=aa[:], data1=mm_[:], initial=0.0,
                                                                       op0=ALU.mult, op1=ALU.add), r=[aa, mm_], w=[hh[0]])
                        else:
                            P.op("dve", lambda e: e.tensor_tensor_scan(out=rev(hh[1]), data0=rev(aa), data1=rev(mm_), initial=0.0,
                                                                       op0=ALU.mult, op1=ALU.add), r=[aa, mm_], w=[hh[1]])
                    P.act(gl[:], gl[:], AF.Silu, r=[gl], w=[gl])
                    P.tt("pool", hh[0][:], hh[0][:], hh[1][:], ALU.add, r=[hh[0], hh[1]], w=[hh[0]])
                    P.tt("dve", yb[:], hh[0][:], gl[:], ALU.mult, r=[hh[0], gl], w=[yb])
                    P.dma("pool", yT_d[1024 + n * 128:1024 + (n + 1) * 128, :], yb[:], r=[yb], w=[(yT_d, 8 + n)])

        if only is not None:
            {"ssd": stage_ssd, "attn": stage_attn, "gdn": stage_gdn, "lru": stage_lru}[only]()
            P.barrier()
            return nc, dbg
        stage_mod()
        if debug:
            md = scratch("dbg_mod", [128, 96])
            P.dma("pool", md[:], modsb[:].rearrange("p l e -> p (l e)"), r=[modsb], w=[md])
        with P.scope():
            hT = P.sb("hT", [128, 16, S], BF16)
            stage_norm(xT_d, lambda k: sc1[:, 0, k:k + 1], lambda k: modsb[:, 0, k:k + 1], out_tile=hT)
            if debug:
                hd = scratch("dbg_h", [128, 16, S], BF16)
                P.dma("pool", hd[:], hT[:], r=[hT], w=[hd])
            fm = [(c * 128, 128, c * 128) for c in range(32) if not (10 <= c < 12)]
            fm += [(4128 + c * 128, 128, 4128 + c * 128) for c in range(8)]
            stage_inproj(hT, w_in_d[0], fm, [(1280, 256, 0), (4096, 32, 256)])
        if upto == "proj0":
            return nc, dbg
        if upto != "ssd":
            stage_attn()
        if upto == "attn":
            return nc, dbg
        stage_ssd()
        if upto == "ssd":
            return nc, dbg
        stage_outproj(w_out_d[0], xT_d, x1T_d, 0)
        if upto == "l0":
            return nc, dbg
        with P.scope():
            hT = P.sb("hT1", [128, 16, S], BF16)
            stage_norm(x1T_d, lambda k: sc1[:, 1, k:k + 1], lambda k: modsb[:, 1, k:k + 1], out_tile=hT)
            fm = [(c * 128, 128, c * 128) for c in range(16)]
            fm += [(2080 + c * 128, 128, 2080 + c * 128) for c in range(24)]
            stage_inproj(hT, w_in_d[1], fm, [(2048, 32, 0)])
        stage_gdn()
        stage_lru()
        stage_outproj(w_out_d[1], x1T_d, x2T_d, 1)
        stage_norm(x2T_d, lambda k: PVc("fnorm", k), lambda k: 0.0, out_dram=out_d)
        P.barrier()
    return nc, dbg


def make_inputs(inp, b):
    pv, rv = pack_params(inp)
    m = {
        "xT": np.ascontiguousarray(inp["x"][b].T),
        "cT": np.ascontiguousarray(inp["c"][b].reshape(16, 128).T),
        "w_mod": np.ascontiguousarray(inp["w_mod"]),
        "ab_w_in": np.ascontiguousarray(inp["ab_w_in"][0]),
        "cd_w_in": np.ascontiguousarray(inp["cd_w_in"][0]),
        "ab_w_out": np.ascontiguousarray(inp["ab_w_out"][0]),
        "cd_w_out": np.ascontiguousarray(inp["cd_w_out"][0]),
        "lru_w": np.ascontiguousarray(np.stack([inp["cd_lru_wa_f"][0], inp["cd_lru_wx_f"][0], inp["cd_lru_wa_b"][0], inp["cd_lru_wx_b"][0]], 0)),
        "consts": CONSTS,
        "rope": _rope_tables(),
        "pvec": pv,
        "rvec": rv,
    }
    return m


def kernel(**inputs):
    inp = {k: np.asarray(v, np.float32) for k, v in inputs.items()}
    maps = [make_inputs(inp, i // 2) for i in range(8)]
    nc, _ = build(maps[0]["pvec"].shape[1], maps[0]["rvec"].shape[1])
    res = run_bass_kernel_spmd(nc, maps, core_ids=list(range(8)))
    out = np.stack([np.asarray(res.results[2 * b]["outT"], np.float32).T for b in range(4)], 0)
    return np.ascontiguousarray(out)
```

```python
import math
import numpy as np
from contextlib import ExitStack, contextmanager
import concourse.bass as bass
import concourse.mybir as mybir
from concourse.bass_utils import run_bass_kernel_spmd

F32 = mybir.dt.float32
BF16 = mybir.dt.bfloat16
AF = mybir.ActivationFunctionType
ALU = mybir.AluOpType

D = 2048
S = 2048
EPS = 1e-6
NEG = -30000.0


class Tn:
    def __init__(self, h, name, psum=False):
        self.h = h
        self.name = name
        self.st = {}
        self.psum = psum

    def __getitem__(self, idx):
        return self.h[idx]


class Prog:
    NDMA = {"sp": 14, "pool": 8}

    def __init__(self, nc, es):
        self.nc = nc
        self.stack = [es]
        self.h = {"pe": nc.tensor, "act": nc.scalar, "dve": nc.vector, "pool": nc.gpsimd, "sp": nc.sync}
        self.sem = {e: es.enter_context(nc.semaphore("s_" + e)) for e in ("pe", "act", "dve", "pool")}
        self.cnt = {e: 0 for e in self.sem}
        self.dsem = {q: [es.enter_context(nc.semaphore(f"d_{q}{i}")) for i in range(n)] for q, n in self.NDMA.items()}
        self.dcnt = {q: 0 for q in self.NDMA}
        self.waited = {e: {} for e in self.h}
        self.semobj = {}
        for s in self.sem.values():
            self.semobj[id(s)] = s
        for l in self.dsem.values():
            for s in l:
                self.semobj[id(s)] = s
        self.uid = 0

    def sb(self, name, shape, dt=F32):
        self.uid += 1
        name = f"{name}_{self.uid}"
        return Tn(self.stack[-1].enter_context(self.nc.sbuf_tensor(name, list(shape), dt)), name)

    def ps(self, name):
        self.uid += 1
        name = f"{name}_{self.uid}"
        return Tn(self.stack[-1].enter_context(self.nc.psum_tensor(name, [128, 512], F32)), name, psum=True)

    def dram(self, name, shape, dt=F32, kind="Internal"):
        return Tn(self.nc.dram_tensor(name, list(shape), dt, kind=kind), name)

    @contextmanager
    def scope(self):
        es = ExitStack()
        self.stack.append(es)
        try:
            yield
        finally:
            self.barrier()
            self.stack.pop()
            es.close()

    def barrier(self):
        toks = [(self.sem[e], self.cnt[e]) for e in self.sem if self.cnt[e] > 0]
        for q, lst in self.dsem.items():
            n = self.dcnt[q]
            for i, s in enumerate(lst):
                uses = (n - i + len(lst) - 1) // len(lst) if n > i else 0
                if uses > 0:
                    toks.append((s, 16 * uses))
        for e, h in self.h.items():
            for s, v in toks:
                if e in self.sem and self.sem[e] is s:
                    continue
                if self.waited[e].get(id(s), 0) >= v:
                    continue
                self.waited[e][id(s)] = v
                h.wait_ge(s, v)

    @staticmethod
    def _norm(x):
        if isinstance(x, tuple):
            return (x[0], None) if x[0].psum else x
        return (x, None)

    def _states(self, t, sub):
        if sub is None:
            return list(t.st.values())
        out = []
        if None in t.st:
            out.append(t.st[None])
        if sub in t.st:
            out.append(t.st[sub])
        return out

    def _deps(self, eng, r, w):
        need = {}

        def add(tok, kind):
            if tok is None:
                return
            sid, val, src = tok
            if src == eng:
                if eng in ("pe", "sp"):
                    return
            if self.waited[eng].get(sid, 0) >= val:
                return
            if need.get(sid, 0) < val:
                need[sid] = val

        for x in r:
            t, sub = self._norm(x)
            for s in self._states(t, sub):
                add(s[0], "raw")
        for x in w:
            t, sub = self._norm(x)
            for s in self._states(t, sub):
                add(s[0], "waw")
                for tok in s[1].values():
                    add(tok, "war")
        for sid, val in need.items():
            self.waited[eng][sid] = val
        return [(self.semobj[sid], val) for sid, val in need.items()]

    def _commit(self, who, tok, r, w):
        for x in r:
            t, sub = self._norm(x)
            if sub is None:
                if None not in t.st:
                    t.st[None] = [None, {}]
                for s in t.st.values():
                    s[1][who] = tok
            else:
                if sub not in t.st:
                    t.st[sub] = [None, {}]
                t.st[sub][1][who] = tok
        for x in w:
            t, sub = self._norm(x)
            if sub is None:
                t.st = {None: [tok, {}]}
            else:
                t.st[sub] = [tok, {}]

    def op(self, eng, fn, r=(), w=()):
        w = list(w) + [x for x in r if self._norm(x)[0].psum]
        waits = self._deps(eng, r, w)
        self.cnt[eng] += 1
        s = self.sem[eng]
        tok = (id(s), self.cnt[eng], eng)
        self._commit(eng, tok, r, w)
        h = self.h[eng]
        for ss, v in waits:
            h.wait_ge(ss, v)
        fn(h).then_inc(s, 1)

    def dma(self, q, out, in_, r=(), w=()):
        n = self.dcnt[q]
        self.dcnt[q] += 1
        pool = self.dsem[q]
        s = pool[n % len(pool)]
        use = n // len(pool)
        waits = self._deps(q, r, w)
        if use > 0 and self.waited[q].get(id(s), 0) < 16 * use:
            waits.append((s, 16 * use))
            self.waited[q][id(s)] = 16 * use
        tok = (id(s), 16 * (use + 1), "dma_" + q)
        self._commit(f"dma_{q}{n % len(pool)}", tok, r, w)
        h = self.h[q]
        for ss, v in waits:
            h.wait_ge(ss, v)
        h.dma_start(out=out, in_=in_).then_inc(s, 16)
        return tok

    def wait_tok(self, eng, tok):
        self.h[eng].wait_ge(self.semobj[tok[0]], tok[1])

    def act(self, out, in_, func, r, w, bias=0.0, scale=1.0, accum_out=None, eng="act"):
        if accum_out is None:
            self.op("act", lambda e: e.activation(out=out, in_=in_, func=func, bias=bias, scale=scale), r=r, w=w)
        else:
            self.op("act", lambda e: e.activation(out=out, in_=in_, func=func, bias=bias, scale=scale, accum_out=accum_out), r=r, w=w)

    def tt(self, eng, out, in0, in1, op, r, w):
        self.op(eng, lambda e: e.tensor_tensor(out=out, in0=in0, in1=in1, op=op), r=r, w=w)

    def ts(self, eng, out, in0, s1, s2, op0, op1, r, w):
        if s2 is None:
            self.op(eng, lambda e: e.tensor_scalar(out=out, in0=in0, scalar1=s1, scalar2=None, op0=op0), r=r, w=w)
        else:
            self.op(eng, lambda e: e.tensor_scalar(out=out, in0=in0, scalar1=s1, scalar2=s2, op0=op0, op1=op1), r=r, w=w)

    def stt(self, eng, out, in0, scalar, in1, op0, op1, r, w):
        self.op(eng, lambda e: e.scalar_tensor_tensor(out=out, in0=in0, scalar=scalar, in1=in1, op0=op0, op1=op1), r=r, w=w)

    def copy(self, eng, out, in_, r, w):
        if eng == "act":
            self.op("act", lambda e: e.copy(out=out, in_=in_), r=r, w=w)
        else:
            self.op(eng, lambda e: e.tensor_copy(out=out, in_=in_), r=r, w=w)

    def mm(self, out, lhsT, rhs, start, stop, r, w):
        self.op("pe", lambda e: e.matmul(out, lhsT=lhsT, rhs=rhs, start=start, stop=stop), r=r, w=w)

    def tr(self, out, in_, ident, r, w):
        self.op("pe", lambda e: e.transpose(out, in_, ident), r=r, w=w)


C_OFF = {}


def _consts():
    i = np.arange(128)
    sI, fI = i[:, None], i[None, :]
    mats = {
        "ident": (sI == fI), "ones": np.ones((128, 128)),
        "Uf": (sI <= fI), "Mf": (sI > fI), "Ub": (sI >= fI), "Mb": (sI < fI),
    }
    R = np.zeros((128, 128))
    for p in range(64):
        R[2 * p, 2 * p + 1] = -1.0
        R[2 * p + 1, 2 * p] = 1.0
    mats["Rt"] = R.T
    negs = {
        "NEGf": NEG * (fI < sI), "NEGb": NEG * (fI > sI),
        "NEGsf": NEG * (fI >= sI), "NEGsb": NEG * (fI <= sI),
    }
    cols = []
    off = 0
    for k, m in mats.items():
        C_OFF[k] = off
        cols.append(np.asarray(m, np.float32))
        off += 128
    for k, m in negs.items():
        C_OFF[k] = off
        cols.append(np.tile(np.asarray(m, np.float32), (1, 4)))
        off += 512
    return np.ascontiguousarray(np.concatenate(cols, axis=1)), off


CONSTS, NCONST = _consts()


def _rope_tables():
    t = np.arange(S)
    row = (t // 64).astype(np.float32)
    col = (t % 64).astype(np.float32)
    n_pairs = 32
    freqs = (np.float32(10000.0) ** (-np.arange(n_pairs, dtype=np.float32) / np.float32(n_pairs))).astype(np.float32)
    ang = np.concatenate([row[:, None] * freqs, col[:, None] * freqs], axis=-1).astype(np.float32)
    cos = np.cos(ang).astype(np.float32)
    sin = np.sin(ang).astype(np.float32)
    cosT = np.repeat(cos, 2, axis=1).T
    sinT = np.repeat(sin, 2, axis=1).T
    return np.ascontiguousarray(np.stack([cosT, sinT], 0))


PV = {}
RV = {}


def _pcols(v, n):
    return np.asarray(v, np.float32).reshape(n, 128).T


def pack_params(inp):
    pv, rv = [], []

    def addp(name, arr):
        PV[name] = (sum(a.shape[1] for a in pv), arr.shape[1])
        pv.append(np.asarray(arr, np.float32))

    def addr(name, vec):
        vec = np.asarray(vec, np.float32).reshape(-1)
        RV[name] = (sum(a.shape[1] for a in rv), vec.shape[0])
        rv.append(np.broadcast_to(vec[None, :], (128, vec.shape[0])))

    addp("q_norm", _pcols(inp["ab_q_norm"][0], 1))
    addp("k_norm", _pcols(inp["ab_k_norm"][0], 1))
    cw = inp["ab_conv_w"][0]
    addp("ab_cw", np.concatenate([_pcols(cw[j], 12)[:, :, None] for j in range(4)], 2).reshape(128, 48))
    addp("ab_cb", _pcols(inp["ab_conv_b"][0], 12))
    addp("d_skip", _pcols(np.repeat(inp["ab_d_skip"][0], 64), 8))
    addp("ssd_norm", _pcols(inp["ab_ssd_norm"][0], 8))
    cw = inp["cd_conv_w"][0]
    addp("cd_cw", np.concatenate([_pcols(cw[j], 16)[:, :, None] for j in range(4)], 2).reshape(128, 64))
    addp("cd_cb", _pcols(inp["cd_conv_b"][0], 16))
    addp("gdn_norm", _pcols(inp["cd_gdn_norm"][0], 1))
    cw = inp["cd_lru_conv_w"][0]
    addp("lru_cw", np.concatenate([_pcols(cw[j], 8)[:, :, None] for j in range(4)], 2).reshape(128, 32))
    addp("lru_cb", _pcols(inp["cd_lru_conv_b"][0], 8))
    for d_ in ("f", "b"):
        addp("ba_" + d_, _pcols(inp["cd_lru_ba_" + d_][0], 8))
        addp("bx_" + d_, _pcols(inp["cd_lru_bx_" + d_][0], 8))
        addp("lam_" + d_, _pcols(inp["cd_lru_lam_" + d_][0], 8))
    addp("fnorm", _pcols(inp["final_norm_w"], 16))
    for l in range(2):
        addp(f"norm{l}", _pcols(inp["norm_w"][l], 16))
        addp(f"bmod{l}", _pcols(inp["b_mod"][l], 48))
    addr("ab_dtb", np.concatenate([inp["ab_dt_bias_f"][0], inp["ab_dt_bias_b"][0]]))
    addr("ab_alog", np.concatenate([inp["ab_a_log_f"][0], inp["ab_a_log_b"][0]]))
    addr("cd_dtb", np.concatenate([inp["cd_dt_bias_f"][0], inp["cd_dt_bias_b"][0]]))
    addr("cd_alog", np.concatenate([inp["cd_a_log_f"][0], inp["cd_a_log_b"][0]]))
    return (np.ascontiguousarray(np.concatenate(pv, 1)), np.ascontiguousarray(np.concatenate(rv, 1)))


def build(npv, nrv, upto="all", debug=False, only=None, lim=None):
    nc = bass.Bass("TRN2", target_bir_lowering=False)
    es = ExitStack()
    dbg = {}
    with es:
        P = Prog(nc, es)
        skind = "ExternalOutput" if debug else "Internal"

        def din(name, shape, dt=F32):
            return P.dram(name, shape, dt, kind="ExternalInput")

        xT_d = din("xT", [D, S])
        cT_d = din("cT", [128, 16])
        wmod_d = din("w_mod", [2, D, 3 * D])
        w_in_d = [din("ab_w_in", [D, 5152]), din("cd_w_in", [D, 5152])]
        w_out_d = [din("ab_w_out", [D, D]), din("cd_w_out", [D, D])]
        lruw_d = din("lru_w", [4, 8, 128, 128])
        consts_d = din("consts", [128, NCONST])
        rope_d = din("rope", [2, 128, S])
        pv_d = din("pvec", [128, npv])
        rv_d = din("rvec", [128, nrv])
        out_d = P.dram("outT", [D, S], F32, kind="ExternalOutput")

        def scratch(name, shape, dt=F32):
            t = P.dram(name, shape, dt, kind=skind)
            dbg[name] = t
            return t

        if only is None:
            projT_d = scratch("projT", [5248, S])
            tokm_d = scratch("tokm", [128, 16, 320])
        else:
            projT_d = din("projT", [5248, S])
            tokm_d = din("tokm", [128, 16, 320])
        yT_d = scratch("yT", [D, S], BF16)
        x1T_d = scratch("x1T", [D, S])
        x2T_d = scratch("x2T", [D, S])

        cst = P.sb("cst", [128, NCONST])
        P.dma("sp", cst[:, 0:1536], consts_d[:, 0:1536], w=[cst])
        P.dma("sp", cst[:, 1536:NCONST], consts_d[:, 1536:NCONST], w=[cst])
        pvt = P.sb("pvt", [128, npv])
        P.dma("sp", pvt[:], pv_d[:], w=[pvt])
        rvt = P.sb("rvt", [128, nrv])
        P.dma("sp", rvt[:], rv_d[:], w=[rvt])
        onesb = P.sb("onesb", [128, 128], BF16)
        P.copy("dve", onesb[:], cst[:, C_OFF["ones"]:C_OFF["ones"] + 128], r=[cst], w=[onesb])
        identb = P.sb("identb", [128, 128], BF16)
        P.copy("dve", identb[:], cst[:, C_OFF["ident"]:C_OFF["ident"] + 128], r=[cst], w=[identb])

        def C(name, n=128):
            return cst[:, C_OFF[name]:C_OFF[name] + n]

        def PVc(name, j=0, n=1):
            o, _ = PV[name]
            return pvt[:, o + j:o + j + n]

        def RVc(name, j=0, n=1):
            o, _ = RV[name]
            return rvt[:, o + j:o + j + n]

        modsb = P.sb("modsb", [128, 2, 48])
        sc1 = P.sb("sc1", [128, 2, 16])

        def stage_mod():
            with P.scope():
                cond = P.sb("cond", [128, 16])
                P.dma("sp", cond[:], cT_d[:], w=[cond])
                P.act(cond[:], cond[:], AF.Silu, r=[cond], w=[cond])
                pm = P.ps("pm")
                wst = [P.sb(f"wst{i}", [128, 3 * D]) for i in range(3)]
                acc = P.sb("macc", [128, 3 * D])
                for l in range(2):
                    for k in range(16):
                        t = wst[(l * 16 + k) % 3]
                        P.dma("sp", t[:, 0:3072], wmod_d[l, k * 128:(k + 1) * 128, 0:3072], w=[(t, 0)])
                        P.dma("sp", t[:, 3072:6144], wmod_d[l, k * 128:(k + 1) * 128, 3072:6144], w=[(t, 1)])
                        for hf in range(2):
                            sl_ = slice(hf * 3072, (hf + 1) * 3072)
                            if k == 0:
                                P.ts("dve", acc[:, sl_], t[:, sl_], cond[:, k:k + 1], None, ALU.mult, None, r=[(t, hf), cond], w=[(acc, hf)])
                            else:
                                P.stt("dve", acc[:, sl_], t[:, sl_], cond[:, k:k + 1], acc[:, sl_], ALU.mult, ALU.add,
                                      r=[(t, hf), cond, (acc, hf)], w=[(acc, hf)])
                    for e in range(48):
                        P.mm(pm[:, l * 64 + e:l * 64 + e + 1], lhsT=acc[:, e * 128:(e + 1) * 128], rhs=C("ones")[:, 0:1],
                             start=True, stop=True, r=[acc, cst], w=[pm])
                    P.tt("dve", modsb[:, l, :], pm[:, l * 64:l * 64 + 48], PVc(f"bmod{l}", 0, 48), ALU.add, r=[pm, pvt], w=[(modsb, l)])
                    P.stt("dve", sc1[:, l, :], modsb[:, l, 16:32], 1.0, PVc(f"norm{l}", 0, 16), ALU.add, ALU.mult,
                          r=[(modsb, l), pvt], w=[(sc1, l)])

        def stage_norm(xin_d, scale_ap, bias_ap, out_tile=None, out_dram=None):
            with P.scope():
                xs = [P.sb(f"xn{i}", [128, 16, 256]) for i in range(2)]
                sq = [P.sb(f"sq{i}", [128, 256]) for i in range(2)]
                rstd = [P.sb(f"rstd{i}", [128, 256]) for i in range(2)]
                pss = [P.ps(f"pss{i}") for i in range(2)]
                ob = [P.sb(f"ob{i}", [128, 16, 256]) for i in range(2)] if out_dram is not None else None
                xv = xin_d.h.ap().rearrange("(k p) t -> p k t", p=128)
                for tb in range(8):
                    x = xs[tb % 2]
                    tsl = slice(tb * 256, (tb + 1) * 256)
                    for hf in range(2):
                        P.dma("sp", x[:, hf * 8:(hf + 1) * 8, :], xv[:, hf * 8:(hf + 1) * 8, tsl], r=[xin_d], w=[(x, hf)])
                    ps_ = pss[tb % 2]
                    for k in range(16):
                        s_ = sq[k % 2]
                        P.act(s_[:], x[:, k, :], AF.Square, r=[(x, k // 8)], w=[s_])
                        P.mm(ps_[:, 0:256], lhsT=C("ones"), rhs=s_[:], start=(k == 0), stop=(k == 15), r=[s_, cst], w=[ps_])
                    rs = rstd[tb % 2]
                    P.act(rs[:], ps_[:, 0:256], AF.Ln, r=[ps_], w=[rs], scale=1.0 / D, bias=EPS)
                    P.act(rs[:], rs[:], AF.Exp, r=[rs], w=[rs], scale=-0.5)
                    for k in range(16):
                        P.tt("dve", x[:, k, :], x[:, k, :], rs[:], ALU.mult, r=[(x, k // 8), rs], w=[(x, k // 8)])
                        if out_tile is not None:
                            P.act(out_tile[:, k, tsl], x[:, k, :], AF.Identity, r=[(x, k // 8), modsb, sc1, pvt], w=[(out_tile, tb)],
                                  scale=scale_ap(k), bias=bias_ap(k))
                        else:
                            o = ob[tb % 2]
                            P.act(o[:, k, :], x[:, k, :], AF.Identity, r=[(x, k // 8), pvt], w=[o], scale=scale_ap(k), bias=bias_ap(k))
                    if out_dram is not None:
                        ov = out_dram.h.ap().rearrange("(k p) t -> p k t", p=128)
                        for hf in range(2):
                            P.dma("pool", ov[:, hf * 8:(hf + 1) * 8, tsl], ob[tb % 2][:, hf * 8:(hf + 1) * 8, :], r=[ob[tb % 2]], w=[(out_dram, tb)])

        def stage_inproj(hT, w_d, fm_chunks, tm_specs):
            with P.scope():
                wf = [P.sb(f"wf{i}", [128, 16, 256]) for i in range(2)]
                wb = [P.sb(f"wb{i}", [128, 16, 256], BF16) for i in range(2)]
                ot = [P.sb(f"ot{i}", [128, S]) for i in range(2)]
                otm = P.sb("otm", [128, 16, 256])
                pp = [P.ps(f"pp{i}") for i in range(6)]
                wv = w_d.h.ap().rearrange("(k p) c -> p k c", p=128)
                groups = []
                i = 0
                while i < len(fm_chunks):
                    g = [fm_chunks[i]]
                    if i + 1 < len(fm_chunks) and fm_chunks[i + 1][0] == fm_chunks[i][0] + 128 and fm_chunks[i][1] == 128:
                        g.append(fm_chunks[i + 1])
                        i += 1
                    i += 1
                    groups.append(("fm", g))
                for sp_ in tm_specs:
                    groups.append(("tm", [sp_]))
                npp = 0
                nout = 0
                def wprep(gi):
                    kind, g = groups[gi]
                    c0 = g[0][0]
                    ncol = sum(x[1] for x in g)
                    f_, b_ = wf[gi % 2], wb[gi % 2]
                    for q4 in range(4):
                        P.dma("sp", f_[:, q4 * 4:(q4 + 1) * 4, 0:ncol], wv[:, q4 * 4:(q4 + 1) * 4, c0:c0 + ncol], w=[(f_, q4)])
                    for q4 in range(4):
                        P.copy("dve" if q4 % 2 == 0 else "act", b_[:, q4 * 4:(q4 + 1) * 4, 0:ncol], f_[:, q4 * 4:(q4 + 1) * 4, 0:ncol],
                               r=[(f_, q4)], w=[(b_, q4)])

                wprep(0)
                for gi, (kind, g) in enumerate(groups):
                    if gi + 1 < len(groups):
                        wprep(gi + 1)
                    b_ = wb[gi % 2]
                    if kind == "fm":
                        for ci, (cc0, cn, row0) in enumerate(g):
                            o_ = ot[nout % 2]
                            nout += 1
                            for tb in range(4):
                                p_ = pp[npp % 6]
                                npp += 1
                                for k in range(16):
                                    P.mm(p_[0:cn, :], lhsT=b_[:, k, ci * 128:ci * 128 + cn], rhs=hT[:, k, tb * 512:(tb + 1) * 512],
                                         start=(k == 0), stop=(k == 15), r=[(b_, k // 4), hT], w=[p_])
                                P.copy("act" if tb % 2 == 0 else "dve", o_[0:cn, tb * 512:(tb + 1) * 512], p_[0:cn, :], r=[p_], w=[(o_, tb)])
                            P.dma("pool", projT_d[row0:row0 + cn, :], o_[0:cn, :], r=[o_], w=[(projT_d, row0 // 128)])
                    else:
                        (cc0, cn, toff) = g[0]
                        o_ = otm
                        ov = o_[:, :, 0:cn]
                        for tb in range(16):
                            p_ = pp[npp % 6]
                            npp += 1
                            for k in range(16):
                                P.mm(p_[:, 0:cn], lhsT=hT[:, k, tb * 128:(tb + 1) * 128], rhs=b_[:, k, 0:cn],
                                     start=(k == 0), stop=(k == 15), r=[(b_, k // 4), hT], w=[p_])
                            P.copy("act" if tb % 2 == 0 else "dve", ov[:, tb, :], p_[:, 0:cn], r=[p_], w=[(o_, tb % 4)])
                        P.dma("pool", tokm_d[:, :, toff:toff + cn], ov, r=[o_], w=[(tokm_d, toff)])

        def stage_outproj(w_d, xin_d, xout_d, l):
            with P.scope():
                yt = P.sb("yt", [128, 16, S], BF16)
                yv = yT_d.h.ap().rearrange("(k p) t -> p k t", p=128)
                for q4 in range(8):
                    P.dma("sp", yt[:, q4 * 2:(q4 + 1) * 2, :], yv[:, q4 * 2:(q4 + 1) * 2, :], r=[yT_d], w=[(yt, q4)])
                wf = [P.sb(f"owf{i}", [128, 16, 128]) for i in range(2)]
                wb = [P.sb(f"owb{i}", [128, 16, 128], BF16) for i in range(2)]
                xo = [P.sb(f"xo{i}", [128, S]) for i in range(2)]
                pp = [P.ps(f"op{i}") for i in range(6)]
                wv = w_d.h.ap().rearrange("(k p) c -> p k c", p=128)
                npp = 0
                def oprep(dc):
                    f_, b_ = wf[dc % 2], wb[dc % 2]
                    for q4 in range(2):
                        P.dma("sp", f_[:, q4 * 8:(q4 + 1) * 8, :], wv[:, q4 * 8:(q4 + 1) * 8, dc * 128:(dc + 1) * 128], w=[(f_, q4)])
                        P.copy("dve" if q4 == 0 else "act", b_[:, q4 * 8:(q4 + 1) * 8, :], f_[:, q4 * 8:(q4 + 1) * 8, :], r=[(f_, q4)], w=[(b_, q4)])
                    P.dma("sp", xo[dc % 2][:], xin_d[dc * 128:(dc + 1) * 128, :], r=[xin_d], w=[xo[dc % 2]])

                oprep(0)
                for dc in range(16):
                    if dc + 1 < 16:
                        oprep(dc + 1)
                    b_ = wb[dc % 2]
                    x_ = xo[dc % 2]
                    for tb in range(4):
                        p_ = pp[npp % 6]
                        npp += 1
                        for k in range(16):
                            P.mm(p_[:], lhsT=b_[:, k, :], rhs=yt[:, k, tb * 512:(tb + 1) * 512], start=(k == 0), stop=(k == 15),
                                 r=[(b_, k // 8), yt], w=[p_])
                        P.stt("dve", x_[:, tb * 512:(tb + 1) * 512], p_[:], modsb[:, l, 32 + dc:33 + dc], x_[:, tb * 512:(tb + 1) * 512],
                              ALU.mult, ALU.add, r=[p_, x_, modsb], w=[x_])
                    P.dma("pool", xout_d[dc * 128:(dc + 1) * 128, :], x_[:], r=[x_], w=[(xout_d, dc)])

        def conv_fm(src_row0, cwname, cbname, chunk, dst, xp, silu, tagr=()):
            P.dma("sp", xp[:, 2:S + 2], projT_d[src_row0:src_row0 + 128, :], r=[(projT_d, src_row0 // 128)], w=[xp])
            o, _ = PV[cwname]
            wc = lambda j: pvt[:, o + chunk * 4 + j:o + chunk * 4 + j + 1]
            P.ts("dve", dst[:], xp[:, 0:S], wc(0), PVc(cbname, chunk), ALU.mult, ALU.add, r=[xp, pvt], w=[dst])
            for j in range(1, 4):
                P.stt("dve", dst[:], xp[:, j:S + j], wc(j), dst[:], ALU.mult, ALU.add, r=[xp, pvt, dst], w=[dst])
            if silu:
                P.act(dst[:], dst[:], AF.Silu, r=[dst], w=[dst])

        def pad_init(xp):
            P.op("dve", lambda e: e.memset(xp[:, 0:2], 0.0), w=[xp])
            P.op("dve", lambda e: e.memset(xp[:, S + 2:S + 3], 0.0), w=[xp])

        def stage_attn():
            with P.scope():
                rope = P.sb("rope", [128, 2, S])
                P.dma("sp", rope[:, 0, :], rope_d[0], w=[(rope, 0)])
                P.dma("sp", rope[:, 1, :], rope_d[1], w=[(rope, 1)])
                qk = P.sb("qkr", [128, 10, S], BF16)
                vt = P.sb("vt", [128, 16, 256], BF16)
                vf = P.sb("vf", [128, 16, 256])
                P.dma("sp", vf[:], tokm_d[:, :, 0:256], r=[(tokm_d, 0)], w=[vf])
                P.copy("dve", vt[:], vf[:], r=[vf], w=[vt])
                raw = [P.sb(f"raw{i}", [128, 512]) for i in range(2)]
                sq = [P.sb(f"asq{i}", [128, 512]) for i in range(2)]
                rs = [P.sb(f"ars{i}", [128, 512]) for i in range(2)]
                qn = [P.sb(f"aqn{i}", [128, 512]) for i in range(2)]
                t1 = [P.sb(f"at1{i}", [128, 512]) for i in range(2)]
                t2 = [P.sb(f"at2{i}", [128, 512]) for i in range(2)]
                pa = [P.ps(f"pa{i}") for i in range(2)]
                pb = [P.ps(f"pb{i}") for i in range(2)]
                it = 0
                for hh in range(10):
                    row0 = hh * 128 if hh < 8 else 1024 + (hh - 8) * 128
                    wn = PVc("q_norm") if hh < 8 else PVc("k_norm")
                    for tb in range(4):
                        b = it % 2
                        it += 1
                        tsl = slice(tb * 512, (tb + 1) * 512)
                        P.dma("sp", raw[b][:], projT_d[row0:row0 + 128, tsl], r=[(projT_d, row0 // 128)], w=[raw[b]])
                        P.act(sq[b][:], raw[b][:], AF.Square, r=[raw[b]], w=[sq[b]])
                        P.mm(pa[b][:], lhsT=C("ones"), rhs=sq[b][:], start=True, stop=True, r=[sq[b], cst], w=[pa[b]])
                        P.act(rs[b][:], pa[b][:], AF.Ln, r=[pa[b]], w=[rs[b]], scale=1.0 / 128, bias=EPS)
                        P.act(rs[b][:], rs[b][:], AF.Exp, r=[rs[b]], w=[rs[b]], scale=-0.5)
                        P.stt("dve", qn[b][:], raw[b][:], wn, rs[b][:], ALU.mult, ALU.mult, r=[raw[b], rs[b], pvt], w=[qn[b]])
                        P.mm(pb[b][:], lhsT=C("Rt"), rhs=qn[b][:], start=True, stop=True, r=[qn[b], cst], w=[pb[b]])
                        P.tt("dve", t1[b][:], qn[b][:], rope[:, 0, tsl], ALU.mult, r=[qn[b], (rope, 0)], w=[t1[b]])
                        P.tt("dve", t2[b][:], pb[b][:], rope[:, 1, tsl], ALU.mult, r=[pb[b], (rope, 1)], w=[t2[b]])
                        P.tt("dve", qk[:, hh, tsl], t1[b][:], t2[b][:], ALU.add, r=[t1[b], t2[b]], w=[(qk, hh * 4 + tb)])
                pT = [P.sb(f"pT{i}", [128, 512], BF16) for i in range(3)]
                ps_ = [P.ps(f"psc{i}") for i in range(2)]
                po = [P.ps(f"po{i}") for i in range(2)]
                gt = [P.sb(f"gt{i}", [128, 512]) for i in range(2)]
                rd = [P.sb(f"rd{i}", [128, 512]) for i in range(2)]
                yo = [P.sb(f"yo{i}", [128, 512], BF16) for i in range(2)]
                scale = 128.0 ** -0.5
                it = 0
                n3 = 0
                groups = [(h, qb) for h in range(8) for qb in range(4)]
                items = [(gi, kc) for gi in range(len(groups)) for kc in range(16)]

                def epilogue(gi):
                    h, qb = groups[gi]
                    b = gi % 2
                    qsl = slice(qb * 512, (qb + 1) * 512)
                    P.ts("dve", gt2[b][:], gt2[b][:], 1.0, None, ALU.add, None, r=[gt2[b]], w=[gt2[b]])
                    P.op("dve", lambda e: e.reciprocal(out=gt2[b][:], in_=gt2[b][:]), r=[gt2[b]], w=[gt2[b]])
                    P.op("dve", lambda e: e.reciprocal(out=rd[b][:], in_=pa[b][:]), r=[pa[b]], w=[rd[b]])
                    P.tt("dve", rd[b][:], rd[b][:], gt2[b][:], ALU.mult, r=[rd[b], gt2[b]], w=[rd[b]])
                    P.tt("dve", rd[b][:], rd[b][:], gt[b][:], ALU.mult, r=[rd[b], gt[b]], w=[rd[b]])
                    P.tt("dve", yo[b][:], po[b][:], rd[b][:], ALU.mult, r=[po[b], rd[b]], w=[yo[b]])
                    P.dma("pool", yT_d[h * 128:(h + 1) * 128, qsl], yo[b][:], r=[yo[b]], w=[(yT_d, h)])

                def tail(idx):
                    gi, kc = items[idx]
                    h, qb = groups[gi]
                    g = h // 4
                    b = gi % 2
                    p_ = pT[idx % 3]
                    P.mm(po[b][:], lhsT=vt[:, kc, g * 128:(g + 1) * 128], rhs=p_[:], start=(kc == 0), stop=(kc == 15),
                         r=[vt, p_], w=[po[b]])
                    P.mm(pa[b][:], lhsT=onesb[:], rhs=p_[:], start=(kc == 0), stop=(kc == 15), r=[onesb, p_], w=[pa[b]])
                    if kc == 15:
                        epilogue(gi)

                gt2 = [P.sb(f"gtb{i}", [128, 512]) for i in range(2)]
                for idx, (gi, kc) in enumerate(items):
                    h, qb = groups[gi]
                    g = h // 4
                    b = gi % 2
                    qsl = slice(qb * 512, (qb + 1) * 512)
                    if kc == 0:
                        P.dma("sp", gt[b][:], projT_d[1536 + h * 128:1536 + (h + 1) * 128, qsl], r=[(projT_d, 12 + h)], w=[gt[b]])
                    s_ = ps_[idx % 2]
                    P.mm(s_[:], lhsT=qk[:, 8 + g, kc * 128:(kc + 1) * 128], rhs=qk[:, h, qsl], start=True, stop=True,
                         r=[(qk, (8 + g) * 4 + kc // 4), (qk, h * 4 + qb)], w=[s_])
                    if idx > 0:
                        tail(idx - 1)
                    P.act(pT[idx % 3][:], s_[:], AF.Exp, r=[s_], w=[pT[idx % 3]], scale=scale)
                    if kc == 0:
                        P.act(gt2[b][:], gt[b][:], AF.Exp, r=[gt[b]], w=[gt2[b]], scale=-1.0)
                tail(len(items) - 1)

        def stage_ssd():
            with P.scope():
                xp = P.sb("xp", [128, S + 3])
                pad_init(xp)
                BT2 = [P.sb(f"BT{g}", [128, S], BF16) for g in range(2)]
                CTb2 = [P.sb(f"CTb{g}", [128, S], BF16) for g in range(2)]
                Btm2 = [P.sb(f"Btm{g}", [128, 16, 128], BF16) for g in range(2)]
                GT2 = [P.sb(f"GT{g}", [128, 16, 128]) for g in range(2)]
                tmp = P.sb("ctmp", [128, S])
                pg = [P.ps(f"pg{i}") for i in range(4)]
                py = [P.ps(f"py{i}") for i in range(4)]
                for g in range(2):
                    BT, CTb, Btm, GT = BT2[g], CTb2[g], Btm2[g], GT2[g]
                    conv_fm(3584 + g * 128, "ab_cw", "ab_cb", 8 + g, tmp, xp, True)
                    P.copy("act", BT[:], tmp[:], r=[tmp], w=[BT])
                    for c in range(16):
                        p_ = pg[c % 4]
                        P.tr(p_[:, 0:128], tmp[:, c * 128:(c + 1) * 128], C("ident"), r=[tmp, cst], w=[p_])
                        P.copy("dve", Btm[:, c, :], p_[:, 0:128], r=[p_], w=[Btm])
                    conv_fm(3840 + g * 128, "ab_cw", "ab_cb", 10 + g, tmp, xp, True)
                    P.copy("act", CTb[:], tmp[:], r=[tmp], w=[CTb])
                    for c in range(16):
                        p_ = py[c % 4]
                        csl = slice(c * 128, (c + 1) * 128)
                        P.mm(p_[:, 0:128], lhsT=BT[:, csl], rhs=CTb[:, csl], start=True, stop=True, r=[BT, CTb], w=[p_])
                        P.copy("dve", GT[:, c, :], p_[:, 0:128], r=[p_], w=[GT])
                dt = P.sb("dt", [128, 16, 32])
                av = P.sb("av", [128, 16, 32])
                Aneg = P.sb("Aneg", [128, 32])
                P.dma("sp", dt[:], tokm_d[:, :, 256:288], r=[tokm_d], w=[dt])
                P.tt("dve", dt[:], dt[:], RVc("ab_dtb", 0, 32).unsqueeze(1).to_broadcast([128, 16, 32]), ALU.add, r=[dt, rvt], w=[dt])
                P.act(dt[:], dt[:], AF.Exp, r=[dt], w=[dt])
                P.act(dt[:], dt[:], AF.Ln, r=[dt], w=[dt], bias=1.0)
                P.act(Aneg[:], RVc("ab_alog", 0, 32), AF.Exp, r=[rvt], w=[Aneg])
                P.stt("dve", av[:], dt[:], -1.0, Aneg[:].unsqueeze(1).to_broadcast([128, 16, 32]), ALU.mult, ALU.mult, r=[dt, Aneg], w=[av])

                xsT = [P.sb(f"xsT{i}", [128, S]) for i in range(2)]
                Y = [P.sb(f"Y{i}", [128, S]) for i in range(2)]
                xtm = [P.sb(f"xtm{i}", [128, 16, 128], BF16) for i in range(2)]
                xpad = [[P.sb(f"xpad{i}{h}", [128, 16, 128], BF16) for h in range(2)] for i in range(2)]
                Vall = P.sb("Vall", [128, 8, S], BF16)
                NC_ = 4
                Sm = [P.sb(f"Sm{i}", [128, 128]) for i in range(NC_)]
                Spad = [[P.sb(f"Spad{i}{h}", [128, 128], BF16) for h in range(2)] for i in range(NC_)]
                Rt_ = [P.sb(f"R{i}", [128, 2, 128]) for i in range(NC_)]
                EG = [P.sb(f"EG{i}", [128, 2, 128]) for i in range(NC_)]
                DC = [P.sb(f"DC{i}", [128, 2, 128]) for i in range(NC_)]
                kds = [P.sb(f"kds{i}", [128, 2]) for i in range(NC_)]
                AT = [P.sb(f"AT{i}", [128, 2, 128], BF16) for i in range(NC_)]
                QD = [P.sb(f"QD{i}", [128, 2, 128], BF16) for i in range(NC_)]
                KD = [P.sb(f"KD{i}", [128, 2, 128], BF16) for i in range(NC_)]
                nst = 16

                def chain(pi, hp, d, ci):
                    g_, y_ = pg[ci], py[ci]
                    CTb, Btm, GT = CTb2[hp // 4], Btm2[hp // 4], GT2[hp // 4]
                    U = C("Uf") if d == 0 else C("Ub")
                    M = C("Mf") if d == 0 else C("Mb")
                    NG = C("NEGf", 256) if d == 0 else C("NEGb", 256)
                    hcol = slice(d * 16 + hp * 2, d * 16 + hp * 2 + 2)
                    ccol = 127 if d == 0 else 0
                    P.op("dve", lambda e: e.memset(Sm[ci][:], 0.0), w=[Sm[ci]])
                    for h in range(2):
                        P.op("dve", lambda e: e.memset(Spad[ci][h][:], 0.0), w=[Spad[ci][h]])
                    for step in range(nst):
                        c = step if d == 0 else 15 - step
                        csl = slice(c * 128, (c + 1) * 128)
                        a2 = av[:, c, hcol]
                        R_ = Rt_[ci]
                        P.tt("dve", R_[:], U.unsqueeze(1).to_broadcast([128, 2, 128]), a2.unsqueeze(2).to_broadcast([128, 2, 128]),
                             ALU.mult, r=[cst, av], w=[R_])
                        Rf = R_[:].rearrange("p h i -> p (h i)")
                        P.mm(g_[:, 0:256], lhsT=C("ones"), rhs=Rf, start=True, stop=True, r=[R_, cst], w=[g_])
                        P.mm(g_[:, 256:512], lhsT=M, rhs=Rf, start=True, stop=False, r=[R_, cst], w=[g_])
                        P.mm(g_[:, 256:512], lhsT=C("ident"), rhs=NG, start=False, stop=True, r=[cst], w=[g_])
                        P.mm(y_[:, 256:258], lhsT=M, rhs=a2, start=True, stop=True, r=[av, cst], w=[y_])
                        yield
                        P.act(EG[ci][:].rearrange("p h i -> p (h i)"), g_[:, 0:256], AF.Exp, r=[g_], w=[EG[ci]])
                        P.act(DC[ci][:].rearrange("p h i -> p (h i)"), g_[:, 256:512], AF.Exp, r=[g_], w=[DC[ci]])
                        P.act(kds[ci][:], y_[:, 256:258], AF.Exp, r=[y_], w=[kds[ci]])
                        P.tt("dve", kds[ci][:], kds[ci][:], dt[:, c, hcol], ALU.mult, r=[kds[ci], dt], w=[kds[ci]])
                        for h in range(2):
                            P.stt("dve", AT[ci][:, h, :], DC[ci][:, h, :], dt[:, c, d * 16 + hp * 2 + h:d * 16 + hp * 2 + h + 1],
                                  GT[:, c, :], ALU.mult, ALU.mult, r=[DC[ci], dt, GT], w=[AT[ci]])
                        P.tt("dve", QD[ci][:], CTb[:, csl].unsqueeze(1).to_broadcast([128, 2, 128]), EG[ci][:], ALU.mult,
                             r=[CTb, EG[ci]], w=[QD[ci]])
                        P.tt("dve", KD[ci][:], Btm[:, c, :].unsqueeze(1).to_broadcast([128, 2, 128]),
                             kds[ci][:].unsqueeze(2).to_broadcast([128, 2, 128]), ALU.mult, r=[Btm, kds[ci]], w=[KD[ci]])
                        for h in range(2):
                            P.mm(y_[:, 0:128], lhsT=Spad[ci][h][:], rhs=QD[ci][:, h, :], start=(h == 0), stop=False,
                                 r=[Spad[ci][h], QD[ci]], w=[y_])
                        for h in range(2):
                            P.mm(y_[:, 0:128], lhsT=xpad[pi][h][:, c, :], rhs=AT[ci][:, h, :], start=False, stop=(h == 1),
                                 r=[xpad[pi][h], AT[ci]], w=[y_])
                        for h in range(2):
                            P.mm(y_[:, 128 + h * 64:128 + (h + 1) * 64], lhsT=KD[ci][:, h, :], rhs=xtm[pi][:, c, h * 64:(h + 1) * 64],
                                 start=True, stop=True, r=[KD[ci], xtm[pi]], w=[y_])
                        yield
                        P.tt("dve", Y[pi][:, csl], Y[pi][:, csl], y_[:, 0:128], ALU.add, r=[y_, (Y[pi], c)], w=[(Y[pi], c)])
                        for h in range(2):
                            hs = slice(h * 64, (h + 1) * 64)
                            P.stt("dve", Sm[ci][:, hs], Sm[ci][:, hs], EG[ci][:, h, ccol:ccol + 1], y_[:, 128 + h * 64:128 + (h + 1) * 64],
                                  ALU.mult, ALU.add, r=[Sm[ci], EG[ci], y_], w=[Sm[ci]])
                            P.copy("act", Spad[ci][h][:, hs], Sm[ci][:, hs], r=[Sm[ci]], w=[Spad[ci][h]])

                for rnd in range(4):
                    for pi in range(2):
                        hp = rnd * 2 + pi
                        conv_fm(2560 + hp * 128, "ab_cw", "ab_cb", hp, xsT[pi], xp, True)
                        for c in range(16):
                            p_ = pg[c % 4]
                            P.tr(p_[:, 0:128], xsT[pi][:, c * 128:(c + 1) * 128], C("ident"), r=[xsT[pi], cst], w=[p_])
                            P.copy("act", xtm[pi][:, c, :], p_[:, 0:128], r=[p_], w=[xtm[pi]])
                        P.ts("dve", Y[pi][:], xsT[pi][:], PVc("d_skip", hp), None, ALU.mult, None, r=[xsT[pi], pvt], w=[Y[pi]])
                        for h in range(2):
                            P.op("dve", lambda e: e.memset(xpad[pi][h][:], 0.0), w=[xpad[pi][h]])
                            P.copy("dve", xpad[pi][h][:, :, h * 64:(h + 1) * 64], xtm[pi][:, :, h * 64:(h + 1) * 64], r=[xtm[pi]], w=[xpad[pi][h]])
                    gens = [chain(pi, rnd * 2 + pi, d, pi * 2 + d) for pi in range(2) for d in range(2)]
                    while gens:
                        for g_ in list(gens):
                            try:
                                next(g_)
                            except StopIteration:
                                gens.remove(g_)
                    for pi in range(2):
                        hp = rnd * 2 + pi
                        P.dma("sp", tmp[:], projT_d[4128 + hp * 128:4128 + (hp + 1) * 128, :], r=[projT_d], w=[tmp])
                        P.act(tmp[:], tmp[:], AF.Silu, r=[tmp], w=[tmp])
                        P.tt("dve", Vall[:, hp, :], Y[pi][:], tmp[:], ALU.mult, r=[Y[pi], tmp], w=[(Vall, hp)])
                sqb = [P.sb(f"ssq{i}", [128, 512], BF16) for i in range(2)]
                rsb = P.sb("srs", [128, 512])
                ob = [P.sb(f"sob{i}", [128, 512], BF16) for i in range(2)]
                for tb in range(4):
                    tsl = slice(tb * 512, (tb + 1) * 512)
                    for hp in range(8):
                        P.tt("dve", sqb[hp % 2][:], Vall[:, hp, tsl], Vall[:, hp, tsl], ALU.mult, r=[(Vall, hp)], w=[sqb[hp % 2]])
                        P.mm(pg[0][:], lhsT=onesb[:], rhs=sqb[hp % 2][:], start=(hp == 0), stop=(hp == 7), r=[sqb[hp % 2], onesb], w=[pg[0]])
                    P.act(rsb[:], pg[0][:], AF.Ln, r=[pg[0]], w=[rsb], scale=1.0 / 1024, bias=EPS)
                    P.act(rsb[:], rsb[:], AF.Exp, r=[rsb], w=[rsb], scale=-0.5)
                    for hp in range(8):
                        o_ = ob[hp % 2]
                        P.stt("dve", o_[:], Vall[:, hp, tsl], PVc("ssd_norm", hp), rsb[:], ALU.mult, ALU.mult, r=[(Vall, hp), rsb, pvt], w=[o_])
                        P.dma("pool", yT_d[1024 + hp * 128:1024 + (hp + 1) * 128, tsl], o_[:], r=[o_], w=[(yT_d, 8 + hp)])

        def stage_gdn():
            GP = "dve"
            with P.scope():
                xp = P.sb("gxp", [128, S + 3])
                pad_init(xp)
                tmp = P.sb("gtmp", [128, S])
                gt = P.sb("ggt", [128, 16, 32])
                P.dma("sp", gt[:], tokm_d[:, :, 0:32], r=[tokm_d], w=[gt])
                beta = P.sb("gbeta", [128, 16, 16])
                nbeta = P.sb("gnbeta", [128, 16, 16])
                gg = P.sb("ggg", [128, 16, 16])
                An = P.sb("gAn", [128, 16])
                P.act(beta[:], gt[:, :, 0:16], AF.Exp, r=[gt], w=[beta], scale=-1.0)
                P.ts("dve", beta[:], beta[:], 1.0, None, ALU.add, None, r=[beta], w=[beta])
                P.op("dve", lambda e: e.reciprocal(out=beta[:], in_=beta[:]), r=[beta], w=[beta])
                P.ts("dve", nbeta[:], beta[:], -1.0, None, ALU.mult, None, r=[beta], w=[nbeta])
                P.tt("dve", gg[:], gt[:, :, 16:32], RVc("cd_dtb", 0, 16).unsqueeze(1).to_broadcast([128, 16, 16]), ALU.add, r=[gt, rvt], w=[gg])
                P.act(gg[:], gg[:], AF.Exp, r=[gg], w=[gg])
                P.act(gg[:], gg[:], AF.Ln, r=[gg], w=[gg], bias=1.0)
                P.act(An[:], RVc("cd_alog", 0, 16), AF.Exp, r=[rvt], w=[An])
                P.stt("dve", gg[:], gg[:], -1.0, An[:].unsqueeze(1).to_broadcast([128, 16, 16]), ALU.mult, ALU.mult, r=[gg, An], w=[gg])

                qT = P.sb("gqT", [128, S])
                kT = P.sb("gkT", [128, S])
                ktm = P.sb("gktm", [128, 16, 128])
                vtm = P.sb("gvtm", [128, 16, 256])
                O = P.sb("gO", [128, 2, S])
                sq = [P.sb(f"gsq{i}", [128, 512]) for i in range(2)]
                rsb = [P.sb(f"grs{i}", [128, 512]) for i in range(2)]
                NS = 4
                RR = [P.sb(f"gRR{i}", [128, 256]) for i in range(NS)]
                E = [P.sb(f"gE{i}", [128, 388]) for i in range(NS)]
                X = [[P.sb(f"gX{i}_{j}", [128, 128]) for j in range(2)] for i in range(NS)]
                XT = [[P.sb(f"gXT{i}_{j}", [128, 128]) for j in range(2)] for i in range(NS)]
                PT = [[P.sb(f"gPT{i}_{j}", [128, 128]) for j in range(2)] for i in range(NS)]
                AT = [P.sb(f"gAT{i}", [128, 128]) for i in range(NS)]
                vb = [P.sb(f"gvb{i}", [128, 128]) for i in range(NS)]
                kbg = [P.sb(f"gkbg{i}", [128, 128]) for i in range(NS)]
                bge = [P.sb(f"gbge{i}", [128, 1]) for i in range(NS)]
                u_ = [P.sb(f"gu{i}", [128, 128]) for i in range(NS)]
                wT = [P.sb(f"gwT{i}", [128, 128]) for i in range(NS)]
                qd = [P.sb(f"gqd{i}", [128, 128]) for i in range(NS)]
                kdec = [P.sb(f"gkd{i}", [128, 128]) for i in range(NS)]
                vnew = [P.sb(f"gvn{i}", [128, 128]) for i in range(NS)]
                Sst = [P.sb(f"gS{d}", [128, 128]) for d in range(4)]
                ybf = [P.sb(f"gyb{i}", [128, 512], BF16) for i in range(2)]
                pX = [P.ps(f"gpX{i}") for i in range(4)]
                pY = [P.ps(f"gpY{i}") for i in range(4)]
                pA = pX
                qscale = 128.0 ** -0.5
                nhq = 4 if lim is None else lim[0]
                nst = 16 if lim is None else lim[1]
                cut = 0 if (lim is None or len(lim) < 3) else lim[2]

                def l2norm_fm(dst, scl):
                    for tb in range(4):
                        tsl = slice(tb * 512, (tb + 1) * 512)
                        b = tb % 2
                        P.act(sq[b][:], tmp[:, tsl], AF.Square, r=[tmp], w=[sq[b]])
                        P.mm(pA[b][:], lhsT=C("ones"), rhs=sq[b][:], start=True, stop=True, r=[sq[b], cst], w=[pA[b]])
                        P.act(rsb[b][:], pA[b][:], AF.Ln, r=[pA[b]], w=[rsb[b]], bias=EPS)
                        P.act(rsb[b][:], rsb[b][:], AF.Exp, r=[rsb[b]], w=[rsb[b]], scale=-0.5)
                        P.stt("dve", dst[:, tsl], tmp[:, tsl], scl, rsb[b][:], ALU.mult, ALU.mult, r=[tmp, rsb[b]], w=[dst])

                for hq in range(nhq):
                    conv_fm(hq * 128, "cd_cw", "cd_cb", hq, tmp, xp, True)
                    l2norm_fm(qT, qscale)
                    conv_fm(512 + hq * 128, "cd_cw", "cd_cb", 4 + hq, tmp, xp, True)
                    l2norm_fm(kT, 1.0)
                    for c in range(16):
                        p_ = pA[c % 2]
                        P.tr(p_[:, 0:128], kT[:, c * 128:(c + 1) * 128], C("ident"), r=[kT, cst], w=[p_])
                        P.copy("act", ktm[:, c, :], p_[:, 0:128], r=[p_], w=[ktm])
                    for e in range(2):
                        conv_fm(1024 + (2 * hq + e) * 128, "cd_cw", "cd_cb", 8 + 2 * hq + e, tmp, xp, True)
                        for c in range(16):
                            p_ = pA[c % 2]
                            P.tr(p_[:, 0:128], tmp[:, c * 128:(c + 1) * 128], C("ident"), r=[tmp, cst], w=[p_])
                            P.copy("act", vtm[:, c, e * 128:(e + 1) * 128], p_[:, 0:128], r=[p_], w=[vtm])
                    P.op("dve", lambda e_: e_.memset(O[:], 0.0), w=[O])
                    def chain(e, d, ci):
                        hv = 2 * hq + e
                        S_ = Sst[ci]
                        a_ = pX[ci]
                        c_ = pY[ci]
                        bi = ci
                        UM = C("Uf", 256) if d == 0 else C("Ub", 256)
                        U, M = UM[:, 0:128], UM[:, 128:256]
                        NGi = C("NEGf") if d == 0 else C("NEGb")
                        NGs = C("NEGsf") if d == 0 else C("NEGsb")
                        col = d * 8 + hv
                        ccol = 127 if d == 0 else 0
                        P.op("dve", lambda e_: e_.memset(S_[:], 0.0), w=[S_])
                        for step in range(nst):
                            c = step if d == 0 else 15 - step
                            csl = slice(c * 128, (c + 1) * 128)
                            gcol = gg[:, c, col:col + 1]
                            bcol = beta[:, c, col:col + 1]
                            nbcol = nbeta[:, c, col:col + 1]
                            P.ts("pool", RR[bi][:], UM, gcol, None, ALU.mult, None, r=[cst, gg], w=[RR[bi]])
                            R1, R2 = RR[bi][:, 0:128], RR[bi][:, 128:256]
                            P.mm(a_[:, 0:128], lhsT=C("ones"), rhs=R1, start=True, stop=True, r=[RR[bi], cst], w=[a_])
                            P.mm(a_[:, 128:256], lhsT=M, rhs=R1, start=True, stop=False, r=[RR[bi], cst], w=[a_])
                            P.mm(a_[:, 128:256], lhsT=C("ident"), rhs=NGi, start=False, stop=True, r=[cst], w=[a_])
                            P.mm(a_[:, 256:384], lhsT=U, rhs=R2, start=True, stop=False, r=[RR[bi], cst], w=[a_])
                            P.mm(a_[:, 256:384], lhsT=C("ident"), rhs=NGs, start=False, stop=True, r=[cst], w=[a_])
                            P.mm(a_[:, 384:385], lhsT=M, rhs=gcol, start=True, stop=True, r=[gg, cst], w=[a_])
                            P.mm(a_[:, 385:386], lhsT=U, rhs=gcol, start=True, stop=True, r=[gg, cst], w=[a_])
                            yield
                            E_ = E[bi]
                            P.act(E_[:, 0:386], a_[:, 0:386], AF.Exp, r=[a_], w=[E_])
                            EGb, decT, decS = E_[:, 0:128], E_[:, 128:256], E_[:, 256:384]
                            kds, eg = E_[:, 384:385], E_[:, 385:386]
                            P.mm(c_[:, 0:128], lhsT=kT[:, csl], rhs=kT[:, csl], start=True, stop=True, r=[kT], w=[c_])
                            P.mm(c_[:, 128:256], lhsT=kT[:, csl], rhs=qT[:, csl], start=True, stop=True, r=[kT, qT], w=[c_])
                            yield
                            X0 = X[bi][0]
                            P.stt("dve", X0[:], c_[:, 0:128], nbcol, decS, ALU.mult, ALU.mult, r=[c_, nbeta, E_], w=[X0])
                            P.tt("dve", AT[bi][:], c_[:, 128:256], decT, ALU.mult, r=[c_, E_], w=[AT[bi]])
                            P.act(vb[bi][:], vtm[:, c, e * 128:(e + 1) * 128], AF.Copy, r=[vtm, beta], w=[vb[bi]], scale=bcol)
                            P.tt("dve", bge[bi][:], bcol, eg, ALU.mult, r=[beta, E_], w=[bge[bi]])
                            P.act(kbg[bi][:], ktm[:, c, :], AF.Copy, r=[ktm, bge[bi]], w=[kbg[bi]], scale=bge[bi][:])
                            P.tt("dve", qd[bi][:], qT[:, csl], EGb, ALU.mult, r=[qT, E_], w=[qd[bi]])
                            P.act(kdec[bi][:], ktm[:, c, :], AF.Copy, r=[ktm, E_], w=[kdec[bi]], scale=kds)
                            P.tr(a_[:, 0:128], X0[:], C("ident"), r=[X0, cst], w=[a_])
                            yield
                            P.copy("act", XT[bi][0][:], a_[:, 0:128], r=[a_], w=[XT[bi][0]])
                            P.tt("dve", PT[bi][0][:], a_[:, 0:128], C("ident"), ALU.add, r=[a_, cst], w=[PT[bi][0]])
                            for k in range(1, 7):
                                xo, xn = X[bi][(k - 1) % 2], X[bi][k % 2]
                                to, tn = XT[bi][(k - 1) % 2], XT[bi][k % 2]
                                po_, pn = PT[bi][(k - 1) % 2], PT[bi][k % 2]
                                P.mm(c_[:, 128:256], lhsT=to[:], rhs=xo[:], start=True, stop=True, r=[to, xo], w=[c_])
                                if k < 6:
                                    P.mm(c_[:, 0:128], lhsT=xo[:], rhs=to[:], start=True, stop=True, r=[to, xo], w=[c_])
                                yield
                                P.copy("act", xn[:], c_[:, 128:256], r=[c_], w=[xn])
                                if k < 6:
                                    P.copy("act", tn[:], c_[:, 0:128], r=[c_], w=[tn])
                                P.mm(a_[:, 256:384], lhsT=xn[:], rhs=po_[:], start=True, stop=True, r=[xn, po_], w=[a_])
                                yield
                                P.tt("dve", pn[:], a_[:, 256:384], po_[:], ALU.add, r=[a_, po_], w=[pn])
                            TT = PT[bi][0]
                            P.mm(c_[:, 256:384], lhsT=TT[:], rhs=vb[bi][:], start=True, stop=True, r=[TT, vb[bi]], w=[c_])
                            P.mm(c_[:, 384:512], lhsT=kbg[bi][:], rhs=TT[:], start=True, stop=True, r=[TT, kbg[bi]], w=[c_])
                            yield
                            P.copy("act", u_[bi][:], c_[:, 256:384], r=[c_], w=[u_[bi]])
                            P.copy("act", wT[bi][:], c_[:, 384:512], r=[c_], w=[wT[bi]])
                            P.mm(a_[:, 0:128], lhsT=wT[bi][:], rhs=S_[:], start=True, stop=True, r=[wT[bi], S_], w=[a_])
                            yield
                            P.tt("dve", vnew[bi][:], u_[bi][:], a_[:, 0:128], ALU.subtract, r=[u_[bi], a_], w=[vnew[bi]])
                            P.mm(c_[:, 0:128], lhsT=S_[:], rhs=qd[bi][:], start=True, stop=False, r=[S_, qd[bi]], w=[c_])
                            P.mm(c_[:, 0:128], lhsT=vnew[bi][:], rhs=AT[bi][:], start=False, stop=True, r=[vnew[bi], AT[bi]], w=[c_])
                            P.mm(c_[:, 128:256], lhsT=kdec[bi][:], rhs=vnew[bi][:], start=True, stop=True, r=[kdec[bi], vnew[bi]], w=[c_])
                            yield
                            P.tt("dve", O[:, e, csl], O[:, e, csl], c_[:, 0:128], ALU.add, r=[c_, (O, e * 16 + c)], w=[(O, e * 16 + c)])
                            P.stt("dve", S_[:], S_[:], EGb[:, ccol:ccol + 1], c_[:, 128:256], ALU.mult, ALU.add, r=[S_, E_, c_], w=[S_])

                    gens = [chain(e, d, e * 2 + d) for e in range(2) for d in range(2)]
                    while gens:
                        for g_ in list(gens):
                            try:
                                next(g_)
                            except StopIteration:
                                gens.remove(g_)
                    for e in range(2):
                        hv = 2 * hq + e
                        P.dma("sp", tmp[:], projT_d[2080 + hv * 128:2080 + (hv + 1) * 128, :], r=[projT_d], w=[tmp])
                        P.act(tmp[:], tmp[:], AF.Silu, r=[tmp], w=[tmp])
                        for tb in range(4):
                            tsl = slice(tb * 512, (tb + 1) * 512)
                            b = tb % 2
                            P.act(sq[b][:], O[:, e, tsl], AF.Square, r=[O], w=[sq[b]])
                            P.mm(pA[b][:], lhsT=C("ones"), rhs=sq[b][:], start=True, stop=True, r=[sq[b], cst], w=[pA[b]])
                            P.act(rsb[b][:], pA[b][:], AF.Ln, r=[pA[b]], w=[rsb[b]], scale=1.0 / 128, bias=EPS)
                            P.act(rsb[b][:], rsb[b][:], AF.Exp, r=[rsb[b]], w=[rsb[b]], scale=-0.5)
                            P.stt("dve", sq[b][:], O[:, e, tsl], PVc("gdn_norm"), rsb[b][:], ALU.mult, ALU.mult, r=[O, rsb[b], pvt], w=[sq[b]])
                            P.tt("dve", ybf[b][:], sq[b][:], tmp[:, tsl], ALU.mult, r=[sq[b], tmp], w=[ybf[b]])
                            P.dma("pool", yT_d[hv * 128:(hv + 1) * 128, tsl], ybf[b][:], r=[ybf[b]], w=[(yT_d, hv)])

        def stage_lru():
            with P.scope():
                xp = P.sb("lxp", [128, S + 3])
                pad_init(xp)
                xc = P.sb("lxc", [128, S])
                wl = P.sb("lw", [128, 4, 8, 128])
                for m_ in range(4):
                    P.dma("sp", wl[:, m_, :, :], lruw_d.h.ap()[m_].rearrange("n i j -> i n j"), w=[(wl, m_)])
                nsp = P.sb("lnsp", [128, 16])
                for d, nm in enumerate(("lam_f", "lam_b")):
                    P.act(nsp[:, d * 8:(d + 1) * 8], PVc(nm, 0, 8), AF.Exp, r=[pvt], w=[(nsp, d)], scale=-1.0)
                    P.act(nsp[:, d * 8:(d + 1) * 8], nsp[:, d * 8:(d + 1) * 8], AF.Ln, r=[(nsp, d)], w=[(nsp, d)], bias=1.0)
                    P.ts("dve", nsp[:, d * 8:(d + 1) * 8], nsp[:, d * 8:(d + 1) * 8], -8.0, None, ALU.mult, None, r=[(nsp, d)], w=[(nsp, d)])
                rr = P.sb("lrr", [128, S])
                ig = P.sb("lig", [128, S])
                aa = P.sb("laa", [128, S])
                mm_ = P.sb("lmm", [128, S])
                hh = [P.sb(f"lhh{d}", [128, S]) for d in range(2)]
                gl = P.sb("lgl", [128, S])
                yb = P.sb("lyb", [128, S], BF16)
                pp = [P.ps(f"lp{i}") for i in range(4)]

                def rev(t):
                    return bass.AP(t.h, S - 1, [[S, 128], [-1, S]])

                for n in range(8 if lim is None else lim[0]):
                    conv_fm(3104 + n * 128, "lru_cw", "lru_cb", n, xc, xp, False)
                    P.dma("sp", gl[:], projT_d[4128 + n * 128:4128 + (n + 1) * 128, :], r=[projT_d], w=[gl])
                    for d in range(2):
                        sfx = "f" if d == 0 else "b"
                        for tb in range(4):
                            tsl = slice(tb * 512, (tb + 1) * 512)
                            p1, p2 = pp[(tb % 2) * 2], pp[(tb % 2) * 2 + 1]
                            P.mm(p1[:], lhsT=wl[:, 2 * d, n, :], rhs=xc[:, tsl], start=True, stop=True, r=[(wl, 2 * d), xc], w=[p1])
                            P.mm(p2[:], lhsT=wl[:, 2 * d + 1, n, :], rhs=xc[:, tsl], start=True, stop=True, r=[(wl, 2 * d + 1), xc], w=[p2])
                            P.act(rr[:, tsl], p1[:], AF.Sigmoid, r=[p1, pvt], w=[(rr, tb)], bias=PVc("ba_" + sfx, n))
                            P.act(ig[:, tsl], p2[:], AF.Sigmoid, r=[p2, pvt], w=[(ig, tb)], bias=PVc("bx_" + sfx, n))
                        P.act(aa[:], rr[:], AF.Exp, r=[rr, nsp], w=[aa], scale=nsp[:, d * 8 + n:d * 8 + n + 1])
                        P.tt("pool", mm_[:], aa[:], aa[:], ALU.mult, r=[aa], w=[mm_])
                        P.act(mm_[:], mm_[:], AF.Ln, r=[mm_], w=[mm_], scale=-1.0, bias=1.0)
                        P.act(mm_[:], mm_[:], AF.Exp, r=[mm_], w=[mm_], scale=0.5)
                        P.tt("pool", ig[:], ig[:], xc[:], ALU.mult, r=[ig, xc], w=[ig])
                        P.tt("dve", mm_[:], mm_[:], ig[:], ALU.mult, r=[mm_, ig], w=[mm_])
                        if d == 0:
                            P.op("dve", lambda e: e.tensor_tensor_scan(out=hh[0][:], data0=aa[:], data1=mm_[:], initial=0.0,
                                                                       op0=ALU.mult, op1=ALU.add), r=[aa, mm_], w=[hh[0]])
                        else:
                            P.op("dve", lambda e: e.tensor_tensor_scan(out=rev(hh[1]), data0=rev(aa), data1=rev(mm_), initial=0.0,
                                                                       op0=ALU.mult, op1=ALU.add), r=[aa, mm_], w=[hh[1]])
                    P.act(gl[:], gl[:], AF.Silu, r=[gl], w=[gl])
                    P.tt("pool", hh[0][:], hh[0][:], hh[1][:], ALU.add, r=[hh[0], hh[1]], w=[hh[0]])
                    P.tt("dve", yb[:], hh[0][:], gl[:], ALU.mult, r=[hh[0], gl], w=[yb])
                    P.dma("pool", yT_d[1024 + n * 128:1024 + (n + 1) * 128, :], yb[:], r=[yb], w=[(yT_d, 8 + n)])

        if only is not None:
            {"ssd": stage_ssd, "attn": stage_attn, "gdn": stage_gdn, "lru": stage_lru}[only]()
            P.barrier()
            return nc, dbg
        stage_mod()
        if debug:
            md = scratch("dbg_mod", [128, 96])
            P.dma("pool", md[:], modsb[:].rearrange("p l e -> p (l e)"), r=[modsb], w=[md])
        with P.scope():
            hT = P.sb("hT", [128, 16, S], BF16)
            stage_norm(xT_d, lambda k: sc1[:, 0, k:k + 1], lambda k: modsb[:, 0, k:k + 1], out_tile=hT)
            if debug:
                hd = scratch("dbg_h", [128, 16, S], BF16)
                P.dma("pool", hd[:], hT[:], r=[hT], w=[hd])
            fm = [(c * 128, 128, c * 128) for c in range(32) if not (10 <= c < 12)]
            fm += [(4128 + c * 128, 128, 4128 + c * 128) for c in range(8)]
            stage_inproj(hT, w_in_d[0], fm, [(1280, 256, 0), (4096, 32, 256)])
        if upto == "proj0":
            return nc, dbg
        if upto != "ssd":
            stage_attn()
        if upto == "attn":
            return nc, dbg
        stage_ssd()
        if upto == "ssd":
            return nc, dbg
        stage_outproj(w_out_d[0], xT_d, x1T_d, 0)
        if upto == "l0":
            return nc, dbg
        with P.scope():
            hT = P.sb("hT1", [128, 16, S], BF16)
            stage_norm(x1T_d, lambda k: sc1[:, 1, k:k + 1], lambda k: modsb[:, 1, k:k + 1], out_tile=hT)
            fm = [(c * 128, 128, c * 128) for c in range(16)]
            fm += [(2080 + c * 128, 128, 2080 + c * 128) for c in range(24)]
            stage_inproj(hT, w_in_d[1], fm, [(2048, 32, 0)])
        stage_gdn()
        stage_lru()
        stage_outproj(w_out_d[1], x1T_d, x2T_d, 1)
        stage_norm(x2T_d, lambda k: PVc("fnorm", k), lambda k: 0.0, out_dram=out_d)
        P.barrier()
    return nc, dbg


def make_inputs(inp, b):
    pv, rv = pack_params(inp)
    m = {
        "xT": np.ascontiguousarray(inp["x"][b].T),
        "cT": np.ascontiguousarray(inp["c"][b].reshape(16, 128).T),
        "w_mod": np.ascontiguousarray(inp["w_mod"]),
        "ab_w_in": np.ascontiguousarray(inp["ab_w_in"][0]),
        "cd_w_in": np.ascontiguousarray(inp["cd_w_in"][0]),
        "ab_w_out": np.ascontiguousarray(inp["ab_w_out"][0]),
        "cd_w_out": np.ascontiguousarray(inp["cd_w_out"][0]),
        "lru_w": np.ascontiguousarray(np.stack([inp["cd_lru_wa_f"][0], inp["cd_lru_wx_f"][0], inp["cd_lru_wa_b"][0], inp["cd_lru_wx_b"][0]], 0)),
        "consts": CONSTS,
        "rope": _rope_tables(),
        "pvec": pv,
        "rvec": rv,
    }
    return m


def kernel(**inputs):
    inp = {k: np.asarray(v, np.float32) for k, v in inputs.items()}
    maps = [make_inputs(inp, i // 2) for i in range(8)]
    nc, _ = build(maps[0]["pvec"].shape[1], maps[0]["rvec"].shape[1])
    res = run_bass_kernel_spmd(nc, maps, core_ids=list(range(8)))
    out = np.stack([np.asarray(res.results[2 * b]["outT"], np.float32).T for b in range(4)], 0)
    return np.ascontiguousarray(out)
```
